# Optimizing a Trainium2 kernel written in Bass

```python
import math
import jax, jax.numpy as jnp
from jax import lax
import numpy as np

D_MODEL = 1024
BATCH = 16
SEQ = 2048
DEPTH = 2

D_MIX = 1024
HEAD_DIM = 64
ATT_HEADS = 6
ATT_WIDTH = ATT_HEADS * HEAD_DIM
DILATED_PAIRS = ((128, 1), (512, 4), (2048, 16))
ATT_BLOCK = 128
SSD_HEADS = 6
SSD_HEADDIM = 64
SSD_WIDTH = SSD_HEADS * SSD_HEADDIM
SSD_GROUPS = 2
SSD_STATE = 128
SSD_CONV = 4
SSD_CHUNK = 128
SSD_CONV_DIM = SSD_WIDTH + 2 * SSD_GROUPS * SSD_STATE
SGU_GROUPS = 4
SGU_GROUP_DIM = 64
SGU_WIDTH = SGU_GROUPS * SGU_GROUP_DIM
SGU_CHUNK = 128
D_FF = 2816
D_IN = 3 * ATT_WIDTH + SSD_WIDTH + SSD_CONV_DIM + SSD_HEADS + 2 * SGU_WIDTH
RMS_EPS = 1e-6
LN_EPS = 1e-5

kernel_name = 'hybrid_dilated_ssd_sgu_macaron'


def _rmsnorm(x, g):
    xf = x.astype(jnp.float32)
    y = xf * lax.rsqrt(jnp.mean(xf * xf, axis=-1, keepdims=True) + RMS_EPS)
    return (y * g.astype(jnp.float32)).astype(x.dtype)


def _swiglu(x, w_gate, w_up, w_down):
    return (jax.nn.silu(x @ w_gate) * (x @ w_up)) @ w_down


def _to_blocks(t, dil, n_blocks):
    b, s, h, e = t.shape
    L = s // dil
    t = t.reshape(b, L, dil, h, e).transpose(0, 2, 1, 3, 4)
    t = jnp.pad(t, ((0, 0), (0, 0), (0, n_blocks * ATT_BLOCK - L), (0, 0), (0, 0)))
    return t.reshape(b, dil, n_blocks, ATT_BLOCK, h, e)


def _with_prev_block(t):
    prev = jnp.pad(t, ((0, 0), (0, 0), (1, 0), (0, 0), (0, 0), (0, 0)))[:, :, :-1]
    return jnp.concatenate([prev, t], axis=3)


def _dilated_branch(q, k, v, window, dil):
    b, s, h, e = q.shape
    L = s // dil
    nb = -(-L // ATT_BLOCK)
    span = window // dil
    qb = _to_blocks(q, dil, nb)
    kk = _with_prev_block(_to_blocks(k, dil, nb))
    vv = _with_prev_block(_to_blocks(v, dil, nb))
    scores = jnp.einsum('bdnqhe,bdnkhe->bdnhqk', qb, kk,
                        preferred_element_type=jnp.float32) * (e ** -0.5)
    qi = jnp.arange(ATT_BLOCK)[:, None]
    kj = jnp.arange(2 * ATT_BLOCK)[None, :]
    dist = qi + ATT_BLOCK - kj
    band = (dist >= 0) & (dist <= span)
    valid_key = (jnp.arange(nb)[:, None, None] > 0) | (kj[None] >= ATT_BLOCK)
    mask = (band[None] & valid_key)[:, None]
    scores = jnp.where(mask, scores, -jnp.inf)
    m = jnp.max(scores, axis=-1, keepdims=True)
    p = jnp.exp(scores - m)
    den = jnp.sum(p, axis=-1)
    o = jnp.einsum('bdnhqk,bdnkhe->bdnqhe', p.astype(v.dtype), vv)
    o = o / den.transpose(0, 1, 2, 4, 3)[..., None]
    lse = (m[..., 0] + jnp.log(den)).transpose(0, 1, 2, 4, 3)
    o = o.reshape(b, dil, nb * ATT_BLOCK, h, e)[:, :, :L].transpose(0, 2, 1, 3, 4).reshape(b, s, h, e)
    lse = lse.reshape(b, dil, nb * ATT_BLOCK, h)[:, :, :L].transpose(0, 2, 1, 3).reshape(b, s, h)
    return o, lse


def _dilated_attention(q, k, v):
    outs, lses = [], []
    for window, dil in DILATED_PAIRS:
        o, lse = _dilated_branch(q, k, v, window, dil)
        outs.append(o)
        lses.append(lse)
    w = jax.nn.softmax(jnp.stack(lses, axis=0), axis=0)
    out = w[0][..., None] * outs[0]
    for i in range(1, len(outs)):
        out = out + w[i][..., None] * outs[i]
    return out.astype(q.dtype)


def _ssd_mixer(z, xbc, dt_raw, conv_w, conv_b, dt_bias, a_log, d_skip, norm_g):
    b, s, _ = xbc.shape
    G, J, P, N = SSD_GROUPS, SSD_HEADS // SSD_GROUPS, SSD_HEADDIM, SSD_STATE
    c, l = s // SSD_CHUNK, SSD_CHUNK
    xbc = lax.conv_general_dilated(xbc, conv_w[:, None, :].astype(xbc.dtype), window_strides=(1,),
                                   padding=[(SSD_CONV - 1, 0)],
                                   dimension_numbers=('NWC', 'WIO', 'NWC'),
                                   feature_group_count=SSD_CONV_DIM)
    xbc = jax.nn.silu(xbc + conv_b)
    xs, bm, cm = jnp.split(xbc, [SSD_WIDTH, SSD_WIDTH + G * N], axis=-1)
    dt = jax.nn.softplus(dt_raw.astype(jnp.float32) + dt_bias.astype(jnp.float32))
    a = dt * (-jnp.exp(a_log.astype(jnp.float32)))
    xs_h = xs.astype(jnp.float32).reshape(b, s, SSD_HEADS, P)
    X = (xs_h * dt[..., None]).reshape(b, c, l, G, J, P)
    Bc = bm.astype(jnp.float32).reshape(b, c, l, G, N)
    Cc = cm.astype(jnp.float32).reshape(b, c, l, G, N)
    a_cs = jnp.cumsum(a.reshape(b, c, l, G, J), axis=2)
    causal = jnp.tril(jnp.ones((l, l), dtype=bool))[:, :, None, None]
    seg = a_cs[:, :, :, None] - a_cs[:, :, None, :]
    decay_in = jnp.exp(jnp.where(causal, seg, -jnp.inf))
    cb = jnp.einsum('bclgn,bcsgn->bclsg', Cc, Bc)
    y_diag = jnp.einsum('bclsg,bclsgj,bcsgjp->bclgjp', cb, decay_in, X)
    decay_to_end = jnp.exp(a_cs[:, :, -1:] - a_cs)
    chunk_states = jnp.einsum('bclgn,bclgj,bclgjp->bcgjpn', Bc, decay_to_end, X)
    chunk_decay = jnp.exp(a_cs[:, :, -1])

    def step(h, inp):
        st, dec = inp
        return h * dec[..., None, None] + st, h

    h0 = jnp.zeros((b, G, J, P, N), jnp.float32)
    _, prev = lax.scan(step, h0, (chunk_states.transpose(1, 0, 2, 3, 4, 5),
                                  chunk_decay.transpose(1, 0, 2, 3)))
    prev = prev.transpose(1, 0, 2, 3, 4, 5)
    y_off = jnp.einsum('bclgn,bcgjpn,bclgj->bclgjp', Cc, prev, jnp.exp(a_cs))
    y = (y_diag + y_off).reshape(b, s, SSD_HEADS, P) + d_skip.astype(jnp.float32)[:, None] * xs_h
    y = y.reshape(b, s, G, J * P) * jax.nn.silu(z.astype(jnp.float32)).reshape(b, s, G, J * P)
    y = y * lax.rsqrt(jnp.mean(y * y, axis=-1, keepdims=True) + RMS_EPS)
    y = y.reshape(b, s, SSD_WIDTH) * norm_g.astype(jnp.float32)
    return y.astype(z.dtype)


def _sgu_mixer(uv, ln_g, ln_b, w_s, b_s):
    b, s, _ = uv.shape
    uv = jax.nn.gelu(uv, approximate=False)
    u, v = jnp.split(uv, 2, axis=-1)
    vf = v.astype(jnp.float32)
    mu = jnp.mean(vf, axis=-1, keepdims=True)
    var = jnp.mean(jnp.square(vf - mu), axis=-1, keepdims=True)
    vn = (vf - mu) * lax.rsqrt(var + LN_EPS) * ln_g.astype(jnp.float32) + ln_b.astype(jnp.float32)
    vc = vn.reshape(b, s // SGU_CHUNK, SGU_CHUNK, SGU_GROUPS, SGU_GROUP_DIM)
    tri = jnp.tril(jnp.ones((SGU_CHUNK, SGU_CHUNK), dtype=bool))[None]
    w_causal = jnp.where(tri, w_s.astype(jnp.float32), 0.0)
    mixed = jnp.einsum('gts,bnsgc->bntgc', w_causal, vc) + b_s.astype(jnp.float32).T[:, :, None]
    return (u.astype(jnp.float32) * mixed.reshape(b, s, SGU_WIDTH)).astype(uv.dtype)


def setup_inputs(seed: int = 0) -> dict:
    key = jax.random.key(seed)
    ks = jax.random.split(key, 32)
    f32 = jnp.float32
    nrm = lambda k, shape, scale: jax.random.normal(k, shape, f32) * scale
    gain = lambda k, shape: jnp.ones(shape, f32) + 0.02 * jax.random.normal(k, shape, f32)
    dt0 = jnp.exp(jax.random.uniform(ks[9], (DEPTH, SSD_HEADS), f32, math.log(1e-3), math.log(1e-1)))
    return {
        'x': jax.random.normal(ks[0], (BATCH, SEQ, D_MODEL), f32),
        'ffn1_norm': gain(ks[1], (DEPTH, D_MODEL)),
        'ffn1_w_gate': nrm(ks[2], (DEPTH, D_MODEL, D_FF), D_MODEL ** -0.5),
        'ffn1_w_up': nrm(ks[3], (DEPTH, D_MODEL, D_FF), D_MODEL ** -0.5),
        'ffn1_w_down': nrm(ks[4], (DEPTH, D_FF, D_MODEL), D_FF ** -0.5),
        'mix_norm': gain(ks[5], (DEPTH, D_MODEL)),
        'w_in': nrm(ks[6], (DEPTH, D_MODEL, D_IN), D_MODEL ** -0.5),
        'conv_w': nrm(ks[7], (DEPTH, SSD_CONV, SSD_CONV_DIM), SSD_CONV ** -0.5),
        'conv_b': nrm(ks[8], (DEPTH, SSD_CONV_DIM), 0.02),
        'dt_bias': dt0 + jnp.log(-jnp.expm1(-dt0)),
        'a_log': jnp.log(jax.random.uniform(ks[10], (DEPTH, SSD_HEADS), f32, 1.0, 16.0)),
        'd_skip': jnp.ones((DEPTH, SSD_HEADS), f32) + 0.1 * jax.random.normal(ks[11], (DEPTH, SSD_HEADS), f32),
        'ssd_norm': gain(ks[12], (DEPTH, SSD_WIDTH)),
        'sgu_ln_g': gain(ks[13], (DEPTH, SGU_WIDTH)),
        'sgu_ln_b': nrm(ks[14], (DEPTH, SGU_WIDTH), 0.02),
        'sgu_w': nrm(ks[15], (DEPTH, SGU_GROUPS, SGU_CHUNK, SGU_CHUNK), SGU_CHUNK ** -0.5),
        'sgu_b': jnp.ones((DEPTH, SGU_GROUPS, SGU_CHUNK), f32) + 0.1 * jax.random.normal(ks[16], (DEPTH, SGU_GROUPS, SGU_CHUNK), f32),
        'w_out': nrm(ks[17], (DEPTH, D_MIX, D_MODEL), D_MIX ** -0.5),
        'ffn2_norm': gain(ks[18], (DEPTH, D_MODEL)),
        'ffn2_w_gate': nrm(ks[19], (DEPTH, D_MODEL, D_FF), D_MODEL ** -0.5),
        'ffn2_w_up': nrm(ks[20], (DEPTH, D_MODEL, D_FF), D_MODEL ** -0.5),
        'ffn2_w_down': nrm(ks[21], (DEPTH, D_FF, D_MODEL), D_FF ** -0.5),
        'final_norm': gain(ks[22], (D_MODEL,)),
    }


def reference(x, ffn1_norm, ffn1_w_gate, ffn1_w_up, ffn1_w_down, mix_norm, w_in, conv_w, conv_b,
              dt_bias, a_log, d_skip, ssd_norm, sgu_ln_g, sgu_ln_b, sgu_w, sgu_b, w_out,
              ffn2_norm, ffn2_w_gate, ffn2_w_up, ffn2_w_down, final_norm):
    b, s, _ = x.shape
    widths = [ATT_WIDTH, ATT_WIDTH, ATT_WIDTH, SSD_WIDTH, SSD_CONV_DIM, SSD_HEADS]
    offsets = []
    acc = 0
    for wdt in widths:
        acc += wdt
        offsets.append(acc)
    for i in range(DEPTH):
        x = x + 0.5 * _swiglu(_rmsnorm(x, ffn1_norm[i]), ffn1_w_gate[i], ffn1_w_up[i], ffn1_w_down[i])
        h = _rmsnorm(x, mix_norm[i])
        proj = h @ w_in[i]
        q, k, v, z, xbc, dt_raw, uv = jnp.split(proj, offsets, axis=-1)
        hs = (b, s, ATT_HEADS, HEAD_DIM)
        y_att = _dilated_attention(q.reshape(hs), k.reshape(hs), v.reshape(hs)).reshape(b, s, ATT_WIDTH)
        y_ssd = _ssd_mixer(z, xbc, dt_raw, conv_w[i], conv_b[i], dt_bias[i], a_log[i], d_skip[i], ssd_norm[i])
        y_sgu = _sgu_mixer(uv, sgu_ln_g[i], sgu_ln_b[i], sgu_w[i], sgu_b[i])
        x = x + jnp.concatenate([y_att, y_ssd, y_sgu], axis=-1) @ w_out[i]
        x = x + 0.5 * _swiglu(_rmsnorm(x, ffn2_norm[i]), ffn2_w_gate[i], ffn2_w_up[i], ffn2_w_down[i])
    return _rmsnorm(x, final_norm)
```

```python
import itertools
from contextlib import ExitStack

import numpy as np
import concourse.bass as bass
import concourse.mybir as mybir
from concourse.bass_utils import run_bass_kernel_spmd

F32 = mybir.dt.float32
BF16 = mybir.dt.bfloat16
AF = mybir.ActivationFunctionType
ALU = mybir.AluOpType
ESZ = {F32: 4, BF16: 2}

NCORES = 8
SEQ = 2048
D = 1024
NSEQ = 2
DFF = 2816
NF = DFF // 128
NK = D // 128
TT_ = 512
NT = SEQ // TT_
DEPTH = 2
D_IN = 2950
RMS_EPS = 1e-6
LN_EPS = 1e-5

SB_CELL = 256
SGU_STOP = 99
SGU_VAR = ''
PS_CELL = 2048


class V:
    __slots__ = ("ap", "cells")

    def __init__(self, ap, cells):
        self.ap = ap
        self.cells = cells

    def m(self, f):
        return V(f(self.ap), self.cells)


class TT:
    def __init__(self, handle, shape, dtype, space, base, cell, tid):
        self.h = handle
        self.shape = list(shape)
        self.dtype = dtype
        self.space = space
        self.base = base
        self.cell = cell
        self.tid = tid
        self._cache = {}
        esz = ESZ[dtype]
        dims = self.shape if space == "D" else self.shape[1:]
        st = []
        acc = esz
        for d in reversed(dims):
            st.append(acc)
            acc *= d
        self.strides = list(reversed(st))
        self.dims = dims
        self.esz = esz

    def __getitem__(self, idx):
        if not isinstance(idx, tuple):
            idx = (idx,)
        key = tuple((i.start, i.stop, i.step) if isinstance(i, slice) else i for i in idx)
        c = self._cache.get(key)
        if c is None:
            c = self._cells(idx)
            self._cache[key] = c
        return V(self.h[idx], c)

    def _cells(self, idx):
        fidx = list(idx) if self.space == "D" else list(idx[1:])
        while len(fidx) < len(self.dims):
            fidx.append(slice(None))
        rngs = []
        for i, d in zip(fidx, self.dims):
            if isinstance(i, slice):
                s, e, stp = i.indices(d)
                rngs.append((s, e, stp))
            else:
                rngs.append((i, i + 1, 1))
        cells = set()
        outer = [range(s, e, stp) for (s, e, stp) in rngs[:-1]]
        ls, le, lstp = rngs[-1]
        last_lo = ls * self.strides[-1]
        last_hi = (ls + ((le - 1 - ls) // lstp) * lstp) * self.strides[-1] + self.esz
        for combo in itertools.product(*outer):
            b = self.base + sum(i * s for i, s in zip(combo, self.strides[:-1]))
            for c in range((b + last_lo) // self.cell, (b + last_hi - 1) // self.cell + 1):
                cells.add((self.tid, c))
        return tuple(cells)


class Op:
    __slots__ = ("eng", "idx", "fn", "deps", "dma_deps", "signal", "sigval", "dma_key", "dma_cnt", "stage")


ENGS = ["pe", "act", "dve", "pool", "sp"]


class Prog:
    def __init__(self, nc):
        self.nc = nc
        self.ops = {e: [] for e in ENGS}
        self.cellstate = {}
        self.dma_cnt = {}
        self.plan = False
        self.stage = ""
        self.ins_stage = {}
        self.n_tid = 0
        self.tts = {}
        self.arena = None
        self.arena_base = 0
        self.sb_ptr = 0
        self.sb_cap = 0
        self.psum_banks = []

    def init_mem(self, sb_bytes):
        self.arena_base = (self.nc.sbuf_base + 63) // 64 * 64
        self.sb_cap = min(sb_bytes, (self.nc.sbuf_top - self.arena_base) // 64 * 64)
        for b in range(8):
            self.psum_banks.append(self.nc.alloc_psum_tensor(f"bank{b}", [128, 512], F32))

    def sb(self, name, shape, dtype, off=None):
        if name in self.tts:
            return self.tts[name]
        n = ESZ[dtype]
        for d in shape[1:]:
            n *= d
        if off is None:
            off = (self.sb_ptr + 63) // 64 * 64
            self.sb_ptr = off + n
        assert off + n <= self.sb_cap, f"SBUF overflow {name}: {off}+{n} > {self.sb_cap}"
        h = self.nc.alloc_sbuf_tensor_at(name, list(shape), dtype, offset=self.arena_base + off)
        t = TT(h, shape, dtype, "S", off, SB_CELL, "S")
        self.tts[name] = t
        return t

    def ps(self, name, bank, shape, dtype, byte_off=0):
        if name in self.tts:
            return self.tts[name]
        n = ESZ[dtype]
        for d in shape[1:]:
            n *= d
        assert byte_off + n <= 2048
        t = PsTT(self.psum_banks[bank], shape, dtype, bank, byte_off)
        self.tts[name] = t
        return t

    def dram(self, name, shape, dtype, kind="Internal", cell=None):
        if name in self.tts:
            return self.tts[name]
        h = self.nc.dram_tensor(name, list(shape), dtype, kind=kind)
        self.n_tid += 1
        t = TT(h, shape, dtype, "D", 0, cell or (1 << 40), f"D{self.n_tid}")
        self.tts[name] = t
        return t

    def op(self, eng, fn, reads=(), writes=(), dma_key=None):
        if self.plan:
            return
        o = Op()
        o.eng = eng
        o.fn = fn
        o.signal = False
        o.sigval = None
        o.dma_key = dma_key
        o.idx = len(self.ops[eng])
        o.stage = self.stage
        deps = {}
        dma_deps = {}

        def add(p, raw):
            if p is None:
                return
            if p.dma_key is not None:
                k = p.dma_key
                dma_deps[k] = self.dma_cnt[k]
                return
            if p.eng == eng and dma_key is None:
                if eng == "pe":
                    return
            if deps.get(p.eng, -1) < p.idx:
                deps[p.eng] = p.idx

        cs = self.cellstate
        for v in reads:
            for c in v.cells:
                st = cs.get(c)
                if st is not None:
                    add(st[0], True)
                    if c[0] == "P":
                        for r in st[1]:
                            if r.eng != eng:
                                add(r, False)
        for v in writes:
            for c in v.cells:
                st = cs.get(c)
                if st is not None:
                    add(st[0], False)
                    for r in st[1]:
                        add(r, False)
        for v in writes:
            for c in v.cells:
                cs[c] = [o, []]
        for v in reads:
            for c in v.cells:
                st = cs.get(c)
                if st is None:
                    cs[c] = [None, [o]]
                elif st[0] is not o:
                    st[1].append(o)
        if dma_key is not None:
            self.dma_cnt[dma_key] = self.dma_cnt.get(dma_key, 0) + 1
            o.dma_cnt = self.dma_cnt[dma_key]
        else:
            o.dma_cnt = 0
        for e, i in deps.items():
            self.ops[e][i].signal = True
        o.deps = deps
        o.dma_deps = dma_deps
        self.ops[eng].append(o)

    def emit(self):
        nc = self.nc
        for e in ENGS:
            cnt = 0
            for o in self.ops[e]:
                if o.signal:
                    cnt += 1
                    o.sigval = cnt
        with ExitStack() as es:
            esem = {e: es.enter_context(nc.semaphore(f"sem_{e}")) for e in ENGS if e != "sp"}
            dsem = {k: es.enter_context(nc.semaphore(f"dma_{k}")) for k in self.dma_cnt}
            block = es.enter_context(nc.Block())

            def run(ename, eng):
                waited = {}
                for o in self.ops[ename]:
                    for pe_, pi in o.deps.items():
                        val = self.ops[pe_][pi].sigval
                        key = ("e", pe_)
                        if waited.get(key, 0) < val:
                            eng.wait_ge(esem[pe_], val)
                            waited[key] = val
                    for k, c in o.dma_deps.items():
                        key = ("d", k)
                        if waited.get(key, 0) < 16 * c:
                            eng.wait_ge(dsem[k], 16 * c)
                            waited[key] = 16 * c
                    ins = o.fn(eng)
                    try:
                        self.ins_stage[ins.ins.name] = (ename, o.stage)
                    except Exception:
                        pass
                    if o.dma_key is not None:
                        ins.then_inc(dsem[o.dma_key], 16)
                    elif o.signal:
                        ins.then_inc(esem[ename], 1)
                if ename == "sp":
                    for k, c in self.dma_cnt.items():
                        if waited.get(("d", k), 0) < 16 * c:
                            eng.wait_ge(dsem[k], 16 * c)

            @block.tensor
            def _(eng):
                run("pe", eng)

            @block.scalar
            def _(eng):
                run("act", eng)

            @block.vector
            def _(eng):
                run("dve", eng)

            @block.gpsimd
            def _(eng):
                run("pool", eng)

            @block.sync
            def _(eng):
                run("sp", eng)


class PsTT(TT):
    def __init__(self, bank_handle, shape, dtype, bank, byte_off):
        n_el = 2048 // ESZ[dtype]
        full = bank_handle[:].bitcast(dtype) if dtype != F32 else bank_handle[:]
        self.full = full
        self.shape = list(shape)
        self.dtype = dtype
        self.space = "P"
        self.base = bank * 2048 + byte_off
        self.cell = PS_CELL
        self.tid = "P"
        self._cache = {}
        esz = ESZ[dtype]
        self.esz = esz
        dims = self.shape[1:]
        st = []
        acc = esz
        for d in reversed(dims):
            st.append(acc)
            acc *= d
        self.strides = list(reversed(st))
        self.dims = dims
        n = 1
        for d in dims:
            n *= d
        e0 = byte_off // esz
        flat = full[:, e0:e0 + n]
        if len(dims) == 1:
            self.view = flat
        elif len(dims) == 2:
            self.view = flat.rearrange("p (a b) -> p a b", b=dims[1])
        elif len(dims) == 3:
            self.view = flat.rearrange("p (a b c) -> p a b c", b=dims[1], c=dims[2])
        else:
            raise ValueError

    def __getitem__(self, idx):
        if not isinstance(idx, tuple):
            idx = (idx,)
        key = tuple((i.start, i.stop, i.step) if isinstance(i, slice) else i for i in idx)
        c = self._cache.get(key)
        if c is None:
            c = self._cells(idx)
            self._cache[key] = c
        return V(self.view[idx], c)


SLOT_ELEMS = NF * 128
NSLOT = 5
PREFETCH = 3


class WStream:
    def __init__(self, prog):
        self.prog = prog
        self.plan_list = []
        self.i_next = 0
        self.i_issued = 0
        self.slots = None

    def reset_for_real(self):
        self.i_next = 0
        self.i_issued = 0

    def _issue(self, i):
        src_fn, nelem = self.plan_list[i]
        s = i % NSLOT
        dst = self.slots[:, s, 0:nelem]
        src = src_fn()
        self.prog.op("sp", lambda e, d=dst, s_=src: e.dma_start(out=d.ap, in_=s_.ap),
                     reads=[src], writes=[dst], dma_key=f"w{s}")

    def next(self, src_fn, nelem):
        if self.prog.plan:
            self.plan_list.append((src_fn, nelem))
            return self.slots[:, 0, 0:nelem], 0
        i = self.i_next
        self.i_next += 1
        while self.i_issued < min(len(self.plan_list), i + 1 + PREFETCH):
            self._issue(self.i_issued)
            self.i_issued += 1
        return self.slots[:, i % NSLOT, 0:nelem], i % NSLOT


def _cols(v):
    v = np.asarray(v, np.float32)
    return np.ascontiguousarray(v.reshape(-1, 128).T)


VEC_LAYOUT = {}


N_CMAT = 15


def build_vecs(inp):
    cols = []
    pos = 0
    VEC_LAYOUT.clear()

    def add(name, arr):
        nonlocal pos
        arr = np.asarray(arr, np.float32)
        VEC_LAYOUT[name] = (pos, arr.shape[1])
        cols.append(arr)
        pos += arr.shape[1]

    def bc(v):
        v = np.asarray(v, np.float32).reshape(1, -1)
        return np.broadcast_to(v, (128, v.shape[1]))

    for l in range(DEPTH):
        add(f"ffn1_norm{l}", _cols(inp["ffn1_norm"][l]))
        add(f"mix_norm{l}", _cols(inp["mix_norm"][l]))
        add(f"ffn2_norm{l}", _cols(inp["ffn2_norm"][l]))
    add("final_norm", _cols(inp["final_norm"]))
    for l in range(DEPTH):
        for j in range(4):
            add(f"conv_w{l}_{j}", _cols(inp["conv_w"][l][j]))
        add(f"conv_b{l}", _cols(inp["conv_b"][l]))
        add(f"ssd_norm{l}", _cols(inp["ssd_norm"][l]))
        add(f"dcol{l}", _cols(np.repeat(np.asarray(inp["d_skip"][l], np.float32), 64)))
        add(f"sgu_ln_g{l}", _cols(inp["sgu_ln_g"][l]))
        add(f"dt_bias{l}", bc(np.tile(np.asarray(inp["dt_bias"][l], np.float32), 4)))
        add(f"a_log{l}", bc(np.tile(np.asarray(inp["a_log"][l], np.float32), 4)))
        add(f"sgu_ln_b{l}", bc(inp["sgu_ln_b"][l]))
    return np.ascontiguousarray(np.concatenate(cols, axis=1))


def n_vec_cols():
    return DEPTH * 3 * NK + NK + DEPTH * (28 + 7 + 3 + 3 + 2 + 24 + 24 + 256)


def build_cmat():
    i = np.arange(128)
    m = []
    m.append(np.eye(128))
    m.append((i[:, None] <= i[None, :]) * 1.0)
    m.append(np.where(i[:, None] <= i[None, :], 0.0, -30000.0))
    m.append((i[None, :] >= i[:, None]) * 1.0)
    m.append((i[None, :] <= i[:, None]) * 1.0)
    for k in range(3):
        for mm in range(3):
            gi = (128 * k + i) // 192
            go = (128 * mm + i) // 192
            m.append((gi[:, None] == go[None, :]) * 1.0)
    m.append((i[None, :] <= i[:, None]) * 1.0)
    return np.ascontiguousarray(np.concatenate(m, axis=1).astype(np.float32))


class Builder:
    def __init__(self, stages=None, dump=None, parts=("att", "ssd", "sgu")):
        self.parts = parts
        self.nc = bass.Bass("TRN2", target_bir_lowering=False, dynamic_dma_scratch_size=64)
        self.P = Prog(self.nc)
        self.ws = WStream(self.P)
        self.stages = stages
        self.dump = dump
        self.rr = 0

    def declare(self):
        P = self.P
        nc = self.nc
        ext = lambda n, s, dt=F32: P.dram(n, s, dt, kind="ExternalInput")
        self.x_in = ext("x", [NSEQ, SEQ, D])
        self.w = {}
        for nm in ("ffn1", "ffn2"):
            self.w[nm + "_w_gate"] = ext(nm + "_w_gate", [DEPTH, D, DFF])
            self.w[nm + "_w_up"] = ext(nm + "_w_up", [DEPTH, D, DFF])
            self.w[nm + "_w_down"] = ext(nm + "_w_down", [DEPTH, DFF, D])
        self.w["w_in"] = ext("w_in", [DEPTH, D, D_IN])
        self.w["w_out"] = ext("w_out", [DEPTH, D, D])
        self.vecs_in = ext("vecs", [128, n_vec_cols()])
        self.out = P.dram("out", [NSEQ, SEQ, D], F32, kind="ExternalOutput")
        self.WguR = P.dram("WguR", [DEPTH * 2, NF, 128, 2 * NK * 128], BF16, cell=128 * 2 * NK * 128 * 2)
        self.WdR = P.dram("WdR", [DEPTH * 2, NK, 128, NF * 128], BF16, cell=128 * NF * 128 * 2)
        self.WinR = P.dram("WinR", [DEPTH, 24, 128, NK * 128], BF16, cell=128 * NK * 128 * 2)
        self.WoutR = P.dram("WoutR", [DEPTH, NK, 128, NK * 128], BF16, cell=128 * NK * 128 * 2)
        self.cmat_in = ext("cmat", [128, N_CMAT * 128])
        self.sgu_w_in = ext("sgu_w", [DEPTH, 4, 128, 128])
        self.sgu_b_in = ext("sgu_b", [DEPTH, 4 * 128])

    def alloc_global(self):
        P = self.P
        P.init_mem(229056)
        self.xT = P.sb("xT", [128, NK, SEQ], F32)
        self.vecs = P.sb("vecs_sb", [128, n_vec_cols()], F32)
        self.cmat = P.sb("cmat_sb", [128, N_CMAT, 128], F32)
        self.ident = self.cmat_view(0)
        self.identbf = P.sb("identbf", [128, 128], BF16)
        self.mask2 = P.sb("mask2", [128, 256], BF16)
        self.ones32 = P.sb("ones32", [128, 128], F32)
        self.selbf = P.sb("selbf", [128, 9, 128], BF16)
        self.ones_bf = P.sb("ones_bf", [128, 128], BF16)
        self.cst = P.sb("cst", [128, 8], F32)
        self.ws.slots = P.sb("wslots", [128, NSLOT, SLOT_ELEMS], BF16)
        self.glob_end = P.sb_ptr

    def cmat_view(self, i):
        class _C:
            def __getitem__(s_, idx):
                if not isinstance(idx, tuple):
                    idx = (idx,)
                return self.cmat[(idx[0], i) + tuple(idx[1:])]
        return _C()

    def phase(self):
        self.P.sb_ptr = self.glob_end
        self.phase_id = getattr(self, "phase_id", 0) + 1
        return f"ph{self.phase_id}_"

    def vcol(self, name, k):
        pos, n = VEC_LAYOUT[name]
        return self.vecs[:, pos + k:pos + k + 1]

    def any_eng(self):
        self.rr += 1
        return ["act", "dve", "pool"][self.rr % 3]

    def load_consts(self):
        P = self.P
        P.op("sp", lambda e: e.dma_start(out=self.vecs[:, :].ap, in_=self.vecs_in[:, :].ap),
             reads=[], writes=[self.vecs[:, :]], dma_key="c0")
        cm = self.cmat[:, :, :]
        P.op("sp", lambda e: e.dma_start(out=cm.ap, in_=self.cmat_in[:, :].m(
            lambda a: a.rearrange("p (c i) -> p c i", i=128)).ap), reads=[], writes=[cm], dma_key="c0")
        P.op("dve", lambda e: e.tensor_copy(out=self.identbf[:, :].ap, in_=self.ident[:, :].ap),
             reads=[self.ident[:, :]], writes=[self.identbf[:, :]])
        P.op("dve", lambda e: e.tensor_copy(out=self.mask2[:, :].ap, in_=self.cmat[:, 3:5, :].m(
            lambda a: a.rearrange("p c i -> p (c i)")).ap), reads=[self.cmat[:, 3:5, :]], writes=[self.mask2[:, :]])
        P.op("pool", lambda e: e.memset(self.ones32[:, :].ap, 1.0), writes=[self.ones32[:, :]])
        P.op("dve", lambda e: e.tensor_copy(out=self.selbf[:, :, :].ap, in_=self.cmat[:, 5:14, :].ap),
             reads=[self.cmat[:, 5:14, :]], writes=[self.selbf[:, :, :]])
        P.op("pool", lambda e: e.memset(self.cst[:, 3:4].ap, 1.0), writes=[self.cst[:, 3:4]])
        P.op("pool", lambda e: e.memset(self.ones_bf[:, :].ap, 1.0), writes=[self.ones_bf[:, :]])
        P.op("pool", lambda e: e.memset(self.cst[:, 0:1].ap, RMS_EPS), writes=[self.cst[:, 0:1]])
        P.op("pool", lambda e: e.memset(self.cst[:, 1:2].ap, LN_EPS), writes=[self.cst[:, 1:2]])
        P.op("pool", lambda e: e.memset(self.cst[:, 2:3].ap, 0.0), writes=[self.cst[:, 2:3]])

    def prepass(self, sl):
        P = self.P
        pre = self.phase()
        CW = 256
        NB_ = 3
        st32 = [P.sb(pre + f"st32_{i}", [128, NF, CW], F32) for i in range(NB_)]
        stbf = [P.sb(pre + f"stbf_{i}", [128, 2, NF, 128], BF16) for i in range(NB_)]
        it = 0

        def one(src_t, l, nk, c0, dst_fn, ncols=CW):
            nonlocal it
            b = it % NB_
            eng = "dve" if it % 2 == 0 else "act"
            it += 1
            s32 = st32[b][:, 0:nk, 0:ncols]
            src = src_t[l, :, c0:c0 + ncols].m(lambda a: a.rearrange("(k p) n -> p k n", p=128))
            P.op("sp", lambda e: e.dma_start(out=s32.ap, in_=src.ap), reads=[], writes=[s32], dma_key=f"pi{b}")
            if ncols == CW:
                sbf = stbf[b][:, :, 0:nk, :]
                s32p = s32.m(lambda a: a.rearrange("p k (j i) -> p j k i", j=2))
            else:
                sbf = stbf[b][:, 0, 0:nk, 0:ncols]
                s32p = s32
            if eng == "act":
                P.op("act", lambda e: e.copy(out=sbf.ap, in_=s32p.ap), reads=[s32], writes=[sbf])
            else:
                P.op(eng, lambda e: e.tensor_copy(out=sbf.ap, in_=s32p.ap), reads=[s32], writes=[sbf])
            dst = dst_fn()
            P.op("act", lambda e: e.dma_start(out=dst.ap, in_=sbf.ap), reads=[sbf], writes=[dst], dma_key=f"po{b}")

        for l in range(DEPTH):
            for fi, nm in enumerate(("ffn1", "ffn2")):
                if (nm, l) not in sl:
                    continue
                idx = l * 2 + fi
                for gu, wn in enumerate(("_w_gate", "_w_up")):
                    for g in range(NF // 2):
                        one(self.w[nm + wn], l, NK, g * CW,
                            lambda idx=idx, g=g, gu=gu: self.WguR[idx, 2 * g:2 * g + 2, :, gu * NK * 128:(gu + 1) * NK * 128]
                            .m(lambda a: a.rearrange("f p (k i) -> p f k i", i=128)))
                for g in range(NK // 2):
                    one(self.w[nm + "_w_down"], l, NF, g * CW,
                        lambda idx=idx, g=g: self.WdR[idx, 2 * g:2 * g + 2, :, :]
                        .m(lambda a: a.rearrange("c p (f i) -> p c f i", i=128)))
        for l in range(DEPTH):
            if ("mix", l) not in sl:
                continue

            def win_dst(l, c, n):
                if n == 2:
                    return lambda: self.WinR[l, c:c + 2, :, :].m(lambda a: a.rearrange("c p (k i) -> p c k i", i=128))
                return None
            for g in range(9):
                one(self.w["w_in"], l, NK, g * CW, win_dst(l, 2 * g, 2))
            one(self.w["w_in"], l, NK, 2304, lambda l=l: self.WinR[l, 18, :, :].m(
                lambda a: a.rearrange("p (k i) -> p k i", i=128)), ncols=128)
            one(self.w["w_in"], l, NK, 2432, lambda l=l: self.WinR[l, 19, :, :].m(
                lambda a: a.rearrange("p (k i) -> p k i", i=128)), ncols=128)
            one(self.w["w_in"], l, NK, 2438, win_dst(l, 20, 2))
            one(self.w["w_in"], l, NK, 2694, win_dst(l, 22, 2))
            for g in range(4):
                one(self.w["w_out"], l, NK, g * CW,
                    lambda l=l, g=g: self.WoutR[l, 2 * g:2 * g + 2, :, :]
                    .m(lambda a: a.rearrange("c p (k i) -> p c k i", i=128)))

    def load_x(self, s):
        P = self.P
        pre = self.phase()
        stg = [P.sb(pre + f"xs{i}", [128, D], F32) for i in range(3)]
        pst = [P.ps(f"tp{i}", i, [128, 4, 128], F32) for i in range(4)]
        n = 0
        for b in range(SEQ // 128):
            sg = stg[b % 3]
            src = self.x_in[s, b * 128:(b + 1) * 128, :]
            P.op("sp", lambda e, sg=sg, src=src: e.dma_start(out=sg[:, :].ap, in_=src.ap),
                 reads=[], writes=[sg[:, :]], dma_key=f"xi{b % 3}")
            for half in range(2):
                pt = pst[n % 4]
                n += 1
                for j in range(4):
                    k = half * 4 + j
                    P.op("pe", lambda e, pt=pt, j=j, k=k, sg=sg: e.transpose(
                        out=pt[:, j, :].ap, in_=sg[:, k * 128:(k + 1) * 128].ap, identity=self.ident[:, :].ap),
                        reads=[sg[:, k * 128:(k + 1) * 128], self.ident[:, :]], writes=[pt[:, j, :]])
                dst = self.xT[:, half * 4:half * 4 + 4, b * 128:(b + 1) * 128]
                if n % 2 == 0:
                    P.op("act", lambda e, pt=pt, dst=dst: e.copy(out=dst.ap, in_=pt[:, :, :].ap),
                         reads=[pt[:, :, :]], writes=[dst])
                else:
                    P.op("dve", lambda e, pt=pt, dst=dst: e.tensor_copy(out=dst.ap, in_=pt[:, :, :].ap),
                         reads=[pt[:, :, :]], writes=[dst])

    def rms_tile(self, pre, t, gname, hT, ss_ps, sq, rstd, dst_fn=None):
        P = self.P
        tl = slice(t * TT_, (t + 1) * TT_)
        for k in range(NK):
            q = sq[k % 2]
            xin = self.xT[:, k, tl]
            P.op("act", lambda e, q=q, xin=xin: e.activation(out=q[:, :].ap, in_=xin.ap, func=AF.Square),
                 reads=[xin], writes=[q[:, :]])
            P.op("pe", lambda e, q=q, k=k: e.matmul(ss_ps[:, :].ap, lhsT=self.ones_bf[:, :].ap, rhs=q[:, :].ap,
                                                    start=(k == 0), stop=(k == NK - 1)),
                 reads=[q[:, :], self.ones_bf[:, :]], writes=[ss_ps[:, :]])
        self.rsqrt_mean(rstd[:, :], ss_ps[:, :], 1.0 / D, RMS_EPS)
        for k in range(NK):
            xin = self.xT[:, k, tl]
            g = self.vcol(gname, k)
            dst = hT[:, k, :] if dst_fn is None else dst_fn(k)
            eng = "dve"
            P.op(eng, lambda e, xin=xin, g=g, dst=dst: e.scalar_tensor_tensor(
                out=dst.ap, in0=xin.ap, scalar=g.ap, in1=rstd[:, :].ap, op0=ALU.mult, op1=ALU.mult),
                reads=[xin, g, rstd[:, :]], writes=[dst])

    def rsqrt_mean(self, dst, src_ps, scale, eps):
        P = self.P
        ec = self.cst[:, 0:1] if eps == RMS_EPS else self.cst[:, 1:2]
        P.op("act", lambda e: e.activation(out=dst.ap, in_=src_ps.ap, func=AF.Sqrt, bias=ec.ap, scale=scale),
             reads=[src_ps, ec], writes=[dst])
        P.op("dve", lambda e: e.reciprocal(out=dst.ap, in_=dst.ap), reads=[dst], writes=[dst])

    def ffn(self, l, fi):
        P = self.P
        pre = self.phase()
        idx = l * 2 + fi
        gname = f"ffn{fi + 1}_norm{l}"
        hT = [P.sb(pre + f"hT{i}", [128, NK, TT_], BF16) for i in range(2)]
        sq = [P.sb(pre + f"sq{i}", [128, TT_], BF16) for i in range(2)]
        rstd = P.sb(pre + "rstd", [128, TT_], F32)
        sg = [P.sb(pre + f"sg{i}", [128, TT_], F32) for i in range(2)]
        aT = P.sb(pre + "aT", [128, NF, TT_], BF16)
        ss_ps = P.ps("ffn_ss", 0, [128, TT_], F32)
        pg = [P.ps(f"ffn_pg{i}", 1 + i, [128, TT_], F32) for i in range(2)]
        pu = [P.ps(f"ffn_pu{i}", 3 + i, [128, TT_], F32) for i in range(2)]
        py = [P.ps(f"ffn_py{i}", 5 + i, [128, TT_], F32) for i in range(2)]
        self.rms_tile(pre, 0, gname, hT[0], ss_ps, sq, rstd)
        for t in range(NT):
            tl = slice(t * TT_, (t + 1) * TT_)
            h = hT[t % 2]
            for f in range(NF):
                wv, _ = self.ws.next(lambda f=f: self.WguR[idx, f, :, :], 2 * NK * 128)
                g_ps = pg[f % 2]
                u_ps = pu[f % 2]
                for gu, ps_ in ((0, g_ps), (1, u_ps)):
                    for k in range(NK):
                        lw = wv.m(lambda a, gu=gu, k=k: a[:, (gu * NK + k) * 128:(gu * NK + k + 1) * 128])
                        P.op("pe", lambda e, ps_=ps_, lw=lw, h=h, k=k: e.matmul(
                            ps_[:, :].ap, lhsT=lw.ap, rhs=h[:, k, :].ap, start=(k == 0), stop=(k == NK - 1)),
                            reads=[wv, h[:, k, :]], writes=[ps_[:, :]])
                s_ = sg[f % 2]
                P.op("act", lambda e, s_=s_, g_ps=g_ps: e.activation(out=s_[:, :].ap, in_=g_ps[:, :].ap, func=AF.Silu),
                     reads=[g_ps[:, :]], writes=[s_[:, :]])
                dst = aT[:, f, :]
                P.op("dve", lambda e, s_=s_, u_ps=u_ps, dst=dst: e.tensor_tensor(
                    out=dst.ap, in0=u_ps[:, :].ap, in1=s_[:, :].ap, op=ALU.mult),
                    reads=[u_ps[:, :], s_[:, :]], writes=[dst])
            if t + 1 < NT:
                self.rms_tile(pre, t + 1, gname, hT[(t + 1) % 2], ss_ps, sq, rstd)
            for c in range(NK):
                wv, _ = self.ws.next(lambda c=c: self.WdR[idx, c, :, :], NF * 128)
                y_ps = py[c % 2]
                for f in range(NF):
                    lw = wv.m(lambda a, f=f: a[:, f * 128:(f + 1) * 128])
                    P.op("pe", lambda e, y_ps=y_ps, lw=lw, f=f: e.matmul(
                        y_ps[:, :].ap, lhsT=lw.ap, rhs=aT[:, f, :].ap, start=(f == 0), stop=(f == NF - 1)),
                        reads=[wv, aT[:, f, :]], writes=[y_ps[:, :]])
                xin = self.xT[:, c, tl]
                P.op("dve", lambda e, y_ps=y_ps, xin=xin: e.scalar_tensor_tensor(
                    out=xin.ap, in0=y_ps[:, :].ap, scalar=0.5, in1=xin.ap, op0=ALU.mult, op1=ALU.add),
                    reads=[y_ps[:, :], xin], writes=[xin])

    def store_out(self, s, normalize=True):
        P = self.P
        pre = self.phase()
        sq = [P.sb(pre + f"sq{i}", [128, TT_], BF16) for i in range(2)]
        rstd = P.sb(pre + "rstd", [128, TT_], F32)
        yT = [P.sb(pre + f"yT{i}", [128, NK, TT_], F32) for i in range(2)]
        stg = [P.sb(pre + f"os{i}", [128, D], F32) for i in range(3)]
        ss_ps = P.ps("ffn_ss", 0, [128, TT_], F32)
        pst = [P.ps(f"otp{i}", 1 + i, [128, 4, 128], F32) for i in range(4)]
        n = 0
        nb = 0
        for t in range(NT):
            tl = slice(t * TT_, (t + 1) * TT_)
            y = yT[t % 2]
            if normalize:
                for k in range(NK):
                    q = sq[k % 2]
                    xin = self.xT[:, k, tl]
                    P.op("act", lambda e, q=q, xin=xin: e.activation(out=q[:, :].ap, in_=xin.ap, func=AF.Square),
                         reads=[xin], writes=[q[:, :]])
                    P.op("pe", lambda e, q=q, k=k: e.matmul(ss_ps[:, :].ap, lhsT=self.ones_bf[:, :].ap, rhs=q[:, :].ap,
                                                            start=(k == 0), stop=(k == NK - 1)),
                         reads=[q[:, :], self.ones_bf[:, :]], writes=[ss_ps[:, :]])
                self.rsqrt_mean(rstd[:, :], ss_ps[:, :], 1.0 / D, RMS_EPS)
                for k in range(NK):
                    xin = self.xT[:, k, tl]
                    g = self.vcol("final_norm", k)
                    dst = y[:, k, :]
                    eng = "dve"
                    P.op(eng, lambda e, xin=xin, g=g, dst=dst: e.scalar_tensor_tensor(
                        out=dst.ap, in0=xin.ap, scalar=g.ap, in1=rstd[:, :].ap, op0=ALU.mult, op1=ALU.mult),
                        reads=[xin, g, rstd[:, :]], writes=[dst])
            for bb in range(TT_ // 128):
                b = t * (TT_ // 128) + bb
                sg = stg[nb % 3]
                nb += 1
                for half in range(2):
                    pt = pst[n % 4]
                    n += 1
                    for j in range(4):
                        k = half * 4 + j
                        src = (y[:, k, bb * 128:(bb + 1) * 128] if normalize
                               else self.xT[:, k, b * 128:(b + 1) * 128])
                        P.op("pe", lambda e, pt=pt, j=j, src=src: e.transpose(
                            out=pt[:, j, :].ap, in_=src.ap, identity=self.ident[:, :].ap),
                            reads=[src, self.ident[:, :]], writes=[pt[:, j, :]])
                    dst = sg[:, half * 512:(half + 1) * 512]
                    pin = pt[:, :, :].m(lambda a: a.rearrange("p a b -> p (a b)"))
                    if n % 2 == 0:
                        P.op("act", lambda e, pin=pin, dst=dst: e.copy(out=dst.ap, in_=pin.ap),
                             reads=[pin], writes=[dst])
                    else:
                        P.op("dve", lambda e, pin=pin, dst=dst: e.tensor_copy(out=dst.ap, in_=pin.ap),
                             reads=[pin], writes=[dst])
                dsto = self.out[s, b * 128:(b + 1) * 128, :]
                P.op("sp", lambda e, sg=sg, dsto=dsto: e.dma_start(out=dsto.ap, in_=sg[:, :].ap),
                     reads=[sg[:, :]], writes=[], dma_key=f"xo{(nb - 1) % 3}")

    def body(self):
        st = self.stages
        full = [(n, l) for l in range(DEPTH) for n in ("ffn1", "mix", "ffn2")]
        if isinstance(st, list):
            sl = st
        elif st is None:
            sl = full
        else:
            sl = full[:st]
        self.P.stage = "consts"
        self.load_consts()
        self.P.stage = "prepass"
        self.prepass(sl)
        for s in range(NSEQ):
            self.P.stage = f"s{s}.load_x"
            self.load_x(s)
            for name, l in sl:
                self.P.stage = f"s{s}.{name}{l}"
                if name == "ffn1":
                    self.ffn(l, 0)
                elif name == "ffn2":
                    self.ffn(l, 1)
                else:
                    self.mixer(l)
            self.P.stage = f"s{s}.store"
            self.store_out(s, normalize=(st is None))

    def mixer(self, l):
        P = self.P
        pre = self.phase()
        parts = self.parts
        hT = P.sb(pre + "hT", [128, NK, SEQ], BF16)
        ymix = P.sb(pre + "ymix", [128, NK, SEQ], BF16)
        self.mix_base = P.sb_ptr
        sq = [P.sb(pre + f"sq{i}", [128, TT_], BF16) for i in range(2)]
        rstd = P.sb(pre + "rstd", [128, TT_], F32)
        ss_ps = P.ps("ffn_ss", 0, [128, TT_], F32)
        for t in range(NT):
            tl = slice(t * TT_, (t + 1) * TT_)
            self.rms_tile(pre, t, f"mix_norm{l}", None, ss_ps, sq, rstd, dst_fn=lambda k, tl=tl: hT[:, k, tl])
        if len(parts) < 3:
            for k in range(NK):
                P.op("pool", lambda e, k=k: e.memset(ymix[:, k, :].ap, 0.0), writes=[ymix[:, k, :]])
        st0 = P.stage
        if "att" in parts:
            P.stage = st0 + ".att"
            self.attention(l, pre, hT, ymix)
        if "ssd" in parts:
            P.stage = st0 + ".ssd"
            self.ssd(l, pre, hT, ymix)
        if "sgu" in parts:
            P.stage = st0 + ".sgu"
            self.sgu(l, pre, hT, ymix)
        P.stage = st0 + ".wout"
        self.wout(l, ymix)

    def proj_w(self, l, c):
        wv, _ = self.ws.next(lambda: self.WinR[l, c, :, :], NK * 128)
        return wv

    def proj_mm(self, wv, ncols, rhs_fn, ps):
        P = self.P
        for k in range(NK):
            lw = wv.m(lambda a, k=k: a[:, k * 128:k * 128 + ncols])
            r = rhs_fn(k)
            P.op("pe", lambda e, lw=lw, r=r, k=k: e.matmul(ps.ap, lhsT=lw.ap, rhs=r.ap, start=(k == 0),
                                                           stop=(k == NK - 1)),
                 reads=[wv, r], writes=[ps])

    def wout(self, l, ymix):
        P = self.P
        py = [P.ps(f"ffn_py{i}", 5 + i, [128, TT_], F32) for i in range(2)]
        n = 0
        for co in range(NK):
            wv, _ = self.ws.next(lambda co=co: self.WoutR[l, co, :, :], NK * 128)
            for t in range(NT):
                tl = slice(t * TT_, (t + 1) * TT_)
                y_ps = py[n % 2]
                n += 1
                self.proj_mm(wv, 128, lambda k, tl=tl: ymix[:, k, tl], y_ps[:, :])
                xin = self.xT[:, co, tl]
                P.op("dve", lambda e, y_ps=y_ps, xin=xin: e.tensor_tensor(
                    out=xin.ap, in0=y_ps[:, :].ap, in1=xin.ap, op=ALU.add),
                    reads=[y_ps[:, :], xin], writes=[xin])

    def attention(self, l, pre0, hT, ymix):
        P = self.P
        P.sb_ptr = self.mix_base
        pre = pre0 + "att_"
        qT = P.sb(pre + "qT", [128, SEQ], BF16)
        kT = P.sb(pre + "kT", [128, SEQ], BF16)
        vT = P.sb(pre + "vT", [128, SEQ], BF16)
        Vtok = P.sb(pre + "Vtok", [128, 16, 128], BF16)
        PT = [P.sb(pre + f"PT{i}", [128, 256], BF16) for i in range(6)]
        accN = P.sb(pre + "accN", [128, SEQ], F32)
        accD = P.sb(pre + "accD", [128, SEQ], F32)
        ps_proj = [P.ps(f"att_pp{i}", i, [128, 512], F32) for i in range(2)]
        ps_tr = [P.ps(f"att_tr{i}", 0, [128, 4, 128], BF16, byte_off=(i % 2) * 1024) for i in range(2)]
        ps_S = [P.ps(f"att_S{i}", 1 + i, [128, 256], F32) for i in range(3)]
        ps_ON = [P.ps(f"att_ON{i}", 4 + i, [128, 512], F32) for i in range(2)]
        ps_OD = [P.ps(f"att_OD{i}", 6 + i, [128, 512], F32) for i in range(2)]
        cnt = {"pp": 0, "tr": 0, "S": 0, "PT": 0, "ev": 0}
        gbase = 0
        for c in range(3):
            for which, dstT in ((0, qT), (1, kT), (2, vT)):
                wv = self.proj_w(l, 3 * which + c)
                for t in range(NT):
                    tl = slice(t * TT_, (t + 1) * TT_)
                    ps = ps_proj[cnt["pp"] % 2]
                    cnt["pp"] += 1
                    self.proj_mm(wv, 128, lambda k, tl=tl: hT[:, k, tl], ps[:, :])
                    dst = dstT[:, tl]
                    if which == 0:
                        P.op("act", lambda e, ps=ps, dst=dst: e.mul(out=dst.ap, in_=ps[:, :].ap, mul=0.125),
                             reads=[ps[:, :]], writes=[dst])
                    elif which == 1:
                        P.op("dve", lambda e, ps=ps, dst=dst: e.tensor_copy(out=dst.ap, in_=ps[:, :].ap),
                             reads=[ps[:, :]], writes=[dst])
                    else:
                        P.op("act", lambda e, ps=ps, dst=dst: e.copy(out=dst.ap, in_=ps[:, :].ap),
                             reads=[ps[:, :]], writes=[dst])
            for br, d in enumerate((1, 4, 16)):
                nb = 16 // d

                def tokstart(blk):
                    if d == 1:
                        return 128 * blk
                    if d == 4:
                        return (blk // 4) + 512 * (blk % 4)
                    return blk
                for g4 in range(4):
                    pt = ps_tr[cnt["tr"] % 2]
                    cnt["tr"] += 1
                    for j in range(4):
                        st = tokstart(g4 * 4 + j)
                        src = vT[:, st:st + 127 * d + 1:d]
                        P.op("pe", lambda e, pt=pt, j=j, src=src: e.transpose(
                            out=pt[:, j, :].ap, in_=src.ap, identity=self.identbf[:, :].ap),
                            reads=[src, self.identbf[:, :]], writes=[pt[:, j, :]])
                    dst = Vtok[:, g4 * 4:(g4 + 1) * 4, :]
                    if g4 % 2 == 0:
                        P.op("act", lambda e, pt=pt, dst=dst: e.copy(out=dst.ap, in_=pt[:, :, :].ap),
                             reads=[pt[:, :, :]], writes=[dst])
                    else:
                        P.op("dve", lambda e, pt=pt, dst=dst: e.tensor_copy(out=dst.ap, in_=pt[:, :, :].ap),
                             reads=[pt[:, :, :]], writes=[dst])
                its = []
                for r in range(d):
                    for n in range(nb):
                        for h in range(2):
                            its.append((r, n, h))
                LAG = 4
                pend = {}
                for ii in range(len(its) + LAG):
                    if ii < len(its):
                        r, n, h = its[ii]
                        st = r + d * 128 * n
                        nq = 256 if n + 1 < nb else 128
                        kset = slice(st, st + 127 * d + 1, d)
                        qset = slice(st, st + (nq - 1) * d + 1, d)
                        hp = slice(64 * h, 64 * h + 64)
                        S = ps_S[cnt["S"] % len(ps_S)]
                        cnt["S"] += 1
                        kk = kT[hp, kset]
                        qq = qT[hp, qset]
                        Sv = S[:, 0:nq]
                        P.op("pe", lambda e, Sv=Sv, kk=kk, qq=qq: e.matmul(Sv.ap, lhsT=kk.ap, rhs=qq.ap,
                                                                           start=True, stop=True),
                             reads=[kk, qq], writes=[Sv])
                        pt_ = PT[cnt["PT"] % len(PT)]
                        cnt["PT"] += 1
                        pv = pt_[:, 0:nq]
                        P.op("act", lambda e, pv=pv, Sv=Sv: e.activation(out=pv.ap, in_=Sv.ap, func=AF.Exp),
                             reads=[Sv], writes=[pv])
                        mk = self.mask2[:, 0:nq]
                        P.op("pool", lambda e, pv=pv, mk=mk: e.tensor_tensor(out=pv.ap, in0=pv.ap, in1=mk.ap,
                                                                             op=ALU.mult),
                             reads=[pv, mk], writes=[pv])
                        pend[ii] = pt_
                    jj = ii - LAG
                    if jj < 0:
                        continue
                    r, n, h = its[jj]
                    pt_ = pend.pop(jj)
                    blk = n if d == 1 else (r * 4 + n if d == 4 else r)
                    nq = 256 if n + 1 < nb else 128
                    hp = slice(64 * h, 64 * h + 64)
                    vv = Vtok[:, blk, 64 * h:64 * h + 64]
                    on1 = self.ones_bf[:, 0:64]
                    for half in range(nq // 128):
                        qi = blk + half
                        G = gbase + qi // 4
                        pos = qi % 4
                        cs_ = slice(pos * 128, (pos + 1) * 128)
                        rhs = pt_[:, half * 128:(half + 1) * 128]
                        if half == 0:
                            fl = dict(start=(n == 0), stop=True)
                        else:
                            fl = dict(start=True, stop=False)
                        for lhs, dstp in ((vv, ps_ON[G % 2][hp, cs_]), (on1, ps_OD[G % 2][hp, cs_])):
                            P.op("pe", lambda e, lhs=lhs, dstp=dstp, rhs=rhs, fl=fl: e.matmul(
                                dstp.ap, lhsT=lhs.ap, rhs=rhs.ap, skip_group_check=True, **fl),
                                reads=[lhs, rhs], writes=[dstp])
                    if h == 1 and blk % 4 == 3:
                        Gl = blk // 4
                        G = gbase + Gl
                        if d == 1:
                            vw = lambda a, Gl=Gl: a[:, 512 * Gl:512 * Gl + 512]
                            pw = lambda a: a
                        elif d == 4:
                            vw = lambda a, Gl=Gl: a[:, Gl:SEQ:4]
                            pw = lambda a: a
                        else:
                            vw = lambda a, Gl=Gl: a.rearrange("p (i r) -> p r i", r=16)[:, 4 * Gl:4 * Gl + 4, :]
                            pw = lambda a: a.rearrange("p (r i) -> p r i", i=128)
                        for acc, psb, eng0 in ((accN, ps_ON[G % 2], "act"), (accD, ps_OD[G % 2], "dve")):
                            full = acc[:, :]
                            if d == 1:
                                av = acc[:, 512 * Gl:512 * Gl + 512]
                            else:
                                av = V(vw(full.ap), full.cells)
                            pp = psb[:, :]
                            pin = V(pw(pp.ap), pp.cells)
                            if br == 0:
                                if eng0 == "act":
                                    P.op("act", lambda e, av=av, pin=pin: e.copy(out=av.ap, in_=pin.ap),
                                         reads=[pin], writes=[av])
                                else:
                                    P.op("dve", lambda e, av=av, pin=pin: e.tensor_copy(out=av.ap, in_=pin.ap),
                                         reads=[pin], writes=[av])
                            else:
                                P.op("dve", lambda e, av=av, pin=pin: e.tensor_tensor(
                                    out=av.ap, in0=pin.ap, in1=av.ap, op=ALU.add),
                                    reads=[pin, av], writes=[av])
                gbase += 4
            for t in range(NT):
                tl = slice(t * TT_, (t + 1) * TT_)
                P.op("dve", lambda e, tl=tl: e.reciprocal(out=accD[:, tl].ap, in_=accD[:, tl].ap),
                     reads=[accD[:, tl]], writes=[accD[:, tl]])
                dst = ymix[:, c, tl]
                P.op("dve", lambda e, tl=tl, dst=dst: e.tensor_tensor(out=dst.ap, in0=accN[:, tl].ap,
                                                                      in1=accD[:, tl].ap, op=ALU.mult),
                     reads=[accN[:, tl], accD[:, tl]], writes=[dst])

    def sgu(self, l, pre0, hT, ymix):
        P = self.P
        P.sb_ptr = self.mix_base
        pre = pre0 + "sgu_"
        wraw = P.sb(pre + "wraw", [128, 4, 128], F32)
        WcT32 = P.sb(pre + "WcT32", [128, 4, 128], F32)
        WcTb = P.sb(pre + "WcTb", [128, 4, 128], BF16)
        bsrow = P.sb(pre + "bsrow", [128, 512], F32)
        Kt4 = P.sb(pre + "Kt4", [128, 2, 4, 128], F32)
        gu = [P.sb(pre + f"gu{i}", [128, 2, TT_], BF16) for i in range(2)]
        vg = [P.sb(pre + f"vg{i}", [128, 256], F32) for i in range(2)]
        cen = P.sb(pre + "cen", [128, 4, 256], F32)
        sqc = P.sb(pre + "sqc", [128, 256], F32)
        stat = [P.sb(pre + f"stat{i}", [128, 16], F32) for i in range(2)]
        nbf = [P.sb(pre + f"nbf{i}", [128, 256], BF16) for i in range(2)]
        tmp = P.sb(pre + "tmp", [128, TT_], F32)
        pp = [P.ps(f"att_pp{i}", i, [128, 512], F32) for i in range(2)]
        vps = [P.ps(f"sgu_v{i}", 2, [128, 256], F32, byte_off=i * 1024) for i in range(2)]
        mps = [P.ps(f"sgu_m{i}", 3 + i, [128, 512], F32) for i in range(2)]
        trps = P.ps("sgu_tr", 5, [128, 128], F32)
        kps = P.ps("sgu_k", 5, [128, 128], F32, byte_off=1024)
        tril = self.cmat_view(14)
        lbpos = VEC_LAYOUT[f"sgu_ln_b{l}"][0]
        P.op("sp", lambda e: e.dma_start(out=wraw[:, :, :].ap, in_=self.sgu_w_in[l, :, :, :].m(
            lambda a: a.rearrange("g t s -> t g s")).ap), reads=[], writes=[wraw[:, :, :]], dma_key="sg")
        P.op("sp", lambda e: e.dma_start(out=bsrow[0:1, :].ap, in_=self.sgu_b_in[l:l + 1, :].ap),
             reads=[], writes=[bsrow[0:1, :]], dma_key="sg")
        if SGU_STOP <= 1:
            return
        for g in range(4):
            w_ = wraw[:, g, :]
            if "nomask" not in SGU_VAR:
                P.op("dve", lambda e, w_=w_: e.tensor_tensor(out=w_.ap, in0=w_.ap, in1=tril[:, :].ap, op=ALU.mult),
                     reads=[w_, tril[:, :]], writes=[w_])
            if "notr" in SGU_VAR:
                continue
            P.op("pe", lambda e, w_=w_: e.transpose(out=trps[:, :].ap, in_=w_.ap, identity=self.ident[:, :].ap),
                 reads=[w_, self.ident[:, :]], writes=[trps[:, :]])
            P.op("act", lambda e, g=g: e.copy(out=WcT32[:, g, :].ap, in_=trps[:, :].ap),
                 reads=[trps[:, :]], writes=[WcT32[:, g, :]])
            P.op("dve", lambda e, g=g: e.tensor_copy(out=WcTb[:, g, :].ap, in_=WcT32[:, g, :].ap),
                 reads=[WcT32[:, g, :]], writes=[WcTb[:, g, :]])
        if SGU_STOP <= 2:
            return
        for cc in range(2):
            for gg in range(2):
                g = 2 * cc + gg
                kp = kps[64 * gg:64 * gg + 64, :]
                lb = self.vecs[:, lbpos + g * 64:lbpos + (g + 1) * 64]
                P.op("pe", lambda e, kp=kp, lb=lb, g=g: e.matmul(kp.ap, lhsT=lb.ap, rhs=WcT32[:, g, :].ap,
                                                                 start=True, stop=False, skip_group_check=True),
                     reads=[lb, WcT32[:, g, :]], writes=[kp])
                on = self.ones32[0:1, 0:64]
                br_ = bsrow[0:1, g * 128:(g + 1) * 128]
                P.op("pe", lambda e, kp=kp, on=on, br_=br_: e.matmul(kp.ap, lhsT=on.ap, rhs=br_.ap,
                                                                     start=False, stop=True, skip_group_check=True),
                     reads=[on, br_], writes=[kp])
            for b in range(4):
                dst = Kt4[:, cc, b, :]
                if b % 2 == 0:
                    P.op("act", lambda e, dst=dst: e.copy(out=dst.ap, in_=kps[:, :].ap), reads=[kps[:, :]], writes=[dst])
                else:
                    P.op("dve", lambda e, dst=dst: e.tensor_copy(out=dst.ap, in_=kps[:, :].ap),
                         reads=[kps[:, :]], writes=[dst])
        nb_ = 0
        if SGU_STOP <= 3:
            return
        for t in range(NT):
            tl = slice(t * TT_, (t + 1) * TT_)
            gut = gu[t % 2]
            st_ = stat[t % 2]
            for cc in range(2):
                wv = self.proj_w(l, 20 + cc)
                ps = pp[cc]
                self.proj_mm(wv, 128, lambda k, tl=tl: hT[:, k, tl], ps[:, :])
                P.op("act", lambda e, ps=ps, cc=cc, gut=gut: e.activation(out=gut[:, cc, :].ap, in_=ps[:, :].ap,
                                                                          func=AF.Gelu),
                     reads=[ps[:, :]], writes=[gut[:, cc, :]])
            if SGU_STOP <= 4:
                continue
            w0 = self.proj_w(l, 22)
            w1 = self.proj_w(l, 23)
            for b in range(4):
                bl = slice(t * TT_ + b * 128, t * TT_ + (b + 1) * 128)
                vp = vps[b % 2]
                for half, w in ((0, w0), (1, w1)):
                    vph = vp[:, half * 128:(half + 1) * 128]
                    for k in range(NK):
                        hk = hT[:, k, bl]
                        wk = w.m(lambda a, k=k: a[:, k * 128:(k + 1) * 128])
                        P.op("pe", lambda e, vph=vph, hk=hk, wk=wk, k=k: e.matmul(
                            vph.ap, lhsT=hk.ap, rhs=wk.ap, start=(k == 0), stop=(k == NK - 1)),
                            reads=[hk, w], writes=[vph])
                v_ = vg[b % 2]
                P.op("act", lambda e, v_=v_, vp=vp: e.activation(out=v_[:, :].ap, in_=vp[:, :].ap, func=AF.Gelu),
                     reads=[vp[:, :]], writes=[v_[:, :]])
                sm = st_[:, b:b + 1]
                nm = st_[:, 4 + b:5 + b]
                vs = st_[:, 8 + b:9 + b]
                P.op("dve", lambda e, v_=v_, sm=sm: e.reduce_sum(out=sm.ap, in_=v_[:, :].ap, axis=mybir.AxisListType.X),
                     reads=[v_[:, :]], writes=[sm])
                P.op("dve", lambda e, sm=sm, nm=nm: e.tensor_scalar(out=nm.ap, in0=sm.ap, scalar1=-1.0 / 256,
                                                                     scalar2=None, op0=ALU.mult),
                     reads=[sm], writes=[nm])
                cb = cen[:, b, :]
                P.op("dve", lambda e, cb=cb, v_=v_, nm=nm: e.tensor_scalar(out=cb.ap, in0=v_[:, :].ap, scalar1=nm.ap,
                                                                           scalar2=None, op0=ALU.add),
                     reads=[v_[:, :], nm], writes=[cb])
                P.op("pool", lambda e, cb=cb: e.tensor_tensor(out=sqc[:, :].ap, in0=cb.ap, in1=cb.ap, op=ALU.mult),
                     reads=[cb], writes=[sqc[:, :]])
                P.op("dve", lambda e, vs=vs: e.reduce_sum(out=vs.ap, in_=sqc[:, :].ap, axis=mybir.AxisListType.X),
                     reads=[sqc[:, :]], writes=[vs])
            if SGU_STOP <= 5:
                continue
            rs = st_[:, 12:16]
            ec = self.cst[:, 1:2]
            P.op("act", lambda e, rs=rs, st_=st_, ec=ec: e.activation(out=rs.ap, in_=st_[:, 8:12].ap, func=AF.Sqrt,
                                                                      bias=ec.ap, scale=1.0 / 256),
                 reads=[st_[:, 8:12], ec], writes=[rs])
            P.op("dve", lambda e, rs=rs: e.reciprocal(out=rs.ap, in_=rs.ap), reads=[rs], writes=[rs])
            if SGU_STOP <= 6:
                continue
            for b in range(4):
                n_ = nbf[nb_ % 2]
                nb_ += 1
                cb = cen[:, b, :]
                rb = st_[:, 12 + b:13 + b]
                P.op("dve", lambda e, n_=n_, cb=cb, rb=rb: e.tensor_scalar(out=n_[:, :].ap, in0=cb.ap, scalar1=rb.ap,
                                                                           scalar2=None, op0=ALU.mult),
                     reads=[cb, rb], writes=[n_[:, :]])
                for g in range(4):
                    mp = mps[g // 2][64 * (g % 2):64 * (g % 2) + 64, b * 128:(b + 1) * 128]
                    ng = n_[:, g * 64:(g + 1) * 64]
                    P.op("pe", lambda e, mp=mp, ng=ng, g=g: e.matmul(mp.ap, lhsT=ng.ap, rhs=WcTb[:, g, :].ap,
                                                                     start=True, stop=True, skip_group_check=True),
                         reads=[ng, WcTb[:, g, :]], writes=[mp])
            for cc in range(2):
                gcol = self.vcol(f"sgu_ln_g{l}", cc)
                k4 = Kt4[:, cc, :, :].m(lambda a: a.rearrange("p b i -> p (b i)"))
                P.op("dve", lambda e, cc=cc, gcol=gcol, k4=k4: e.scalar_tensor_tensor(
                    out=tmp[:, :].ap, in0=mps[cc][:, :].ap, scalar=gcol.ap, in1=k4.ap, op0=ALU.mult, op1=ALU.add),
                    reads=[mps[cc][:, :], gcol, k4], writes=[tmp[:, :]])
                dst = ymix[:, 6 + cc, tl]
                P.op("dve", lambda e, cc=cc, dst=dst, gut=gut: e.tensor_tensor(
                    out=dst.ap, in0=tmp[:, :].ap, in1=gut[:, cc, :].ap, op=ALU.mult),
                    reads=[tmp[:, :], gut[:, cc, :]], writes=[dst])

    def ssd(self, l, pre0, hT, ymix):
        P = self.P
        P.sb_ptr = self.mix_base
        pre = pre0 + "ssd_"
        zs = P.sb(pre + "zs", [128, 3, TT_], BF16)
        stgb = [P.sb(pre + f"stg{i}", [128, TT_ + 3], F32) for i in range(2)]
        halo = P.sb(pre + "halo", [128, 7, 4], F32)
        cacc = [P.sb(pre + f"cacc{i}", [128, TT_], F32) for i in range(2)]
        xact = P.sb(pre + "xact", [128, 7, TT_], BF16)
        negA = P.sb(pre + "negA", [128, 24], F32)
        sm = [{n: P.sb(pre + f"{n}{i}", [128, 24], F32) for n in ("t1", "dt", "a", "acs", "last", "dte", "dA", "dtd")}
              for i in range(2)]
        NBUF = 3
        arep = [P.sb(pre + f"arep{i}", [128, 128], F32) for i in range(NBUF)]
        tmpL = [P.sb(pre + f"tmpL{i}", [128, 128], F32) for i in range(NBUF)]
        LT = [P.sb(pre + f"LT{i}", [128, 128], F32) for i in range(NBUF)]
        Eb = [P.sb(pre + f"E{i}", [128, 128], BF16) for i in range(NBUF)]
        MT = [P.sb(pre + f"MT{i}", [128, 128], BF16) for i in range(12)]
        CsT = [P.sb(pre + f"CsT{i}", [128, 128], BF16) for i in range(12)]
        Xs = [P.sb(pre + f"X{i}", [128, 384], BF16) for i in range(2)]
        Xd = [P.sb(pre + f"Xd{i}", [128, 384], BF16) for i in range(2)]
        Btok = [P.sb(pre + f"Btok{i}", [128, 2, 128], BF16) for i in range(2)]
        H32 = P.sb(pre + "H32", [128, 6, 64], F32)
        Hbf = P.sb(pre + "Hbf", [128, 6, 64], BF16)
        ycat = cacc[0]
        rst = cacc[1]
        yg = P.sb(pre + "yg", [128, 3, TT_], F32)
        sqg = P.sb(pre + "sqg", [128, 3, TT_], BF16)
        pp = [P.ps(f"att_pp{i}", i, [128, 512], F32) for i in range(2)]
        yps = [P.ps(f"ssd_y{i}", 2 + i, [128, 512], F32) for i in range(3)]
        BCp = [P.ps(f"ssd_bc{i}", b_, [128, 128], F32) for i, b_ in enumerate((0, 1, 7))]
        GTp = [P.ps(f"ssd_gt{i}", 5, [128, 128], F32, byte_off=512 * i) for i in range(4)]
        trp = [P.ps(f"ssd_tr{i}", 6, [128, 128], BF16, byte_off=256 * i) for i in range(4)]
        Hps = [P.ps(f"ssd_h{i}", 6, [128, 64], F32, byte_off=1024 + 256 * i) for i in range(2)]
        dtp = P.ps("ssd_dt", 6, [128, 24], F32, byte_off=1536)
        acp = P.ps("ssd_ac", 6, [128, 24], F32, byte_off=1664)
        lap = P.ps("ssd_la", 6, [128, 24], F32, byte_off=1792)
        ssp = P.ps("ssd_ss", 7, [128, 512], F32)
        tri = self.cmat_view(1)
        negm = self.cmat_view(2)
        one_c = self.cst[:, 3:4]
        p0 = VEC_LAYOUT[f"dt_bias{l}"][0]
        dtb = self.vecs[:, p0:p0 + 24]
        p1 = VEC_LAYOUT[f"a_log{l}"][0]
        alog = self.vecs[:, p1:p1 + 24]
        P.op("act", lambda e: e.activation(out=negA[:, :].ap, in_=alog.ap, func=AF.Exp), reads=[alog], writes=[negA[:, :]])
        P.op("dve", lambda e: e.tensor_scalar(out=negA[:, :].ap, in0=negA[:, :].ap, scalar1=-1.0, scalar2=None,
                                              op0=ALU.mult), reads=[negA[:, :]], writes=[negA[:, :]])
        P.op("pool", lambda e: e.memset(H32[:, :, :].ap, 0.0), writes=[H32[:, :, :]])
        P.op("pool", lambda e: e.memset(Hbf[:, :, :].ap, 0.0), writes=[Hbf[:, :, :]])
        P.op("pool", lambda e: e.memset(halo[:, :, :].ap, 0.0), writes=[halo[:, :, :]])
        cn = {"pp": 0, "tr": 0, "ch": 0, "hd": 0, "hp": 0}
        for t in range(NT):
            tl = slice(t * TT_, (t + 1) * TT_)
            for c in range(3):
                wv = self.proj_w(l, 9 + c)
                ps = pp[cn["pp"] % 2]
                cn["pp"] += 1
                self.proj_mm(wv, 128, lambda k, tl=tl: hT[:, k, tl], ps[:, :])
                P.op("act", lambda e, ps=ps, c=c: e.activation(out=zs[:, c, :].ap, in_=ps[:, :].ap, func=AF.Silu),
                     reads=[ps[:, :]], writes=[zs[:, c, :]])
            for c in range(7):
                wv = self.proj_w(l, 12 + c)
                ps = pp[cn["pp"] % 2]
                cn["pp"] += 1
                self.proj_mm(wv, 128, lambda k, tl=tl: hT[:, k, tl], ps[:, :])
                stg = stgb[c % 2]
                sg_ = stg[:, 3:TT_ + 3]
                P.op("pool", lambda e, stg=stg, c=c: e.tensor_copy(out=stg[:, 0:3].ap, in_=halo[:, c, 0:3].ap),
                     reads=[halo[:, c, 0:3]], writes=[stg[:, 0:3]])
                P.op("act", lambda e, ps=ps, sg_=sg_: e.copy(out=sg_.ap, in_=ps[:, :].ap), reads=[ps[:, :]], writes=[sg_])
                ca = cacc[c % 2]
                for j in range(4):
                    wj = self.vcol(f"conv_w{l}_{j}", c)
                    sj = stg[:, j:j + TT_]
                    if j == 0:
                        P.op("dve", lambda e, ca=ca, sj=sj, wj=wj: e.tensor_scalar(
                            out=ca[:, :].ap, in0=sj.ap, scalar1=wj.ap, scalar2=None, op0=ALU.mult),
                            reads=[sj, wj], writes=[ca[:, :]])
                    else:
                        P.op("dve", lambda e, ca=ca, sj=sj, wj=wj: e.scalar_tensor_tensor(
                            out=ca[:, :].ap, in0=sj.ap, scalar=wj.ap, in1=ca[:, :].ap, op0=ALU.mult, op1=ALU.add),
                            reads=[sj, wj, ca[:, :]], writes=[ca[:, :]])
                cb_ = self.vcol(f"conv_b{l}", c)
                P.op("act", lambda e, ca=ca, c=c, cb_=cb_: e.activation(out=xact[:, c, :].ap, in_=ca[:, :].ap,
                                                                        func=AF.Silu, bias=cb_.ap, scale=1.0),
                     reads=[ca[:, :], cb_], writes=[xact[:, c, :]])
                P.op("pool", lambda e, c=c, stg=stg: e.tensor_copy(out=halo[:, c, 0:3].ap, in_=stg[:, TT_:TT_ + 3].ap),
                     reads=[stg[:, TT_:TT_ + 3]], writes=[halo[:, c, 0:3]])
            wdt = self.proj_w(l, 19)
            S_ = sm[t % 2]
            for ch in range(4):
                tok = slice(t * TT_ + ch * 128, t * TT_ + (ch + 1) * 128)
                dpc = dtp[:, ch * 6:(ch + 1) * 6]
                for k in range(NK):
                    hk = hT[:, k, tok]
                    wk = wdt.m(lambda a, k=k: a[:, k * 128:k * 128 + 6])
                    P.op("pe", lambda e, hk=hk, wk=wk, k=k, dpc=dpc: e.matmul(dpc.ap, lhsT=hk.ap, rhs=wk.ap,
                                                                              start=(k == 0), stop=(k == NK - 1)),
                         reads=[hk, wdt], writes=[dpc])
            t1, dt, a_, acs, last, dte, dA, dtd = (S_[n][:, :] for n in ("t1", "dt", "a", "acs", "last", "dte", "dA", "dtd"))
            P.op("dve", lambda e, t1=t1: e.tensor_tensor(out=t1.ap, in0=dtp[:, :].ap, in1=dtb.ap, op=ALU.add),
                 reads=[dtp[:, :], dtb], writes=[t1])
            P.op("act", lambda e, t1=t1: e.activation(out=t1.ap, in_=t1.ap, func=AF.Exp), reads=[t1], writes=[t1])
            P.op("act", lambda e, t1=t1, dt=dt: e.activation(out=dt.ap, in_=t1.ap, func=AF.Ln, bias=one_c.ap, scale=1.0),
                 reads=[t1, one_c], writes=[dt])
            P.op("dve", lambda e, a_=a_, dt=dt: e.tensor_tensor(out=a_.ap, in0=dt.ap, in1=negA[:, :].ap, op=ALU.mult),
                 reads=[dt, negA[:, :]], writes=[a_])
            P.op("pe", lambda e, a_=a_: e.matmul(acp[:, :].ap, lhsT=tri[:, :].ap, rhs=a_.ap, start=True, stop=True),
                 reads=[tri[:, :], a_], writes=[acp[:, :]])
            P.op("pe", lambda e, a_=a_: e.matmul(lap[:, :].ap, lhsT=self.ones32[:, :].ap, rhs=a_.ap, start=True, stop=True),
                 reads=[self.ones32[:, :], a_], writes=[lap[:, :]])
            P.op("dve", lambda e, acs=acs: e.tensor_copy(out=acs.ap, in_=acp[:, :].ap), reads=[acp[:, :]], writes=[acs])
            P.op("dve", lambda e, last=last: e.tensor_copy(out=last.ap, in_=lap[:, :].ap), reads=[lap[:, :]], writes=[last])
            P.op("dve", lambda e, dte=dte, last=last, acs=acs: e.tensor_tensor(out=dte.ap, in0=last.ap, in1=acs.ap,
                                                                              op=ALU.subtract),
                 reads=[last, acs], writes=[dte])
            P.op("act", lambda e, dte=dte: e.activation(out=dte.ap, in_=dte.ap, func=AF.Exp), reads=[dte], writes=[dte])
            P.op("act", lambda e, dA=dA, last=last: e.activation(out=dA.ap, in_=last.ap, func=AF.Exp),
                 reads=[last], writes=[dA])
            P.op("dve", lambda e, dtd=dtd, dt=dt, dte=dte: e.tensor_tensor(out=dtd.ap, in0=dt.ap, in1=dte.ap, op=ALU.mult),
                 reads=[dt, dte], writes=[dtd])
            def prologue(ch):
                lt = slice(ch * 128, (ch + 1) * 128)
                X = Xs[ch % 2]
                XD = Xd[ch % 2]
                BT = Btok[ch % 2]
                for c in range(3):
                    tp = trp[cn["tr"] % 4]
                    cn["tr"] += 1
                    src = xact[:, c, lt]
                    P.op("pe", lambda e, tp=tp, src=src: e.transpose(out=tp[:, :].ap, in_=src.ap,
                                                                     identity=self.identbf[:, :].ap),
                         reads=[src, self.identbf[:, :]], writes=[tp[:, :]])
                    for hh in range(2):
                        h = 2 * c + hh
                        xh = X[:, h * 64:(h + 1) * 64]
                        xdh = XD[:, h * 64:(h + 1) * 64]
                        tph = tp[:, hh * 64:(hh + 1) * 64]
                        dth = S_["dt"][:, ch * 6 + h:ch * 6 + h + 1]
                        ddh = S_["dtd"][:, ch * 6 + h:ch * 6 + h + 1]
                        P.op("dve", lambda e, xh=xh, tph=tph, dth=dth: e.tensor_scalar(
                            out=xh.ap, in0=tph.ap, scalar1=dth.ap, scalar2=None, op0=ALU.mult),
                            reads=[tph, dth], writes=[xh])
                        P.op("dve", lambda e, xdh=xdh, tph=tph, ddh=ddh: e.tensor_scalar(
                            out=xdh.ap, in0=tph.ap, scalar1=ddh.ap, scalar2=None, op0=ALU.mult),
                            reads=[tph, ddh], writes=[xdh])
                for g in range(2):
                    tp = trp[cn["tr"] % 4]
                    cn["tr"] += 1
                    src = xact[:, 3 + g, lt]
                    P.op("pe", lambda e, tp=tp, src=src: e.transpose(out=tp[:, :].ap, in_=src.ap,
                                                                     identity=self.identbf[:, :].ap),
                         reads=[src, self.identbf[:, :]], writes=[tp[:, :]])
                    P.op("act", lambda e, tp=tp, g=g, BT=BT: e.copy(out=BT[:, g, :].ap, in_=tp[:, :].ap),
                         reads=[tp[:, :]], writes=[BT[:, g, :]])
                for g in range(2):
                    gt = GTp[(ch % 2) * 2 + g]
                    bT = xact[:, 3 + g, lt]
                    cT = xact[:, 5 + g, lt]
                    P.op("pe", lambda e, gt=gt, bT=bT, cT=cT: e.matmul(gt[:, :].ap, lhsT=bT.ap, rhs=cT.ap,
                                                                       start=True, stop=True),
                         reads=[bT, cT], writes=[gt[:, :]])
            def stageA(ch):
                lt = slice(ch * 128, (ch + 1) * 128)
                for h in range(6):
                    g = h // 3
                    gt = GTp[(ch % 2) * 2 + g]
                    cT = xact[:, 5 + g, lt]
                    i3 = cn["hd"] % NBUF
                    cn["hd"] += 1
                    i12 = (ch % 2) * 6 + h
                    ar, tL, L_, E_, M_, C_ = arep[i3], tmpL[i3], LT[i3], Eb[i3], MT[i12], CsT[i12]
                    bc = BCp[i3]
                    ah = S_["a"][:, ch * 6 + h:ch * 6 + h + 1]
                    ach = S_["acs"][:, ch * 6 + h:ch * 6 + h + 1]
                    P.op("act", lambda e, ar=ar, ah=ah: e.activation(out=ar[:, :].ap, in_=self.ones32[:, :].ap,
                                                                     func=AF.Copy, scale=ah.ap),
                         reads=[self.ones32[:, :], ah], writes=[ar[:, :]])
                    P.op("pe", lambda e, bc=bc, ar=ar: e.matmul(bc[:, :].ap, lhsT=ar[:, :].ap, rhs=tri[:, :].ap,
                                                                start=True, stop=True),
                         reads=[ar[:, :], tri[:, :]], writes=[bc[:, :]])
                    P.op("dve", lambda e, tL=tL, bc=bc, ach=ach: e.scalar_tensor_tensor(
                        out=tL[:, :].ap, in0=bc[:, :].ap, scalar=ach.ap, in1=negm[:, :].ap,
                        op0=ALU.subtract, op1=ALU.add),
                        reads=[bc[:, :], ach, negm[:, :]], writes=[tL[:, :]])
                    P.op("act", lambda e, E_=E_, bc=bc: e.activation(out=E_[:, :].ap, in_=bc[:, :].ap, func=AF.Exp),
                         reads=[bc[:, :]], writes=[E_[:, :]])
                    P.op("act", lambda e, L_=L_, tL=tL: e.activation(out=L_[:, :].ap, in_=tL[:, :].ap, func=AF.Exp),
                         reads=[tL[:, :]], writes=[L_[:, :]])
                    P.op("dve", lambda e, M_=M_, gt=gt, L_=L_: e.tensor_tensor(out=M_[:, :].ap, in0=gt[:, :].ap,
                                                                               in1=L_[:, :].ap, op=ALU.mult),
                         reads=[gt[:, :], L_[:, :]], writes=[M_[:, :]])
                    P.op("pool", lambda e, C_=C_, cT=cT, E_=E_: e.tensor_tensor(out=C_[:, :].ap, in0=cT.ap,
                                                                                in1=E_[:, :].ap, op=ALU.mult),
                         reads=[cT, E_[:, :]], writes=[C_[:, :]])
                    bufs[(ch, h)] = (M_, C_)

            def stageB(ch):
                lt = slice(ch * 128, (ch + 1) * 128)
                X = Xs[ch % 2]
                XD = Xd[ch % 2]
                BT = Btok[ch % 2]
                for h in range(6):
                    g = h // 3
                    c, hh = h // 2, h % 2
                    M_, C_ = bufs.pop((ch, h))
                    dah = S_["dA"][:, ch * 6 + h:ch * 6 + h + 1]
                    yp = yps[c][64 * hh:64 * hh + 64, lt]
                    xh = X[:, h * 64:(h + 1) * 64]
                    xdh = XD[:, h * 64:(h + 1) * 64]
                    hb = Hbf[:, h, :]
                    P.op("pe", lambda e, yp=yp, xh=xh, M_=M_: e.matmul(yp.ap, lhsT=xh.ap, rhs=M_[:, :].ap, start=True,
                                                                       stop=False, skip_group_check=True),
                         reads=[xh, M_[:, :]], writes=[yp])
                    P.op("pe", lambda e, yp=yp, hb=hb, C_=C_: e.matmul(yp.ap, lhsT=hb.ap, rhs=C_[:, :].ap, start=False,
                                                                       stop=True, skip_group_check=True),
                         reads=[hb, C_[:, :]], writes=[yp])
                    hp_ = Hps[cn["hp"] % 2]
                    cn["hp"] += 1
                    P.op("pe", lambda e, hp_=hp_, BT=BT, g=g, xdh=xdh: e.matmul(hp_[:, :].ap, lhsT=BT[:, g, :].ap,
                                                                                rhs=xdh.ap, start=True, stop=True),
                         reads=[BT[:, g, :], xdh], writes=[hp_[:, :]])
                    h32 = H32[:, h, :]
                    P.op("dve", lambda e, h32=h32, dah=dah, hp_=hp_: e.scalar_tensor_tensor(
                        out=h32.ap, in0=h32.ap, scalar=dah.ap, in1=hp_[:, :].ap, op0=ALU.mult, op1=ALU.add),
                        reads=[h32, dah, hp_[:, :]], writes=[h32])
                    P.op("pool", lambda e, hb=hb, h32=h32: e.tensor_copy(out=hb.ap, in_=h32.ap),
                         reads=[h32], writes=[hb])
            bufs = {}
            prologue(0)
            stageA(0)
            for ch in range(4):
                if ch + 1 < 4:
                    prologue(ch + 1)
                    stageA(ch + 1)
                stageB(ch)

            for c in range(3):
                dc = self.vcol(f"dcol{l}", c)
                P.op("dve", lambda e, c=c, dc=dc: e.scalar_tensor_tensor(
                    out=ycat[:, :].ap, in0=xact[:, c, :].ap, scalar=dc.ap, in1=yps[c][:, :].ap, op0=ALU.mult, op1=ALU.add),
                    reads=[xact[:, c, :], dc, yps[c][:, :]], writes=[ycat[:, :]])
                P.op("dve", lambda e, c=c: e.tensor_tensor(out=yg[:, c, :].ap, in0=ycat[:, :].ap, in1=zs[:, c, :].ap,
                                                           op=ALU.mult),
                     reads=[ycat[:, :], zs[:, c, :]], writes=[yg[:, c, :]])
                P.op("act", lambda e, c=c: e.activation(out=sqg[:, c, :].ap, in_=yg[:, c, :].ap, func=AF.Square),
                     reads=[yg[:, c, :]], writes=[sqg[:, c, :]])
            for m in range(3):
                ks = [k for k in range(3) if abs(k - m) <= 1]
                for i, k in enumerate(ks):
                    sel = self.selbf[:, 3 * k + m, :]
                    P.op("pe", lambda e, sel=sel, k=k, i=i, ks=ks: e.matmul(ssp[:, :].ap, lhsT=sel.ap, rhs=sqg[:, k, :].ap,
                                                                           start=(i == 0), stop=(i == len(ks) - 1)),
                         reads=[sel, sqg[:, k, :]], writes=[ssp[:, :]])
                self.rsqrt_mean(rst[:, :], ssp[:, :], 1.0 / 192, RMS_EPS)
                gcol = self.vcol(f"ssd_norm{l}", m)
                dst = ymix[:, 3 + m, tl]
                P.op("dve", lambda e, m=m, gcol=gcol, dst=dst: e.scalar_tensor_tensor(
                    out=dst.ap, in0=yg[:, m, :].ap, scalar=gcol.ap, in1=rst[:, :].ap, op0=ALU.mult, op1=ALU.mult),
                    reads=[yg[:, m, :], gcol, rst[:, :]], writes=[dst])

    def build(self):
        self.declare()
        self.alloc_global()
        self.P.plan = True
        self.body()
        self.P.plan = False
        self.ws.reset_for_real()
        self.phase_id = 0
        self.rr = 0
        self.body()
        self.P.emit()
        return self.nc


def make_in_maps(inp):
    vecs = build_vecs(inp)
    x = np.ascontiguousarray(np.asarray(inp["x"], np.float32))
    shared = {}
    for nm in ("ffn1", "ffn2"):
        for wn in ("_w_gate", "_w_up", "_w_down"):
            shared[nm + wn] = np.ascontiguousarray(np.asarray(inp[nm + wn], np.float32))
    shared["w_in"] = np.ascontiguousarray(np.asarray(inp["w_in"], np.float32))
    shared["w_out"] = np.ascontiguousarray(np.asarray(inp["w_out"], np.float32))
    shared["vecs"] = vecs
    shared["cmat"] = build_cmat()
    shared["sgu_w"] = np.ascontiguousarray(np.asarray(inp["sgu_w"], np.float32))
    shared["sgu_b"] = np.ascontiguousarray(np.asarray(inp["sgu_b"], np.float32).reshape(DEPTH, 4 * 128))
    maps = []
    for c in range(NCORES):
        m = dict(shared)
        m["x"] = x[c * NSEQ:(c + 1) * NSEQ]
        maps.append(m)
    return maps


LAST_BUILDER = None


def run(inp, stages=None, trace=False, parts=("att", "ssd", "sgu")):
    global LAST_BUILDER
    b = Builder(stages=stages, parts=parts)
    LAST_BUILDER = b
    maps = make_in_maps(inp)
    nc = b.build()
    res = run_bass_kernel_spmd(nc, maps, core_ids=list(range(NCORES)), trace=trace)
    out = np.concatenate([np.asarray(r["out"]) for r in res.results], axis=0)
    return out.astype(np.float32), res


def kernel(**inputs):
    out, _ = run(inputs)
    return out
```

```python
import itertools
from contextlib import ExitStack

import numpy as np
import concourse.bass as bass
import concourse.mybir as mybir
from concourse.bass_utils import run_bass_kernel_spmd

F32 = mybir.dt.float32
BF16 = mybir.dt.bfloat16
AF = mybir.ActivationFunctionType
ALU = mybir.AluOpType
ESZ = {F32: 4, BF16: 2}

NCORES = 8
SEQ = 2048
D = 1024
NSEQ = 2
DFF = 2816
NF = DFF // 128
NK = D // 128
TT_ = 512
NT = SEQ // TT_
DEPTH = 2
D_IN = 2950
RMS_EPS = 1e-6
LN_EPS = 1e-5

SB_CELL = 256
SGU_STOP = 99
SGU_VAR = ''
PS_CELL = 2048


class V:
    __slots__ = ("ap", "cells")

    def __init__(self, ap, cells):
        self.ap = ap
        self.cells = cells

    def m(self, f):
        return V(f(self.ap), self.cells)


class TT:
    def __init__(self, handle, shape, dtype, space, base, cell, tid):
        self.h = handle
        self.shape = list(shape)
        self.dtype = dtype
        self.space = space
        self.base = base
        self.cell = cell
        self.tid = tid
        self._cache = {}
        esz = ESZ[dtype]
        dims = self.shape if space == "D" else self.shape[1:]
        st = []
        acc = esz
        for d in reversed(dims):
            st.append(acc)
            acc *= d
        self.strides = list(reversed(st))
        self.dims = dims
        self.esz = esz

    def __getitem__(self, idx):
        if not isinstance(idx, tuple):
            idx = (idx,)
        key = tuple((i.start, i.stop, i.step) if isinstance(i, slice) else i for i in idx)
        c = self._cache.get(key)
        if c is None:
            c = self._cells(idx)
            self._cache[key] = c
        return V(self.h[idx], c)

    def _cells(self, idx):
        fidx = list(idx) if self.space == "D" else list(idx[1:])
        while len(fidx) < len(self.dims):
            fidx.append(slice(None))
        rngs = []
        for i, d in zip(fidx, self.dims):
            if isinstance(i, slice):
                s, e, stp = i.indices(d)
                rngs.append((s, e, stp))
            else:
                rngs.append((i, i + 1, 1))
        cells = set()
        outer = [range(s, e, stp) for (s, e, stp) in rngs[:-1]]
        ls, le, lstp = rngs[-1]
        last_lo = ls * self.strides[-1]
        last_hi = (ls + ((le - 1 - ls) // lstp) * lstp) * self.strides[-1] + self.esz
        for combo in itertools.product(*outer):
            b = self.base + sum(i * s for i, s in zip(combo, self.strides[:-1]))
            for c in range((b + last_lo) // self.cell, (b + last_hi - 1) // self.cell + 1):
                cells.add((self.tid, c))
        return tuple(cells)


class Op:
    __slots__ = ("eng", "idx", "fn", "deps", "dma_deps", "signal", "sigval", "dma_key", "dma_cnt", "stage")


ENGS = ["pe", "act", "dve", "pool", "sp"]


class Prog:
    def __init__(self, nc):
        self.nc = nc
        self.ops = {e: [] for e in ENGS}
        self.cellstate = {}
        self.dma_cnt = {}
        self.plan = False
        self.stage = ""
        self.ins_stage = {}
        self.n_tid = 0
        self.tts = {}
        self.arena = None
        self.arena_base = 0
        self.sb_ptr = 0
        self.sb_cap = 0
        self.psum_banks = []

    def init_mem(self, sb_bytes):
        self.arena_base = (self.nc.sbuf_base + 63) // 64 * 64
        self.sb_cap = min(sb_bytes, (self.nc.sbuf_top - self.arena_base) // 64 * 64)
        for b in range(8):
            self.psum_banks.append(self.nc.alloc_psum_tensor(f"bank{b}", [128, 512], F32))

    def sb(self, name, shape, dtype, off=None):
        if name in self.tts:
            return self.tts[name]
        n = ESZ[dtype]
        for d in shape[1:]:
            n *= d
        if off is None:
            off = (self.sb_ptr + 63) // 64 * 64
            self.sb_ptr = off + n
        assert off + n <= self.sb_cap, f"SBUF overflow {name}: {off}+{n} > {self.sb_cap}"
        h = self.nc.alloc_sbuf_tensor_at(name, list(shape), dtype, offset=self.arena_base + off)
        t = TT(h, shape, dtype, "S", off, SB_CELL, "S")
        self.tts[name] = t
        return t

    def ps(self, name, bank, shape, dtype, byte_off=0):
        if name in self.tts:
            return self.tts[name]
        n = ESZ[dtype]
        for d in shape[1:]:
            n *= d
        assert byte_off + n <= 2048
        t = PsTT(self.psum_banks[bank], shape, dtype, bank, byte_off)
        self.tts[name] = t
        return t

    def dram(self, name, shape, dtype, kind="Internal", cell=None):
        if name in self.tts:
            return self.tts[name]
        h = self.nc.dram_tensor(name, list(shape), dtype, kind=kind)
        self.n_tid += 1
        t = TT(h, shape, dtype, "D", 0, cell or (1 << 40), f"D{self.n_tid}")
        self.tts[name] = t
        return t

    def op(self, eng, fn, reads=(), writes=(), dma_key=None):
        if self.plan:
            return
        o = Op()
        o.eng = eng
        o.fn = fn
        o.signal = False
        o.sigval = None
        o.dma_key = dma_key
        o.idx = len(self.ops[eng])
        o.stage = self.stage
        deps = {}
        dma_deps = {}

        def add(p, raw):
            if p is None:
                return
            if p.dma_key is not None:
                k = p.dma_key
                dma_deps[k] = self.dma_cnt[k]
                return
            if p.eng == eng and dma_key is None:
                if eng == "pe":
                    return
            if deps.get(p.eng, -1) < p.idx:
                deps[p.eng] = p.idx

        cs = self.cellstate
        for v in reads:
            for c in v.cells:
                st = cs.get(c)
                if st is not None:
                    add(st[0], True)
                    if c[0] == "P":
                        for r in st[1]:
                            if r.eng != eng:
                                add(r, False)
        for v in writes:
            for c in v.cells:
                st = cs.get(c)
                if st is not None:
                    add(st[0], False)
                    for r in st[1]:
                        add(r, False)
        for v in writes:
            for c in v.cells:
                cs[c] = [o, []]
        for v in reads:
            for c in v.cells:
                st = cs.get(c)
                if st is None:
                    cs[c] = [None, [o]]
                elif st[0] is not o:
                    st[1].append(o)
        if dma_key is not None:
            self.dma_cnt[dma_key] = self.dma_cnt.get(dma_key, 0) + 1
            o.dma_cnt = self.dma_cnt[dma_key]
        else:
            o.dma_cnt = 0
        for e, i in deps.items():
            self.ops[e][i].signal = True
        o.deps = deps
        o.dma_deps = dma_deps
        self.ops[eng].append(o)

    def emit(self):
        nc = self.nc
        for e in ENGS:
            cnt = 0
            for o in self.ops[e]:
                if o.signal:
                    cnt += 1
                    o.sigval = cnt
        with ExitStack() as es:
            esem = {e: es.enter_context(nc.semaphore(f"sem_{e}")) for e in ENGS if e != "sp"}
            dsem = {k: es.enter_context(nc.semaphore(f"dma_{k}")) for k in self.dma_cnt}
            block = es.enter_context(nc.Block())

            def run(ename, eng):
                waited = {}
                for o in self.ops[ename]:
                    for pe_, pi in o.deps.items():
                        val = self.ops[pe_][pi].sigval
                        key = ("e", pe_)
                        if waited.get(key, 0) < val:
                            eng.wait_ge(esem[pe_], val)
                            waited[key] = val
                    for k, c in o.dma_deps.items():
                        key = ("d", k)
                        if waited.get(key, 0) < 16 * c:
                            eng.wait_ge(dsem[k], 16 * c)
                            waited[key] = 16 * c
                    ins = o.fn(eng)
                    try:
                        self.ins_stage[ins.ins.name] = (ename, o.stage)
                    except Exception:
                        pass
                    if o.dma_key is not None:
                        ins.then_inc(dsem[o.dma_key], 16)
                    elif o.signal:
                        ins.then_inc(esem[ename], 1)
                if ename == "sp":
                    for k, c in self.dma_cnt.items():
                        if waited.get(("d", k), 0) < 16 * c:
                            eng.wait_ge(dsem[k], 16 * c)

            @block.tensor
            def _(eng):
                run("pe", eng)

            @block.scalar
            def _(eng):
                run("act", eng)

            @block.vector
            def _(eng):
                run("dve", eng)

            @block.gpsimd
            def _(eng):
                run("pool", eng)

            @block.sync
            def _(eng):
                run("sp", eng)


class PsTT(TT):
    def __init__(self, bank_handle, shape, dtype, bank, byte_off):
        n_el = 2048 // ESZ[dtype]
        full = bank_handle[:].bitcast(dtype) if dtype != F32 else bank_handle[:]
        self.full = full
        self.shape = list(shape)
        self.dtype = dtype
        self.space = "P"
        self.base = bank * 2048 + byte_off
        self.cell = PS_CELL
        self.tid = "P"
        self._cache = {}
        esz = ESZ[dtype]
        self.esz = esz
        dims = self.shape[1:]
        st = []
        acc = esz
        for d in reversed(dims):
            st.append(acc)
            acc *= d
        self.strides = list(reversed(st))
        self.dims = dims
        n = 1
        for d in dims:
            n *= d
        e0 = byte_off // esz
        flat = full[:, e0:e0 + n]
        if len(dims) == 1:
            self.view = flat
        elif len(dims) == 2:
            self.view = flat.rearrange("p (a b) -> p a b", b=dims[1])
        elif len(dims) == 3:
            self.view = flat.rearrange("p (a b c) -> p a b c", b=dims[1], c=dims[2])
        else:
            raise ValueError

    def __getitem__(self, idx):
        if not isinstance(idx, tuple):
            idx = (idx,)
        key = tuple((i.start, i.stop, i.step) if isinstance(i, slice) else i for i in idx)
        c = self._cache.get(key)
        if c is None:
            c = self._cells(idx)
            self._cache[key] = c
        return V(self.view[idx], c)


SLOT_ELEMS = NF * 128
NSLOT = 6
PREFETCH = 4


class WStream:
    def __init__(self, prog):
        self.prog = prog
        self.plan_list = []
        self.i_next = 0
        self.i_issued = 0
        self.slots = None

    def reset_for_real(self):
        self.i_next = 0
        self.i_issued = 0

    def _issue(self, i):
        src_fn, nelem = self.plan_list[i]
        s = i % NSLOT
        dst = self.slots[:, s, 0:nelem]
        src = src_fn()
        self.prog.op("sp", lambda e, d=dst, s_=src: e.dma_start(out=d.ap, in_=s_.ap),
                     reads=[src], writes=[dst], dma_key=f"w{s}")

    def next(self, src_fn, nelem):
        if self.prog.plan:
            self.plan_list.append((src_fn, nelem))
            return self.slots[:, 0, 0:nelem], 0
        i = self.i_next
        self.i_next += 1
        while self.i_issued < min(len(self.plan_list), i + 1 + PREFETCH):
            self._issue(self.i_issued)
            self.i_issued += 1
        return self.slots[:, i % NSLOT, 0:nelem], i % NSLOT


def _cols(v):
    v = np.asarray(v, np.float32)
    return np.ascontiguousarray(v.reshape(-1, 128).T)


VEC_LAYOUT = {}


N_CMAT = 15


def build_vecs(inp):
    cols = []
    pos = 0
    VEC_LAYOUT.clear()

    def add(name, arr):
        nonlocal pos
        arr = np.asarray(arr, np.float32)
        VEC_LAYOUT[name] = (pos, arr.shape[1])
        cols.append(arr)
        pos += arr.shape[1]

    def bc(v):
        v = np.asarray(v, np.float32).reshape(1, -1)
        return np.broadcast_to(v, (128, v.shape[1]))

    for l in range(DEPTH):
        add(f"ffn1_norm{l}", _cols(inp["ffn1_norm"][l]))
        add(f"mix_norm{l}", _cols(inp["mix_norm"][l]))
        add(f"ffn2_norm{l}", _cols(inp["ffn2_norm"][l]))
    add("final_norm", _cols(inp["final_norm"]))
    for l in range(DEPTH):
        for j in range(4):
            add(f"conv_w{l}_{j}", _cols(inp["conv_w"][l][j]))
        add(f"conv_b{l}", _cols(inp["conv_b"][l]))
        add(f"ssd_norm{l}", _cols(inp["ssd_norm"][l]))
        add(f"dcol{l}", _cols(np.repeat(np.asarray(inp["d_skip"][l], np.float32), 64)))
        add(f"sgu_ln_g{l}", _cols(inp["sgu_ln_g"][l]))
        add(f"dt_bias{l}", bc(np.tile(np.asarray(inp["dt_bias"][l], np.float32), 4)))
        add(f"a_log{l}", bc(np.tile(np.asarray(inp["a_log"][l], np.float32), 4)))
        add(f"sgu_ln_b{l}", bc(inp["sgu_ln_b"][l]))
    return np.ascontiguousarray(np.concatenate(cols, axis=1))


def n_vec_cols():
    return DEPTH * 3 * NK + NK + DEPTH * (28 + 7 + 3 + 3 + 2 + 24 + 24 + 256)


def build_cmat():
    i = np.arange(128)
    m = []
    m.append(np.eye(128))
    m.append((i[:, None] <= i[None, :]) * 1.0)
    m.append(np.where(i[:, None] <= i[None, :], 0.0, -30000.0))
    m.append((i[None, :] >= i[:, None]) * 1.0)
    m.append((i[None, :] <= i[:, None]) * 1.0)
    for k in range(3):
        for mm in range(3):
            gi = (128 * k + i) // 192
            go = (128 * mm + i) // 192
            m.append((gi[:, None] == go[None, :]) * 1.0)
    m.append((i[None, :] <= i[:, None]) * 1.0)
    return np.ascontiguousarray(np.concatenate(m, axis=1).astype(np.float32))


class Builder:
    def __init__(self, stages=None, dump=None, parts=("att", "ssd", "sgu")):
        self.parts = parts
        self.nc = bass.Bass("TRN2", target_bir_lowering=False, dynamic_dma_scratch_size=64)
        self.P = Prog(self.nc)
        self.ws = WStream(self.P)
        self.stages = stages
        self.dump = dump
        self.rr = 0

    def declare(self):
        P = self.P
        nc = self.nc
        ext = lambda n, s, dt=F32: P.dram(n, s, dt, kind="ExternalInput")
        self.x_in = ext("x", [NSEQ, SEQ, D])
        self.w = {}
        for nm in ("ffn1", "ffn2"):
            self.w[nm + "_w_gate"] = ext(nm + "_w_gate", [DEPTH, D, DFF])
            self.w[nm + "_w_up"] = ext(nm + "_w_up", [DEPTH, D, DFF])
            self.w[nm + "_w_down"] = ext(nm + "_w_down", [DEPTH, DFF, D])
        self.w["w_in"] = ext("w_in", [DEPTH, D, D_IN])
        self.w["w_out"] = ext("w_out", [DEPTH, D, D])
        self.vecs_in = ext("vecs", [128, n_vec_cols()])
        self.out = P.dram("out", [NSEQ, SEQ, D], F32, kind="ExternalOutput")
        self.WguR = P.dram("WguR", [DEPTH * 2, NF, 128, 2 * NK * 128], BF16, cell=128 * 2 * NK * 128 * 2)
        self.WdR = P.dram("WdR", [DEPTH * 2, NK, 128, NF * 128], BF16, cell=128 * NF * 128 * 2)
        self.WinR = P.dram("WinR", [DEPTH, 24, 128, NK * 128], BF16, cell=128 * NK * 128 * 2)
        self.WoutR = P.dram("WoutR", [DEPTH, NK, 128, NK * 128], BF16, cell=128 * NK * 128 * 2)
        self.cmat_in = ext("cmat", [128, N_CMAT * 128])
        self.sgu_w_in = ext("sgu_w", [DEPTH, 4, 128, 128])
        self.sgu_b_in = ext("sgu_b", [DEPTH, 4 * 128])

    def alloc_global(self):
        P = self.P
        P.init_mem(229056)
        self.xT = P.sb("xT", [128, NK, SEQ], F32)
        self.vecs = P.sb("vecs_sb", [128, n_vec_cols()], F32)
        self.cmat = P.sb("cmat_sb", [128, N_CMAT, 128], F32)
        self.ident = self.cmat_view(0)
        self.identbf = P.sb("identbf", [128, 128], BF16)
        self.mask2 = P.sb("mask2", [128, 256], BF16)
        self.ones32 = P.sb("ones32", [128, 128], F32)
        self.selbf = P.sb("selbf", [128, 9, 128], BF16)
        self.ones_bf = P.sb("ones_bf", [128, 128], BF16)
        self.cst = P.sb("cst", [128, 8], F32)
        self.ws.slots = P.sb("wslots", [128, NSLOT, SLOT_ELEMS], BF16)
        self.glob_end = P.sb_ptr

    def cmat_view(self, i):
        class _C:
            def __getitem__(s_, idx):
                if not isinstance(idx, tuple):
                    idx = (idx,)
                return self.cmat[(idx[0], i) + tuple(idx[1:])]
        return _C()

    def phase(self):
        self.P.sb_ptr = self.glob_end
        self.phase_id = getattr(self, "phase_id", 0) + 1
        return f"ph{self.phase_id}_"

    def vcol(self, name, k):
        pos, n = VEC_LAYOUT[name]
        return self.vecs[:, pos + k:pos + k + 1]

    def any_eng(self):
        self.rr += 1
        return ["act", "dve", "pool"][self.rr % 3]

    def load_consts(self):
        P = self.P
        P.op("sp", lambda e: e.dma_start(out=self.vecs[:, :].ap, in_=self.vecs_in[:, :].ap),
             reads=[], writes=[self.vecs[:, :]], dma_key="c0")
        cm = self.cmat[:, :, :]
        P.op("sp", lambda e: e.dma_start(out=cm.ap, in_=self.cmat_in[:, :].m(
            lambda a: a.rearrange("p (c i) -> p c i", i=128)).ap), reads=[], writes=[cm], dma_key="c0")
        P.op("dve", lambda e: e.tensor_copy(out=self.identbf[:, :].ap, in_=self.ident[:, :].ap),
             reads=[self.ident[:, :]], writes=[self.identbf[:, :]])
        P.op("dve", lambda e: e.tensor_copy(out=self.mask2[:, :].ap, in_=self.cmat[:, 3:5, :].m(
            lambda a: a.rearrange("p c i -> p (c i)")).ap), reads=[self.cmat[:, 3:5, :]], writes=[self.mask2[:, :]])
        P.op("pool", lambda e: e.memset(self.ones32[:, :].ap, 1.0), writes=[self.ones32[:, :]])
        P.op("dve", lambda e: e.tensor_copy(out=self.selbf[:, :, :].ap, in_=self.cmat[:, 5:14, :].ap),
             reads=[self.cmat[:, 5:14, :]], writes=[self.selbf[:, :, :]])
        P.op("pool", lambda e: e.memset(self.cst[:, 3:4].ap, 1.0), writes=[self.cst[:, 3:4]])
        P.op("pool", lambda e: e.memset(self.ones_bf[:, :].ap, 1.0), writes=[self.ones_bf[:, :]])
        P.op("pool", lambda e: e.memset(self.cst[:, 0:1].ap, RMS_EPS), writes=[self.cst[:, 0:1]])
        P.op("pool", lambda e: e.memset(self.cst[:, 1:2].ap, LN_EPS), writes=[self.cst[:, 1:2]])
        P.op("pool", lambda e: e.memset(self.cst[:, 2:3].ap, 0.0), writes=[self.cst[:, 2:3]])

    def prepass(self, sl):
        P = self.P
        pre = self.phase()
        CW = 256
        NB_ = 3
        st32 = [P.sb(pre + f"st32_{i}", [128, NF, CW], F32) for i in range(NB_)]
        stbf = [P.sb(pre + f"stbf_{i}", [128, 2, NF, 128], BF16) for i in range(NB_)]
        it = 0

        def one(src_t, l, nk, c0, dst_fn, ncols=CW):
            nonlocal it
            b = it % NB_
            eng = "dve" if it % 2 == 0 else "act"
            it += 1
            s32 = st32[b][:, 0:nk, 0:ncols]
            src = src_t[l, :, c0:c0 + ncols].m(lambda a: a.rearrange("(k p) n -> p k n", p=128))
            P.op("sp", lambda e: e.dma_start(out=s32.ap, in_=src.ap), reads=[], writes=[s32], dma_key=f"pi{b}")
            if ncols == CW:
                sbf = stbf[b][:, :, 0:nk, :]
                s32p = s32.m(lambda a: a.rearrange("p k (j i) -> p j k i", j=2))
            else:
                sbf = stbf[b][:, 0, 0:nk, 0:ncols]
                s32p = s32
            if eng == "act":
                P.op("act", lambda e: e.copy(out=sbf.ap, in_=s32p.ap), reads=[s32], writes=[sbf])
            else:
                P.op(eng, lambda e: e.tensor_copy(out=sbf.ap, in_=s32p.ap), reads=[s32], writes=[sbf])
            dst = dst_fn()
            P.op("act", lambda e: e.dma_start(out=dst.ap, in_=sbf.ap), reads=[sbf], writes=[dst], dma_key=f"po{b}")

        for l in range(DEPTH):
            for fi, nm in enumerate(("ffn1", "ffn2")):
                if (nm, l) not in sl:
                    continue
                idx = l * 2 + fi
                for gu, wn in enumerate(("_w_gate", "_w_up")):
                    for g in range(NF // 2):
                        one(self.w[nm + wn], l, NK, g * CW,
                            lambda idx=idx, g=g, gu=gu: self.WguR[idx, 2 * g:2 * g + 2, :, gu * NK * 128:(gu + 1) * NK * 128]
                            .m(lambda a: a.rearrange("f p (k i) -> p f k i", i=128)))
                for g in range(NK // 2):
                    one(self.w[nm + "_w_down"], l, NF, g * CW,
                        lambda idx=idx, g=g: self.WdR[idx, 2 * g:2 * g + 2, :, :]
                        .m(lambda a: a.rearrange("c p (f i) -> p c f i", i=128)))
        for l in range(DEPTH):
            if ("mix", l) not in sl:
                continue

            def win_dst(l, c, n):
                if n == 2:
                    return lambda: self.WinR[l, c:c + 2, :, :].m(lambda a: a.rearrange("c p (k i) -> p c k i", i=128))
                return None
            for g in range(9):
                one(self.w["w_in"], l, NK, g * CW, win_dst(l, 2 * g, 2))
            one(self.w["w_in"], l, NK, 2304, lambda l=l: self.WinR[l, 18, :, :].m(
                lambda a: a.rearrange("p (k i) -> p k i", i=128)), ncols=128)
            one(self.w["w_in"], l, NK, 2432, lambda l=l: self.WinR[l, 19, :, :].m(
                lambda a: a.rearrange("p (k i) -> p k i", i=128)), ncols=128)
            one(self.w["w_in"], l, NK, 2438, win_dst(l, 20, 2))
            one(self.w["w_in"], l, NK, 2694, win_dst(l, 22, 2))
            for g in range(4):
                one(self.w["w_out"], l, NK, g * CW,
                    lambda l=l, g=g: self.WoutR[l, 2 * g:2 * g + 2, :, :]
                    .m(lambda a: a.rearrange("c p (k i) -> p c k i", i=128)))

    def load_x(self, s):
        P = self.P
        pre = self.phase()
        stg = [P.sb(pre + f"xs{i}", [128, D], F32) for i in range(3)]
        pst = [P.ps(f"tp{i}", i, [128, 4, 128], F32) for i in range(4)]
        n = 0
        for b in range(SEQ // 128):
            sg = stg[b % 3]
            src = self.x_in[s, b * 128:(b + 1) * 128, :]
            P.op("sp", lambda e, sg=sg, src=src: e.dma_start(out=sg[:, :].ap, in_=src.ap),
                 reads=[], writes=[sg[:, :]], dma_key=f"xi{b % 3}")
            for half in range(2):
                pt = pst[n % 4]
                n += 1
                for j in range(4):
                    k = half * 4 + j
                    P.op("pe", lambda e, pt=pt, j=j, k=k, sg=sg: e.transpose(
                        out=pt[:, j, :].ap, in_=sg[:, k * 128:(k + 1) * 128].ap, identity=self.ident[:, :].ap),
                        reads=[sg[:, k * 128:(k + 1) * 128], self.ident[:, :]], writes=[pt[:, j, :]])
                dst = self.xT[:, half * 4:half * 4 + 4, b * 128:(b + 1) * 128]
                if n % 2 == 0:
                    P.op("act", lambda e, pt=pt, dst=dst: e.copy(out=dst.ap, in_=pt[:, :, :].ap),
                         reads=[pt[:, :, :]], writes=[dst])
                else:
                    P.op("dve", lambda e, pt=pt, dst=dst: e.tensor_copy(out=dst.ap, in_=pt[:, :, :].ap),
                         reads=[pt[:, :, :]], writes=[dst])

    def rms_tile(self, pre, t, gname, hT, ss_ps, sq, rstd, dst_fn=None):
        P = self.P
        tl = slice(t * TT_, (t + 1) * TT_)
        for k in range(NK):
            q = sq[k % 2]
            xin = self.xT[:, k, tl]
            P.op("act", lambda e, q=q, xin=xin: e.activation(out=q[:, :].ap, in_=xin.ap, func=AF.Square),
                 reads=[xin], writes=[q[:, :]])
            P.op("pe", lambda e, q=q, k=k: e.matmul(ss_ps[:, :].ap, lhsT=self.ones_bf[:, :].ap, rhs=q[:, :].ap,
                                                    start=(k == 0), stop=(k == NK - 1)),
                 reads=[q[:, :], self.ones_bf[:, :]], writes=[ss_ps[:, :]])
        self.rsqrt_mean(rstd[:, :], ss_ps[:, :], 1.0 / D, RMS_EPS)
        for k in range(NK):
            xin = self.xT[:, k, tl]
            g = self.vcol(gname, k)
            dst = hT[:, k, :] if dst_fn is None else dst_fn(k)
            eng = "dve"
            P.op(eng, lambda e, xin=xin, g=g, dst=dst: e.scalar_tensor_tensor(
                out=dst.ap, in0=xin.ap, scalar=g.ap, in1=rstd[:, :].ap, op0=ALU.mult, op1=ALU.mult),
                reads=[xin, g, rstd[:, :]], writes=[dst])

    def rsqrt_mean(self, dst, src_ps, scale, eps):
        P = self.P
        ec = self.cst[:, 0:1] if eps == RMS_EPS else self.cst[:, 1:2]
        P.op("act", lambda e: e.activation(out=dst.ap, in_=src_ps.ap, func=AF.Sqrt, bias=ec.ap, scale=scale),
             reads=[src_ps, ec], writes=[dst])
        P.op("dve", lambda e: e.reciprocal(out=dst.ap, in_=dst.ap), reads=[dst], writes=[dst])

    def ffn(self, l, fi):
        P = self.P
        pre = self.phase()
        idx = l * 2 + fi
        gname = f"ffn{fi + 1}_norm{l}"
        hT = [P.sb(pre + f"hT{i}", [128, NK, TT_], BF16) for i in range(2)]
        sq = [P.sb(pre + f"sq{i}", [128, TT_], BF16) for i in range(2)]
        rstd = P.sb(pre + "rstd", [128, TT_], F32)
        sg = [P.sb(pre + f"sg{i}", [128, TT_], F32) for i in range(2)]
        aT = P.sb(pre + "aT", [128, NF, TT_], BF16)
        ss_ps = P.ps("ffn_ss", 0, [128, TT_], F32)
        pg = [P.ps(f"ffn_pg{i}", 1 + i, [128, TT_], F32) for i in range(2)]
        pu = [P.ps(f"ffn_pu{i}", 3 + i, [128, TT_], F32) for i in range(2)]
        py = [P.ps(f"ffn_py{i}", 5 + i, [128, TT_], F32) for i in range(2)]
        self.rms_tile(pre, 0, gname, hT[0], ss_ps, sq, rstd)
        for t in range(NT):
            tl = slice(t * TT_, (t + 1) * TT_)
            h = hT[t % 2]
            for f in range(NF):
                wv, _ = self.ws.next(lambda f=f: self.WguR[idx, f, :, :], 2 * NK * 128)
                g_ps = pg[f % 2]
                u_ps = pu[f % 2]
                for gu, ps_ in ((0, g_ps), (1, u_ps)):
                    for k in range(NK):
                        lw = wv.m(lambda a, gu=gu, k=k: a[:, (gu * NK + k) * 128:(gu * NK + k + 1) * 128])
                        P.op("pe", lambda e, ps_=ps_, lw=lw, h=h, k=k: e.matmul(
                            ps_[:, :].ap, lhsT=lw.ap, rhs=h[:, k, :].ap, start=(k == 0), stop=(k == NK - 1)),
                            reads=[wv, h[:, k, :]], writes=[ps_[:, :]])
                s_ = sg[f % 2]
                P.op("act", lambda e, s_=s_, g_ps=g_ps: e.activation(out=s_[:, :].ap, in_=g_ps[:, :].ap, func=AF.Silu),
                     reads=[g_ps[:, :]], writes=[s_[:, :]])
                dst = aT[:, f, :]
                P.op("dve", lambda e, s_=s_, u_ps=u_ps, dst=dst: e.tensor_tensor(
                    out=dst.ap, in0=u_ps[:, :].ap, in1=s_[:, :].ap, op=ALU.mult),
                    reads=[u_ps[:, :], s_[:, :]], writes=[dst])
            if t + 1 < NT:
                self.rms_tile(pre, t + 1, gname, hT[(t + 1) % 2], ss_ps, sq, rstd)
            for c in range(NK):
                wv, _ = self.ws.next(lambda c=c: self.WdR[idx, c, :, :], NF * 128)
                y_ps = py[c % 2]
                for f in range(NF):
                    lw = wv.m(lambda a, f=f: a[:, f * 128:(f + 1) * 128])
                    P.op("pe", lambda e, y_ps=y_ps, lw=lw, f=f: e.matmul(
                        y_ps[:, :].ap, lhsT=lw.ap, rhs=aT[:, f, :].ap, start=(f == 0), stop=(f == NF - 1)),
                        reads=[wv, aT[:, f, :]], writes=[y_ps[:, :]])
                xin = self.xT[:, c, tl]
                P.op("dve", lambda e, y_ps=y_ps, xin=xin: e.scalar_tensor_tensor(
                    out=xin.ap, in0=y_ps[:, :].ap, scalar=0.5, in1=xin.ap, op0=ALU.mult, op1=ALU.add),
                    reads=[y_ps[:, :], xin], writes=[xin])

    def store_out(self, s, normalize=True):
        P = self.P
        pre = self.phase()
        sq = [P.sb(pre + f"sq{i}", [128, TT_], BF16) for i in range(2)]
        rstd = P.sb(pre + "rstd", [128, TT_], F32)
        yT = [P.sb(pre + f"yT{i}", [128, NK, TT_], F32) for i in range(2)]
        stg = [P.sb(pre + f"os{i}", [128, D], F32) for i in range(3)]
        ss_ps = P.ps("ffn_ss", 0, [128, TT_], F32)
        pst = [P.ps(f"otp{i}", 1 + i, [128, 4, 128], F32) for i in range(4)]
        n = 0
        nb = 0
        for t in range(NT):
            tl = slice(t * TT_, (t + 1) * TT_)
            y = yT[t % 2]
            if normalize:
                for k in range(NK):
                    q = sq[k % 2]
                    xin = self.xT[:, k, tl]
                    P.op("act", lambda e, q=q, xin=xin: e.activation(out=q[:, :].ap, in_=xin.ap, func=AF.Square),
                         reads=[xin], writes=[q[:, :]])
                    P.op("pe", lambda e, q=q, k=k: e.matmul(ss_ps[:, :].ap, lhsT=self.ones_bf[:, :].ap, rhs=q[:, :].ap,
                                                            start=(k == 0), stop=(k == NK - 1)),
                         reads=[q[:, :], self.ones_bf[:, :]], writes=[ss_ps[:, :]])
                self.rsqrt_mean(rstd[:, :], ss_ps[:, :], 1.0 / D, RMS_EPS)
                for k in range(NK):
                    xin = self.xT[:, k, tl]
                    g = self.vcol("final_norm", k)
                    dst = y[:, k, :]
                    eng = "dve"
                    P.op(eng, lambda e, xin=xin, g=g, dst=dst: e.scalar_tensor_tensor(
                        out=dst.ap, in0=xin.ap, scalar=g.ap, in1=rstd[:, :].ap, op0=ALU.mult, op1=ALU.mult),
                        reads=[xin, g, rstd[:, :]], writes=[dst])
            for bb in range(TT_ // 128):
                b = t * (TT_ // 128) + bb
                sg = stg[nb % 3]
                nb += 1
                for half in range(2):
                    pt = pst[n % 4]
                    n += 1
                    for j in range(4):
                        k = half * 4 + j
                        src = (y[:, k, bb * 128:(bb + 1) * 128] if normalize
                               else self.xT[:, k, b * 128:(b + 1) * 128])
                        P.op("pe", lambda e, pt=pt, j=j, src=src: e.transpose(
                            out=pt[:, j, :].ap, in_=src.ap, identity=self.ident[:, :].ap),
                            reads=[src, self.ident[:, :]], writes=[pt[:, j, :]])
                    dst = sg[:, half * 512:(half + 1) * 512]
                    pin = pt[:, :, :].m(lambda a: a.rearrange("p a b -> p (a b)"))
                    if n % 2 == 0:
                        P.op("act", lambda e, pin=pin, dst=dst: e.copy(out=dst.ap, in_=pin.ap),
                             reads=[pin], writes=[dst])
                    else:
                        P.op("dve", lambda e, pin=pin, dst=dst: e.tensor_copy(out=dst.ap, in_=pin.ap),
                             reads=[pin], writes=[dst])
                dsto = self.out[s, b * 128:(b + 1) * 128, :]
                P.op("sp", lambda e, sg=sg, dsto=dsto: e.dma_start(out=dsto.ap, in_=sg[:, :].ap),
                     reads=[sg[:, :]], writes=[], dma_key=f"xo{(nb - 1) % 3}")

    def body(self):
        st = self.stages
        full = [(n, l) for l in range(DEPTH) for n in ("ffn1", "mix", "ffn2")]
        if isinstance(st, list):
            sl = st
        elif st is None:
            sl = full
        else:
            sl = full[:st]
        self.P.stage = "consts"
        self.load_consts()
        self.P.stage = "prepass"
        self.prepass(sl)
        for s in range(NSEQ):
            self.P.stage = f"s{s}.load_x"
            self.load_x(s)
            for name, l in sl:
                self.P.stage = f"s{s}.{name}{l}"
                if name == "ffn1":
                    self.ffn(l, 0)
                elif name == "ffn2":
                    self.ffn(l, 1)
                else:
                    self.mixer(l)
            self.P.stage = f"s{s}.store"
            self.store_out(s, normalize=(st is None))

    def mixer(self, l):
        P = self.P
        pre = self.phase()
        parts = self.parts
        hT = P.sb(pre + "hT", [128, NK, SEQ], BF16)
        ymix = P.sb(pre + "ymix", [128, NK, SEQ], BF16)
        self.mix_base = P.sb_ptr
        sq = [P.sb(pre + f"sq{i}", [128, TT_], BF16) for i in range(2)]
        rstd = P.sb(pre + "rstd", [128, TT_], F32)
        ss_ps = P.ps("ffn_ss", 0, [128, TT_], F32)
        for t in range(NT):
            tl = slice(t * TT_, (t + 1) * TT_)
            self.rms_tile(pre, t, f"mix_norm{l}", None, ss_ps, sq, rstd, dst_fn=lambda k, tl=tl: hT[:, k, tl])
        if len(parts) < 3:
            for k in range(NK):
                P.op("pool", lambda e, k=k: e.memset(ymix[:, k, :].ap, 0.0), writes=[ymix[:, k, :]])
        st0 = P.stage
        if "att" in parts:
            P.stage = st0 + ".att"
            self.attention(l, pre, hT, ymix)
        if "ssd" in parts:
            P.stage = st0 + ".ssd"
            self.ssd(l, pre, hT, ymix)
        if "sgu" in parts:
            P.stage = st0 + ".sgu"
            self.sgu(l, pre, hT, ymix)
        P.stage = st0 + ".wout"
        self.wout(l, ymix)

    def proj_w(self, l, c):
        wv, _ = self.ws.next(lambda: self.WinR[l, c, :, :], NK * 128)
        return wv

    def proj_mm(self, wv, ncols, rhs_fn, ps):
        P = self.P
        for k in range(NK):
            lw = wv.m(lambda a, k=k: a[:, k * 128:k * 128 + ncols])
            r = rhs_fn(k)
            P.op("pe", lambda e, lw=lw, r=r, k=k: e.matmul(ps.ap, lhsT=lw.ap, rhs=r.ap, start=(k == 0),
                                                           stop=(k == NK - 1)),
                 reads=[wv, r], writes=[ps])

    def wout(self, l, ymix):
        P = self.P
        py = [P.ps(f"ffn_py{i}", 5 + i, [128, TT_], F32) for i in range(2)]
        n = 0
        for co in range(NK):
            wv, _ = self.ws.next(lambda co=co: self.WoutR[l, co, :, :], NK * 128)
            for t in range(NT):
                tl = slice(t * TT_, (t + 1) * TT_)
                y_ps = py[n % 2]
                n += 1
                self.proj_mm(wv, 128, lambda k, tl=tl: ymix[:, k, tl], y_ps[:, :])
                xin = self.xT[:, co, tl]
                P.op("dve", lambda e, y_ps=y_ps, xin=xin: e.tensor_tensor(
                    out=xin.ap, in0=y_ps[:, :].ap, in1=xin.ap, op=ALU.add),
                    reads=[y_ps[:, :], xin], writes=[xin])

    def attention(self, l, pre0, hT, ymix):
        P = self.P
        P.sb_ptr = self.mix_base
        pre = pre0 + "att_"
        qT = P.sb(pre + "qT", [128, SEQ], BF16)
        kT = P.sb(pre + "kT", [128, SEQ], BF16)
        vT = P.sb(pre + "vT", [128, SEQ], BF16)
        Vtok = P.sb(pre + "Vtok", [128, 16, 128], BF16)
        PT = [P.sb(pre + f"PT{i}", [128, 256], BF16) for i in range(6)]
        accN = P.sb(pre + "accN", [128, SEQ], F32)
        accD = P.sb(pre + "accD", [128, SEQ], F32)
        ps_proj = [P.ps(f"att_pp{i}", i, [128, 512], F32) for i in range(2)]
        ps_tr = [P.ps(f"att_tr{i}", 0, [128, 4, 128], BF16, byte_off=(i % 2) * 1024) for i in range(2)]
        ps_S = [P.ps(f"att_S{i}", 1 + i, [128, 256], F32) for i in range(3)]
        ps_ON = [P.ps(f"att_ON{i}", 4 + i, [128, 512], F32) for i in range(2)]
        ps_OD = [P.ps(f"att_OD{i}", 6 + i, [128, 512], F32) for i in range(2)]
        cnt = {"pp": 0, "tr": 0, "S": 0, "PT": 0, "ev": 0}
        gbase = 0
        for c in range(3):
            for t in range(NT):
                tl = slice(t * TT_, (t + 1) * TT_)
                P.op("pool", lambda e, tl=tl: e.memset(accN[:, tl].ap, 0.0), writes=[accN[:, tl]])
                P.op("pool", lambda e, tl=tl: e.memset(accD[:, tl].ap, 0.0), writes=[accD[:, tl]])
            for which, dstT in ((0, qT), (1, kT), (2, vT)):
                wv = self.proj_w(l, 3 * which + c)
                for t in range(NT):
                    tl = slice(t * TT_, (t + 1) * TT_)
                    ps = ps_proj[cnt["pp"] % 2]
                    cnt["pp"] += 1
                    self.proj_mm(wv, 128, lambda k, tl=tl: hT[:, k, tl], ps[:, :])
                    dst = dstT[:, tl]
                    if which == 0:
                        P.op("act", lambda e, ps=ps, dst=dst: e.mul(out=dst.ap, in_=ps[:, :].ap, mul=0.125),
                             reads=[ps[:, :]], writes=[dst])
                    elif which == 1:
                        P.op("dve", lambda e, ps=ps, dst=dst: e.tensor_copy(out=dst.ap, in_=ps[:, :].ap),
                             reads=[ps[:, :]], writes=[dst])
                    else:
                        P.op("act", lambda e, ps=ps, dst=dst: e.copy(out=dst.ap, in_=ps[:, :].ap),
                             reads=[ps[:, :]], writes=[dst])
            for br, d in enumerate((1, 4, 16)):
                nb = 16 // d

                def tokstart(blk):
                    if d == 1:
                        return 128 * blk
                    if d == 4:
                        return (blk // 4) + 512 * (blk % 4)
                    return blk
                for g4 in range(4):
                    pt = ps_tr[cnt["tr"] % 2]
                    cnt["tr"] += 1
                    for j in range(4):
                        st = tokstart(g4 * 4 + j)
                        src = vT[:, st:st + 127 * d + 1:d]
                        P.op("pe", lambda e, pt=pt, j=j, src=src: e.transpose(
                            out=pt[:, j, :].ap, in_=src.ap, identity=self.identbf[:, :].ap),
                            reads=[src, self.identbf[:, :]], writes=[pt[:, j, :]])
                    dst = Vtok[:, g4 * 4:(g4 + 1) * 4, :]
                    if g4 % 2 == 0:
                        P.op("act", lambda e, pt=pt, dst=dst: e.copy(out=dst.ap, in_=pt[:, :, :].ap),
                             reads=[pt[:, :, :]], writes=[dst])
                    else:
                        P.op("dve", lambda e, pt=pt, dst=dst: e.tensor_copy(out=dst.ap, in_=pt[:, :, :].ap),
                             reads=[pt[:, :, :]], writes=[dst])
                its = []
                for r in range(d):
                    for n in range(nb):
                        for h in range(2):
                            its.append((r, n, h))
                LAG = 4
                pend = {}
                for ii in range(len(its) + LAG):
                    if ii < len(its):
                        r, n, h = its[ii]
                        st = r + d * 128 * n
                        nq = 256 if n + 1 < nb else 128
                        kset = slice(st, st + 127 * d + 1, d)
                        qset = slice(st, st + (nq - 1) * d + 1, d)
                        hp = slice(64 * h, 64 * h + 64)
                        S = ps_S[cnt["S"] % len(ps_S)]
                        cnt["S"] += 1
                        kk = kT[hp, kset]
                        qq = qT[hp, qset]
                        Sv = S[:, 0:nq]
                        P.op("pe", lambda e, Sv=Sv, kk=kk, qq=qq: e.matmul(Sv.ap, lhsT=kk.ap, rhs=qq.ap,
                                                                           start=True, stop=True),
                             reads=[kk, qq], writes=[Sv])
                        pt_ = PT[cnt["PT"] % len(PT)]
                        cnt["PT"] += 1
                        pv = pt_[:, 0:nq]
                        P.op("act", lambda e, pv=pv, Sv=Sv: e.activation(out=pv.ap, in_=Sv.ap, func=AF.Exp),
                             reads=[Sv], writes=[pv])
                        mk = self.mask2[:, 0:nq]
                        P.op("pool", lambda e, pv=pv, mk=mk: e.tensor_tensor(out=pv.ap, in0=pv.ap, in1=mk.ap,
                                                                             op=ALU.mult),
                             reads=[pv, mk], writes=[pv])
                        pend[ii] = pt_
                    jj = ii - LAG
                    if jj < 0:
                        continue
                    r, n, h = its[jj]
                    pt_ = pend.pop(jj)
                    blk = n if d == 1 else (r * 4 + n if d == 4 else r)
                    nq = 256 if n + 1 < nb else 128
                    hp = slice(64 * h, 64 * h + 64)
                    vv = Vtok[:, blk, 64 * h:64 * h + 64]
                    on1 = self.ones_bf[:, 0:64]
                    itn = cnt["ev"]
                    rhs = pt_[:, 0:nq]
                    for lhs, dstp in ((vv, ps_ON[itn % 2][hp, 0:nq]), (on1, ps_OD[itn % 2][hp, 0:nq])):
                        P.op("pe", lambda e, lhs=lhs, dstp=dstp, rhs=rhs: e.matmul(
                            dstp.ap, lhsT=lhs.ap, rhs=rhs.ap, start=True, stop=True, skip_group_check=True),
                            reads=[lhs, rhs], writes=[dstp])
                    if h == 1:
                        cnt["ev"] += 1
                        st = r + d * 128 * n
                        for acc, psb in ((accN, ps_ON[itn % 2]), (accD, ps_OD[itn % 2])):
                            full = acc[:, :]
                            if d == 1:
                                av = acc[:, st:st + nq]
                            else:
                                av = V(full.ap[:, st:st + (nq - 1) * d + 1:d], full.cells)
                            pin = psb[:, 0:nq]
                            P.op("dve", lambda e, av=av, pin=pin: e.tensor_tensor(
                                out=av.ap, in0=pin.ap, in1=av.ap, op=ALU.add),
                                reads=[pin, av], writes=[av])
                gbase += 4
            for t in range(NT):
                tl = slice(t * TT_, (t + 1) * TT_)
                P.op("dve", lambda e, tl=tl: e.reciprocal(out=accD[:, tl].ap, in_=accD[:, tl].ap),
                     reads=[accD[:, tl]], writes=[accD[:, tl]])
                dst = ymix[:, c, tl]
                P.op("dve", lambda e, tl=tl, dst=dst: e.tensor_tensor(out=dst.ap, in0=accN[:, tl].ap,
                                                                      in1=accD[:, tl].ap, op=ALU.mult),
                     reads=[accN[:, tl], accD[:, tl]], writes=[dst])

    def sgu(self, l, pre0, hT, ymix):
        P = self.P
        P.sb_ptr = self.mix_base
        pre = pre0 + "sgu_"
        wraw = P.sb(pre + "wraw", [128, 4, 128], F32)
        WcT32 = P.sb(pre + "WcT32", [128, 4, 128], F32)
        WcTb = P.sb(pre + "WcTb", [128, 4, 128], BF16)
        bsrow = P.sb(pre + "bsrow", [128, 512], F32)
        Kt4 = P.sb(pre + "Kt4", [128, 2, 4, 128], F32)
        gu = [P.sb(pre + f"gu{i}", [128, 2, TT_], BF16) for i in range(2)]
        vg = [P.sb(pre + f"vg{i}", [128, 256], F32) for i in range(2)]
        cen = P.sb(pre + "cen", [128, 4, 256], F32)
        sqc = P.sb(pre + "sqc", [128, 256], F32)
        stat = [P.sb(pre + f"stat{i}", [128, 16], F32) for i in range(2)]
        nbf = [P.sb(pre + f"nbf{i}", [128, 256], BF16) for i in range(2)]
        tmp = P.sb(pre + "tmp", [128, TT_], F32)
        pp = [P.ps(f"att_pp{i}", i, [128, 512], F32) for i in range(2)]
        vps = [P.ps(f"sgu_v{i}", 2, [128, 256], F32, byte_off=i * 1024) for i in range(2)]
        mps = [P.ps(f"sgu_m{i}", 3 + i, [128, 512], F32) for i in range(2)]
        trps = P.ps("sgu_tr", 5, [128, 128], F32)
        kps = P.ps("sgu_k", 5, [128, 128], F32, byte_off=1024)
        tril = self.cmat_view(14)
        lbpos = VEC_LAYOUT[f"sgu_ln_b{l}"][0]
        P.op("sp", lambda e: e.dma_start(out=wraw[:, :, :].ap, in_=self.sgu_w_in[l, :, :, :].m(
            lambda a: a.rearrange("g t s -> t g s")).ap), reads=[], writes=[wraw[:, :, :]], dma_key="sg")
        P.op("sp", lambda e: e.dma_start(out=bsrow[0:1, :].ap, in_=self.sgu_b_in[l:l + 1, :].ap),
             reads=[], writes=[bsrow[0:1, :]], dma_key="sg")
        if SGU_STOP <= 1:
            return
        for g in range(4):
            w_ = wraw[:, g, :]
            if "nomask" not in SGU_VAR:
                P.op("dve", lambda e, w_=w_: e.tensor_tensor(out=w_.ap, in0=w_.ap, in1=tril[:, :].ap, op=ALU.mult),
                     reads=[w_, tril[:, :]], writes=[w_])
            if "notr" in SGU_VAR:
                continue
            P.op("pe", lambda e, w_=w_: e.transpose(out=trps[:, :].ap, in_=w_.ap, identity=self.ident[:, :].ap),
                 reads=[w_, self.ident[:, :]], writes=[trps[:, :]])
            P.op("act", lambda e, g=g: e.copy(out=WcT32[:, g, :].ap, in_=trps[:, :].ap),
                 reads=[trps[:, :]], writes=[WcT32[:, g, :]])
            P.op("dve", lambda e, g=g: e.tensor_copy(out=WcTb[:, g, :].ap, in_=WcT32[:, g, :].ap),
                 reads=[WcT32[:, g, :]], writes=[WcTb[:, g, :]])
        if SGU_STOP <= 2:
            return
        for cc in range(2):
            for gg in range(2):
                g = 2 * cc + gg
                kp = kps[64 * gg:64 * gg + 64, :]
                lb = self.vecs[:, lbpos + g * 64:lbpos + (g + 1) * 64]
                P.op("pe", lambda e, kp=kp, lb=lb, g=g: e.matmul(kp.ap, lhsT=lb.ap, rhs=WcT32[:, g, :].ap,
                                                                 start=True, stop=False, skip_group_check=True),
                     reads=[lb, WcT32[:, g, :]], writes=[kp])
                on = self.ones32[0:1, 0:64]
                br_ = bsrow[0:1, g * 128:(g + 1) * 128]
                P.op("pe", lambda e, kp=kp, on=on, br_=br_: e.matmul(kp.ap, lhsT=on.ap, rhs=br_.ap,
                                                                     start=False, stop=True, skip_group_check=True),
                     reads=[on, br_], writes=[kp])
            for b in range(4):
                dst = Kt4[:, cc, b, :]
                if b % 2 == 0:
                    P.op("act", lambda e, dst=dst: e.copy(out=dst.ap, in_=kps[:, :].ap), reads=[kps[:, :]], writes=[dst])
                else:
                    P.op("dve", lambda e, dst=dst: e.tensor_copy(out=dst.ap, in_=kps[:, :].ap),
                         reads=[kps[:, :]], writes=[dst])
        nb_ = 0
        if SGU_STOP <= 3:
            return
        for t in range(NT):
            tl = slice(t * TT_, (t + 1) * TT_)
            gut = gu[t % 2]
            st_ = stat[t % 2]
            for cc in range(2):
                wv = self.proj_w(l, 20 + cc)
                ps = pp[cc]
                self.proj_mm(wv, 128, lambda k, tl=tl: hT[:, k, tl], ps[:, :])
                P.op("act", lambda e, ps=ps, cc=cc, gut=gut: e.activation(out=gut[:, cc, :].ap, in_=ps[:, :].ap,
                                                                          func=AF.Gelu),
                     reads=[ps[:, :]], writes=[gut[:, cc, :]])
            if SGU_STOP <= 4:
                continue
            w0 = self.proj_w(l, 22)
            w1 = self.proj_w(l, 23)
            for b in range(4):
                bl = slice(t * TT_ + b * 128, t * TT_ + (b + 1) * 128)
                vp = vps[b % 2]
                for half, w in ((0, w0), (1, w1)):
                    vph = vp[:, half * 128:(half + 1) * 128]
                    for k in range(NK):
                        hk = hT[:, k, bl]
                        wk = w.m(lambda a, k=k: a[:, k * 128:(k + 1) * 128])
                        P.op("pe", lambda e, vph=vph, hk=hk, wk=wk, k=k: e.matmul(
                            vph.ap, lhsT=hk.ap, rhs=wk.ap, start=(k == 0), stop=(k == NK - 1)),
                            reads=[hk, w], writes=[vph])
                v_ = vg[b % 2]
                P.op("act", lambda e, v_=v_, vp=vp: e.activation(out=v_[:, :].ap, in_=vp[:, :].ap, func=AF.Gelu),
                     reads=[vp[:, :]], writes=[v_[:, :]])
                sm = st_[:, b:b + 1]
                nm = st_[:, 4 + b:5 + b]
                vs = st_[:, 8 + b:9 + b]
                P.op("dve", lambda e, v_=v_, sm=sm: e.reduce_sum(out=sm.ap, in_=v_[:, :].ap, axis=mybir.AxisListType.X),
                     reads=[v_[:, :]], writes=[sm])
                P.op("dve", lambda e, sm=sm, nm=nm: e.tensor_scalar(out=nm.ap, in0=sm.ap, scalar1=-1.0 / 256,
                                                                     scalar2=None, op0=ALU.mult),
                     reads=[sm], writes=[nm])
                cb = cen[:, b, :]
                P.op("dve", lambda e, cb=cb, v_=v_, nm=nm: e.tensor_scalar(out=cb.ap, in0=v_[:, :].ap, scalar1=nm.ap,
                                                                           scalar2=None, op0=ALU.add),
                     reads=[v_[:, :], nm], writes=[cb])
                P.op("pool", lambda e, cb=cb: e.tensor_tensor(out=sqc[:, :].ap, in0=cb.ap, in1=cb.ap, op=ALU.mult),
                     reads=[cb], writes=[sqc[:, :]])
                P.op("dve", lambda e, vs=vs: e.reduce_sum(out=vs.ap, in_=sqc[:, :].ap, axis=mybir.AxisListType.X),
                     reads=[sqc[:, :]], writes=[vs])
            if SGU_STOP <= 5:
                continue
            rs = st_[:, 12:16]
            ec = self.cst[:, 1:2]
            P.op("act", lambda e, rs=rs, st_=st_, ec=ec: e.activation(out=rs.ap, in_=st_[:, 8:12].ap, func=AF.Sqrt,
                                                                      bias=ec.ap, scale=1.0 / 256),
                 reads=[st_[:, 8:12], ec], writes=[rs])
            P.op("dve", lambda e, rs=rs: e.reciprocal(out=rs.ap, in_=rs.ap), reads=[rs], writes=[rs])
            if SGU_STOP <= 6:
                continue
            for b in range(4):
                n_ = nbf[nb_ % 2]
                nb_ += 1
                cb = cen[:, b, :]
                rb = st_[:, 12 + b:13 + b]
                P.op("dve", lambda e, n_=n_, cb=cb, rb=rb: e.tensor_scalar(out=n_[:, :].ap, in0=cb.ap, scalar1=rb.ap,
                                                                           scalar2=None, op0=ALU.mult),
                     reads=[cb, rb], writes=[n_[:, :]])
                for g in range(4):
                    mp = mps[g // 2][64 * (g % 2):64 * (g % 2) + 64, b * 128:(b + 1) * 128]
                    ng = n_[:, g * 64:(g + 1) * 64]
                    P.op("pe", lambda e, mp=mp, ng=ng, g=g: e.matmul(mp.ap, lhsT=ng.ap, rhs=WcTb[:, g, :].ap,
                                                                     start=True, stop=True, skip_group_check=True),
                         reads=[ng, WcTb[:, g, :]], writes=[mp])
            for cc in range(2):
                gcol = self.vcol(f"sgu_ln_g{l}", cc)
                k4 = Kt4[:, cc, :, :].m(lambda a: a.rearrange("p b i -> p (b i)"))
                P.op("dve", lambda e, cc=cc, gcol=gcol, k4=k4: e.scalar_tensor_tensor(
                    out=tmp[:, :].ap, in0=mps[cc][:, :].ap, scalar=gcol.ap, in1=k4.ap, op0=ALU.mult, op1=ALU.add),
                    reads=[mps[cc][:, :], gcol, k4], writes=[tmp[:, :]])
                dst = ymix[:, 6 + cc, tl]
                P.op("dve", lambda e, cc=cc, dst=dst, gut=gut: e.tensor_tensor(
                    out=dst.ap, in0=tmp[:, :].ap, in1=gut[:, cc, :].ap, op=ALU.mult),
                    reads=[tmp[:, :], gut[:, cc, :]], writes=[dst])

    def ssd(self, l, pre0, hT, ymix):
        P = self.P
        P.sb_ptr = self.mix_base
        pre = pre0 + "ssd_"
        zs = P.sb(pre + "zs", [128, 3, TT_], BF16)
        stgb = [P.sb(pre + f"stg{i}", [128, TT_ + 3], F32) for i in range(2)]
        halo = P.sb(pre + "halo", [128, 7, 4], F32)
        cacc = [P.sb(pre + f"cacc{i}", [128, TT_], F32) for i in range(2)]
        xact = P.sb(pre + "xact", [128, 7, TT_], BF16)
        negA = P.sb(pre + "negA", [128, 24], F32)
        sm = [{n: P.sb(pre + f"{n}{i}", [128, 24], F32) for n in ("t1", "dt", "a", "acs", "last", "dte", "dA", "dtd")}
              for i in range(2)]
        NBUF = 3
        arep = [P.sb(pre + f"arep{i}", [128, 128], F32) for i in range(NBUF)]
        tmpL = [P.sb(pre + f"tmpL{i}", [128, 128], F32) for i in range(NBUF)]
        LT = [P.sb(pre + f"LT{i}", [128, 128], F32) for i in range(NBUF)]
        Eb = [P.sb(pre + f"E{i}", [128, 128], BF16) for i in range(NBUF)]
        MT = [P.sb(pre + f"MT{i}", [128, 128], BF16) for i in range(NBUF)]
        CsT = [P.sb(pre + f"CsT{i}", [128, 128], BF16) for i in range(NBUF)]
        Xs = [P.sb(pre + f"X{i}", [128, 384], BF16) for i in range(2)]
        Xd = [P.sb(pre + f"Xd{i}", [128, 384], BF16) for i in range(2)]
        Btok = [P.sb(pre + f"Btok{i}", [128, 2, 128], BF16) for i in range(2)]
        H32 = P.sb(pre + "H32", [128, 6, 64], F32)
        Hbf = P.sb(pre + "Hbf", [128, 6, 64], BF16)
        ycat = cacc[0]
        rst = cacc[1]
        yg = P.sb(pre + "yg", [128, 3, TT_], F32)
        sqg = P.sb(pre + "sqg", [128, 3, TT_], BF16)
        pp = [P.ps(f"att_pp{i}", i, [128, 512], F32) for i in range(2)]
        yps = [P.ps(f"ssd_y{i}", 2 + i, [128, 512], F32) for i in range(3)]
        BCp = [P.ps(f"ssd_bc{i}", b_, [128, 128], F32) for i, b_ in enumerate((0, 1, 7))]
        GTp = [P.ps(f"ssd_gt{i}", 5, [128, 128], F32, byte_off=1024 * i) for i in range(2)]
        trp = [P.ps(f"ssd_tr{i}", 6, [128, 128], BF16, byte_off=256 * i) for i in range(4)]
        Hps = [P.ps(f"ssd_h{i}", 6, [128, 64], F32, byte_off=1024 + 256 * i) for i in range(2)]
        dtp = P.ps("ssd_dt", 6, [128, 24], F32, byte_off=1536)
        acp = P.ps("ssd_ac", 6, [128, 24], F32, byte_off=1664)
        lap = P.ps("ssd_la", 6, [128, 24], F32, byte_off=1792)
        ssp = P.ps("ssd_ss", 7, [128, 512], F32)
        tri = self.cmat_view(1)
        negm = self.cmat_view(2)
        one_c = self.cst[:, 3:4]
        p0 = VEC_LAYOUT[f"dt_bias{l}"][0]
        dtb = self.vecs[:, p0:p0 + 24]
        p1 = VEC_LAYOUT[f"a_log{l}"][0]
        alog = self.vecs[:, p1:p1 + 24]
        P.op("act", lambda e: e.activation(out=negA[:, :].ap, in_=alog.ap, func=AF.Exp), reads=[alog], writes=[negA[:, :]])
        P.op("dve", lambda e: e.tensor_scalar(out=negA[:, :].ap, in0=negA[:, :].ap, scalar1=-1.0, scalar2=None,
                                              op0=ALU.mult), reads=[negA[:, :]], writes=[negA[:, :]])
        P.op("pool", lambda e: e.memset(H32[:, :, :].ap, 0.0), writes=[H32[:, :, :]])
        P.op("pool", lambda e: e.memset(Hbf[:, :, :].ap, 0.0), writes=[Hbf[:, :, :]])
        P.op("pool", lambda e: e.memset(halo[:, :, :].ap, 0.0), writes=[halo[:, :, :]])
        cn = {"pp": 0, "tr": 0, "ch": 0, "hd": 0, "hp": 0}
        for t in range(NT):
            tl = slice(t * TT_, (t + 1) * TT_)
            for c in range(3):
                wv = self.proj_w(l, 9 + c)
                ps = pp[cn["pp"] % 2]
                cn["pp"] += 1
                self.proj_mm(wv, 128, lambda k, tl=tl: hT[:, k, tl], ps[:, :])
                P.op("act", lambda e, ps=ps, c=c: e.activation(out=zs[:, c, :].ap, in_=ps[:, :].ap, func=AF.Silu),
                     reads=[ps[:, :]], writes=[zs[:, c, :]])
            for c in range(7):
                wv = self.proj_w(l, 12 + c)
                ps = pp[cn["pp"] % 2]
                cn["pp"] += 1
                self.proj_mm(wv, 128, lambda k, tl=tl: hT[:, k, tl], ps[:, :])
                stg = stgb[c % 2]
                sg_ = stg[:, 3:TT_ + 3]
                P.op("pool", lambda e, stg=stg, c=c: e.tensor_copy(out=stg[:, 0:3].ap, in_=halo[:, c, 0:3].ap),
                     reads=[halo[:, c, 0:3]], writes=[stg[:, 0:3]])
                P.op("act", lambda e, ps=ps, sg_=sg_: e.copy(out=sg_.ap, in_=ps[:, :].ap), reads=[ps[:, :]], writes=[sg_])
                ca = cacc[c % 2]
                for j in range(4):
                    wj = self.vcol(f"conv_w{l}_{j}", c)
                    sj = stg[:, j:j + TT_]
                    if j == 0:
                        P.op("dve", lambda e, ca=ca, sj=sj, wj=wj: e.tensor_scalar(
                            out=ca[:, :].ap, in0=sj.ap, scalar1=wj.ap, scalar2=None, op0=ALU.mult),
                            reads=[sj, wj], writes=[ca[:, :]])
                    else:
                        P.op("dve", lambda e, ca=ca, sj=sj, wj=wj: e.scalar_tensor_tensor(
                            out=ca[:, :].ap, in0=sj.ap, scalar=wj.ap, in1=ca[:, :].ap, op0=ALU.mult, op1=ALU.add),
                            reads=[sj, wj, ca[:, :]], writes=[ca[:, :]])
                cb_ = self.vcol(f"conv_b{l}", c)
                P.op("act", lambda e, ca=ca, c=c, cb_=cb_: e.activation(out=xact[:, c, :].ap, in_=ca[:, :].ap,
                                                                        func=AF.Silu, bias=cb_.ap, scale=1.0),
                     reads=[ca[:, :], cb_], writes=[xact[:, c, :]])
                P.op("pool", lambda e, c=c, stg=stg: e.tensor_copy(out=halo[:, c, 0:3].ap, in_=stg[:, TT_:TT_ + 3].ap),
                     reads=[stg[:, TT_:TT_ + 3]], writes=[halo[:, c, 0:3]])
            wdt = self.proj_w(l, 19)
            S_ = sm[t % 2]
            for ch in range(4):
                tok = slice(t * TT_ + ch * 128, t * TT_ + (ch + 1) * 128)
                dpc = dtp[:, ch * 6:(ch + 1) * 6]
                for k in range(NK):
                    hk = hT[:, k, tok]
                    wk = wdt.m(lambda a, k=k: a[:, k * 128:k * 128 + 6])
                    P.op("pe", lambda e, hk=hk, wk=wk, k=k, dpc=dpc: e.matmul(dpc.ap, lhsT=hk.ap, rhs=wk.ap,
                                                                              start=(k == 0), stop=(k == NK - 1)),
                         reads=[hk, wdt], writes=[dpc])
            t1, dt, a_, acs, last, dte, dA, dtd = (S_[n][:, :] for n in ("t1", "dt", "a", "acs", "last", "dte", "dA", "dtd"))
            P.op("dve", lambda e, t1=t1: e.tensor_tensor(out=t1.ap, in0=dtp[:, :].ap, in1=dtb.ap, op=ALU.add),
                 reads=[dtp[:, :], dtb], writes=[t1])
            P.op("act", lambda e, t1=t1: e.activation(out=t1.ap, in_=t1.ap, func=AF.Exp), reads=[t1], writes=[t1])
            P.op("act", lambda e, t1=t1, dt=dt: e.activation(out=dt.ap, in_=t1.ap, func=AF.Ln, bias=one_c.ap, scale=1.0),
                 reads=[t1, one_c], writes=[dt])
            P.op("dve", lambda e, a_=a_, dt=dt: e.tensor_tensor(out=a_.ap, in0=dt.ap, in1=negA[:, :].ap, op=ALU.mult),
                 reads=[dt, negA[:, :]], writes=[a_])
            P.op("pe", lambda e, a_=a_: e.matmul(acp[:, :].ap, lhsT=tri[:, :].ap, rhs=a_.ap, start=True, stop=True),
                 reads=[tri[:, :], a_], writes=[acp[:, :]])
            P.op("pe", lambda e, a_=a_: e.matmul(lap[:, :].ap, lhsT=self.ones32[:, :].ap, rhs=a_.ap, start=True, stop=True),
                 reads=[self.ones32[:, :], a_], writes=[lap[:, :]])
            P.op("dve", lambda e, acs=acs: e.tensor_copy(out=acs.ap, in_=acp[:, :].ap), reads=[acp[:, :]], writes=[acs])
            P.op("dve", lambda e, last=last: e.tensor_copy(out=last.ap, in_=lap[:, :].ap), reads=[lap[:, :]], writes=[last])
            P.op("dve", lambda e, dte=dte, last=last, acs=acs: e.tensor_tensor(out=dte.ap, in0=last.ap, in1=acs.ap,
                                                                              op=ALU.subtract),
                 reads=[last, acs], writes=[dte])
            P.op("act", lambda e, dte=dte: e.activation(out=dte.ap, in_=dte.ap, func=AF.Exp), reads=[dte], writes=[dte])
            P.op("act", lambda e, dA=dA, last=last: e.activation(out=dA.ap, in_=last.ap, func=AF.Exp),
                 reads=[last], writes=[dA])
            P.op("dve", lambda e, dtd=dtd, dt=dt, dte=dte: e.tensor_tensor(out=dtd.ap, in0=dt.ap, in1=dte.ap, op=ALU.mult),
                 reads=[dt, dte], writes=[dtd])
            for ch in range(4):
                lt = slice(ch * 128, (ch + 1) * 128)
                X = Xs[cn["ch"] % 2]
                XD = Xd[cn["ch"] % 2]
                BT = Btok[cn["ch"] % 2]
                cn["ch"] += 1
                for c in range(3):
                    tp = trp[cn["tr"] % 4]
                    cn["tr"] += 1
                    src = xact[:, c, lt]
                    P.op("pe", lambda e, tp=tp, src=src: e.transpose(out=tp[:, :].ap, in_=src.ap,
                                                                     identity=self.identbf[:, :].ap),
                         reads=[src, self.identbf[:, :]], writes=[tp[:, :]])
                    for hh in range(2):
                        h = 2 * c + hh
                        xh = X[:, h * 64:(h + 1) * 64]
                        xdh = XD[:, h * 64:(h + 1) * 64]
                        tph = tp[:, hh * 64:(hh + 1) * 64]
                        dth = S_["dt"][:, ch * 6 + h:ch * 6 + h + 1]
                        ddh = S_["dtd"][:, ch * 6 + h:ch * 6 + h + 1]
                        P.op("dve", lambda e, xh=xh, tph=tph, dth=dth: e.tensor_scalar(
                            out=xh.ap, in0=tph.ap, scalar1=dth.ap, scalar2=None, op0=ALU.mult),
                            reads=[tph, dth], writes=[xh])
                        P.op("dve", lambda e, xdh=xdh, tph=tph, ddh=ddh: e.tensor_scalar(
                            out=xdh.ap, in0=tph.ap, scalar1=ddh.ap, scalar2=None, op0=ALU.mult),
                            reads=[tph, ddh], writes=[xdh])
                for g in range(2):
                    tp = trp[cn["tr"] % 4]
                    cn["tr"] += 1
                    src = xact[:, 3 + g, lt]
                    P.op("pe", lambda e, tp=tp, src=src: e.transpose(out=tp[:, :].ap, in_=src.ap,
                                                                     identity=self.identbf[:, :].ap),
                         reads=[src, self.identbf[:, :]], writes=[tp[:, :]])
                    P.op("act", lambda e, tp=tp, g=g, BT=BT: e.copy(out=BT[:, g, :].ap, in_=tp[:, :].ap),
                         reads=[tp[:, :]], writes=[BT[:, g, :]])
                for g in range(2):
                    gt = GTp[g]
                    bT = xact[:, 3 + g, lt]
                    cT = xact[:, 5 + g, lt]
                    P.op("pe", lambda e, gt=gt, bT=bT, cT=cT: e.matmul(gt[:, :].ap, lhsT=bT.ap, rhs=cT.ap,
                                                                       start=True, stop=True),
                         reads=[bT, cT], writes=[gt[:, :]])
                LAG = 2
                bufs = {}
                for ii in range(6 + LAG):
                    if ii < 6:
                        h = ii
                        g = h // 3
                        gt = GTp[g]
                        cT = xact[:, 5 + g, lt]
                        i3 = cn["hd"] % NBUF
                        cn["hd"] += 1
                        ar, tL, L_, E_, M_, C_ = arep[i3], tmpL[i3], LT[i3], Eb[i3], MT[i3], CsT[i3]
                        bc = BCp[i3]
                        ah = S_["a"][:, ch * 6 + h:ch * 6 + h + 1]
                        ach = S_["acs"][:, ch * 6 + h:ch * 6 + h + 1]
                        P.op("act", lambda e, ar=ar, ah=ah: e.activation(out=ar[:, :].ap, in_=self.ones32[:, :].ap,
                                                                         func=AF.Copy, scale=ah.ap),
                             reads=[self.ones32[:, :], ah], writes=[ar[:, :]])
                        P.op("pe", lambda e, bc=bc, ar=ar: e.matmul(bc[:, :].ap, lhsT=ar[:, :].ap, rhs=tri[:, :].ap,
                                                                    start=True, stop=True),
                             reads=[ar[:, :], tri[:, :]], writes=[bc[:, :]])
                        P.op("dve", lambda e, tL=tL, bc=bc, ach=ach: e.scalar_tensor_tensor(
                            out=tL[:, :].ap, in0=bc[:, :].ap, scalar=ach.ap, in1=negm[:, :].ap,
                            op0=ALU.subtract, op1=ALU.add),
                            reads=[bc[:, :], ach, negm[:, :]], writes=[tL[:, :]])
                        P.op("act", lambda e, E_=E_, bc=bc: e.activation(out=E_[:, :].ap, in_=bc[:, :].ap, func=AF.Exp),
                             reads=[bc[:, :]], writes=[E_[:, :]])
                        P.op("act", lambda e, L_=L_, tL=tL: e.activation(out=L_[:, :].ap, in_=tL[:, :].ap, func=AF.Exp),
                             reads=[tL[:, :]], writes=[L_[:, :]])
                        P.op("dve", lambda e, M_=M_, gt=gt, L_=L_: e.tensor_tensor(out=M_[:, :].ap, in0=gt[:, :].ap,
                                                                                   in1=L_[:, :].ap, op=ALU.mult),
                             reads=[gt[:, :], L_[:, :]], writes=[M_[:, :]])
                        P.op("pool", lambda e, C_=C_, cT=cT, E_=E_: e.tensor_tensor(out=C_[:, :].ap, in0=cT.ap,
                                                                                    in1=E_[:, :].ap, op=ALU.mult),
                             reads=[cT, E_[:, :]], writes=[C_[:, :]])
                        bufs[ii] = (M_, C_)
                    jj = ii - LAG
                    if jj < 0:
                        continue
                    h = jj
                    g = h // 3
                    c, hh = h // 2, h % 2
                    M_, C_ = bufs.pop(jj)
                    dah = S_["dA"][:, ch * 6 + h:ch * 6 + h + 1]
                    yp = yps[c][64 * hh:64 * hh + 64, lt]
                    xh = X[:, h * 64:(h + 1) * 64]
                    xdh = XD[:, h * 64:(h + 1) * 64]
                    hb = Hbf[:, h, :]
                    P.op("pe", lambda e, yp=yp, xh=xh, M_=M_: e.matmul(yp.ap, lhsT=xh.ap, rhs=M_[:, :].ap, start=True,
                                                                       stop=False, skip_group_check=True),
                         reads=[xh, M_[:, :]], writes=[yp])
                    P.op("pe", lambda e, yp=yp, hb=hb, C_=C_: e.matmul(yp.ap, lhsT=hb.ap, rhs=C_[:, :].ap, start=False,
                                                                       stop=True, skip_group_check=True),
                         reads=[hb, C_[:, :]], writes=[yp])
                    hp_ = Hps[cn["hp"] % 2]
                    cn["hp"] += 1
                    P.op("pe", lambda e, hp_=hp_, BT=BT, g=g, xdh=xdh: e.matmul(hp_[:, :].ap, lhsT=BT[:, g, :].ap,
                                                                                rhs=xdh.ap, start=True, stop=True),
                         reads=[BT[:, g, :], xdh], writes=[hp_[:, :]])
                    h32 = H32[:, h, :]
                    P.op("dve", lambda e, h32=h32, dah=dah, hp_=hp_: e.scalar_tensor_tensor(
                        out=h32.ap, in0=h32.ap, scalar=dah.ap, in1=hp_[:, :].ap, op0=ALU.mult, op1=ALU.add),
                        reads=[h32, dah, hp_[:, :]], writes=[h32])
                    P.op("pool", lambda e, hb=hb, h32=h32: e.tensor_copy(out=hb.ap, in_=h32.ap),
                         reads=[h32], writes=[hb])
            for c in range(3):
                dc = self.vcol(f"dcol{l}", c)
                P.op("dve", lambda e, c=c, dc=dc: e.scalar_tensor_tensor(
                    out=ycat[:, :].ap, in0=xact[:, c, :].ap, scalar=dc.ap, in1=yps[c][:, :].ap, op0=ALU.mult, op1=ALU.add),
                    reads=[xact[:, c, :], dc, yps[c][:, :]], writes=[ycat[:, :]])
                P.op("dve", lambda e, c=c: e.tensor_tensor(out=yg[:, c, :].ap, in0=ycat[:, :].ap, in1=zs[:, c, :].ap,
                                                           op=ALU.mult),
                     reads=[ycat[:, :], zs[:, c, :]], writes=[yg[:, c, :]])
                P.op("act", lambda e, c=c: e.activation(out=sqg[:, c, :].ap, in_=yg[:, c, :].ap, func=AF.Square),
                     reads=[yg[:, c, :]], writes=[sqg[:, c, :]])
            for m in range(3):
                ks = [k for k in range(3) if abs(k - m) <= 1]
                for i, k in enumerate(ks):
                    sel = self.selbf[:, 3 * k + m, :]
                    P.op("pe", lambda e, sel=sel, k=k, i=i, ks=ks: e.matmul(ssp[:, :].ap, lhsT=sel.ap, rhs=sqg[:, k, :].ap,
                                                                           start=(i == 0), stop=(i == len(ks) - 1)),
                         reads=[sel, sqg[:, k, :]], writes=[ssp[:, :]])
                self.rsqrt_mean(rst[:, :], ssp[:, :], 1.0 / 192, RMS_EPS)
                gcol = self.vcol(f"ssd_norm{l}", m)
                dst = ymix[:, 3 + m, tl]
                P.op("dve", lambda e, m=m, gcol=gcol, dst=dst: e.scalar_tensor_tensor(
                    out=dst.ap, in0=yg[:, m, :].ap, scalar=gcol.ap, in1=rst[:, :].ap, op0=ALU.mult, op1=ALU.mult),
                    reads=[yg[:, m, :], gcol, rst[:, :]], writes=[dst])

    def build(self):
        self.declare()
        self.alloc_global()
        self.P.plan = True
        self.body()
        self.P.plan = False
        self.ws.reset_for_real()
        self.phase_id = 0
        self.rr = 0
        self.body()
        self.P.emit()
        return self.nc


def make_in_maps(inp):
    vecs = build_vecs(inp)
    x = np.ascontiguousarray(np.asarray(inp["x"], np.float32))
    shared = {}
    for nm in ("ffn1", "ffn2"):
        for wn in ("_w_gate", "_w_up", "_w_down"):
            shared[nm + wn] = np.ascontiguousarray(np.asarray(inp[nm + wn], np.float32))
    shared["w_in"] = np.ascontiguousarray(np.asarray(inp["w_in"], np.float32))
    shared["w_out"] = np.ascontiguousarray(np.asarray(inp["w_out"], np.float32))
    shared["vecs"] = vecs
    shared["cmat"] = build_cmat()
    shared["sgu_w"] = np.ascontiguousarray(np.asarray(inp["sgu_w"], np.float32))
    shared["sgu_b"] = np.ascontiguousarray(np.asarray(inp["sgu_b"], np.float32).reshape(DEPTH, 4 * 128))
    maps = []
    for c in range(NCORES):
        m = dict(shared)
        m["x"] = x[c * NSEQ:(c + 1) * NSEQ]
        maps.append(m)
    return maps


LAST_BUILDER = None


def run(inp, stages=None, trace=False, parts=("att", "ssd", "sgu")):
    global LAST_BUILDER
    b = Builder(stages=stages, parts=parts)
    LAST_BUILDER = b
    maps = make_in_maps(inp)
    nc = b.build()
    res = run_bass_kernel_spmd(nc, maps, core_ids=list(range(NCORES)), trace=trace)
    out = np.concatenate([np.asarray(r["out"]) for r in res.results], axis=0)
    return out.astype(np.float32), res


def kernel(**inputs):
    out, _ = run(inputs)
    return out
```

```python
import itertools
from contextlib import ExitStack

import numpy as np
import concourse.bass as bass
import concourse.mybir as mybir
from concourse.bass_utils import run_bass_kernel_spmd

F32 = mybir.dt.float32
BF16 = mybir.dt.bfloat16
AF = mybir.ActivationFunctionType
ALU = mybir.AluOpType
ESZ = {F32: 4, BF16: 2}

NCORES = 8
SEQ = 2048
D = 1024
NSEQ = 2
DFF = 2816
NF = DFF // 128
NK = D // 128
TT_ = 512
NT = SEQ // TT_
DEPTH = 2
D_IN = 2950
RMS_EPS = 1e-6
LN_EPS = 1e-5

SB_CELL = 256
SGU_STOP = 99
SGU_VAR = ''
PS_CELL = 2048


class V:
    __slots__ = ("ap", "cells")

    def __init__(self, ap, cells):
        self.ap = ap
        self.cells = cells

    def m(self, f):
        return V(f(self.ap), self.cells)


class TT:
    def __init__(self, handle, shape, dtype, space, base, cell, tid):
        self.h = handle
        self.shape = list(shape)
        self.dtype = dtype
        self.space = space
        self.base = base
        self.cell = cell
        self.tid = tid
        self._cache = {}
        esz = ESZ[dtype]
        dims = self.shape if space == "D" else self.shape[1:]
        st = []
        acc = esz
        for d in reversed(dims):
            st.append(acc)
            acc *= d
        self.strides = list(reversed(st))
        self.dims = dims
        self.esz = esz

    def __getitem__(self, idx):
        if not isinstance(idx, tuple):
            idx = (idx,)
        key = tuple((i.start, i.stop, i.step) if isinstance(i, slice) else i for i in idx)
        c = self._cache.get(key)
        if c is None:
            c = self._cells(idx)
            self._cache[key] = c
        return V(self.h[idx], c)

    def _cells(self, idx):
        fidx = list(idx) if self.space == "D" else list(idx[1:])
        while len(fidx) < len(self.dims):
            fidx.append(slice(None))
        rngs = []
        for i, d in zip(fidx, self.dims):
            if isinstance(i, slice):
                s, e, stp = i.indices(d)
                rngs.append((s, e, stp))
            else:
                rngs.append((i, i + 1, 1))
        cells = set()
        outer = [range(s, e, stp) for (s, e, stp) in rngs[:-1]]
        ls, le, lstp = rngs[-1]
        last_lo = ls * self.strides[-1]
        last_hi = (ls + ((le - 1 - ls) // lstp) * lstp) * self.strides[-1] + self.esz
        for combo in itertools.product(*outer):
            b = self.base + sum(i * s for i, s in zip(combo, self.strides[:-1]))
            for c in range((b + last_lo) // self.cell, (b + last_hi - 1) // self.cell + 1):
                cells.add((self.tid, c))
        return tuple(cells)


class Op:
    __slots__ = ("eng", "idx", "fn", "deps", "dma_deps", "signal", "sigval", "dma_key", "dma_cnt", "stage")


ENGS = ["pe", "act", "dve", "pool", "sp"]


class Prog:
    def __init__(self, nc):
        self.nc = nc
        self.ops = {e: [] for e in ENGS}
        self.cellstate = {}
        self.dma_cnt = {}
        self.plan = False
        self.stage = ""
        self.ins_stage = {}
        self.n_tid = 0
        self.tts = {}
        self.arena = None
        self.arena_base = 0
        self.sb_ptr = 0
        self.sb_cap = 0
        self.psum_banks = []

    def init_mem(self, sb_bytes):
        self.arena_base = (self.nc.sbuf_base + 63) // 64 * 64
        self.sb_cap = min(sb_bytes, (self.nc.sbuf_top - self.arena_base) // 64 * 64)
        for b in range(8):
            self.psum_banks.append(self.nc.alloc_psum_tensor(f"bank{b}", [128, 512], F32))

    def sb(self, name, shape, dtype, off=None):
        if name in self.tts:
            return self.tts[name]
        n = ESZ[dtype]
        for d in shape[1:]:
            n *= d
        if off is None:
            off = (self.sb_ptr + 63) // 64 * 64
            self.sb_ptr = off + n
        assert off + n <= self.sb_cap, f"SBUF overflow {name}: {off}+{n} > {self.sb_cap}"
        h = self.nc.alloc_sbuf_tensor_at(name, list(shape), dtype, offset=self.arena_base + off)
        t = TT(h, shape, dtype, "S", off, SB_CELL, "S")
        self.tts[name] = t
        return t

    def ps(self, name, bank, shape, dtype, byte_off=0):
        if name in self.tts:
            return self.tts[name]
        n = ESZ[dtype]
        for d in shape[1:]:
            n *= d
        assert byte_off + n <= 2048
        t = PsTT(self.psum_banks[bank], shape, dtype, bank, byte_off)
        self.tts[name] = t
        return t

    def dram(self, name, shape, dtype, kind="Internal", cell=None):
        if name in self.tts:
            return self.tts[name]
        h = self.nc.dram_tensor(name, list(shape), dtype, kind=kind)
        self.n_tid += 1
        t = TT(h, shape, dtype, "D", 0, cell or (1 << 40), f"D{self.n_tid}")
        self.tts[name] = t
        return t

    def op(self, eng, fn, reads=(), writes=(), dma_key=None):
        if self.plan:
            return
        o = Op()
        o.eng = eng
        o.fn = fn
        o.signal = False
        o.sigval = None
        o.dma_key = dma_key
        o.idx = len(self.ops[eng])
        o.stage = self.stage
        deps = {}
        dma_deps = {}

        def add(p, raw):
            if p is None:
                return
            if p.dma_key is not None:
                k = p.dma_key
                dma_deps[k] = self.dma_cnt[k]
                return
            if p.eng == eng and dma_key is None:
                if eng == "pe":
                    return
            if deps.get(p.eng, -1) < p.idx:
                deps[p.eng] = p.idx

        cs = self.cellstate
        for v in reads:
            for c in v.cells:
                st = cs.get(c)
                if st is not None:
                    add(st[0], True)
                    if c[0] == "P":
                        for r in st[1]:
                            if r.eng != eng:
                                add(r, False)
        for v in writes:
            for c in v.cells:
                st = cs.get(c)
                if st is not None:
                    add(st[0], False)
                    for r in st[1]:
                        add(r, False)
        for v in writes:
            for c in v.cells:
                cs[c] = [o, []]
        for v in reads:
            for c in v.cells:
                st = cs.get(c)
                if st is None:
                    cs[c] = [None, [o]]
                elif st[0] is not o:
                    st[1].append(o)
        if dma_key is not None:
            self.dma_cnt[dma_key] = self.dma_cnt.get(dma_key, 0) + 1
            o.dma_cnt = self.dma_cnt[dma_key]
        else:
            o.dma_cnt = 0
        for e, i in deps.items():
            self.ops[e][i].signal = True
        o.deps = deps
        o.dma_deps = dma_deps
        self.ops[eng].append(o)

    def emit(self):
        nc = self.nc
        for e in ENGS:
            cnt = 0
            for o in self.ops[e]:
                if o.signal:
                    cnt += 1
                    o.sigval = cnt
        with ExitStack() as es:
            esem = {e: es.enter_context(nc.semaphore(f"sem_{e}")) for e in ENGS if e != "sp"}
            dsem = {k: es.enter_context(nc.semaphore(f"dma_{k}")) for k in self.dma_cnt}
            block = es.enter_context(nc.Block())

            def run(ename, eng):
                waited = {}
                for o in self.ops[ename]:
                    for pe_, pi in o.deps.items():
                        val = self.ops[pe_][pi].sigval
                        key = ("e", pe_)
                        if waited.get(key, 0) < val:
                            eng.wait_ge(esem[pe_], val)
                            waited[key] = val
                    for k, c in o.dma_deps.items():
                        key = ("d", k)
                        if waited.get(key, 0) < 16 * c:
                            eng.wait_ge(dsem[k], 16 * c)
                            waited[key] = 16 * c
                    ins = o.fn(eng)
                    try:
                        self.ins_stage[ins.ins.name] = (ename, o.stage)
                    except Exception:
                        pass
                    if o.dma_key is not None:
                        ins.then_inc(dsem[o.dma_key], 16)
                    elif o.signal:
                        ins.then_inc(esem[ename], 1)
                if ename == "sp":
                    for k, c in self.dma_cnt.items():
                        if waited.get(("d", k), 0) < 16 * c:
                            eng.wait_ge(dsem[k], 16 * c)

            @block.tensor
            def _(eng):
                run("pe", eng)

            @block.scalar
            def _(eng):
                run("act", eng)

            @block.vector
            def _(eng):
                run("dve", eng)

            @block.gpsimd
            def _(eng):
                run("pool", eng)

            @block.sync
            def _(eng):
                run("sp", eng)


class PsTT(TT):
    def __init__(self, bank_handle, shape, dtype, bank, byte_off):
        n_el = 2048 // ESZ[dtype]
        full = bank_handle[:].bitcast(dtype) if dtype != F32 else bank_handle[:]
        self.full = full
        self.shape = list(shape)
        self.dtype = dtype
        self.space = "P"
        self.base = bank * 2048 + byte_off
        self.cell = PS_CELL
        self.tid = "P"
        self._cache = {}
        esz = ESZ[dtype]
        self.esz = esz
        dims = self.shape[1:]
        st = []
        acc = esz
        for d in reversed(dims):
            st.append(acc)
            acc *= d
        self.strides = list(reversed(st))
        self.dims = dims
        n = 1
        for d in dims:
            n *= d
        e0 = byte_off // esz
        flat = full[:, e0:e0 + n]
        if len(dims) == 1:
            self.view = flat
        elif len(dims) == 2:
            self.view = flat.rearrange("p (a b) -> p a b", b=dims[1])
        elif len(dims) == 3:
            self.view = flat.rearrange("p (a b c) -> p a b c", b=dims[1], c=dims[2])
        else:
            raise ValueError

    def __getitem__(self, idx):
        if not isinstance(idx, tuple):
            idx = (idx,)
        key = tuple((i.start, i.stop, i.step) if isinstance(i, slice) else i for i in idx)
        c = self._cache.get(key)
        if c is None:
            c = self._cells(idx)
            self._cache[key] = c
        return V(self.view[idx], c)


SLOT_ELEMS = NF * 128
NSLOT = 5
PREFETCH = 3


class WStream:
    def __init__(self, prog):
        self.prog = prog
        self.plan_list = []
        self.i_next = 0
        self.i_issued = 0
        self.slots = None

    def reset_for_real(self):
        self.i_next = 0
        self.i_issued = 0

    def _issue(self, i):
        src_fn, nelem = self.plan_list[i]
        s = i % NSLOT
        dst = self.slots[:, s, 0:nelem]
        src = src_fn()
        self.prog.op("sp", lambda e, d=dst, s_=src: e.dma_start(out=d.ap, in_=s_.ap),
                     reads=[src], writes=[dst], dma_key=f"w{s}")

    def next(self, src_fn, nelem):
        if self.prog.plan:
            self.plan_list.append((src_fn, nelem))
            return self.slots[:, 0, 0:nelem], 0
        i = self.i_next
        self.i_next += 1
        while self.i_issued < min(len(self.plan_list), i + 1 + PREFETCH):
            self._issue(self.i_issued)
            self.i_issued += 1
        return self.slots[:, i % NSLOT, 0:nelem], i % NSLOT


def _cols(v):
    v = np.asarray(v, np.float32)
    return np.ascontiguousarray(v.reshape(-1, 128).T)


VEC_LAYOUT = {}


N_CMAT = 15


def build_vecs(inp):
    cols = []
    pos = 0
    VEC_LAYOUT.clear()

    def add(name, arr):
        nonlocal pos
        arr = np.asarray(arr, np.float32)
        VEC_LAYOUT[name] = (pos, arr.shape[1])
        cols.append(arr)
        pos += arr.shape[1]

    def bc(v):
        v = np.asarray(v, np.float32).reshape(1, -1)
        return np.broadcast_to(v, (128, v.shape[1]))

    for l in range(DEPTH):
        add(f"ffn1_norm{l}", _cols(inp["ffn1_norm"][l]))
        add(f"mix_norm{l}", _cols(inp["mix_norm"][l]))
        add(f"ffn2_norm{l}", _cols(inp["ffn2_norm"][l]))
    add("final_norm", _cols(inp["final_norm"]))
    for l in range(DEPTH):
        for j in range(4):
            add(f"conv_w{l}_{j}", _cols(inp["conv_w"][l][j]))
        add(f"conv_b{l}", _cols(inp["conv_b"][l]))
        add(f"ssd_norm{l}", _cols(inp["ssd_norm"][l]))
        add(f"dcol{l}", _cols(np.repeat(np.asarray(inp["d_skip"][l], np.float32), 64)))
        add(f"sgu_ln_g{l}", _cols(inp["sgu_ln_g"][l]))
        add(f"dt_bias{l}", bc(np.tile(np.asarray(inp["dt_bias"][l], np.float32), 4)))
        add(f"a_log{l}", bc(np.tile(np.asarray(inp["a_log"][l], np.float32), 4)))
        add(f"sgu_ln_b{l}", bc(inp["sgu_ln_b"][l]))
    return np.ascontiguousarray(np.concatenate(cols, axis=1))


def n_vec_cols():
    return DEPTH * 3 * NK + NK + DEPTH * (28 + 7 + 3 + 3 + 2 + 24 + 24 + 256)


def build_cmat():
    i = np.arange(128)
    m = []
    m.append(np.eye(128))
    m.append((i[:, None] <= i[None, :]) * 1.0)
    m.append(np.where(i[:, None] <= i[None, :], 0.0, -30000.0))
    m.append((i[None, :] >= i[:, None]) * 1.0)
    m.append((i[None, :] <= i[:, None]) * 1.0)
    for k in range(3):
        for mm in range(3):
            gi = (128 * k + i) // 192
            go = (128 * mm + i) // 192
            m.append((gi[:, None] == go[None, :]) * 1.0)
    m.append((i[None, :] <= i[:, None]) * 1.0)
    return np.ascontiguousarray(np.concatenate(m, axis=1).astype(np.float32))


class Builder:
    def __init__(self, stages=None, dump=None, parts=("att", "ssd", "sgu")):
        self.parts = parts
        self.nc = bass.Bass("TRN2", target_bir_lowering=False, dynamic_dma_scratch_size=64)
        self.P = Prog(self.nc)
        self.ws = WStream(self.P)
        self.stages = stages
        self.dump = dump
        self.rr = 0

    def declare(self):
        P = self.P
        nc = self.nc
        ext = lambda n, s, dt=F32: P.dram(n, s, dt, kind="ExternalInput")
        self.x_in = ext("x", [NSEQ, SEQ, D])
        self.w = {}
        for nm in ("ffn1", "ffn2"):
            self.w[nm + "_w_gate"] = ext(nm + "_w_gate", [DEPTH, D, DFF])
            self.w[nm + "_w_up"] = ext(nm + "_w_up", [DEPTH, D, DFF])
            self.w[nm + "_w_down"] = ext(nm + "_w_down", [DEPTH, DFF, D])
        self.w["w_in"] = ext("w_in", [DEPTH, D, D_IN])
        self.w["w_out"] = ext("w_out", [DEPTH, D, D])
        self.vecs_in = ext("vecs", [128, n_vec_cols()])
        self.out = P.dram("out", [NSEQ, SEQ, D], F32, kind="ExternalOutput")
        self.WguR = P.dram("WguR", [DEPTH * 2, NF, 128, 2 * NK * 128], BF16, cell=128 * 2 * NK * 128 * 2)
        self.WdR = P.dram("WdR", [DEPTH * 2, NK, 128, NF * 128], BF16, cell=128 * NF * 128 * 2)
        self.WinR = P.dram("WinR", [DEPTH, 24, 128, NK * 128], BF16, cell=128 * NK * 128 * 2)
        self.WoutR = P.dram("WoutR", [DEPTH, NK, 128, NK * 128], BF16, cell=128 * NK * 128 * 2)
        self.cmat_in = ext("cmat", [128, N_CMAT * 128])
        self.sgu_w_in = ext("sgu_w", [DEPTH, 4, 128, 128])
        self.sgu_b_in = ext("sgu_b", [DEPTH, 4 * 128])

    def alloc_global(self):
        P = self.P
        P.init_mem(229056)
        self.xT = P.sb("xT", [128, NK, SEQ], F32)
        self.vecs = P.sb("vecs_sb", [128, n_vec_cols()], F32)
        self.cmat = P.sb("cmat_sb", [128, N_CMAT, 128], F32)
        self.ident = self.cmat_view(0)
        self.identbf = P.sb("identbf", [128, 128], BF16)
        self.mask2 = P.sb("mask2", [128, 256], BF16)
        self.ones32 = P.sb("ones32", [128, 128], F32)
        self.selbf = P.sb("selbf", [128, 9, 128], BF16)
        self.ones_bf = P.sb("ones_bf", [128, 128], BF16)
        self.cst = P.sb("cst", [128, 8], F32)
        self.ws.slots = P.sb("wslots", [128, NSLOT, SLOT_ELEMS], BF16)
        self.glob_end = P.sb_ptr

    def cmat_view(self, i):
        class _C:
            def __getitem__(s_, idx):
                if not isinstance(idx, tuple):
                    idx = (idx,)
                return self.cmat[(idx[0], i) + tuple(idx[1:])]
        return _C()

    def phase(self):
        self.P.sb_ptr = self.glob_end
        self.phase_id = getattr(self, "phase_id", 0) + 1
        return f"ph{self.phase_id}_"

    def vcol(self, name, k):
        pos, n = VEC_LAYOUT[name]
        return self.vecs[:, pos + k:pos + k + 1]

    def any_eng(self):
        self.rr += 1
        return ["act", "dve", "pool"][self.rr % 3]

    def load_consts(self):
        P = self.P
        P.op("sp", lambda e: e.dma_start(out=self.vecs[:, :].ap, in_=self.vecs_in[:, :].ap),
             reads=[], writes=[self.vecs[:, :]], dma_key="c0")
        cm = self.cmat[:, :, :]
        P.op("sp", lambda e: e.dma_start(out=cm.ap, in_=self.cmat_in[:, :].m(
            lambda a: a.rearrange("p (c i) -> p c i", i=128)).ap), reads=[], writes=[cm], dma_key="c0")
        P.op("dve", lambda e: e.tensor_copy(out=self.identbf[:, :].ap, in_=self.ident[:, :].ap),
             reads=[self.ident[:, :]], writes=[self.identbf[:, :]])
        P.op("dve", lambda e: e.tensor_copy(out=self.mask2[:, :].ap, in_=self.cmat[:, 3:5, :].m(
            lambda a: a.rearrange("p c i -> p (c i)")).ap), reads=[self.cmat[:, 3:5, :]], writes=[self.mask2[:, :]])
        P.op("pool", lambda e: e.memset(self.ones32[:, :].ap, 1.0), writes=[self.ones32[:, :]])
        P.op("dve", lambda e: e.tensor_copy(out=self.selbf[:, :, :].ap, in_=self.cmat[:, 5:14, :].ap),
             reads=[self.cmat[:, 5:14, :]], writes=[self.selbf[:, :, :]])
        P.op("pool", lambda e: e.memset(self.cst[:, 3:4].ap, 1.0), writes=[self.cst[:, 3:4]])
        P.op("pool", lambda e: e.memset(self.ones_bf[:, :].ap, 1.0), writes=[self.ones_bf[:, :]])
        P.op("pool", lambda e: e.memset(self.cst[:, 0:1].ap, RMS_EPS), writes=[self.cst[:, 0:1]])
        P.op("pool", lambda e: e.memset(self.cst[:, 1:2].ap, LN_EPS), writes=[self.cst[:, 1:2]])
        P.op("pool", lambda e: e.memset(self.cst[:, 2:3].ap, 0.0), writes=[self.cst[:, 2:3]])

    def prepass(self, sl):
        P = self.P
        pre = self.phase()
        CW = 256
        NB_ = 3
        st32 = [P.sb(pre + f"st32_{i}", [128, NF, CW], F32) for i in range(NB_)]
        stbf = [P.sb(pre + f"stbf_{i}", [128, 2, NF, 128], BF16) for i in range(NB_)]
        it = 0

        def one(src_t, l, nk, c0, dst_fn, ncols=CW):
            nonlocal it
            b = it % NB_
            eng = "dve" if it % 2 == 0 else "act"
            it += 1
            s32 = st32[b][:, 0:nk, 0:ncols]
            src = src_t[l, :, c0:c0 + ncols].m(lambda a: a.rearrange("(k p) n -> p k n", p=128))
            P.op("sp", lambda e: e.dma_start(out=s32.ap, in_=src.ap), reads=[], writes=[s32], dma_key=f"pi{b}")
            if ncols == CW:
                sbf = stbf[b][:, :, 0:nk, :]
                s32p = s32.m(lambda a: a.rearrange("p k (j i) -> p j k i", j=2))
            else:
                sbf = stbf[b][:, 0, 0:nk, 0:ncols]
                s32p = s32
            if eng == "act":
                P.op("act", lambda e: e.copy(out=sbf.ap, in_=s32p.ap), reads=[s32], writes=[sbf])
            else:
                P.op(eng, lambda e: e.tensor_copy(out=sbf.ap, in_=s32p.ap), reads=[s32], writes=[sbf])
            dst = dst_fn()
            P.op("act", lambda e: e.dma_start(out=dst.ap, in_=sbf.ap), reads=[sbf], writes=[dst], dma_key=f"po{b}")

        for l in range(DEPTH):
            for fi, nm in enumerate(("ffn1", "ffn2")):
                if (nm, l) not in sl:
                    continue
                idx = l * 2 + fi
                for gu, wn in enumerate(("_w_gate", "_w_up")):
                    for g in range(NF // 2):
                        one(self.w[nm + wn], l, NK, g * CW,
                            lambda idx=idx, g=g, gu=gu: self.WguR[idx, 2 * g:2 * g + 2, :, gu * NK * 128:(gu + 1) * NK * 128]
                            .m(lambda a: a.rearrange("f p (k i) -> p f k i", i=128)))
                for g in range(NK // 2):
                    one(self.w[nm + "_w_down"], l, NF, g * CW,
                        lambda idx=idx, g=g: self.WdR[idx, 2 * g:2 * g + 2, :, :]
                        .m(lambda a: a.rearrange("c p (f i) -> p c f i", i=128)))
        for l in range(DEPTH):
            if ("mix", l) not in sl:
                continue

            def win_dst(l, c, n):
                if n == 2:
                    return lambda: self.WinR[l, c:c + 2, :, :].m(lambda a: a.rearrange("c p (k i) -> p c k i", i=128))
                return None
            for g in range(9):
                one(self.w["w_in"], l, NK, g * CW, win_dst(l, 2 * g, 2))
            one(self.w["w_in"], l, NK, 2304, lambda l=l: self.WinR[l, 18, :, :].m(
                lambda a: a.rearrange("p (k i) -> p k i", i=128)), ncols=128)
            one(self.w["w_in"], l, NK, 2432, lambda l=l: self.WinR[l, 19, :, :].m(
                lambda a: a.rearrange("p (k i) -> p k i", i=128)), ncols=128)
            one(self.w["w_in"], l, NK, 2438, win_dst(l, 20, 2))
            one(self.w["w_in"], l, NK, 2694, win_dst(l, 22, 2))
            for g in range(4):
                one(self.w["w_out"], l, NK, g * CW,
                    lambda l=l, g=g: self.WoutR[l, 2 * g:2 * g + 2, :, :]
                    .m(lambda a: a.rearrange("c p (k i) -> p c k i", i=128)))

    def load_x(self, s):
        P = self.P
        pre = self.phase()
        stg = [P.sb(pre + f"xs{i}", [128, D], F32) for i in range(3)]
        pst = [P.ps(f"tp{i}", i, [128, 4, 128], F32) for i in range(4)]
        n = 0
        for b in range(SEQ // 128):
            sg = stg[b % 3]
            src = self.x_in[s, b * 128:(b + 1) * 128, :]
            P.op("sp", lambda e, sg=sg, src=src: e.dma_start(out=sg[:, :].ap, in_=src.ap),
                 reads=[], writes=[sg[:, :]], dma_key=f"xi{b % 3}")
            for half in range(2):
                pt = pst[n % 4]
                n += 1
                for j in range(4):
                    k = half * 4 + j
                    P.op("pe", lambda e, pt=pt, j=j, k=k, sg=sg: e.transpose(
                        out=pt[:, j, :].ap, in_=sg[:, k * 128:(k + 1) * 128].ap, identity=self.ident[:, :].ap),
                        reads=[sg[:, k * 128:(k + 1) * 128], self.ident[:, :]], writes=[pt[:, j, :]])
                dst = self.xT[:, half * 4:half * 4 + 4, b * 128:(b + 1) * 128]
                if n % 2 == 0:
                    P.op("act", lambda e, pt=pt, dst=dst: e.copy(out=dst.ap, in_=pt[:, :, :].ap),
                         reads=[pt[:, :, :]], writes=[dst])
                else:
                    P.op("dve", lambda e, pt=pt, dst=dst: e.tensor_copy(out=dst.ap, in_=pt[:, :, :].ap),
                         reads=[pt[:, :, :]], writes=[dst])

    def rms_tile(self, pre, t, gname, hT, ss_ps, sq, rstd, dst_fn=None):
        P = self.P
        tl = slice(t * TT_, (t + 1) * TT_)
        for k in range(NK):
            q = sq[k % 2]
            xin = self.xT[:, k, tl]
            P.op("act", lambda e, q=q, xin=xin: e.activation(out=q[:, :].ap, in_=xin.ap, func=AF.Square),
                 reads=[xin], writes=[q[:, :]])
            P.op("pe", lambda e, q=q, k=k: e.matmul(ss_ps[:, :].ap, lhsT=self.ones_bf[:, :].ap, rhs=q[:, :].ap,
                                                    start=(k == 0), stop=(k == NK - 1)),
                 reads=[q[:, :], self.ones_bf[:, :]], writes=[ss_ps[:, :]])
        self.rsqrt_mean(rstd[:, :], ss_ps[:, :], 1.0 / D, RMS_EPS)
        for k in range(NK):
            xin = self.xT[:, k, tl]
            g = self.vcol(gname, k)
            dst = hT[:, k, :] if dst_fn is None else dst_fn(k)
            eng = "dve"
            P.op(eng, lambda e, xin=xin, g=g, dst=dst: e.scalar_tensor_tensor(
                out=dst.ap, in0=xin.ap, scalar=g.ap, in1=rstd[:, :].ap, op0=ALU.mult, op1=ALU.mult),
                reads=[xin, g, rstd[:, :]], writes=[dst])

    def rsqrt_mean(self, dst, src_ps, scale, eps):
        P = self.P
        ec = self.cst[:, 0:1] if eps == RMS_EPS else self.cst[:, 1:2]
        P.op("act", lambda e: e.activation(out=dst.ap, in_=src_ps.ap, func=AF.Sqrt, bias=ec.ap, scale=scale),
             reads=[src_ps, ec], writes=[dst])
        P.op("dve", lambda e: e.reciprocal(out=dst.ap, in_=dst.ap), reads=[dst], writes=[dst])

    def ffn(self, l, fi):
        P = self.P
        pre = self.phase()
        idx = l * 2 + fi
        gname = f"ffn{fi + 1}_norm{l}"
        hT = [P.sb(pre + f"hT{i}", [128, NK, TT_], BF16) for i in range(2)]
        sq = [P.sb(pre + f"sq{i}", [128, TT_], BF16) for i in range(2)]
        rstd = P.sb(pre + "rstd", [128, TT_], F32)
        sg = [P.sb(pre + f"sg{i}", [128, TT_], F32) for i in range(2)]
        aT = P.sb(pre + "aT", [128, NF, TT_], BF16)
        ss_ps = P.ps("ffn_ss", 0, [128, TT_], F32)
        pg = [P.ps(f"ffn_pg{i}", 1 + i, [128, TT_], F32) for i in range(2)]
        pu = [P.ps(f"ffn_pu{i}", 3 + i, [128, TT_], F32) for i in range(2)]
        py = [P.ps(f"ffn_py{i}", 5 + i, [128, TT_], F32) for i in range(2)]
        self.rms_tile(pre, 0, gname, hT[0], ss_ps, sq, rstd)
        for t in range(NT):
            tl = slice(t * TT_, (t + 1) * TT_)
            h = hT[t % 2]
            for f in range(NF):
                wv, _ = self.ws.next(lambda f=f: self.WguR[idx, f, :, :], 2 * NK * 128)
                g_ps = pg[f % 2]
                u_ps = pu[f % 2]
                for gu, ps_ in ((0, g_ps), (1, u_ps)):
                    for k in range(NK):
                        lw = wv.m(lambda a, gu=gu, k=k: a[:, (gu * NK + k) * 128:(gu * NK + k + 1) * 128])
                        P.op("pe", lambda e, ps_=ps_, lw=lw, h=h, k=k: e.matmul(
                            ps_[:, :].ap, lhsT=lw.ap, rhs=h[:, k, :].ap, start=(k == 0), stop=(k == NK - 1)),
                            reads=[wv, h[:, k, :]], writes=[ps_[:, :]])
                s_ = sg[f % 2]
                P.op("act", lambda e, s_=s_, g_ps=g_ps: e.activation(out=s_[:, :].ap, in_=g_ps[:, :].ap, func=AF.Silu),
                     reads=[g_ps[:, :]], writes=[s_[:, :]])
                dst = aT[:, f, :]
                P.op("dve", lambda e, s_=s_, u_ps=u_ps, dst=dst: e.tensor_tensor(
                    out=dst.ap, in0=u_ps[:, :].ap, in1=s_[:, :].ap, op=ALU.mult),
                    reads=[u_ps[:, :], s_[:, :]], writes=[dst])
            if t + 1 < NT:
                self.rms_tile(pre, t + 1, gname, hT[(t + 1) % 2], ss_ps, sq, rstd)
            for c in range(NK):
                wv, _ = self.ws.next(lambda c=c: self.WdR[idx, c, :, :], NF * 128)
                y_ps = py[c % 2]
                for f in range(NF):
                    lw = wv.m(lambda a, f=f: a[:, f * 128:(f + 1) * 128])
                    P.op("pe", lambda e, y_ps=y_ps, lw=lw, f=f: e.matmul(
                        y_ps[:, :].ap, lhsT=lw.ap, rhs=aT[:, f, :].ap, start=(f == 0), stop=(f == NF - 1)),
                        reads=[wv, aT[:, f, :]], writes=[y_ps[:, :]])
                xin = self.xT[:, c, tl]
                P.op("dve", lambda e, y_ps=y_ps, xin=xin: e.scalar_tensor_tensor(
                    out=xin.ap, in0=y_ps[:, :].ap, scalar=0.5, in1=xin.ap, op0=ALU.mult, op1=ALU.add),
                    reads=[y_ps[:, :], xin], writes=[xin])

    def store_out(self, s, normalize=True):
        P = self.P
        pre = self.phase()
        sq = [P.sb(pre + f"sq{i}", [128, TT_], BF16) for i in range(2)]
        rstd = P.sb(pre + "rstd", [128, TT_], F32)
        yT = [P.sb(pre + f"yT{i}", [128, NK, TT_], F32) for i in range(2)]
        stg = [P.sb(pre + f"os{i}", [128, D], F32) for i in range(3)]
        ss_ps = P.ps("ffn_ss", 0, [128, TT_], F32)
        pst = [P.ps(f"otp{i}", 1 + i, [128, 4, 128], F32) for i in range(4)]
        n = 0
        nb = 0
        for t in range(NT):
            tl = slice(t * TT_, (t + 1) * TT_)
            y = yT[t % 2]
            if normalize:
                for k in range(NK):
                    q = sq[k % 2]
                    xin = self.xT[:, k, tl]
                    P.op("act", lambda e, q=q, xin=xin: e.activation(out=q[:, :].ap, in_=xin.ap, func=AF.Square),
                         reads=[xin], writes=[q[:, :]])
                    P.op("pe", lambda e, q=q, k=k: e.matmul(ss_ps[:, :].ap, lhsT=self.ones_bf[:, :].ap, rhs=q[:, :].ap,
                                                            start=(k == 0), stop=(k == NK - 1)),
                         reads=[q[:, :], self.ones_bf[:, :]], writes=[ss_ps[:, :]])
                self.rsqrt_mean(rstd[:, :], ss_ps[:, :], 1.0 / D, RMS_EPS)
                for k in range(NK):
                    xin = self.xT[:, k, tl]
                    g = self.vcol("final_norm", k)
                    dst = y[:, k, :]
                    eng = "dve"
                    P.op(eng, lambda e, xin=xin, g=g, dst=dst: e.scalar_tensor_tensor(
                        out=dst.ap, in0=xin.ap, scalar=g.ap, in1=rstd[:, :].ap, op0=ALU.mult, op1=ALU.mult),
                        reads=[xin, g, rstd[:, :]], writes=[dst])
            for bb in range(TT_ // 128):
                b = t * (TT_ // 128) + bb
                sg = stg[nb % 3]
                nb += 1
                for half in range(2):
                    pt = pst[n % 4]
                    n += 1
                    for j in range(4):
                        k = half * 4 + j
                        src = (y[:, k, bb * 128:(bb + 1) * 128] if normalize
                               else self.xT[:, k, b * 128:(b + 1) * 128])
                        P.op("pe", lambda e, pt=pt, j=j, src=src: e.transpose(
                            out=pt[:, j, :].ap, in_=src.ap, identity=self.ident[:, :].ap),
                            reads=[src, self.ident[:, :]], writes=[pt[:, j, :]])
                    dst = sg[:, half * 512:(half + 1) * 512]
                    pin = pt[:, :, :].m(lambda a: a.rearrange("p a b -> p (a b)"))
                    if n % 2 == 0:
                        P.op("act", lambda e, pin=pin, dst=dst: e.copy(out=dst.ap, in_=pin.ap),
                             reads=[pin], writes=[dst])
                    else:
                        P.op("dve", lambda e, pin=pin, dst=dst: e.tensor_copy(out=dst.ap, in_=pin.ap),
                             reads=[pin], writes=[dst])
                dsto = self.out[s, b * 128:(b + 1) * 128, :]
                P.op("sp", lambda e, sg=sg, dsto=dsto: e.dma_start(out=dsto.ap, in_=sg[:, :].ap),
                     reads=[sg[:, :]], writes=[], dma_key=f"xo{(nb - 1) % 3}")

    def body(self):
        st = self.stages
        full = [(n, l) for l in range(DEPTH) for n in ("ffn1", "mix", "ffn2")]
        if isinstance(st, list):
            sl = st
        elif st is None:
            sl = full
        else:
            sl = full[:st]
        self.P.stage = "consts"
        self.load_consts()
        self.P.stage = "prepass"
        self.prepass(sl)
        for s in range(NSEQ):
            self.P.stage = f"s{s}.load_x"
            self.load_x(s)
            for name, l in sl:
                self.P.stage = f"s{s}.{name}{l}"
                if name == "ffn1":
                    self.ffn(l, 0)
                elif name == "ffn2":
                    self.ffn(l, 1)
                else:
                    self.mixer(l)
            self.P.stage = f"s{s}.store"
            self.store_out(s, normalize=(st is None))

    def mixer(self, l):
        P = self.P
        pre = self.phase()
        parts = self.parts
        hT = P.sb(pre + "hT", [128, NK, SEQ], BF16)
        ymix = P.sb(pre + "ymix", [128, NK, SEQ], BF16)
        self.mix_base = P.sb_ptr
        sq = [P.sb(pre + f"sq{i}", [128, TT_], BF16) for i in range(2)]
        rstd = P.sb(pre + "rstd", [128, TT_], F32)
        ss_ps = P.ps("ffn_ss", 0, [128, TT_], F32)
        for t in range(NT):
            tl = slice(t * TT_, (t + 1) * TT_)
            self.rms_tile(pre, t, f"mix_norm{l}", None, ss_ps, sq, rstd, dst_fn=lambda k, tl=tl: hT[:, k, tl])
        if len(parts) < 3:
            for k in range(NK):
                P.op("pool", lambda e, k=k: e.memset(ymix[:, k, :].ap, 0.0), writes=[ymix[:, k, :]])
        st0 = P.stage
        if "att" in parts:
            P.stage = st0 + ".att"
            self.attention(l, pre, hT, ymix)
        if "ssd" in parts:
            P.stage = st0 + ".ssd"
            self.ssd(l, pre, hT, ymix)
        if "sgu" in parts:
            P.stage = st0 + ".sgu"
            self.sgu(l, pre, hT, ymix)
        P.stage = st0 + ".wout"
        self.wout(l, ymix)

    def proj_w(self, l, c):
        wv, _ = self.ws.next(lambda: self.WinR[l, c, :, :], NK * 128)
        return wv

    def proj_mm(self, wv, ncols, rhs_fn, ps):
        P = self.P
        for k in range(NK):
            lw = wv.m(lambda a, k=k: a[:, k * 128:k * 128 + ncols])
            r = rhs_fn(k)
            P.op("pe", lambda e, lw=lw, r=r, k=k: e.matmul(ps.ap, lhsT=lw.ap, rhs=r.ap, start=(k == 0),
                                                           stop=(k == NK - 1)),
                 reads=[wv, r], writes=[ps])

    def wout(self, l, ymix):
        P = self.P
        py = [P.ps(f"ffn_py{i}", 5 + i, [128, TT_], F32) for i in range(2)]
        n = 0
        for co in range(NK):
            wv, _ = self.ws.next(lambda co=co: self.WoutR[l, co, :, :], NK * 128)
            for t in range(NT):
                tl = slice(t * TT_, (t + 1) * TT_)
                y_ps = py[n % 2]
                n += 1
                self.proj_mm(wv, 128, lambda k, tl=tl: ymix[:, k, tl], y_ps[:, :])
                xin = self.xT[:, co, tl]
                P.op("dve", lambda e, y_ps=y_ps, xin=xin: e.tensor_tensor(
                    out=xin.ap, in0=y_ps[:, :].ap, in1=xin.ap, op=ALU.add),
                    reads=[y_ps[:, :], xin], writes=[xin])

    def attention(self, l, pre0, hT, ymix):
        P = self.P
        P.sb_ptr = self.mix_base
        pre = pre0 + "att_"
        qT = P.sb(pre + "qT", [128, SEQ], BF16)
        kT = P.sb(pre + "kT", [128, SEQ], BF16)
        vT = P.sb(pre + "vT", [128, SEQ], BF16)
        Vtok = P.sb(pre + "Vtok", [128, 16, 128], BF16)
        PT = [P.sb(pre + f"PT{i}", [128, 256], BF16) for i in range(4)]
        accN = P.sb(pre + "accN", [128, SEQ], F32)
        accD = P.sb(pre + "accD", [128, SEQ], F32)
        ps_proj = [P.ps(f"att_pp{i}", i, [128, 512], F32) for i in range(2)]
        ps_tr = [P.ps(f"att_tr{i}", 0, [128, 4, 128], BF16, byte_off=(i % 2) * 1024) for i in range(2)]
        ps_S = [P.ps(f"att_S{i}", 1 + i, [128, 256], F32) for i in range(3)]
        ps_ON = [P.ps(f"att_ON{i}", 4 + i, [128, 512], F32) for i in range(2)]
        ps_OD = [P.ps(f"att_OD{i}", 6 + i, [128, 512], F32) for i in range(2)]
        cnt = {"pp": 0, "tr": 0, "S": 0, "PT": 0, "ev": 0}
        gbase = 0
        for c in range(3):
            for which, dstT in ((0, qT), (1, kT), (2, vT)):
                wv = self.proj_w(l, 3 * which + c)
                for t in range(NT):
                    tl = slice(t * TT_, (t + 1) * TT_)
                    ps = ps_proj[cnt["pp"] % 2]
                    cnt["pp"] += 1
                    self.proj_mm(wv, 128, lambda k, tl=tl: hT[:, k, tl], ps[:, :])
                    dst = dstT[:, tl]
                    if which == 0:
                        P.op("act", lambda e, ps=ps, dst=dst: e.mul(out=dst.ap, in_=ps[:, :].ap, mul=0.125),
                             reads=[ps[:, :]], writes=[dst])
                    elif which == 1:
                        P.op("dve", lambda e, ps=ps, dst=dst: e.tensor_copy(out=dst.ap, in_=ps[:, :].ap),
                             reads=[ps[:, :]], writes=[dst])
                    else:
                        P.op("act", lambda e, ps=ps, dst=dst: e.copy(out=dst.ap, in_=ps[:, :].ap),
                             reads=[ps[:, :]], writes=[dst])
            for br, d in enumerate((1, 4, 16)):
                nb = 16 // d

                def tokstart(blk):
                    if d == 1:
                        return 128 * blk
                    if d == 4:
                        return (blk // 4) + 512 * (blk % 4)
                    return blk
                for g4 in range(4):
                    pt = ps_tr[cnt["tr"] % 2]
                    cnt["tr"] += 1
                    for j in range(4):
                        st = tokstart(g4 * 4 + j)
                        src = vT[:, st:st + 127 * d + 1:d]
                        P.op("pe", lambda e, pt=pt, j=j, src=src: e.transpose(
                            out=pt[:, j, :].ap, in_=src.ap, identity=self.identbf[:, :].ap),
                            reads=[src, self.identbf[:, :]], writes=[pt[:, j, :]])
                    dst = Vtok[:, g4 * 4:(g4 + 1) * 4, :]
                    if g4 % 2 == 0:
                        P.op("act", lambda e, pt=pt, dst=dst: e.copy(out=dst.ap, in_=pt[:, :, :].ap),
                             reads=[pt[:, :, :]], writes=[dst])
                    else:
                        P.op("dve", lambda e, pt=pt, dst=dst: e.tensor_copy(out=dst.ap, in_=pt[:, :, :].ap),
                             reads=[pt[:, :, :]], writes=[dst])
                its = []
                for r in range(d):
                    for n in range(nb):
                        for h in range(2):
                            its.append((r, n, h))
                LAG = 2
                pend = {}
                for ii in range(len(its) + LAG):
                    if ii < len(its):
                        r, n, h = its[ii]
                        st = r + d * 128 * n
                        nq = 256 if n + 1 < nb else 128
                        kset = slice(st, st + 127 * d + 1, d)
                        qset = slice(st, st + (nq - 1) * d + 1, d)
                        hp = slice(64 * h, 64 * h + 64)
                        S = ps_S[cnt["S"] % len(ps_S)]
                        cnt["S"] += 1
                        kk = kT[hp, kset]
                        qq = qT[hp, qset]
                        Sv = S[:, 0:nq]
                        P.op("pe", lambda e, Sv=Sv, kk=kk, qq=qq: e.matmul(Sv.ap, lhsT=kk.ap, rhs=qq.ap,
                                                                           start=True, stop=True),
                             reads=[kk, qq], writes=[Sv])
                        pt_ = PT[cnt["PT"] % len(PT)]
                        cnt["PT"] += 1
                        pv = pt_[:, 0:nq]
                        P.op("act", lambda e, pv=pv, Sv=Sv: e.activation(out=pv.ap, in_=Sv.ap, func=AF.Exp),
                             reads=[Sv], writes=[pv])
                        mk = self.mask2[:, 0:nq]
                        P.op("pool", lambda e, pv=pv, mk=mk: e.tensor_tensor(out=pv.ap, in0=pv.ap, in1=mk.ap,
                                                                             op=ALU.mult),
                             reads=[pv, mk], writes=[pv])
                        pend[ii] = pt_
                    jj = ii - LAG
                    if jj < 0:
                        continue
                    r, n, h = its[jj]
                    pt_ = pend.pop(jj)
                    blk = n if d == 1 else (r * 4 + n if d == 4 else r)
                    nq = 256 if n + 1 < nb else 128
                    hp = slice(64 * h, 64 * h + 64)
                    vv = Vtok[:, blk, 64 * h:64 * h + 64]
                    on1 = self.ones_bf[:, 0:64]
                    for half in range(nq // 128):
                        qi = blk + half
                        G = gbase + qi // 4
                        pos = qi % 4
                        cs_ = slice(pos * 128, (pos + 1) * 128)
                        rhs = pt_[:, half * 128:(half + 1) * 128]
                        if half == 0:
                            fl = dict(start=(n == 0), stop=True)
                        else:
                            fl = dict(start=True, stop=False)
                        for lhs, dstp in ((vv, ps_ON[G % 2][hp, cs_]), (on1, ps_OD[G % 2][hp, cs_])):
                            P.op("pe", lambda e, lhs=lhs, dstp=dstp, rhs=rhs, fl=fl: e.matmul(
                                dstp.ap, lhsT=lhs.ap, rhs=rhs.ap, skip_group_check=True, **fl),
                                reads=[lhs, rhs], writes=[dstp])
                    if h == 1 and blk % 4 == 3:
                        Gl = blk // 4
                        G = gbase + Gl
                        if d == 1:
                            vw = lambda a, Gl=Gl: a[:, 512 * Gl:512 * Gl + 512]
                            pw = lambda a: a
                        elif d == 4:
                            vw = lambda a, Gl=Gl: a[:, Gl:SEQ:4]
                            pw = lambda a: a
                        else:
                            vw = lambda a, Gl=Gl: a.rearrange("p (i r) -> p r i", r=16)[:, 4 * Gl:4 * Gl + 4, :]
                            pw = lambda a: a.rearrange("p (r i) -> p r i", i=128)
                        for acc, psb, eng0 in ((accN, ps_ON[G % 2], "act"), (accD, ps_OD[G % 2], "dve")):
                            full = acc[:, :]
                            if d == 1:
                                av = acc[:, 512 * Gl:512 * Gl + 512]
                            else:
                                av = V(vw(full.ap), full.cells)
                            pp = psb[:, :]
                            pin = V(pw(pp.ap), pp.cells)
                            if br == 0:
                                if eng0 == "act":
                                    P.op("act", lambda e, av=av, pin=pin: e.copy(out=av.ap, in_=pin.ap),
                                         reads=[pin], writes=[av])
                                else:
                                    P.op("dve", lambda e, av=av, pin=pin: e.tensor_copy(out=av.ap, in_=pin.ap),
                                         reads=[pin], writes=[av])
                            else:
                                P.op("dve", lambda e, av=av, pin=pin: e.tensor_tensor(
                                    out=av.ap, in0=pin.ap, in1=av.ap, op=ALU.add),
                                    reads=[pin, av], writes=[av])
                gbase += 4
            for t in range(NT):
                tl = slice(t * TT_, (t + 1) * TT_)
                P.op("dve", lambda e, tl=tl: e.reciprocal(out=accD[:, tl].ap, in_=accD[:, tl].ap),
                     reads=[accD[:, tl]], writes=[accD[:, tl]])
                dst = ymix[:, c, tl]
                P.op("dve", lambda e, tl=tl, dst=dst: e.tensor_tensor(out=dst.ap, in0=accN[:, tl].ap,
                                                                      in1=accD[:, tl].ap, op=ALU.mult),
                     reads=[accN[:, tl], accD[:, tl]], writes=[dst])

    def sgu(self, l, pre0, hT, ymix):
        P = self.P
        P.sb_ptr = self.mix_base
        pre = pre0 + "sgu_"
        wraw = P.sb(pre + "wraw", [128, 4, 128], F32)
        WcT32 = P.sb(pre + "WcT32", [128, 4, 128], F32)
        WcTb = P.sb(pre + "WcTb", [128, 4, 128], BF16)
        bsrow = P.sb(pre + "bsrow", [128, 512], F32)
        Kt4 = P.sb(pre + "Kt4", [128, 2, 4, 128], F32)
        gu = [P.sb(pre + f"gu{i}", [128, 2, TT_], BF16) for i in range(2)]
        vg = [P.sb(pre + f"vg{i}", [128, 256], F32) for i in range(2)]
        cen = P.sb(pre + "cen", [128, 4, 256], F32)
        sqc = P.sb(pre + "sqc", [128, 256], F32)
        stat = [P.sb(pre + f"stat{i}", [128, 16], F32) for i in range(2)]
        nbf = [P.sb(pre + f"nbf{i}", [128, 256], BF16) for i in range(2)]
        tmp = P.sb(pre + "tmp", [128, TT_], F32)
        pp = [P.ps(f"att_pp{i}", i, [128, 512], F32) for i in range(2)]
        vps = [P.ps(f"sgu_v{i}", 2, [128, 256], F32, byte_off=i * 1024) for i in range(2)]
        mps = [P.ps(f"sgu_m{i}", 3 + i, [128, 512], F32) for i in range(2)]
        trps = P.ps("sgu_tr", 5, [128, 128], F32)
        kps = P.ps("sgu_k", 5, [128, 128], F32, byte_off=1024)
        tril = self.cmat_view(14)
        lbpos = VEC_LAYOUT[f"sgu_ln_b{l}"][0]
        P.op("sp", lambda e: e.dma_start(out=wraw[:, :, :].ap, in_=self.sgu_w_in[l, :, :, :].m(
            lambda a: a.rearrange("g t s -> t g s")).ap), reads=[], writes=[wraw[:, :, :]], dma_key="sg")
        P.op("sp", lambda e: e.dma_start(out=bsrow[0:1, :].ap, in_=self.sgu_b_in[l:l + 1, :].ap),
             reads=[], writes=[bsrow[0:1, :]], dma_key="sg")
        if SGU_STOP <= 1:
            return
        for g in range(4):
            w_ = wraw[:, g, :]
            if "nomask" not in SGU_VAR:
                P.op("dve", lambda e, w_=w_: e.tensor_tensor(out=w_.ap, in0=w_.ap, in1=tril[:, :].ap, op=ALU.mult),
                     reads=[w_, tril[:, :]], writes=[w_])
            if "notr" in SGU_VAR:
                continue
            P.op("pe", lambda e, w_=w_: e.transpose(out=trps[:, :].ap, in_=w_.ap, identity=self.ident[:, :].ap),
                 reads=[w_, self.ident[:, :]], writes=[trps[:, :]])
            P.op("act", lambda e, g=g: e.copy(out=WcT32[:, g, :].ap, in_=trps[:, :].ap),
                 reads=[trps[:, :]], writes=[WcT32[:, g, :]])
            P.op("dve", lambda e, g=g: e.tensor_copy(out=WcTb[:, g, :].ap, in_=WcT32[:, g, :].ap),
                 reads=[WcT32[:, g, :]], writes=[WcTb[:, g, :]])
        if SGU_STOP <= 2:
            return
        for cc in range(2):
            for gg in range(2):
                g = 2 * cc + gg
                kp = kps[64 * gg:64 * gg + 64, :]
                lb = self.vecs[:, lbpos + g * 64:lbpos + (g + 1) * 64]
                P.op("pe", lambda e, kp=kp, lb=lb, g=g: e.matmul(kp.ap, lhsT=lb.ap, rhs=WcT32[:, g, :].ap,
                                                                 start=True, stop=False, skip_group_check=True),
                     reads=[lb, WcT32[:, g, :]], writes=[kp])
                on = self.ones32[0:1, 0:64]
                br_ = bsrow[0:1, g * 128:(g + 1) * 128]
                P.op("pe", lambda e, kp=kp, on=on, br_=br_: e.matmul(kp.ap, lhsT=on.ap, rhs=br_.ap,
                                                                     start=False, stop=True, skip_group_check=True),
                     reads=[on, br_], writes=[kp])
            for b in range(4):
                dst = Kt4[:, cc, b, :]
                if b % 2 == 0:
                    P.op("act", lambda e, dst=dst: e.copy(out=dst.ap, in_=kps[:, :].ap), reads=[kps[:, :]], writes=[dst])
                else:
                    P.op("dve", lambda e, dst=dst: e.tensor_copy(out=dst.ap, in_=kps[:, :].ap),
                         reads=[kps[:, :]], writes=[dst])
        nb_ = 0
        if SGU_STOP <= 3:
            return
        for t in range(NT):
            tl = slice(t * TT_, (t + 1) * TT_)
            gut = gu[t % 2]
            st_ = stat[t % 2]
            for cc in range(2):
                wv = self.proj_w(l, 20 + cc)
                ps = pp[cc]
                self.proj_mm(wv, 128, lambda k, tl=tl: hT[:, k, tl], ps[:, :])
                P.op("act", lambda e, ps=ps, cc=cc, gut=gut: e.activation(out=gut[:, cc, :].ap, in_=ps[:, :].ap,
                                                                          func=AF.Gelu),
                     reads=[ps[:, :]], writes=[gut[:, cc, :]])
            if SGU_STOP <= 4:
                continue
            w0 = self.proj_w(l, 22)
            w1 = self.proj_w(l, 23)
            for b in range(4):
                bl = slice(t * TT_ + b * 128, t * TT_ + (b + 1) * 128)
                vp = vps[b % 2]
                for half, w in ((0, w0), (1, w1)):
                    vph = vp[:, half * 128:(half + 1) * 128]
                    for k in range(NK):
                        hk = hT[:, k, bl]
                        wk = w.m(lambda a, k=k: a[:, k * 128:(k + 1) * 128])
                        P.op("pe", lambda e, vph=vph, hk=hk, wk=wk, k=k: e.matmul(
                            vph.ap, lhsT=hk.ap, rhs=wk.ap, start=(k == 0), stop=(k == NK - 1)),
                            reads=[hk, w], writes=[vph])
                v_ = vg[b % 2]
                P.op("act", lambda e, v_=v_, vp=vp: e.activation(out=v_[:, :].ap, in_=vp[:, :].ap, func=AF.Gelu),
                     reads=[vp[:, :]], writes=[v_[:, :]])
                sm = st_[:, b:b + 1]
                nm = st_[:, 4 + b:5 + b]
                vs = st_[:, 8 + b:9 + b]
                P.op("dve", lambda e, v_=v_, sm=sm: e.reduce_sum(out=sm.ap, in_=v_[:, :].ap, axis=mybir.AxisListType.X),
                     reads=[v_[:, :]], writes=[sm])
                P.op("dve", lambda e, sm=sm, nm=nm: e.tensor_scalar(out=nm.ap, in0=sm.ap, scalar1=-1.0 / 256,
                                                                     scalar2=None, op0=ALU.mult),
                     reads=[sm], writes=[nm])
                cb = cen[:, b, :]
                P.op("dve", lambda e, cb=cb, v_=v_, nm=nm: e.tensor_scalar(out=cb.ap, in0=v_[:, :].ap, scalar1=nm.ap,
                                                                           scalar2=None, op0=ALU.add),
                     reads=[v_[:, :], nm], writes=[cb])
                P.op("pool", lambda e, cb=cb: e.tensor_tensor(out=sqc[:, :].ap, in0=cb.ap, in1=cb.ap, op=ALU.mult),
                     reads=[cb], writes=[sqc[:, :]])
                P.op("dve", lambda e, vs=vs: e.reduce_sum(out=vs.ap, in_=sqc[:, :].ap, axis=mybir.AxisListType.X),
                     reads=[sqc[:, :]], writes=[vs])
            if SGU_STOP <= 5:
                continue
            rs = st_[:, 12:16]
            ec = self.cst[:, 1:2]
            P.op("act", lambda e, rs=rs, st_=st_, ec=ec: e.activation(out=rs.ap, in_=st_[:, 8:12].ap, func=AF.Sqrt,
                                                                      bias=ec.ap, scale=1.0 / 256),
                 reads=[st_[:, 8:12], ec], writes=[rs])
            P.op("dve", lambda e, rs=rs: e.reciprocal(out=rs.ap, in_=rs.ap), reads=[rs], writes=[rs])
            if SGU_STOP <= 6:
                continue
            for b in range(4):
                n_ = nbf[nb_ % 2]
                nb_ += 1
                cb = cen[:, b, :]
                rb = st_[:, 12 + b:13 + b]
                P.op("dve", lambda e, n_=n_, cb=cb, rb=rb: e.tensor_scalar(out=n_[:, :].ap, in0=cb.ap, scalar1=rb.ap,
                                                                           scalar2=None, op0=ALU.mult),
                     reads=[cb, rb], writes=[n_[:, :]])
                for g in range(4):
                    mp = mps[g // 2][64 * (g % 2):64 * (g % 2) + 64, b * 128:(b + 1) * 128]
                    ng = n_[:, g * 64:(g + 1) * 64]
                    P.op("pe", lambda e, mp=mp, ng=ng, g=g: e.matmul(mp.ap, lhsT=ng.ap, rhs=WcTb[:, g, :].ap,
                                                                     start=True, stop=True, skip_group_check=True),
                         reads=[ng, WcTb[:, g, :]], writes=[mp])
            for cc in range(2):
                gcol = self.vcol(f"sgu_ln_g{l}", cc)
                k4 = Kt4[:, cc, :, :].m(lambda a: a.rearrange("p b i -> p (b i)"))
                P.op("dve", lambda e, cc=cc, gcol=gcol, k4=k4: e.scalar_tensor_tensor(
                    out=tmp[:, :].ap, in0=mps[cc][:, :].ap, scalar=gcol.ap, in1=k4.ap, op0=ALU.mult, op1=ALU.add),
                    reads=[mps[cc][:, :], gcol, k4], writes=[tmp[:, :]])
                dst = ymix[:, 6 + cc, tl]
                P.op("dve", lambda e, cc=cc, dst=dst, gut=gut: e.tensor_tensor(
                    out=dst.ap, in0=tmp[:, :].ap, in1=gut[:, cc, :].ap, op=ALU.mult),
                    reads=[tmp[:, :], gut[:, cc, :]], writes=[dst])

    def ssd(self, l, pre0, hT, ymix):
        P = self.P
        P.sb_ptr = self.mix_base
        pre = pre0 + "ssd_"
        zs = P.sb(pre + "zs", [128, 3, TT_], BF16)
        stgb = [P.sb(pre + f"stg{i}", [128, TT_ + 3], F32) for i in range(2)]
        halo = P.sb(pre + "halo", [128, 7, 4], F32)
        cacc = [P.sb(pre + f"cacc{i}", [128, TT_], F32) for i in range(2)]
        xact = P.sb(pre + "xact", [128, 7, TT_], BF16)
        negA = P.sb(pre + "negA", [128, 24], F32)
        sm = [{n: P.sb(pre + f"{n}{i}", [128, 24], F32) for n in ("t1", "dt", "a", "acs", "last", "dte", "dA", "dtd")}
              for i in range(2)]
        NBUF = 6
        arep = [P.sb(pre + f"arep{i}", [128, 128], F32) for i in range(NBUF)]
        tmpL = [P.sb(pre + f"tmpL{i}", [128, 128], F32) for i in range(NBUF)]
        LT = [P.sb(pre + f"LT{i}", [128, 128], F32) for i in range(NBUF)]
        Eb = [P.sb(pre + f"E{i}", [128, 128], BF16) for i in range(NBUF)]
        MT = [P.sb(pre + f"MT{i}", [128, 128], BF16) for i in range(NBUF)]
        CsT = [P.sb(pre + f"CsT{i}", [128, 128], BF16) for i in range(NBUF)]
        Xs = [P.sb(pre + f"X{i}", [128, 384], BF16) for i in range(2)]
        Xd = [P.sb(pre + f"Xd{i}", [128, 384], BF16) for i in range(2)]
        Btok = [P.sb(pre + f"Btok{i}", [128, 2, 128], BF16) for i in range(2)]
        H32 = P.sb(pre + "H32", [128, 6, 64], F32)
        Hbf = P.sb(pre + "Hbf", [128, 6, 64], BF16)
        ycat = cacc[0]
        rst = cacc[1]
        yg = P.sb(pre + "yg", [128, 3, TT_], F32)
        sqg = P.sb(pre + "sqg", [128, 3, TT_], BF16)
        pp = [P.ps(f"att_pp{i}", i, [128, 512], F32) for i in range(2)]
        yps = [P.ps(f"ssd_y{i}", 2 + i, [128, 512], F32) for i in range(3)]
        BCp = [P.ps(f"ssd_bc{i}", (0, 1, 7)[i // 2], [128, 128], F32, byte_off=1024 * (i % 2)) for i in range(6)]
        GTp = [P.ps(f"ssd_gt{i}", 5, [128, 128], F32, byte_off=1024 * i) for i in range(2)]
        trp = [P.ps(f"ssd_tr{i}", 6, [128, 128], BF16, byte_off=256 * i) for i in range(4)]
        Hps = [P.ps(f"ssd_h{i}", 6, [128, 64], F32, byte_off=1024 + 256 * i) for i in range(2)]
        dtp = P.ps("ssd_dt", 6, [128, 24], F32, byte_off=1536)
        acp = P.ps("ssd_ac", 6, [128, 24], F32, byte_off=1664)
        lap = P.ps("ssd_la", 6, [128, 24], F32, byte_off=1792)
        ssp = P.ps("ssd_ss", 7, [128, 512], F32)
        tri = self.cmat_view(1)
        negm = self.cmat_view(2)
        one_c = self.cst[:, 3:4]
        p0 = VEC_LAYOUT[f"dt_bias{l}"][0]
        dtb = self.vecs[:, p0:p0 + 24]
        p1 = VEC_LAYOUT[f"a_log{l}"][0]
        alog = self.vecs[:, p1:p1 + 24]
        P.op("act", lambda e: e.activation(out=negA[:, :].ap, in_=alog.ap, func=AF.Exp), reads=[alog], writes=[negA[:, :]])
        P.op("dve", lambda e: e.tensor_scalar(out=negA[:, :].ap, in0=negA[:, :].ap, scalar1=-1.0, scalar2=None,
                                              op0=ALU.mult), reads=[negA[:, :]], writes=[negA[:, :]])
        P.op("pool", lambda e: e.memset(H32[:, :, :].ap, 0.0), writes=[H32[:, :, :]])
        P.op("pool", lambda e: e.memset(Hbf[:, :, :].ap, 0.0), writes=[Hbf[:, :, :]])
        P.op("pool", lambda e: e.memset(halo[:, :, :].ap, 0.0), writes=[halo[:, :, :]])
        cn = {"pp": 0, "tr": 0, "ch": 0, "hd": 0, "hp": 0}
        for t in range(NT):
            tl = slice(t * TT_, (t + 1) * TT_)
            for c in range(3):
                wv = self.proj_w(l, 9 + c)
                ps = pp[cn["pp"] % 2]
                cn["pp"] += 1
                self.proj_mm(wv, 128, lambda k, tl=tl: hT[:, k, tl], ps[:, :])
                P.op("act", lambda e, ps=ps, c=c: e.activation(out=zs[:, c, :].ap, in_=ps[:, :].ap, func=AF.Silu),
                     reads=[ps[:, :]], writes=[zs[:, c, :]])
            for c in range(7):
                wv = self.proj_w(l, 12 + c)
                ps = pp[cn["pp"] % 2]
                cn["pp"] += 1
                self.proj_mm(wv, 128, lambda k, tl=tl: hT[:, k, tl], ps[:, :])
                stg = stgb[c % 2]
                sg_ = stg[:, 3:TT_ + 3]
                P.op("pool", lambda e, stg=stg, c=c: e.tensor_copy(out=stg[:, 0:3].ap, in_=halo[:, c, 0:3].ap),
                     reads=[halo[:, c, 0:3]], writes=[stg[:, 0:3]])
                P.op("act", lambda e, ps=ps, sg_=sg_: e.copy(out=sg_.ap, in_=ps[:, :].ap), reads=[ps[:, :]], writes=[sg_])
                ca = cacc[c % 2]
                for j in range(4):
                    wj = self.vcol(f"conv_w{l}_{j}", c)
                    sj = stg[:, j:j + TT_]
                    if j == 0:
                        P.op("dve", lambda e, ca=ca, sj=sj, wj=wj: e.tensor_scalar(
                            out=ca[:, :].ap, in0=sj.ap, scalar1=wj.ap, scalar2=None, op0=ALU.mult),
                            reads=[sj, wj], writes=[ca[:, :]])
                    else:
                        P.op("dve", lambda e, ca=ca, sj=sj, wj=wj: e.scalar_tensor_tensor(
                            out=ca[:, :].ap, in0=sj.ap, scalar=wj.ap, in1=ca[:, :].ap, op0=ALU.mult, op1=ALU.add),
                            reads=[sj, wj, ca[:, :]], writes=[ca[:, :]])
                cb_ = self.vcol(f"conv_b{l}", c)
                P.op("act", lambda e, ca=ca, c=c, cb_=cb_: e.activation(out=xact[:, c, :].ap, in_=ca[:, :].ap,
                                                                        func=AF.Silu, bias=cb_.ap, scale=1.0),
                     reads=[ca[:, :], cb_], writes=[xact[:, c, :]])
                P.op("pool", lambda e, c=c, stg=stg: e.tensor_copy(out=halo[:, c, 0:3].ap, in_=stg[:, TT_:TT_ + 3].ap),
                     reads=[stg[:, TT_:TT_ + 3]], writes=[halo[:, c, 0:3]])
            wdt = self.proj_w(l, 19)
            S_ = sm[t % 2]
            for ch in range(4):
                tok = slice(t * TT_ + ch * 128, t * TT_ + (ch + 1) * 128)
                dpc = dtp[:, ch * 6:(ch + 1) * 6]
                for k in range(NK):
                    hk = hT[:, k, tok]
                    wk = wdt.m(lambda a, k=k: a[:, k * 128:k * 128 + 6])
                    P.op("pe", lambda e, hk=hk, wk=wk, k=k, dpc=dpc: e.matmul(dpc.ap, lhsT=hk.ap, rhs=wk.ap,
                                                                              start=(k == 0), stop=(k == NK - 1)),
                         reads=[hk, wdt], writes=[dpc])
            t1, dt, a_, acs, last, dte, dA, dtd = (S_[n][:, :] for n in ("t1", "dt", "a", "acs", "last", "dte", "dA", "dtd"))
            P.op("dve", lambda e, t1=t1: e.tensor_tensor(out=t1.ap, in0=dtp[:, :].ap, in1=dtb.ap, op=ALU.add),
                 reads=[dtp[:, :], dtb], writes=[t1])
            P.op("act", lambda e, t1=t1: e.activation(out=t1.ap, in_=t1.ap, func=AF.Exp), reads=[t1], writes=[t1])
            P.op("act", lambda e, t1=t1, dt=dt: e.activation(out=dt.ap, in_=t1.ap, func=AF.Ln, bias=one_c.ap, scale=1.0),
                 reads=[t1, one_c], writes=[dt])
            P.op("dve", lambda e, a_=a_, dt=dt: e.tensor_tensor(out=a_.ap, in0=dt.ap, in1=negA[:, :].ap, op=ALU.mult),
                 reads=[dt, negA[:, :]], writes=[a_])
            P.op("pe", lambda e, a_=a_: e.matmul(acp[:, :].ap, lhsT=tri[:, :].ap, rhs=a_.ap, start=True, stop=True),
                 reads=[tri[:, :], a_], writes=[acp[:, :]])
            P.op("pe", lambda e, a_=a_: e.matmul(lap[:, :].ap, lhsT=self.ones32[:, :].ap, rhs=a_.ap, start=True, stop=True),
                 reads=[self.ones32[:, :], a_], writes=[lap[:, :]])
            P.op("dve", lambda e, acs=acs: e.tensor_copy(out=acs.ap, in_=acp[:, :].ap), reads=[acp[:, :]], writes=[acs])
            P.op("dve", lambda e, last=last: e.tensor_copy(out=last.ap, in_=lap[:, :].ap), reads=[lap[:, :]], writes=[last])
            P.op("dve", lambda e, dte=dte, last=last, acs=acs: e.tensor_tensor(out=dte.ap, in0=last.ap, in1=acs.ap,
                                                                              op=ALU.subtract),
                 reads=[last, acs], writes=[dte])
            P.op("act", lambda e, dte=dte: e.activation(out=dte.ap, in_=dte.ap, func=AF.Exp), reads=[dte], writes=[dte])
            P.op("act", lambda e, dA=dA, last=last: e.activation(out=dA.ap, in_=last.ap, func=AF.Exp),
                 reads=[last], writes=[dA])
            P.op("dve", lambda e, dtd=dtd, dt=dt, dte=dte: e.tensor_tensor(out=dtd.ap, in0=dt.ap, in1=dte.ap, op=ALU.mult),
                 reads=[dt, dte], writes=[dtd])
            for ch in range(4):
                lt = slice(ch * 128, (ch + 1) * 128)
                X = Xs[cn["ch"] % 2]
                XD = Xd[cn["ch"] % 2]
                BT = Btok[cn["ch"] % 2]
                cn["ch"] += 1
                for c in range(3):
                    tp = trp[cn["tr"] % 4]
                    cn["tr"] += 1
                    src = xact[:, c, lt]
                    P.op("pe", lambda e, tp=tp, src=src: e.transpose(out=tp[:, :].ap, in_=src.ap,
                                                                     identity=self.identbf[:, :].ap),
                         reads=[src, self.identbf[:, :]], writes=[tp[:, :]])
                    for hh in range(2):
                        h = 2 * c + hh
                        xh = X[:, h * 64:(h + 1) * 64]
                        xdh = XD[:, h * 64:(h + 1) * 64]
                        tph = tp[:, hh * 64:(hh + 1) * 64]
                        dth = S_["dt"][:, ch * 6 + h:ch * 6 + h + 1]
                        ddh = S_["dtd"][:, ch * 6 + h:ch * 6 + h + 1]
                        P.op("dve", lambda e, xh=xh, tph=tph, dth=dth: e.tensor_scalar(
                            out=xh.ap, in0=tph.ap, scalar1=dth.ap, scalar2=None, op0=ALU.mult),
                            reads=[tph, dth], writes=[xh])
                        P.op("dve", lambda e, xdh=xdh, tph=tph, ddh=ddh: e.tensor_scalar(
                            out=xdh.ap, in0=tph.ap, scalar1=ddh.ap, scalar2=None, op0=ALU.mult),
                            reads=[tph, ddh], writes=[xdh])
                for g in range(2):
                    tp = trp[cn["tr"] % 4]
                    cn["tr"] += 1
                    src = xact[:, 3 + g, lt]
                    P.op("pe", lambda e, tp=tp, src=src: e.transpose(out=tp[:, :].ap, in_=src.ap,
                                                                     identity=self.identbf[:, :].ap),
                         reads=[src, self.identbf[:, :]], writes=[tp[:, :]])
                    P.op("act", lambda e, tp=tp, g=g, BT=BT: e.copy(out=BT[:, g, :].ap, in_=tp[:, :].ap),
                         reads=[tp[:, :]], writes=[BT[:, g, :]])
                for g in range(2):
                    gt = GTp[g]
                    bT = xact[:, 3 + g, lt]
                    cT = xact[:, 5 + g, lt]
                    P.op("pe", lambda e, gt=gt, bT=bT, cT=cT: e.matmul(gt[:, :].ap, lhsT=bT.ap, rhs=cT.ap,
                                                                       start=True, stop=True),
                         reads=[bT, cT], writes=[gt[:, :]])
                hs = range(6)
                ahs = [S_["a"][:, ch * 6 + h:ch * 6 + h + 1] for h in hs]
                achs = [S_["acs"][:, ch * 6 + h:ch * 6 + h + 1] for h in hs]
                for h in hs:
                    ar = arep[h]
                    P.op("act", lambda e, ar=ar, ah=ahs[h]: e.activation(out=ar[:, :].ap, in_=self.ones32[:, :].ap,
                                                                         func=AF.Copy, scale=ah.ap),
                         reads=[self.ones32[:, :], ahs[h]], writes=[ar[:, :]])
                for h in hs:
                    ar, bc = arep[h], BCp[h]
                    P.op("pe", lambda e, bc=bc, ar=ar: e.matmul(bc[:, :].ap, lhsT=ar[:, :].ap, rhs=tri[:, :].ap,
                                                                start=True, stop=True),
                         reads=[ar[:, :], tri[:, :]], writes=[bc[:, :]])
                for h in hs:
                    tL, bc = tmpL[h], BCp[h]
                    P.op("dve", lambda e, tL=tL, bc=bc, ach=achs[h]: e.scalar_tensor_tensor(
                        out=tL[:, :].ap, in0=bc[:, :].ap, scalar=ach.ap, in1=negm[:, :].ap,
                        op0=ALU.subtract, op1=ALU.add),
                        reads=[bc[:, :], achs[h], negm[:, :]], writes=[tL[:, :]])
                for h in hs:
                    E_, bc = Eb[h], BCp[h]
                    P.op("act", lambda e, E_=E_, bc=bc: e.activation(out=E_[:, :].ap, in_=bc[:, :].ap, func=AF.Exp),
                         reads=[bc[:, :]], writes=[E_[:, :]])
                for h in hs:
                    L_, tL = LT[h], tmpL[h]
                    P.op("act", lambda e, L_=L_, tL=tL: e.activation(out=L_[:, :].ap, in_=tL[:, :].ap, func=AF.Exp),
                         reads=[tL[:, :]], writes=[L_[:, :]])
                for h in hs:
                    g = h // 3
                    cT = xact[:, 5 + g, lt]
                    C_, E_ = CsT[h], Eb[h]
                    P.op("pool", lambda e, C_=C_, cT=cT, E_=E_: e.tensor_tensor(out=C_[:, :].ap, in0=cT.ap,
                                                                                in1=E_[:, :].ap, op=ALU.mult),
                         reads=[cT, E_[:, :]], writes=[C_[:, :]])
                for h in hs:
                    g = h // 3
                    gt = GTp[g]
                    M_, L_ = MT[h], LT[h]
                    P.op("dve", lambda e, M_=M_, gt=gt, L_=L_: e.tensor_tensor(out=M_[:, :].ap, in0=gt[:, :].ap,
                                                                               in1=L_[:, :].ap, op=ALU.mult),
                         reads=[gt[:, :], L_[:, :]], writes=[M_[:, :]])
                for h in hs:
                    g = h // 3
                    c, hh = h // 2, h % 2
                    M_, C_ = MT[h], CsT[h]
                    dah = S_["dA"][:, ch * 6 + h:ch * 6 + h + 1]
                    yp = yps[c][64 * hh:64 * hh + 64, lt]
                    xh = X[:, h * 64:(h + 1) * 64]
                    xdh = XD[:, h * 64:(h + 1) * 64]
                    hb = Hbf[:, h, :]
                    P.op("pe", lambda e, yp=yp, xh=xh, M_=M_: e.matmul(yp.ap, lhsT=xh.ap, rhs=M_[:, :].ap, start=True,
                                                                       stop=False, skip_group_check=True),
                         reads=[xh, M_[:, :]], writes=[yp])
                    P.op("pe", lambda e, yp=yp, hb=hb, C_=C_: e.matmul(yp.ap, lhsT=hb.ap, rhs=C_[:, :].ap, start=False,
                                                                       stop=True, skip_group_check=True),
                         reads=[hb, C_[:, :]], writes=[yp])
                    hp_ = Hps[cn["hp"] % 2]
                    cn["hp"] += 1
                    P.op("pe", lambda e, hp_=hp_, BT=BT, g=g, xdh=xdh: e.matmul(hp_[:, :].ap, lhsT=BT[:, g, :].ap,
                                                                                rhs=xdh.ap, start=True, stop=True),
                         reads=[BT[:, g, :], xdh], writes=[hp_[:, :]])
                    h32 = H32[:, h, :]
                    P.op("dve", lambda e, h32=h32, dah=dah, hp_=hp_: e.scalar_tensor_tensor(
                        out=h32.ap, in0=h32.ap, scalar=dah.ap, in1=hp_[:, :].ap, op0=ALU.mult, op1=ALU.add),
                        reads=[h32, dah, hp_[:, :]], writes=[h32])
                    P.op("pool", lambda e, hb=hb, h32=h32: e.tensor_copy(out=hb.ap, in_=h32.ap),
                         reads=[h32], writes=[hb])
            for c in range(3):
                dc = self.vcol(f"dcol{l}", c)
                P.op("dve", lambda e, c=c, dc=dc: e.scalar_tensor_tensor(
                    out=ycat[:, :].ap, in0=xact[:, c, :].ap, scalar=dc.ap, in1=yps[c][:, :].ap, op0=ALU.mult, op1=ALU.add),
                    reads=[xact[:, c, :], dc, yps[c][:, :]], writes=[ycat[:, :]])
                P.op("dve", lambda e, c=c: e.tensor_tensor(out=yg[:, c, :].ap, in0=ycat[:, :].ap, in1=zs[:, c, :].ap,
                                                           op=ALU.mult),
                     reads=[ycat[:, :], zs[:, c, :]], writes=[yg[:, c, :]])
                P.op("act", lambda e, c=c: e.activation(out=sqg[:, c, :].ap, in_=yg[:, c, :].ap, func=AF.Square),
                     reads=[yg[:, c, :]], writes=[sqg[:, c, :]])
            for m in range(3):
                ks = [k for k in range(3) if abs(k - m) <= 1]
                for i, k in enumerate(ks):
                    sel = self.selbf[:, 3 * k + m, :]
                    P.op("pe", lambda e, sel=sel, k=k, i=i, ks=ks: e.matmul(ssp[:, :].ap, lhsT=sel.ap, rhs=sqg[:, k, :].ap,
                                                                           start=(i == 0), stop=(i == len(ks) - 1)),
                         reads=[sel, sqg[:, k, :]], writes=[ssp[:, :]])
                self.rsqrt_mean(rst[:, :], ssp[:, :], 1.0 / 192, RMS_EPS)
                gcol = self.vcol(f"ssd_norm{l}", m)
                dst = ymix[:, 3 + m, tl]
                P.op("dve", lambda e, m=m, gcol=gcol, dst=dst: e.scalar_tensor_tensor(
                    out=dst.ap, in0=yg[:, m, :].ap, scalar=gcol.ap, in1=rst[:, :].ap, op0=ALU.mult, op1=ALU.mult),
                    reads=[yg[:, m, :], gcol, rst[:, :]], writes=[dst])

    def build(self):
        self.declare()
        self.alloc_global()
        self.P.plan = True
        self.body()
        self.P.plan = False
        self.ws.reset_for_real()
        self.phase_id = 0
        self.rr = 0
        self.body()
        self.P.emit()
        return self.nc


def make_in_maps(inp):
    vecs = build_vecs(inp)
    x = np.ascontiguousarray(np.asarray(inp["x"], np.float32))
    shared = {}
    for nm in ("ffn1", "ffn2"):
        for wn in ("_w_gate", "_w_up", "_w_down"):
            shared[nm + wn] = np.ascontiguousarray(np.asarray(inp[nm + wn], np.float32))
    shared["w_in"] = np.ascontiguousarray(np.asarray(inp["w_in"], np.float32))
    shared["w_out"] = np.ascontiguousarray(np.asarray(inp["w_out"], np.float32))
    shared["vecs"] = vecs
    shared["cmat"] = build_cmat()
    shared["sgu_w"] = np.ascontiguousarray(np.asarray(inp["sgu_w"], np.float32))
    shared["sgu_b"] = np.ascontiguousarray(np.asarray(inp["sgu_b"], np.float32).reshape(DEPTH, 4 * 128))
    maps = []
    for c in range(NCORES):
        m = dict(shared)
        m["x"] = x[c * NSEQ:(c + 1) * NSEQ]
        maps.append(m)
    return maps


LAST_BUILDER = None


def run(inp, stages=None, trace=False, parts=("att", "ssd", "sgu")):
    global LAST_BUILDER
    b = Builder(stages=stages, parts=parts)
    LAST_BUILDER = b
    maps = make_in_maps(inp)
    nc = b.build()
    res = run_bass_kernel_spmd(nc, maps, core_ids=list(range(NCORES)), trace=trace)
    out = np.concatenate([np.asarray(r["out"]) for r in res.results], axis=0)
    return out.astype(np.float32), res


def kernel(**inputs):
    out, _ = run(inputs)
    return out
```

```python
import itertools
from contextlib import ExitStack

import numpy as np
import concourse.bass as bass
import concourse.mybir as mybir
from concourse.bass_utils import run_bass_kernel_spmd

F32 = mybir.dt.float32
BF16 = mybir.dt.bfloat16
AF = mybir.ActivationFunctionType
ALU = mybir.AluOpType
ESZ = {F32: 4, BF16: 2}

NCORES = 8
SEQ = 2048
D = 1024
NSEQ = 2
DFF = 2816
NF = DFF // 128
NK = D // 128
TT_ = 512
NT = SEQ // TT_
DEPTH = 2
D_IN = 2950
RMS_EPS = 1e-6
LN_EPS = 1e-5

SB_CELL = 256
SGU_STOP = 99
SGU_VAR = ''
PS_CELL = 2048


class V:
    __slots__ = ("ap", "cells")

    def __init__(self, ap, cells):
        self.ap = ap
        self.cells = cells

    def m(self, f):
        return V(f(self.ap), self.cells)


class TT:
    def __init__(self, handle, shape, dtype, space, base, cell, tid):
        self.h = handle
        self.shape = list(shape)
        self.dtype = dtype
        self.space = space
        self.base = base
        self.cell = cell
        self.tid = tid
        self._cache = {}
        esz = ESZ[dtype]
        dims = self.shape if space == "D" else self.shape[1:]
        st = []
        acc = esz
        for d in reversed(dims):
            st.append(acc)
            acc *= d
        self.strides = list(reversed(st))
        self.dims = dims
        self.esz = esz

    def __getitem__(self, idx):
        if not isinstance(idx, tuple):
            idx = (idx,)
        key = tuple((i.start, i.stop, i.step) if isinstance(i, slice) else i for i in idx)
        c = self._cache.get(key)
        if c is None:
            c = self._cells(idx)
            self._cache[key] = c
        return V(self.h[idx], c)

    def _cells(self, idx):
        fidx = list(idx) if self.space == "D" else list(idx[1:])
        while len(fidx) < len(self.dims):
            fidx.append(slice(None))
        rngs = []
        for i, d in zip(fidx, self.dims):
            if isinstance(i, slice):
                s, e, stp = i.indices(d)
                rngs.append((s, e, stp))
            else:
                rngs.append((i, i + 1, 1))
        cells = set()
        outer = [range(s, e, stp) for (s, e, stp) in rngs[:-1]]
        ls, le, lstp = rngs[-1]
        last_lo = ls * self.strides[-1]
        last_hi = (ls + ((le - 1 - ls) // lstp) * lstp) * self.strides[-1] + self.esz
        for combo in itertools.product(*outer):
            b = self.base + sum(i * s for i, s in zip(combo, self.strides[:-1]))
            for c in range((b + last_lo) // self.cell, (b + last_hi - 1) // self.cell + 1):
                cells.add((self.tid, c))
        return tuple(cells)


class Op:
    __slots__ = ("eng", "idx", "fn", "deps", "dma_deps", "signal", "sigval", "dma_key", "dma_cnt", "stage")


ENGS = ["pe", "act", "dve", "pool", "sp"]


class Prog:
    def __init__(self, nc):
        self.nc = nc
        self.ops = {e: [] for e in ENGS}
        self.cellstate = {}
        self.dma_cnt = {}
        self.plan = False
        self.stage = ""
        self.ins_stage = {}
        self.n_tid = 0
        self.tts = {}
        self.arena = None
        self.arena_base = 0
        self.sb_ptr = 0
        self.sb_cap = 0
        self.psum_banks = []

    def init_mem(self, sb_bytes):
        self.arena_base = (self.nc.sbuf_base + 63) // 64 * 64
        self.sb_cap = min(sb_bytes, (self.nc.sbuf_top - self.arena_base) // 64 * 64)
        for b in range(8):
            self.psum_banks.append(self.nc.alloc_psum_tensor(f"bank{b}", [128, 512], F32))

    def sb(self, name, shape, dtype, off=None):
        if name in self.tts:
            return self.tts[name]
        n = ESZ[dtype]
        for d in shape[1:]:
            n *= d
        if off is None:
            off = (self.sb_ptr + 63) // 64 * 64
            self.sb_ptr = off + n
        assert off + n <= self.sb_cap, f"SBUF overflow {name}: {off}+{n} > {self.sb_cap}"
        h = self.nc.alloc_sbuf_tensor_at(name, list(shape), dtype, offset=self.arena_base + off)
        t = TT(h, shape, dtype, "S", off, SB_CELL, "S")
        self.tts[name] = t
        return t

    def ps(self, name, bank, shape, dtype, byte_off=0):
        if name in self.tts:
            return self.tts[name]
        n = ESZ[dtype]
        for d in shape[1:]:
            n *= d
        assert byte_off + n <= 2048
        t = PsTT(self.psum_banks[bank], shape, dtype, bank, byte_off)
        self.tts[name] = t
        return t

    def dram(self, name, shape, dtype, kind="Internal", cell=None):
        if name in self.tts:
            return self.tts[name]
        h = self.nc.dram_tensor(name, list(shape), dtype, kind=kind)
        self.n_tid += 1
        t = TT(h, shape, dtype, "D", 0, cell or (1 << 40), f"D{self.n_tid}")
        self.tts[name] = t
        return t

    def op(self, eng, fn, reads=(), writes=(), dma_key=None):
        if self.plan:
            return
        o = Op()
        o.eng = eng
        o.fn = fn
        o.signal = False
        o.sigval = None
        o.dma_key = dma_key
        o.idx = len(self.ops[eng])
        o.stage = self.stage
        deps = {}
        dma_deps = {}

        def add(p, raw):
            if p is None:
                return
            if p.dma_key is not None:
                k = p.dma_key
                dma_deps[k] = self.dma_cnt[k]
                return
            if p.eng == eng and dma_key is None:
                if eng == "pe":
                    return
            if deps.get(p.eng, -1) < p.idx:
                deps[p.eng] = p.idx

        cs = self.cellstate
        for v in reads:
            for c in v.cells:
                st = cs.get(c)
                if st is not None:
                    add(st[0], True)
                    if c[0] == "P":
                        for r in st[1]:
                            if r.eng != eng:
                                add(r, False)
        for v in writes:
            for c in v.cells:
                st = cs.get(c)
                if st is not None:
                    add(st[0], False)
                    for r in st[1]:
                        add(r, False)
        for v in writes:
            for c in v.cells:
                cs[c] = [o, []]
        for v in reads:
            for c in v.cells:
                st = cs.get(c)
                if st is None:
                    cs[c] = [None, [o]]
                elif st[0] is not o:
                    st[1].append(o)
        if dma_key is not None:
            self.dma_cnt[dma_key] = self.dma_cnt.get(dma_key, 0) + 1
            o.dma_cnt = self.dma_cnt[dma_key]
        else:
            o.dma_cnt = 0
        for e, i in deps.items():
            self.ops[e][i].signal = True
        o.deps = deps
        o.dma_deps = dma_deps
        self.ops[eng].append(o)

    def emit(self):
        nc = self.nc
        for e in ENGS:
            cnt = 0
            for o in self.ops[e]:
                if o.signal:
                    cnt += 1
                    o.sigval = cnt
        with ExitStack() as es:
            esem = {e: es.enter_context(nc.semaphore(f"sem_{e}")) for e in ENGS if e != "sp"}
            dsem = {k: es.enter_context(nc.semaphore(f"dma_{k}")) for k in self.dma_cnt}
            block = es.enter_context(nc.Block())

            def run(ename, eng):
                waited = {}
                for o in self.ops[ename]:
                    for pe_, pi in o.deps.items():
                        val = self.ops[pe_][pi].sigval
                        key = ("e", pe_)
                        if waited.get(key, 0) < val:
                            eng.wait_ge(esem[pe_], val)
                            waited[key] = val
                    for k, c in o.dma_deps.items():
                        key = ("d", k)
                        if waited.get(key, 0) < 16 * c:
                            eng.wait_ge(dsem[k], 16 * c)
                            waited[key] = 16 * c
                    ins = o.fn(eng)
                    try:
                        self.ins_stage[ins.ins.name] = (ename, o.stage)
                    except Exception:
                        pass
                    if o.dma_key is not None:
                        ins.then_inc(dsem[o.dma_key], 16)
                    elif o.signal:
                        ins.then_inc(esem[ename], 1)
                if ename == "sp":
                    for k, c in self.dma_cnt.items():
                        if waited.get(("d", k), 0) < 16 * c:
                            eng.wait_ge(dsem[k], 16 * c)

            @block.tensor
            def _(eng):
                run("pe", eng)

            @block.scalar
            def _(eng):
                run("act", eng)

            @block.vector
            def _(eng):
                run("dve", eng)

            @block.gpsimd
            def _(eng):
                run("pool", eng)

            @block.sync
            def _(eng):
                run("sp", eng)


class PsTT(TT):
    def __init__(self, bank_handle, shape, dtype, bank, byte_off):
        n_el = 2048 // ESZ[dtype]
        full = bank_handle[:].bitcast(dtype) if dtype != F32 else bank_handle[:]
        self.full = full
        self.shape = list(shape)
        self.dtype = dtype
        self.space = "P"
        self.base = bank * 2048 + byte_off
        self.cell = PS_CELL
        self.tid = "P"
        self._cache = {}
        esz = ESZ[dtype]
        self.esz = esz
        dims = self.shape[1:]
        st = []
        acc = esz
        for d in reversed(dims):
            st.append(acc)
            acc *= d
        self.strides = list(reversed(st))
        self.dims = dims
        n = 1
        for d in dims:
            n *= d
        e0 = byte_off // esz
        flat = full[:, e0:e0 + n]
        if len(dims) == 1:
            self.view = flat
        elif len(dims) == 2:
            self.view = flat.rearrange("p (a b) -> p a b", b=dims[1])
        elif len(dims) == 3:
            self.view = flat.rearrange("p (a b c) -> p a b c", b=dims[1], c=dims[2])
        else:
            raise ValueError

    def __getitem__(self, idx):
        if not isinstance(idx, tuple):
            idx = (idx,)
        key = tuple((i.start, i.stop, i.step) if isinstance(i, slice) else i for i in idx)
        c = self._cache.get(key)
        if c is None:
            c = self._cells(idx)
            self._cache[key] = c
        return V(self.view[idx], c)


SLOT_ELEMS = NF * 128
NSLOT = 6
PREFETCH = 4


class WStream:
    def __init__(self, prog):
        self.prog = prog
        self.plan_list = []
        self.i_next = 0
        self.i_issued = 0
        self.slots = None

    def reset_for_real(self):
        self.i_next = 0
        self.i_issued = 0

    def _issue(self, i):
        src_fn, nelem = self.plan_list[i]
        s = i % NSLOT
        dst = self.slots[:, s, 0:nelem]
        src = src_fn()
        self.prog.op("sp", lambda e, d=dst, s_=src: e.dma_start(out=d.ap, in_=s_.ap),
                     reads=[src], writes=[dst], dma_key=f"w{s}")

    def next(self, src_fn, nelem):
        if self.prog.plan:
            self.plan_list.append((src_fn, nelem))
            return self.slots[:, 0, 0:nelem], 0
        i = self.i_next
        self.i_next += 1
        while self.i_issued < min(len(self.plan_list), i + 1 + PREFETCH):
            self._issue(self.i_issued)
            self.i_issued += 1
        return self.slots[:, i % NSLOT, 0:nelem], i % NSLOT


def _cols(v):
    v = np.asarray(v, np.float32)
    return np.ascontiguousarray(v.reshape(-1, 128).T)


VEC_LAYOUT = {}


N_CMAT = 15


def build_vecs(inp):
    cols = []
    pos = 0
    VEC_LAYOUT.clear()

    def add(name, arr):
        nonlocal pos
        arr = np.asarray(arr, np.float32)
        VEC_LAYOUT[name] = (pos, arr.shape[1])
        cols.append(arr)
        pos += arr.shape[1]

    def bc(v):
        v = np.asarray(v, np.float32).reshape(1, -1)
        return np.broadcast_to(v, (128, v.shape[1]))

    for l in range(DEPTH):
        add(f"ffn1_norm{l}", _cols(inp["ffn1_norm"][l]))
        add(f"mix_norm{l}", _cols(inp["mix_norm"][l]))
        add(f"ffn2_norm{l}", _cols(inp["ffn2_norm"][l]))
    add("final_norm", _cols(inp["final_norm"]))
    for l in range(DEPTH):
        for j in range(4):
            add(f"conv_w{l}_{j}", _cols(inp["conv_w"][l][j]))
        add(f"conv_b{l}", _cols(inp["conv_b"][l]))
        add(f"ssd_norm{l}", _cols(inp["ssd_norm"][l]))
        add(f"dcol{l}", _cols(np.repeat(np.asarray(inp["d_skip"][l], np.float32), 64)))
        add(f"sgu_ln_g{l}", _cols(inp["sgu_ln_g"][l]))
        add(f"dt_bias{l}", bc(np.tile(np.asarray(inp["dt_bias"][l], np.float32), 4)))
        add(f"a_log{l}", bc(np.tile(np.asarray(inp["a_log"][l], np.float32), 4)))
        add(f"sgu_ln_b{l}", bc(inp["sgu_ln_b"][l]))
    return np.ascontiguousarray(np.concatenate(cols, axis=1))


def n_vec_cols():
    return DEPTH * 3 * NK + NK + DEPTH * (28 + 7 + 3 + 3 + 2 + 24 + 24 + 256)


def build_cmat():
    i = np.arange(128)
    m = []
    m.append(np.eye(128))
    m.append((i[:, None] <= i[None, :]) * 1.0)
    m.append(np.where(i[:, None] <= i[None, :], 0.0, -30000.0))
    m.append((i[None, :] >= i[:, None]) * 1.0)
    m.append((i[None, :] <= i[:, None]) * 1.0)
    for k in range(3):
        for mm in range(3):
            gi = (128 * k + i) // 192
            go = (128 * mm + i) // 192
            m.append((gi[:, None] == go[None, :]) * 1.0)
    m.append((i[None, :] <= i[:, None]) * 1.0)
    return np.ascontiguousarray(np.concatenate(m, axis=1).astype(np.float32))


class Builder:
    def __init__(self, stages=None, dump=None, parts=("att", "ssd", "sgu")):
        self.parts = parts
        self.nc = bass.Bass("TRN2", target_bir_lowering=False, dynamic_dma_scratch_size=64)
        self.P = Prog(self.nc)
        self.ws = WStream(self.P)
        self.stages = stages
        self.dump = dump
        self.rr = 0

    def declare(self):
        P = self.P
        nc = self.nc
        ext = lambda n, s, dt=F32: P.dram(n, s, dt, kind="ExternalInput")
        self.x_in = ext("x", [NSEQ, SEQ, D])
        self.w = {}
        for nm in ("ffn1", "ffn2"):
            self.w[nm + "_w_gate"] = ext(nm + "_w_gate", [DEPTH, D, DFF])
            self.w[nm + "_w_up"] = ext(nm + "_w_up", [DEPTH, D, DFF])
            self.w[nm + "_w_down"] = ext(nm + "_w_down", [DEPTH, DFF, D])
        self.w["w_in"] = ext("w_in", [DEPTH, D, D_IN])
        self.w["w_out"] = ext("w_out", [DEPTH, D, D])
        self.vecs_in = ext("vecs", [128, n_vec_cols()])
        self.out = P.dram("out", [NSEQ, SEQ, D], F32, kind="ExternalOutput")
        self.WguR = P.dram("WguR", [DEPTH * 2, NF, 128, 2 * NK * 128], BF16, cell=128 * 2 * NK * 128 * 2)
        self.WdR = P.dram("WdR", [DEPTH * 2, NK, 128, NF * 128], BF16, cell=128 * NF * 128 * 2)
        self.WinR = P.dram("WinR", [DEPTH, 24, 128, NK * 128], BF16, cell=128 * NK * 128 * 2)
        self.WoutR = P.dram("WoutR", [DEPTH, NK, 128, NK * 128], BF16, cell=128 * NK * 128 * 2)
        self.cmat_in = ext("cmat", [128, N_CMAT * 128])
        self.sgu_w_in = ext("sgu_w", [DEPTH, 4, 128, 128])
        self.sgu_b_in = ext("sgu_b", [DEPTH, 4 * 128])

    def alloc_global(self):
        P = self.P
        P.init_mem(229056)
        self.xT = P.sb("xT", [128, NK, SEQ], F32)
        self.vecs = P.sb("vecs_sb", [128, n_vec_cols()], F32)
        self.cmat = P.sb("cmat_sb", [128, N_CMAT, 128], F32)
        self.ident = self.cmat_view(0)
        self.identbf = P.sb("identbf", [128, 128], BF16)
        self.mask2 = P.sb("mask2", [128, 256], BF16)
        self.ones32 = P.sb("ones32", [128, 128], F32)
        self.selbf = P.sb("selbf", [128, 9, 128], BF16)
        self.ones_bf = P.sb("ones_bf", [128, 128], BF16)
        self.cst = P.sb("cst", [128, 8], F32)
        self.ws.slots = P.sb("wslots", [128, NSLOT, SLOT_ELEMS], BF16)
        self.glob_end = P.sb_ptr

    def cmat_view(self, i):
        class _C:
            def __getitem__(s_, idx):
                if not isinstance(idx, tuple):
                    idx = (idx,)
                return self.cmat[(idx[0], i) + tuple(idx[1:])]
        return _C()

    def phase(self):
        self.P.sb_ptr = self.glob_end
        self.phase_id = getattr(self, "phase_id", 0) + 1
        return f"ph{self.phase_id}_"

    def vcol(self, name, k):
        pos, n = VEC_LAYOUT[name]
        return self.vecs[:, pos + k:pos + k + 1]

    def any_eng(self):
        self.rr += 1
        return ["act", "dve", "pool"][self.rr % 3]

    def load_consts(self):
        P = self.P
        P.op("sp", lambda e: e.dma_start(out=self.vecs[:, :].ap, in_=self.vecs_in[:, :].ap),
             reads=[], writes=[self.vecs[:, :]], dma_key="c0")
        cm = self.cmat[:, :, :]
        P.op("sp", lambda e: e.dma_start(out=cm.ap, in_=self.cmat_in[:, :].m(
            lambda a: a.rearrange("p (c i) -> p c i", i=128)).ap), reads=[], writes=[cm], dma_key="c0")
        P.op("dve", lambda e: e.tensor_copy(out=self.identbf[:, :].ap, in_=self.ident[:, :].ap),
             reads=[self.ident[:, :]], writes=[self.identbf[:, :]])
        P.op("dve", lambda e: e.tensor_copy(out=self.mask2[:, :].ap, in_=self.cmat[:, 3:5, :].m(
            lambda a: a.rearrange("p c i -> p (c i)")).ap), reads=[self.cmat[:, 3:5, :]], writes=[self.mask2[:, :]])
        P.op("pool", lambda e: e.memset(self.ones32[:, :].ap, 1.0), writes=[self.ones32[:, :]])
        P.op("dve", lambda e: e.tensor_copy(out=self.selbf[:, :, :].ap, in_=self.cmat[:, 5:14, :].ap),
             reads=[self.cmat[:, 5:14, :]], writes=[self.selbf[:, :, :]])
        P.op("pool", lambda e: e.memset(self.cst[:, 3:4].ap, 1.0), writes=[self.cst[:, 3:4]])
        P.op("pool", lambda e: e.memset(self.ones_bf[:, :].ap, 1.0), writes=[self.ones_bf[:, :]])
        P.op("pool", lambda e: e.memset(self.cst[:, 0:1].ap, RMS_EPS), writes=[self.cst[:, 0:1]])
        P.op("pool", lambda e: e.memset(self.cst[:, 1:2].ap, LN_EPS), writes=[self.cst[:, 1:2]])
        P.op("pool", lambda e: e.memset(self.cst[:, 2:3].ap, 0.0), writes=[self.cst[:, 2:3]])

    def prepass(self, sl):
        P = self.P
        pre = self.phase()
        CW = 256
        NB_ = 3
        st32 = [P.sb(pre + f"st32_{i}", [128, NF, CW], F32) for i in range(NB_)]
        stbf = [P.sb(pre + f"stbf_{i}", [128, 2, NF, 128], BF16) for i in range(NB_)]
        it = 0

        def one(src_t, l, nk, c0, dst_fn, ncols=CW):
            nonlocal it
            b = it % NB_
            eng = "dve" if it % 2 == 0 else "act"
            it += 1
            s32 = st32[b][:, 0:nk, 0:ncols]
            src = src_t[l, :, c0:c0 + ncols].m(lambda a: a.rearrange("(k p) n -> p k n", p=128))
            P.op("sp", lambda e: e.dma_start(out=s32.ap, in_=src.ap), reads=[], writes=[s32], dma_key=f"pi{b}")
            if ncols == CW:
                sbf = stbf[b][:, :, 0:nk, :]
                s32p = s32.m(lambda a: a.rearrange("p k (j i) -> p j k i", j=2))
            else:
                sbf = stbf[b][:, 0, 0:nk, 0:ncols]
                s32p = s32
            if eng == "act":
                P.op("act", lambda e: e.copy(out=sbf.ap, in_=s32p.ap), reads=[s32], writes=[sbf])
            else:
                P.op(eng, lambda e: e.tensor_copy(out=sbf.ap, in_=s32p.ap), reads=[s32], writes=[sbf])
            dst = dst_fn()
            P.op("act", lambda e: e.dma_start(out=dst.ap, in_=sbf.ap), reads=[sbf], writes=[dst], dma_key=f"po{b}")

        for l in range(DEPTH):
            for fi, nm in enumerate(("ffn1", "ffn2")):
                if (nm, l) not in sl:
                    continue
                idx = l * 2 + fi
                for gu, wn in enumerate(("_w_gate", "_w_up")):
                    for g in range(NF // 2):
                        one(self.w[nm + wn], l, NK, g * CW,
                            lambda idx=idx, g=g, gu=gu: self.WguR[idx, 2 * g:2 * g + 2, :, gu * NK * 128:(gu + 1) * NK * 128]
                            .m(lambda a: a.rearrange("f p (k i) -> p f k i", i=128)))
                for g in range(NK // 2):
                    one(self.w[nm + "_w_down"], l, NF, g * CW,
                        lambda idx=idx, g=g: self.WdR[idx, 2 * g:2 * g + 2, :, :]
                        .m(lambda a: a.rearrange("c p (f i) -> p c f i", i=128)))
        for l in range(DEPTH):
            if ("mix", l) not in sl:
                continue

            def win_dst(l, c, n):
                if n == 2:
                    return lambda: self.WinR[l, c:c + 2, :, :].m(lambda a: a.rearrange("c p (k i) -> p c k i", i=128))
                return None
            for g in range(9):
                one(self.w["w_in"], l, NK, g * CW, win_dst(l, 2 * g, 2))
            one(self.w["w_in"], l, NK, 2304, lambda l=l: self.WinR[l, 18, :, :].m(
                lambda a: a.rearrange("p (k i) -> p k i", i=128)), ncols=128)
            one(self.w["w_in"], l, NK, 2432, lambda l=l: self.WinR[l, 19, :, :].m(
                lambda a: a.rearrange("p (k i) -> p k i", i=128)), ncols=128)
            one(self.w["w_in"], l, NK, 2438, win_dst(l, 20, 2))
            one(self.w["w_in"], l, NK, 2694, win_dst(l, 22, 2))
            for g in range(4):
                one(self.w["w_out"], l, NK, g * CW,
                    lambda l=l, g=g: self.WoutR[l, 2 * g:2 * g + 2, :, :]
                    .m(lambda a: a.rearrange("c p (k i) -> p c k i", i=128)))

    def load_x(self, s):
        P = self.P
        pre = self.phase()
        stg = [P.sb(pre + f"xs{i}", [128, D], F32) for i in range(3)]
        pst = [P.ps(f"tp{i}", i, [128, 4, 128], F32) for i in range(4)]
        n = 0
        for b in range(SEQ // 128):
            sg = stg[b % 3]
            src = self.x_in[s, b * 128:(b + 1) * 128, :]
            P.op("sp", lambda e, sg=sg, src=src: e.dma_start(out=sg[:, :].ap, in_=src.ap),
                 reads=[], writes=[sg[:, :]], dma_key=f"xi{b % 3}")
            for half in range(2):
                pt = pst[n % 4]
                n += 1
                for j in range(4):
                    k = half * 4 + j
                    P.op("pe", lambda e, pt=pt, j=j, k=k, sg=sg: e.transpose(
                        out=pt[:, j, :].ap, in_=sg[:, k * 128:(k + 1) * 128].ap, identity=self.ident[:, :].ap),
                        reads=[sg[:, k * 128:(k + 1) * 128], self.ident[:, :]], writes=[pt[:, j, :]])
                dst = self.xT[:, half * 4:half * 4 + 4, b * 128:(b + 1) * 128]
                if n % 2 == 0:
                    P.op("act", lambda e, pt=pt, dst=dst: e.copy(out=dst.ap, in_=pt[:, :, :].ap),
                         reads=[pt[:, :, :]], writes=[dst])
                else:
                    P.op("dve", lambda e, pt=pt, dst=dst: e.tensor_copy(out=dst.ap, in_=pt[:, :, :].ap),
                         reads=[pt[:, :, :]], writes=[dst])

    def rms_tile(self, pre, t, gname, hT, ss_ps, sq, rstd, dst_fn=None):
        P = self.P
        tl = slice(t * TT_, (t + 1) * TT_)
        for k in range(NK):
            q = sq[k % 2]
            xin = self.xT[:, k, tl]
            P.op("act", lambda e, q=q, xin=xin: e.activation(out=q[:, :].ap, in_=xin.ap, func=AF.Square),
                 reads=[xin], writes=[q[:, :]])
            P.op("pe", lambda e, q=q, k=k: e.matmul(ss_ps[:, :].ap, lhsT=self.ones_bf[:, :].ap, rhs=q[:, :].ap,
                                                    start=(k == 0), stop=(k == NK - 1)),
                 reads=[q[:, :], self.ones_bf[:, :]], writes=[ss_ps[:, :]])
        self.rsqrt_mean(rstd[:, :], ss_ps[:, :], 1.0 / D, RMS_EPS)
        for k in range(NK):
            xin = self.xT[:, k, tl]
            g = self.vcol(gname, k)
            dst = hT[:, k, :] if dst_fn is None else dst_fn(k)
            eng = "dve"
            P.op(eng, lambda e, xin=xin, g=g, dst=dst: e.scalar_tensor_tensor(
                out=dst.ap, in0=xin.ap, scalar=g.ap, in1=rstd[:, :].ap, op0=ALU.mult, op1=ALU.mult),
                reads=[xin, g, rstd[:, :]], writes=[dst])

    def rsqrt_mean(self, dst, src_ps, scale, eps):
        P = self.P
        ec = self.cst[:, 0:1] if eps == RMS_EPS else self.cst[:, 1:2]
        P.op("act", lambda e: e.activation(out=dst.ap, in_=src_ps.ap, func=AF.Ln, bias=ec.ap, scale=scale),
             reads=[src_ps, ec], writes=[dst])
        P.op("act", lambda e: e.activation(out=dst.ap, in_=dst.ap, func=AF.Exp, scale=-0.5),
             reads=[dst], writes=[dst])

    def ffn(self, l, fi):
        P = self.P
        pre = self.phase()
        idx = l * 2 + fi
        gname = f"ffn{fi + 1}_norm{l}"
        hT = [P.sb(pre + f"hT{i}", [128, NK, TT_], BF16) for i in range(2)]
        sq = [P.sb(pre + f"sq{i}", [128, TT_], BF16) for i in range(2)]
        rstd = P.sb(pre + "rstd", [128, TT_], F32)
        sg = [P.sb(pre + f"sg{i}", [128, TT_], F32) for i in range(2)]
        aT = P.sb(pre + "aT", [128, NF, TT_], BF16)
        ss_ps = P.ps("ffn_ss", 0, [128, TT_], F32)
        pg = [P.ps(f"ffn_pg{i}", 1 + i, [128, TT_], F32) for i in range(2)]
        pu = [P.ps(f"ffn_pu{i}", 3 + i, [128, TT_], F32) for i in range(2)]
        py = [P.ps(f"ffn_py{i}", 5 + i, [128, TT_], F32) for i in range(2)]
        self.rms_tile(pre, 0, gname, hT[0], ss_ps, sq, rstd)
        for t in range(NT):
            tl = slice(t * TT_, (t + 1) * TT_)
            h = hT[t % 2]
            for f in range(NF):
                wv, _ = self.ws.next(lambda f=f: self.WguR[idx, f, :, :], 2 * NK * 128)
                g_ps = pg[f % 2]
                u_ps = pu[f % 2]
                for gu, ps_ in ((0, g_ps), (1, u_ps)):
                    for k in range(NK):
                        lw = wv.m(lambda a, gu=gu, k=k: a[:, (gu * NK + k) * 128:(gu * NK + k + 1) * 128])
                        P.op("pe", lambda e, ps_=ps_, lw=lw, h=h, k=k: e.matmul(
                            ps_[:, :].ap, lhsT=lw.ap, rhs=h[:, k, :].ap, start=(k == 0), stop=(k == NK - 1)),
                            reads=[wv, h[:, k, :]], writes=[ps_[:, :]])
                s_ = sg[f % 2]
                P.op("act", lambda e, s_=s_, g_ps=g_ps: e.activation(out=s_[:, :].ap, in_=g_ps[:, :].ap, func=AF.Silu),
                     reads=[g_ps[:, :]], writes=[s_[:, :]])
                dst = aT[:, f, :]
                P.op("dve", lambda e, s_=s_, u_ps=u_ps, dst=dst: e.tensor_tensor(
                    out=dst.ap, in0=u_ps[:, :].ap, in1=s_[:, :].ap, op=ALU.mult),
                    reads=[u_ps[:, :], s_[:, :]], writes=[dst])
            if t + 1 < NT:
                self.rms_tile(pre, t + 1, gname, hT[(t + 1) % 2], ss_ps, sq, rstd)
            for c in range(NK):
                wv, _ = self.ws.next(lambda c=c: self.WdR[idx, c, :, :], NF * 128)
                y_ps = py[c % 2]
                for f in range(NF):
                    lw = wv.m(lambda a, f=f: a[:, f * 128:(f + 1) * 128])
                    P.op("pe", lambda e, y_ps=y_ps, lw=lw, f=f: e.matmul(
                        y_ps[:, :].ap, lhsT=lw.ap, rhs=aT[:, f, :].ap, start=(f == 0), stop=(f == NF - 1)),
                        reads=[wv, aT[:, f, :]], writes=[y_ps[:, :]])
                xin = self.xT[:, c, tl]
                P.op("dve", lambda e, y_ps=y_ps, xin=xin: e.scalar_tensor_tensor(
                    out=xin.ap, in0=y_ps[:, :].ap, scalar=0.5, in1=xin.ap, op0=ALU.mult, op1=ALU.add),
                    reads=[y_ps[:, :], xin], writes=[xin])

    def store_out(self, s, normalize=True):
        P = self.P
        pre = self.phase()
        sq = [P.sb(pre + f"sq{i}", [128, TT_], BF16) for i in range(2)]
        rstd = P.sb(pre + "rstd", [128, TT_], F32)
        yT = [P.sb(pre + f"yT{i}", [128, NK, TT_], F32) for i in range(2)]
        stg = [P.sb(pre + f"os{i}", [128, D], F32) for i in range(3)]
        ss_ps = P.ps("ffn_ss", 0, [128, TT_], F32)
        pst = [P.ps(f"otp{i}", 1 + i, [128, 4, 128], F32) for i in range(4)]
        n = 0
        nb = 0
        for t in range(NT):
            tl = slice(t * TT_, (t + 1) * TT_)
            y = yT[t % 2]
            if normalize:
                for k in range(NK):
                    q = sq[k % 2]
                    xin = self.xT[:, k, tl]
                    P.op("act", lambda e, q=q, xin=xin: e.activation(out=q[:, :].ap, in_=xin.ap, func=AF.Square),
                         reads=[xin], writes=[q[:, :]])
                    P.op("pe", lambda e, q=q, k=k: e.matmul(ss_ps[:, :].ap, lhsT=self.ones_bf[:, :].ap, rhs=q[:, :].ap,
                                                            start=(k == 0), stop=(k == NK - 1)),
                         reads=[q[:, :], self.ones_bf[:, :]], writes=[ss_ps[:, :]])
                self.rsqrt_mean(rstd[:, :], ss_ps[:, :], 1.0 / D, RMS_EPS)
                for k in range(NK):
                    xin = self.xT[:, k, tl]
                    g = self.vcol("final_norm", k)
                    dst = y[:, k, :]
                    eng = "dve"
                    P.op(eng, lambda e, xin=xin, g=g, dst=dst: e.scalar_tensor_tensor(
                        out=dst.ap, in0=xin.ap, scalar=g.ap, in1=rstd[:, :].ap, op0=ALU.mult, op1=ALU.mult),
                        reads=[xin, g, rstd[:, :]], writes=[dst])
            for bb in range(TT_ // 128):
                b = t * (TT_ // 128) + bb
                sg = stg[nb % 3]
                nb += 1
                for half in range(2):
                    pt = pst[n % 4]
                    n += 1
                    for j in range(4):
                        k = half * 4 + j
                        src = (y[:, k, bb * 128:(bb + 1) * 128] if normalize
                               else self.xT[:, k, b * 128:(b + 1) * 128])
                        P.op("pe", lambda e, pt=pt, j=j, src=src: e.transpose(
                            out=pt[:, j, :].ap, in_=src.ap, identity=self.ident[:, :].ap),
                            reads=[src, self.ident[:, :]], writes=[pt[:, j, :]])
                    dst = sg[:, half * 512:(half + 1) * 512]
                    pin = pt[:, :, :].m(lambda a: a.rearrange("p a b -> p (a b)"))
                    if n % 2 == 0:
                        P.op("act", lambda e, pin=pin, dst=dst: e.copy(out=dst.ap, in_=pin.ap),
                             reads=[pin], writes=[dst])
                    else:
                        P.op("dve", lambda e, pin=pin, dst=dst: e.tensor_copy(out=dst.ap, in_=pin.ap),
                             reads=[pin], writes=[dst])
                dsto = self.out[s, b * 128:(b + 1) * 128, :]
                P.op("sp", lambda e, sg=sg, dsto=dsto: e.dma_start(out=dsto.ap, in_=sg[:, :].ap),
                     reads=[sg[:, :]], writes=[], dma_key=f"xo{(nb - 1) % 3}")

    def body(self):
        st = self.stages
        full = [(n, l) for l in range(DEPTH) for n in ("ffn1", "mix", "ffn2")]
        if isinstance(st, list):
            sl = st
        elif st is None:
            sl = full
        else:
            sl = full[:st]
        self.P.stage = "consts"
        self.load_consts()
        self.P.stage = "prepass"
        self.prepass(sl)
        for s in range(NSEQ):
            self.P.stage = f"s{s}.load_x"
            self.load_x(s)
            for name, l in sl:
                self.P.stage = f"s{s}.{name}{l}"
                if name == "ffn1":
                    self.ffn(l, 0)
                elif name == "ffn2":
                    self.ffn(l, 1)
                else:
                    self.mixer(l)
            self.P.stage = f"s{s}.store"
            self.store_out(s, normalize=(st is None))

    def mixer(self, l):
        P = self.P
        pre = self.phase()
        parts = self.parts
        hT = P.sb(pre + "hT", [128, NK, SEQ], BF16)
        ymix = P.sb(pre + "ymix", [128, NK, SEQ], BF16)
        self.mix_base = P.sb_ptr
        sq = [P.sb(pre + f"sq{i}", [128, TT_], BF16) for i in range(2)]
        rstd = P.sb(pre + "rstd", [128, TT_], F32)
        ss_ps = P.ps("ffn_ss", 0, [128, TT_], F32)
        for t in range(NT):
            tl = slice(t * TT_, (t + 1) * TT_)
            self.rms_tile(pre, t, f"mix_norm{l}", None, ss_ps, sq, rstd, dst_fn=lambda k, tl=tl: hT[:, k, tl])
        if len(parts) < 3:
            for k in range(NK):
                P.op("pool", lambda e, k=k: e.memset(ymix[:, k, :].ap, 0.0), writes=[ymix[:, k, :]])
        st0 = P.stage
        if "att" in parts:
            P.stage = st0 + ".att"
            self.attention(l, pre, hT, ymix)
        if "ssd" in parts:
            P.stage = st0 + ".ssd"
            self.ssd(l, pre, hT, ymix)
        if "sgu" in parts:
            P.stage = st0 + ".sgu"
            self.sgu(l, pre, hT, ymix)
        P.stage = st0 + ".wout"
        self.wout(l, ymix)

    def proj_w(self, l, c):
        wv, _ = self.ws.next(lambda: self.WinR[l, c, :, :], NK * 128)
        return wv

    def proj_mm(self, wv, ncols, rhs_fn, ps):
        P = self.P
        for k in range(NK):
            lw = wv.m(lambda a, k=k: a[:, k * 128:k * 128 + ncols])
            r = rhs_fn(k)
            P.op("pe", lambda e, lw=lw, r=r, k=k: e.matmul(ps.ap, lhsT=lw.ap, rhs=r.ap, start=(k == 0),
                                                           stop=(k == NK - 1)),
                 reads=[wv, r], writes=[ps])

    def wout(self, l, ymix):
        P = self.P
        py = [P.ps(f"ffn_py{i}", 5 + i, [128, TT_], F32) for i in range(2)]
        n = 0
        for co in range(NK):
            wv, _ = self.ws.next(lambda co=co: self.WoutR[l, co, :, :], NK * 128)
            for t in range(NT):
                tl = slice(t * TT_, (t + 1) * TT_)
                y_ps = py[n % 2]
                n += 1
                self.proj_mm(wv, 128, lambda k, tl=tl: ymix[:, k, tl], y_ps[:, :])
                xin = self.xT[:, co, tl]
                P.op("dve", lambda e, y_ps=y_ps, xin=xin: e.tensor_tensor(
                    out=xin.ap, in0=y_ps[:, :].ap, in1=xin.ap, op=ALU.add),
                    reads=[y_ps[:, :], xin], writes=[xin])

    def attention(self, l, pre0, hT, ymix):
        P = self.P
        P.sb_ptr = self.mix_base
        pre = pre0 + "att_"
        qT = P.sb(pre + "qT", [128, SEQ], BF16)
        kT = P.sb(pre + "kT", [128, SEQ], BF16)
        vT = P.sb(pre + "vT", [128, SEQ], BF16)
        Vtok = P.sb(pre + "Vtok", [128, 16, 128], BF16)
        PT = [P.sb(pre + f"PT{i}", [128, 256], BF16) for i in range(4)]
        accN = P.sb(pre + "accN", [128, SEQ], F32)
        accD = P.sb(pre + "accD", [128, SEQ], F32)
        ps_proj = [P.ps(f"att_pp{i}", i, [128, 512], F32) for i in range(2)]
        ps_tr = [P.ps(f"att_tr{i}", 0, [128, 4, 128], BF16, byte_off=(i % 2) * 1024) for i in range(2)]
        ps_S = [P.ps(f"att_S{i}", 1 + i, [128, 256], F32) for i in range(3)]
        ps_ON = [P.ps(f"att_ON{i}", 4 + i, [128, 512], F32) for i in range(2)]
        ps_OD = [P.ps(f"att_OD{i}", 6 + i, [128, 512], F32) for i in range(2)]
        cnt = {"pp": 0, "tr": 0, "S": 0, "PT": 0, "ev": 0}
        gbase = 0
        for c in range(3):
            for which, dstT in ((0, qT), (1, kT), (2, vT)):
                wv = self.proj_w(l, 3 * which + c)
                for t in range(NT):
                    tl = slice(t * TT_, (t + 1) * TT_)
                    ps = ps_proj[cnt["pp"] % 2]
                    cnt["pp"] += 1
                    self.proj_mm(wv, 128, lambda k, tl=tl: hT[:, k, tl], ps[:, :])
                    dst = dstT[:, tl]
                    if which == 0:
                        P.op("act", lambda e, ps=ps, dst=dst: e.mul(out=dst.ap, in_=ps[:, :].ap, mul=0.125),
                             reads=[ps[:, :]], writes=[dst])
                    elif which == 1:
                        P.op("dve", lambda e, ps=ps, dst=dst: e.tensor_copy(out=dst.ap, in_=ps[:, :].ap),
                             reads=[ps[:, :]], writes=[dst])
                    else:
                        P.op("act", lambda e, ps=ps, dst=dst: e.copy(out=dst.ap, in_=ps[:, :].ap),
                             reads=[ps[:, :]], writes=[dst])
            for br, d in enumerate((1, 4, 16)):
                nb = 16 // d

                def tokstart(blk):
                    if d == 1:
                        return 128 * blk
                    if d == 4:
                        return (blk // 4) + 512 * (blk % 4)
                    return blk
                for g4 in range(4):
                    pt = ps_tr[cnt["tr"] % 2]
                    cnt["tr"] += 1
                    for j in range(4):
                        st = tokstart(g4 * 4 + j)
                        src = vT[:, st:st + 127 * d + 1:d]
                        P.op("pe", lambda e, pt=pt, j=j, src=src: e.transpose(
                            out=pt[:, j, :].ap, in_=src.ap, identity=self.identbf[:, :].ap),
                            reads=[src, self.identbf[:, :]], writes=[pt[:, j, :]])
                    dst = Vtok[:, g4 * 4:(g4 + 1) * 4, :]
                    if g4 % 2 == 0:
                        P.op("act", lambda e, pt=pt, dst=dst: e.copy(out=dst.ap, in_=pt[:, :, :].ap),
                             reads=[pt[:, :, :]], writes=[dst])
                    else:
                        P.op("dve", lambda e, pt=pt, dst=dst: e.tensor_copy(out=dst.ap, in_=pt[:, :, :].ap),
                             reads=[pt[:, :, :]], writes=[dst])
                its = []
                for r in range(d):
                    for n in range(nb):
                        for h in range(2):
                            its.append((r, n, h))
                LAG = 2
                pend = {}
                for ii in range(len(its) + LAG):
                    if ii < len(its):
                        r, n, h = its[ii]
                        st = r + d * 128 * n
                        nq = 256 if n + 1 < nb else 128
                        kset = slice(st, st + 127 * d + 1, d)
                        qset = slice(st, st + (nq - 1) * d + 1, d)
                        hp = slice(64 * h, 64 * h + 64)
                        S = ps_S[cnt["S"] % len(ps_S)]
                        cnt["S"] += 1
                        kk = kT[hp, kset]
                        qq = qT[hp, qset]
                        Sv = S[:, 0:nq]
                        P.op("pe", lambda e, Sv=Sv, kk=kk, qq=qq: e.matmul(Sv.ap, lhsT=kk.ap, rhs=qq.ap,
                                                                           start=True, stop=True),
                             reads=[kk, qq], writes=[Sv])
                        pt_ = PT[cnt["PT"] % len(PT)]
                        cnt["PT"] += 1
                        pv = pt_[:, 0:nq]
                        P.op("act", lambda e, pv=pv, Sv=Sv: e.activation(out=pv.ap, in_=Sv.ap, func=AF.Exp),
                             reads=[Sv], writes=[pv])
                        mk = self.mask2[:, 0:nq]
                        P.op("pool", lambda e, pv=pv, mk=mk: e.tensor_tensor(out=pv.ap, in0=pv.ap, in1=mk.ap,
                                                                             op=ALU.mult),
                             reads=[pv, mk], writes=[pv])
                        pend[ii] = pt_
                    jj = ii - LAG
                    if jj < 0:
                        continue
                    r, n, h = its[jj]
                    pt_ = pend.pop(jj)
                    blk = n if d == 1 else (r * 4 + n if d == 4 else r)
                    nq = 256 if n + 1 < nb else 128
                    hp = slice(64 * h, 64 * h + 64)
                    vv = Vtok[:, blk, 64 * h:64 * h + 64]
                    on1 = self.ones_bf[:, 0:64]
                    for half in range(nq // 128):
                        qi = blk + half
                        G = gbase + qi // 4
                        pos = qi % 4
                        cs_ = slice(pos * 128, (pos + 1) * 128)
                        rhs = pt_[:, half * 128:(half + 1) * 128]
                        if half == 0:
                            fl = dict(start=(n == 0), stop=True)
                        else:
                            fl = dict(start=True, stop=False)
                        for lhs, dstp in ((vv, ps_ON[G % 2][hp, cs_]), (on1, ps_OD[G % 2][hp, cs_])):
                            P.op("pe", lambda e, lhs=lhs, dstp=dstp, rhs=rhs, fl=fl: e.matmul(
                                dstp.ap, lhsT=lhs.ap, rhs=rhs.ap, skip_group_check=True, **fl),
                                reads=[lhs, rhs], writes=[dstp])
                    if h == 1 and blk % 4 == 3:
                        Gl = blk // 4
                        G = gbase + Gl
                        if d == 1:
                            vw = lambda a, Gl=Gl: a[:, 512 * Gl:512 * Gl + 512]
                            pw = lambda a: a
                        elif d == 4:
                            vw = lambda a, Gl=Gl: a[:, Gl:SEQ:4]
                            pw = lambda a: a
                        else:
                            vw = lambda a, Gl=Gl: a.rearrange("p (i r) -> p r i", r=16)[:, 4 * Gl:4 * Gl + 4, :]
                            pw = lambda a: a.rearrange("p (r i) -> p r i", i=128)
                        for acc, psb, eng0 in ((accN, ps_ON[G % 2], "act"), (accD, ps_OD[G % 2], "dve")):
                            full = acc[:, :]
                            if d == 1:
                                av = acc[:, 512 * Gl:512 * Gl + 512]
                            else:
                                av = V(vw(full.ap), full.cells)
                            pp = psb[:, :]
                            pin = V(pw(pp.ap), pp.cells)
                            if br == 0:
                                if eng0 == "act":
                                    P.op("act", lambda e, av=av, pin=pin: e.copy(out=av.ap, in_=pin.ap),
                                         reads=[pin], writes=[av])
                                else:
                                    P.op("dve", lambda e, av=av, pin=pin: e.tensor_copy(out=av.ap, in_=pin.ap),
                                         reads=[pin], writes=[av])
                            else:
                                P.op("dve", lambda e, av=av, pin=pin: e.tensor_tensor(
                                    out=av.ap, in0=pin.ap, in1=av.ap, op=ALU.add),
                                    reads=[pin, av], writes=[av])
                gbase += 4
            for t in range(NT):
                tl = slice(t * TT_, (t + 1) * TT_)
                P.op("dve", lambda e, tl=tl: e.reciprocal(out=accD[:, tl].ap, in_=accD[:, tl].ap),
                     reads=[accD[:, tl]], writes=[accD[:, tl]])
                dst = ymix[:, c, tl]
                P.op("dve", lambda e, tl=tl, dst=dst: e.tensor_tensor(out=dst.ap, in0=accN[:, tl].ap,
                                                                      in1=accD[:, tl].ap, op=ALU.mult),
                     reads=[accN[:, tl], accD[:, tl]], writes=[dst])

    def sgu(self, l, pre0, hT, ymix):
        P = self.P
        P.sb_ptr = self.mix_base
        pre = pre0 + "sgu_"
        wraw = P.sb(pre + "wraw", [128, 4, 128], F32)
        WcT32 = P.sb(pre + "WcT32", [128, 4, 128], F32)
        WcTb = P.sb(pre + "WcTb", [128, 4, 128], BF16)
        bsrow = P.sb(pre + "bsrow", [128, 512], F32)
        Kt4 = P.sb(pre + "Kt4", [128, 2, 4, 128], F32)
        gu = [P.sb(pre + f"gu{i}", [128, 2, TT_], BF16) for i in range(2)]
        vg = [P.sb(pre + f"vg{i}", [128, 256], F32) for i in range(2)]
        cen = P.sb(pre + "cen", [128, 4, 256], F32)
        sqc = P.sb(pre + "sqc", [128, 256], F32)
        stat = [P.sb(pre + f"stat{i}", [128, 16], F32) for i in range(2)]
        nbf = [P.sb(pre + f"nbf{i}", [128, 256], BF16) for i in range(2)]
        tmp = P.sb(pre + "tmp", [128, TT_], F32)
        pp = [P.ps(f"att_pp{i}", i, [128, 512], F32) for i in range(2)]
        vps = [P.ps(f"sgu_v{i}", 2, [128, 256], F32, byte_off=i * 1024) for i in range(2)]
        mps = [P.ps(f"sgu_m{i}", 3 + i, [128, 512], F32) for i in range(2)]
        trps = P.ps("sgu_tr", 5, [128, 128], F32)
        kps = P.ps("sgu_k", 5, [128, 128], F32, byte_off=1024)
        tril = self.cmat_view(14)
        lbpos = VEC_LAYOUT[f"sgu_ln_b{l}"][0]
        P.op("sp", lambda e: e.dma_start(out=wraw[:, :, :].ap, in_=self.sgu_w_in[l, :, :, :].m(
            lambda a: a.rearrange("g t s -> t g s")).ap), reads=[], writes=[wraw[:, :, :]], dma_key="sg")
        P.op("sp", lambda e: e.dma_start(out=bsrow[0:1, :].ap, in_=self.sgu_b_in[l:l + 1, :].ap),
             reads=[], writes=[bsrow[0:1, :]], dma_key="sg")
        if SGU_STOP <= 1:
            return
        for g in range(4):
            w_ = wraw[:, g, :]
            if "nomask" not in SGU_VAR:
                P.op("dve", lambda e, w_=w_: e.tensor_tensor(out=w_.ap, in0=w_.ap, in1=tril[:, :].ap, op=ALU.mult),
                     reads=[w_, tril[:, :]], writes=[w_])
            if "notr" in SGU_VAR:
                continue
            P.op("pe", lambda e, w_=w_: e.transpose(out=trps[:, :].ap, in_=w_.ap, identity=self.ident[:, :].ap),
                 reads=[w_, self.ident[:, :]], writes=[trps[:, :]])
            P.op("act", lambda e, g=g: e.copy(out=WcT32[:, g, :].ap, in_=trps[:, :].ap),
                 reads=[trps[:, :]], writes=[WcT32[:, g, :]])
            P.op("dve", lambda e, g=g: e.tensor_copy(out=WcTb[:, g, :].ap, in_=WcT32[:, g, :].ap),
                 reads=[WcT32[:, g, :]], writes=[WcTb[:, g, :]])
        if SGU_STOP <= 2:
            return
        for cc in range(2):
            for gg in range(2):
                g = 2 * cc + gg
                kp = kps[64 * gg:64 * gg + 64, :]
                lb = self.vecs[:, lbpos + g * 64:lbpos + (g + 1) * 64]
                P.op("pe", lambda e, kp=kp, lb=lb, g=g: e.matmul(kp.ap, lhsT=lb.ap, rhs=WcT32[:, g, :].ap,
                                                                 start=True, stop=False, skip_group_check=True),
                     reads=[lb, WcT32[:, g, :]], writes=[kp])
                on = self.ones32[0:1, 0:64]
                br_ = bsrow[0:1, g * 128:(g + 1) * 128]
                P.op("pe", lambda e, kp=kp, on=on, br_=br_: e.matmul(kp.ap, lhsT=on.ap, rhs=br_.ap,
                                                                     start=False, stop=True, skip_group_check=True),
                     reads=[on, br_], writes=[kp])
            for b in range(4):
                dst = Kt4[:, cc, b, :]
                if b % 2 == 0:
                    P.op("act", lambda e, dst=dst: e.copy(out=dst.ap, in_=kps[:, :].ap), reads=[kps[:, :]], writes=[dst])
                else:
                    P.op("dve", lambda e, dst=dst: e.tensor_copy(out=dst.ap, in_=kps[:, :].ap),
                         reads=[kps[:, :]], writes=[dst])
        nb_ = 0
        if SGU_STOP <= 3:
            return
        for t in range(NT):
            tl = slice(t * TT_, (t + 1) * TT_)
            gut = gu[t % 2]
            st_ = stat[t % 2]
            for cc in range(2):
                wv = self.proj_w(l, 20 + cc)
                ps = pp[cc]
                self.proj_mm(wv, 128, lambda k, tl=tl: hT[:, k, tl], ps[:, :])
                P.op("act", lambda e, ps=ps, cc=cc, gut=gut: e.activation(out=gut[:, cc, :].ap, in_=ps[:, :].ap,
                                                                          func=AF.Gelu),
                     reads=[ps[:, :]], writes=[gut[:, cc, :]])
            if SGU_STOP <= 4:
                continue
            w0 = self.proj_w(l, 22)
            w1 = self.proj_w(l, 23)
            for b in range(4):
                bl = slice(t * TT_ + b * 128, t * TT_ + (b + 1) * 128)
                vp = vps[b % 2]
                for half, w in ((0, w0), (1, w1)):
                    vph = vp[:, half * 128:(half + 1) * 128]
                    for k in range(NK):
                        hk = hT[:, k, bl]
                        wk = w.m(lambda a, k=k: a[:, k * 128:(k + 1) * 128])
                        P.op("pe", lambda e, vph=vph, hk=hk, wk=wk, k=k: e.matmul(
                            vph.ap, lhsT=hk.ap, rhs=wk.ap, start=(k == 0), stop=(k == NK - 1)),
                            reads=[hk, w], writes=[vph])
                v_ = vg[b % 2]
                P.op("act", lambda e, v_=v_, vp=vp: e.activation(out=v_[:, :].ap, in_=vp[:, :].ap, func=AF.Gelu),
                     reads=[vp[:, :]], writes=[v_[:, :]])
                sm = st_[:, b:b + 1]
                nm = st_[:, 4 + b:5 + b]
                vs = st_[:, 8 + b:9 + b]
                P.op("dve", lambda e, v_=v_, sm=sm: e.reduce_sum(out=sm.ap, in_=v_[:, :].ap, axis=mybir.AxisListType.X),
                     reads=[v_[:, :]], writes=[sm])
                P.op("dve", lambda e, sm=sm, nm=nm: e.tensor_scalar(out=nm.ap, in0=sm.ap, scalar1=-1.0 / 256,
                                                                     scalar2=None, op0=ALU.mult),
                     reads=[sm], writes=[nm])
                cb = cen[:, b, :]
                P.op("dve", lambda e, cb=cb, v_=v_, nm=nm: e.tensor_scalar(out=cb.ap, in0=v_[:, :].ap, scalar1=nm.ap,
                                                                           scalar2=None, op0=ALU.add),
                     reads=[v_[:, :], nm], writes=[cb])
                P.op("pool", lambda e, cb=cb: e.tensor_tensor(out=sqc[:, :].ap, in0=cb.ap, in1=cb.ap, op=ALU.mult),
                     reads=[cb], writes=[sqc[:, :]])
                P.op("dve", lambda e, vs=vs: e.reduce_sum(out=vs.ap, in_=sqc[:, :].ap, axis=mybir.AxisListType.X),
                     reads=[sqc[:, :]], writes=[vs])
            if SGU_STOP <= 5:
                continue
            rs = st_[:, 12:16]
            ec = self.cst[:, 1:2]
            P.op("act", lambda e, rs=rs, st_=st_, ec=ec: e.activation(out=rs.ap, in_=st_[:, 8:12].ap, func=AF.Sqrt,
                                                                      bias=ec.ap, scale=1.0 / 256),
                 reads=[st_[:, 8:12], ec], writes=[rs])
            P.op("dve", lambda e, rs=rs: e.reciprocal(out=rs.ap, in_=rs.ap), reads=[rs], writes=[rs])
            if SGU_STOP <= 6:
                continue
            for b in range(4):
                n_ = nbf[nb_ % 2]
                nb_ += 1
                cb = cen[:, b, :]
                rb = st_[:, 12 + b:13 + b]
                P.op("dve", lambda e, n_=n_, cb=cb, rb=rb: e.tensor_scalar(out=n_[:, :].ap, in0=cb.ap, scalar1=rb.ap,
                                                                           scalar2=None, op0=ALU.mult),
                     reads=[cb, rb], writes=[n_[:, :]])
                for g in range(4):
                    mp = mps[g // 2][64 * (g % 2):64 * (g % 2) + 64, b * 128:(b + 1) * 128]
                    ng = n_[:, g * 64:(g + 1) * 64]
                    P.op("pe", lambda e, mp=mp, ng=ng, g=g: e.matmul(mp.ap, lhsT=ng.ap, rhs=WcTb[:, g, :].ap,
                                                                     start=True, stop=True, skip_group_check=True),
                         reads=[ng, WcTb[:, g, :]], writes=[mp])
            for cc in range(2):
                gcol = self.vcol(f"sgu_ln_g{l}", cc)
                k4 = Kt4[:, cc, :, :].m(lambda a: a.rearrange("p b i -> p (b i)"))
                P.op("dve", lambda e, cc=cc, gcol=gcol, k4=k4: e.scalar_tensor_tensor(
                    out=tmp[:, :].ap, in0=mps[cc][:, :].ap, scalar=gcol.ap, in1=k4.ap, op0=ALU.mult, op1=ALU.add),
                    reads=[mps[cc][:, :], gcol, k4], writes=[tmp[:, :]])
                dst = ymix[:, 6 + cc, tl]
                P.op("dve", lambda e, cc=cc, dst=dst, gut=gut: e.tensor_tensor(
                    out=dst.ap, in0=tmp[:, :].ap, in1=gut[:, cc, :].ap, op=ALU.mult),
                    reads=[tmp[:, :], gut[:, cc, :]], writes=[dst])

    def ssd(self, l, pre0, hT, ymix):
        P = self.P
        P.sb_ptr = self.mix_base
        pre = pre0 + "ssd_"
        zs = P.sb(pre + "zs", [128, 3, TT_], BF16)
        stgb = [P.sb(pre + f"stg{i}", [128, TT_ + 3], F32) for i in range(2)]
        halo = P.sb(pre + "halo", [128, 7, 4], F32)
        cacc = [P.sb(pre + f"cacc{i}", [128, TT_], F32) for i in range(2)]
        xact = P.sb(pre + "xact", [128, 7, TT_], BF16)
        negA = P.sb(pre + "negA", [128, 24], F32)
        sm = [{n: P.sb(pre + f"{n}{i}", [128, 24], F32) for n in ("t1", "dt", "a", "acs", "last", "dte", "dA", "dtd")}
              for i in range(2)]
        NBUF = 3
        arep = [P.sb(pre + f"arep{i}", [128, 128], F32) for i in range(NBUF)]
        tmpL = [P.sb(pre + f"tmpL{i}", [128, 128], F32) for i in range(NBUF)]
        LT = [P.sb(pre + f"LT{i}", [128, 128], F32) for i in range(NBUF)]
        Eb = [P.sb(pre + f"E{i}", [128, 128], BF16) for i in range(NBUF)]
        MT = [P.sb(pre + f"MT{i}", [128, 128], BF16) for i in range(NBUF)]
        CsT = [P.sb(pre + f"CsT{i}", [128, 128], BF16) for i in range(NBUF)]
        Xs = [P.sb(pre + f"X{i}", [128, 384], BF16) for i in range(2)]
        Xd = [P.sb(pre + f"Xd{i}", [128, 384], BF16) for i in range(2)]
        Btok = [P.sb(pre + f"Btok{i}", [128, 2, 128], BF16) for i in range(2)]
        H32 = P.sb(pre + "H32", [128, 6, 64], F32)
        Hbf = P.sb(pre + "Hbf", [128, 6, 64], BF16)
        ycat = cacc[0]
        rst = cacc[1]
        yg = P.sb(pre + "yg", [128, 3, TT_], F32)
        sqg = P.sb(pre + "sqg", [128, 3, TT_], BF16)
        pp = [P.ps(f"att_pp{i}", i, [128, 512], F32) for i in range(2)]
        yps = [P.ps(f"ssd_y{i}", 2 + i, [128, 512], F32) for i in range(3)]
        BCp = [P.ps(f"ssd_bc{i}", b_, [128, 128], F32) for i, b_ in enumerate((0, 1, 7))]
        GTp = [P.ps(f"ssd_gt{i}", 5, [128, 128], F32, byte_off=1024 * i) for i in range(2)]
        trp = [P.ps(f"ssd_tr{i}", 6, [128, 128], BF16, byte_off=256 * i) for i in range(4)]
        Hps = [P.ps(f"ssd_h{i}", 6, [128, 64], F32, byte_off=1024 + 256 * i) for i in range(2)]
        dtp = P.ps("ssd_dt", 6, [128, 24], F32, byte_off=1536)
        acp = P.ps("ssd_ac", 6, [128, 24], F32, byte_off=1664)
        lap = P.ps("ssd_la", 6, [128, 24], F32, byte_off=1792)
        ssp = P.ps("ssd_ss", 7, [128, 512], F32)
        tri = self.cmat_view(1)
        negm = self.cmat_view(2)
        one_c = self.cst[:, 3:4]
        p0 = VEC_LAYOUT[f"dt_bias{l}"][0]
        dtb = self.vecs[:, p0:p0 + 24]
        p1 = VEC_LAYOUT[f"a_log{l}"][0]
        alog = self.vecs[:, p1:p1 + 24]
        P.op("act", lambda e: e.activation(out=negA[:, :].ap, in_=alog.ap, func=AF.Exp), reads=[alog], writes=[negA[:, :]])
        P.op("dve", lambda e: e.tensor_scalar(out=negA[:, :].ap, in0=negA[:, :].ap, scalar1=-1.0, scalar2=None,
                                              op0=ALU.mult), reads=[negA[:, :]], writes=[negA[:, :]])
        P.op("pool", lambda e: e.memset(H32[:, :, :].ap, 0.0), writes=[H32[:, :, :]])
        P.op("pool", lambda e: e.memset(Hbf[:, :, :].ap, 0.0), writes=[Hbf[:, :, :]])
        P.op("pool", lambda e: e.memset(halo[:, :, :].ap, 0.0), writes=[halo[:, :, :]])
        cn = {"pp": 0, "tr": 0, "ch": 0, "hd": 0, "hp": 0}
        for t in range(NT):
            tl = slice(t * TT_, (t + 1) * TT_)
            for c in range(3):
                wv = self.proj_w(l, 9 + c)
                ps = pp[cn["pp"] % 2]
                cn["pp"] += 1
                self.proj_mm(wv, 128, lambda k, tl=tl: hT[:, k, tl], ps[:, :])
                P.op("act", lambda e, ps=ps, c=c: e.activation(out=zs[:, c, :].ap, in_=ps[:, :].ap, func=AF.Silu),
                     reads=[ps[:, :]], writes=[zs[:, c, :]])
            for c in range(7):
                wv = self.proj_w(l, 12 + c)
                ps = pp[cn["pp"] % 2]
                cn["pp"] += 1
                self.proj_mm(wv, 128, lambda k, tl=tl: hT[:, k, tl], ps[:, :])
                stg = stgb[c % 2]
                sg_ = stg[:, 3:TT_ + 3]
                P.op("pool", lambda e, stg=stg, c=c: e.tensor_copy(out=stg[:, 0:3].ap, in_=halo[:, c, 0:3].ap),
                     reads=[halo[:, c, 0:3]], writes=[stg[:, 0:3]])
                P.op("act", lambda e, ps=ps, sg_=sg_: e.copy(out=sg_.ap, in_=ps[:, :].ap), reads=[ps[:, :]], writes=[sg_])
                ca = cacc[c % 2]
                for j in range(4):
                    wj = self.vcol(f"conv_w{l}_{j}", c)
                    sj = stg[:, j:j + TT_]
                    if j == 0:
                        P.op("dve", lambda e, ca=ca, sj=sj, wj=wj: e.tensor_scalar(
                            out=ca[:, :].ap, in0=sj.ap, scalar1=wj.ap, scalar2=None, op0=ALU.mult),
                            reads=[sj, wj], writes=[ca[:, :]])
                    else:
                        P.op("dve", lambda e, ca=ca, sj=sj, wj=wj: e.scalar_tensor_tensor(
                            out=ca[:, :].ap, in0=sj.ap, scalar=wj.ap, in1=ca[:, :].ap, op0=ALU.mult, op1=ALU.add),
                            reads=[sj, wj, ca[:, :]], writes=[ca[:, :]])
                cb_ = self.vcol(f"conv_b{l}", c)
                P.op("act", lambda e, ca=ca, c=c, cb_=cb_: e.activation(out=xact[:, c, :].ap, in_=ca[:, :].ap,
                                                                        func=AF.Silu, bias=cb_.ap, scale=1.0),
                     reads=[ca[:, :], cb_], writes=[xact[:, c, :]])
                P.op("pool", lambda e, c=c, stg=stg: e.tensor_copy(out=halo[:, c, 0:3].ap, in_=stg[:, TT_:TT_ + 3].ap),
                     reads=[stg[:, TT_:TT_ + 3]], writes=[halo[:, c, 0:3]])
            wdt = self.proj_w(l, 19)
            S_ = sm[t % 2]
            for ch in range(4):
                tok = slice(t * TT_ + ch * 128, t * TT_ + (ch + 1) * 128)
                dpc = dtp[:, ch * 6:(ch + 1) * 6]
                for k in range(NK):
                    hk = hT[:, k, tok]
                    wk = wdt.m(lambda a, k=k: a[:, k * 128:k * 128 + 6])
                    P.op("pe", lambda e, hk=hk, wk=wk, k=k, dpc=dpc: e.matmul(dpc.ap, lhsT=hk.ap, rhs=wk.ap,
                                                                              start=(k == 0), stop=(k == NK - 1)),
                         reads=[hk, wdt], writes=[dpc])
            t1, dt, a_, acs, last, dte, dA, dtd = (S_[n][:, :] for n in ("t1", "dt", "a", "acs", "last", "dte", "dA", "dtd"))
            P.op("dve", lambda e, t1=t1: e.tensor_tensor(out=t1.ap, in0=dtp[:, :].ap, in1=dtb.ap, op=ALU.add),
                 reads=[dtp[:, :], dtb], writes=[t1])
            P.op("act", lambda e, t1=t1: e.activation(out=t1.ap, in_=t1.ap, func=AF.Exp), reads=[t1], writes=[t1])
            P.op("act", lambda e, t1=t1, dt=dt: e.activation(out=dt.ap, in_=t1.ap, func=AF.Ln, bias=one_c.ap, scale=1.0),
                 reads=[t1, one_c], writes=[dt])
            P.op("dve", lambda e, a_=a_, dt=dt: e.tensor_tensor(out=a_.ap, in0=dt.ap, in1=negA[:, :].ap, op=ALU.mult),
                 reads=[dt, negA[:, :]], writes=[a_])
            P.op("pe", lambda e, a_=a_: e.matmul(acp[:, :].ap, lhsT=tri[:, :].ap, rhs=a_.ap, start=True, stop=True),
                 reads=[tri[:, :], a_], writes=[acp[:, :]])
            P.op("pe", lambda e, a_=a_: e.matmul(lap[:, :].ap, lhsT=self.ones32[:, :].ap, rhs=a_.ap, start=True, stop=True),
                 reads=[self.ones32[:, :], a_], writes=[lap[:, :]])
            P.op("dve", lambda e, acs=acs: e.tensor_copy(out=acs.ap, in_=acp[:, :].ap), reads=[acp[:, :]], writes=[acs])
            P.op("dve", lambda e, last=last: e.tensor_copy(out=last.ap, in_=lap[:, :].ap), reads=[lap[:, :]], writes=[last])
            P.op("dve", lambda e, dte=dte, last=last, acs=acs: e.tensor_tensor(out=dte.ap, in0=last.ap, in1=acs.ap,
                                                                              op=ALU.subtract),
                 reads=[last, acs], writes=[dte])
            P.op("act", lambda e, dte=dte: e.activation(out=dte.ap, in_=dte.ap, func=AF.Exp), reads=[dte], writes=[dte])
            P.op("act", lambda e, dA=dA, last=last: e.activation(out=dA.ap, in_=last.ap, func=AF.Exp),
                 reads=[last], writes=[dA])
            P.op("dve", lambda e, dtd=dtd, dt=dt, dte=dte: e.tensor_tensor(out=dtd.ap, in0=dt.ap, in1=dte.ap, op=ALU.mult),
                 reads=[dt, dte], writes=[dtd])
            for ch in range(4):
                lt = slice(ch * 128, (ch + 1) * 128)
                X = Xs[cn["ch"] % 2]
                XD = Xd[cn["ch"] % 2]
                BT = Btok[cn["ch"] % 2]
                cn["ch"] += 1
                for c in range(3):
                    tp = trp[cn["tr"] % 4]
                    cn["tr"] += 1
                    src = xact[:, c, lt]
                    P.op("pe", lambda e, tp=tp, src=src: e.transpose(out=tp[:, :].ap, in_=src.ap,
                                                                     identity=self.identbf[:, :].ap),
                         reads=[src, self.identbf[:, :]], writes=[tp[:, :]])
                    for hh in range(2):
                        h = 2 * c + hh
                        xh = X[:, h * 64:(h + 1) * 64]
                        xdh = XD[:, h * 64:(h + 1) * 64]
                        tph = tp[:, hh * 64:(hh + 1) * 64]
                        dth = S_["dt"][:, ch * 6 + h:ch * 6 + h + 1]
                        ddh = S_["dtd"][:, ch * 6 + h:ch * 6 + h + 1]
                        P.op("dve", lambda e, xh=xh, tph=tph, dth=dth: e.tensor_scalar(
                            out=xh.ap, in0=tph.ap, scalar1=dth.ap, scalar2=None, op0=ALU.mult),
                            reads=[tph, dth], writes=[xh])
                        P.op("dve", lambda e, xdh=xdh, tph=tph, ddh=ddh: e.tensor_scalar(
                            out=xdh.ap, in0=tph.ap, scalar1=ddh.ap, scalar2=None, op0=ALU.mult),
                            reads=[tph, ddh], writes=[xdh])
                for g in range(2):
                    tp = trp[cn["tr"] % 4]
                    cn["tr"] += 1
                    src = xact[:, 3 + g, lt]
                    P.op("pe", lambda e, tp=tp, src=src: e.transpose(out=tp[:, :].ap, in_=src.ap,
                                                                     identity=self.identbf[:, :].ap),
                         reads=[src, self.identbf[:, :]], writes=[tp[:, :]])
                    P.op("act", lambda e, tp=tp, g=g, BT=BT: e.copy(out=BT[:, g, :].ap, in_=tp[:, :].ap),
                         reads=[tp[:, :]], writes=[BT[:, g, :]])
                for g in range(2):
                    gt = GTp[g]
                    bT = xact[:, 3 + g, lt]
                    cT = xact[:, 5 + g, lt]
                    P.op("pe", lambda e, gt=gt, bT=bT, cT=cT: e.matmul(gt[:, :].ap, lhsT=bT.ap, rhs=cT.ap,
                                                                       start=True, stop=True),
                         reads=[bT, cT], writes=[gt[:, :]])
                LAG = 2
                bufs = {}
                for ii in range(6 + LAG):
                    if ii < 6:
                        h = ii
                        g = h // 3
                        gt = GTp[g]
                        cT = xact[:, 5 + g, lt]
                        i3 = cn["hd"] % NBUF
                        cn["hd"] += 1
                        ar, tL, L_, E_, M_, C_ = arep[i3], tmpL[i3], LT[i3], Eb[i3], MT[i3], CsT[i3]
                        bc = BCp[i3]
                        ah = S_["a"][:, ch * 6 + h:ch * 6 + h + 1]
                        ach = S_["acs"][:, ch * 6 + h:ch * 6 + h + 1]
                        P.op("act", lambda e, ar=ar, ah=ah: e.activation(out=ar[:, :].ap, in_=self.ones32[:, :].ap,
                                                                         func=AF.Copy, scale=ah.ap),
                             reads=[self.ones32[:, :], ah], writes=[ar[:, :]])
                        P.op("pe", lambda e, bc=bc, ar=ar: e.matmul(bc[:, :].ap, lhsT=ar[:, :].ap, rhs=tri[:, :].ap,
                                                                    start=True, stop=True),
                             reads=[ar[:, :], tri[:, :]], writes=[bc[:, :]])
                        P.op("dve", lambda e, tL=tL, bc=bc, ach=ach: e.scalar_tensor_tensor(
                            out=tL[:, :].ap, in0=bc[:, :].ap, scalar=ach.ap, in1=negm[:, :].ap,
                            op0=ALU.subtract, op1=ALU.add),
                            reads=[bc[:, :], ach, negm[:, :]], writes=[tL[:, :]])
                        P.op("act", lambda e, E_=E_, bc=bc: e.activation(out=E_[:, :].ap, in_=bc[:, :].ap, func=AF.Exp),
                             reads=[bc[:, :]], writes=[E_[:, :]])
                        P.op("act", lambda e, L_=L_, tL=tL: e.activation(out=L_[:, :].ap, in_=tL[:, :].ap, func=AF.Exp),
                             reads=[tL[:, :]], writes=[L_[:, :]])
                        P.op("dve", lambda e, M_=M_, gt=gt, L_=L_: e.tensor_tensor(out=M_[:, :].ap, in0=gt[:, :].ap,
                                                                                   in1=L_[:, :].ap, op=ALU.mult),
                             reads=[gt[:, :], L_[:, :]], writes=[M_[:, :]])
                        P.op("pool", lambda e, C_=C_, cT=cT, E_=E_: e.tensor_tensor(out=C_[:, :].ap, in0=cT.ap,
                                                                                    in1=E_[:, :].ap, op=ALU.mult),
                             reads=[cT, E_[:, :]], writes=[C_[:, :]])
                        bufs[ii] = (M_, C_)
                    jj = ii - LAG
                    if jj < 0:
                        continue
                    h = jj
                    g = h // 3
                    c, hh = h // 2, h % 2
                    M_, C_ = bufs.pop(jj)
                    dah = S_["dA"][:, ch * 6 + h:ch * 6 + h + 1]
                    yp = yps[c][64 * hh:64 * hh + 64, lt]
                    xh = X[:, h * 64:(h + 1) * 64]
                    xdh = XD[:, h * 64:(h + 1) * 64]
                    hb = Hbf[:, h, :]
                    P.op("pe", lambda e, yp=yp, xh=xh, M_=M_: e.matmul(yp.ap, lhsT=xh.ap, rhs=M_[:, :].ap, start=True,
                                                                       stop=False, skip_group_check=True),
                         reads=[xh, M_[:, :]], writes=[yp])
                    P.op("pe", lambda e, yp=yp, hb=hb, C_=C_: e.matmul(yp.ap, lhsT=hb.ap, rhs=C_[:, :].ap, start=False,
                                                                       stop=True, skip_group_check=True),
                         reads=[hb, C_[:, :]], writes=[yp])
                    hp_ = Hps[cn["hp"] % 2]
                    cn["hp"] += 1
                    P.op("pe", lambda e, hp_=hp_, BT=BT, g=g, xdh=xdh: e.matmul(hp_[:, :].ap, lhsT=BT[:, g, :].ap,
                                                                                rhs=xdh.ap, start=True, stop=True),
                         reads=[BT[:, g, :], xdh], writes=[hp_[:, :]])
                    h32 = H32[:, h, :]
                    P.op("dve", lambda e, h32=h32, dah=dah, hp_=hp_: e.scalar_tensor_tensor(
                        out=h32.ap, in0=h32.ap, scalar=dah.ap, in1=hp_[:, :].ap, op0=ALU.mult, op1=ALU.add),
                        reads=[h32, dah, hp_[:, :]], writes=[h32])
                    P.op("pool", lambda e, hb=hb, h32=h32: e.tensor_copy(out=hb.ap, in_=h32.ap),
                         reads=[h32], writes=[hb])
            for c in range(3):
                dc = self.vcol(f"dcol{l}", c)
                P.op("dve", lambda e, c=c, dc=dc: e.scalar_tensor_tensor(
                    out=ycat[:, :].ap, in0=xact[:, c, :].ap, scalar=dc.ap, in1=yps[c][:, :].ap, op0=ALU.mult, op1=ALU.add),
                    reads=[xact[:, c, :], dc, yps[c][:, :]], writes=[ycat[:, :]])
                P.op("dve", lambda e, c=c: e.tensor_tensor(out=yg[:, c, :].ap, in0=ycat[:, :].ap, in1=zs[:, c, :].ap,
                                                           op=ALU.mult),
                     reads=[ycat[:, :], zs[:, c, :]], writes=[yg[:, c, :]])
                P.op("act", lambda e, c=c: e.activation(out=sqg[:, c, :].ap, in_=yg[:, c, :].ap, func=AF.Square),
                     reads=[yg[:, c, :]], writes=[sqg[:, c, :]])
            for m in range(3):
                ks = [k for k in range(3) if abs(k - m) <= 1]
                for i, k in enumerate(ks):
                    sel = self.selbf[:, 3 * k + m, :]
                    P.op("pe", lambda e, sel=sel, k=k, i=i, ks=ks: e.matmul(ssp[:, :].ap, lhsT=sel.ap, rhs=sqg[:, k, :].ap,
                                                                           start=(i == 0), stop=(i == len(ks) - 1)),
                         reads=[sel, sqg[:, k, :]], writes=[ssp[:, :]])
                self.rsqrt_mean(rst[:, :], ssp[:, :], 1.0 / 192, RMS_EPS)
                gcol = self.vcol(f"ssd_norm{l}", m)
                dst = ymix[:, 3 + m, tl]
                P.op("dve", lambda e, m=m, gcol=gcol, dst=dst: e.scalar_tensor_tensor(
                    out=dst.ap, in0=yg[:, m, :].ap, scalar=gcol.ap, in1=rst[:, :].ap, op0=ALU.mult, op1=ALU.mult),
                    reads=[yg[:, m, :], gcol, rst[:, :]], writes=[dst])

    def build(self):
        self.declare()
        self.alloc_global()
        self.P.plan = True
        self.body()
        self.P.plan = False
        self.ws.reset_for_real()
        self.phase_id = 0
        self.rr = 0
        self.body()
        self.P.emit()
        return self.nc


def make_in_maps(inp):
    vecs = build_vecs(inp)
    x = np.ascontiguousarray(np.asarray(inp["x"], np.float32))
    shared = {}
    for nm in ("ffn1", "ffn2"):
        for wn in ("_w_gate", "_w_up", "_w_down"):
            shared[nm + wn] = np.ascontiguousarray(np.asarray(inp[nm + wn], np.float32))
    shared["w_in"] = np.ascontiguousarray(np.asarray(inp["w_in"], np.float32))
    shared["w_out"] = np.ascontiguousarray(np.asarray(inp["w_out"], np.float32))
    shared["vecs"] = vecs
    shared["cmat"] = build_cmat()
    shared["sgu_w"] = np.ascontiguousarray(np.asarray(inp["sgu_w"], np.float32))
    shared["sgu_b"] = np.ascontiguousarray(np.asarray(inp["sgu_b"], np.float32).reshape(DEPTH, 4 * 128))
    maps = []
    for c in range(NCORES):
        m = dict(shared)
        m["x"] = x[c * NSEQ:(c + 1) * NSEQ]
        maps.append(m)
    return maps


LAST_BUILDER = None


def run(inp, stages=None, trace=False, parts=("att", "ssd", "sgu")):
    global LAST_BUILDER
    b = Builder(stages=stages, parts=parts)
    LAST_BUILDER = b
    maps = make_in_maps(inp)
    nc = b.build()
    res = run_bass_kernel_spmd(nc, maps, core_ids=list(range(NCORES)), trace=trace)
    out = np.concatenate([np.asarray(r["out"]) for r in res.results], axis=0)
    return out.astype(np.float32), res


def kernel(**inputs):
    out, _ = run(inputs)
    return out
```

```python
import itertools
from contextlib import ExitStack

import numpy as np
import concourse.bass as bass
import concourse.mybir as mybir
from concourse.bass_utils import run_bass_kernel_spmd

F32 = mybir.dt.float32
BF16 = mybir.dt.bfloat16
AF = mybir.ActivationFunctionType
ALU = mybir.AluOpType
ESZ = {F32: 4, BF16: 2}

NCORES = 8
SEQ = 2048
D = 1024
NSEQ = 2
DFF = 2816
NF = DFF // 128
NK = D // 128
TT_ = 512
NT = SEQ // TT_
DEPTH = 2
D_IN = 2950
RMS_EPS = 1e-6
LN_EPS = 1e-5

SB_CELL = 256
SGU_STOP = 99
SGU_VAR = ''
PS_CELL = 2048


class V:
    __slots__ = ("ap", "cells")

    def __init__(self, ap, cells):
        self.ap = ap
        self.cells = cells

    def m(self, f):
        return V(f(self.ap), self.cells)


class TT:
    def __init__(self, handle, shape, dtype, space, base, cell, tid):
        self.h = handle
        self.shape = list(shape)
        self.dtype = dtype
        self.space = space
        self.base = base
        self.cell = cell
        self.tid = tid
        self._cache = {}
        esz = ESZ[dtype]
        dims = self.shape if space == "D" else self.shape[1:]
        st = []
        acc = esz
        for d in reversed(dims):
            st.append(acc)
            acc *= d
        self.strides = list(reversed(st))
        self.dims = dims
        self.esz = esz

    def __getitem__(self, idx):
        if not isinstance(idx, tuple):
            idx = (idx,)
        key = tuple((i.start, i.stop, i.step) if isinstance(i, slice) else i for i in idx)
        c = self._cache.get(key)
        if c is None:
            c = self._cells(idx)
            self._cache[key] = c
        return V(self.h[idx], c)

    def _cells(self, idx):
        fidx = list(idx) if self.space == "D" else list(idx[1:])
        while len(fidx) < len(self.dims):
            fidx.append(slice(None))
        rngs = []
        for i, d in zip(fidx, self.dims):
            if isinstance(i, slice):
                s, e, stp = i.indices(d)
                rngs.append((s, e, stp))
            else:
                rngs.append((i, i + 1, 1))
        cells = set()
        outer = [range(s, e, stp) for (s, e, stp) in rngs[:-1]]
        ls, le, lstp = rngs[-1]
        last_lo = ls * self.strides[-1]
        last_hi = (ls + ((le - 1 - ls) // lstp) * lstp) * self.strides[-1] + self.esz
        for combo in itertools.product(*outer):
            b = self.base + sum(i * s for i, s in zip(combo, self.strides[:-1]))
            for c in range((b + last_lo) // self.cell, (b + last_hi - 1) // self.cell + 1):
                cells.add((self.tid, c))
        return tuple(cells)


class Op:
    __slots__ = ("eng", "idx", "fn", "deps", "dma_deps", "signal", "sigval", "dma_key", "dma_cnt", "stage")


ENGS = ["pe", "act", "dve", "pool", "sp"]


class Prog:
    def __init__(self, nc):
        self.nc = nc
        self.ops = {e: [] for e in ENGS}
        self.cellstate = {}
        self.dma_cnt = {}
        self.plan = False
        self.stage = ""
        self.ins_stage = {}
        self.n_tid = 0
        self.tts = {}
        self.arena = None
        self.arena_base = 0
        self.sb_ptr = 0
        self.sb_cap = 0
        self.psum_banks = []

    def init_mem(self, sb_bytes):
        self.arena_base = (self.nc.sbuf_base + 63) // 64 * 64
        self.sb_cap = min(sb_bytes, (self.nc.sbuf_top - self.arena_base) // 64 * 64)
        for b in range(8):
            self.psum_banks.append(self.nc.alloc_psum_tensor(f"bank{b}", [128, 512], F32))

    def sb(self, name, shape, dtype, off=None):
        if name in self.tts:
            return self.tts[name]
        n = ESZ[dtype]
        for d in shape[1:]:
            n *= d
        if off is None:
            off = (self.sb_ptr + 63) // 64 * 64
            self.sb_ptr = off + n
        assert off + n <= self.sb_cap, f"SBUF overflow {name}: {off}+{n} > {self.sb_cap}"
        h = self.nc.alloc_sbuf_tensor_at(name, list(shape), dtype, offset=self.arena_base + off)
        t = TT(h, shape, dtype, "S", off, SB_CELL, "S")
        self.tts[name] = t
        return t

    def ps(self, name, bank, shape, dtype, byte_off=0):
        if name in self.tts:
            return self.tts[name]
        n = ESZ[dtype]
        for d in shape[1:]:
            n *= d
        assert byte_off + n <= 2048
        t = PsTT(self.psum_banks[bank], shape, dtype, bank, byte_off)
        self.tts[name] = t
        return t

    def dram(self, name, shape, dtype, kind="Internal", cell=None):
        if name in self.tts:
            return self.tts[name]
        h = self.nc.dram_tensor(name, list(shape), dtype, kind=kind)
        self.n_tid += 1
        t = TT(h, shape, dtype, "D", 0, cell or (1 << 40), f"D{self.n_tid}")
        self.tts[name] = t
        return t

    def op(self, eng, fn, reads=(), writes=(), dma_key=None):
        if self.plan:
            return
        o = Op()
        o.eng = eng
        o.fn = fn
        o.signal = False
        o.sigval = None
        o.dma_key = dma_key
        o.idx = len(self.ops[eng])
        o.stage = self.stage
        deps = {}
        dma_deps = {}

        def add(p, raw):
            if p is None:
                return
            if p.dma_key is not None:
                k = p.dma_key
                dma_deps[k] = self.dma_cnt[k]
                return
            if p.eng == eng and dma_key is None:
                if eng == "pe":
                    return
            if deps.get(p.eng, -1) < p.idx:
                deps[p.eng] = p.idx

        cs = self.cellstate
        for v in reads:
            for c in v.cells:
                st = cs.get(c)
                if st is not None:
                    add(st[0], True)
                    if c[0] == "P":
                        for r in st[1]:
                            if r.eng != eng:
                                add(r, False)
        for v in writes:
            for c in v.cells:
                st = cs.get(c)
                if st is not None:
                    add(st[0], False)
                    for r in st[1]:
                        add(r, False)
        for v in writes:
            for c in v.cells:
                cs[c] = [o, []]
        for v in reads:
            for c in v.cells:
                st = cs.get(c)
                if st is None:
                    cs[c] = [None, [o]]
                elif st[0] is not o:
                    st[1].append(o)
        if dma_key is not None:
            self.dma_cnt[dma_key] = self.dma_cnt.get(dma_key, 0) + 1
            o.dma_cnt = self.dma_cnt[dma_key]
        else:
            o.dma_cnt = 0
        for e, i in deps.items():
            self.ops[e][i].signal = True
        o.deps = deps
        o.dma_deps = dma_deps
        self.ops[eng].append(o)

    def emit(self):
        nc = self.nc
        for e in ENGS:
            cnt = 0
            for o in self.ops[e]:
                if o.signal:
                    cnt += 1
                    o.sigval = cnt
        with ExitStack() as es:
            esem = {e: es.enter_context(nc.semaphore(f"sem_{e}")) for e in ENGS if e != "sp"}
            dsem = {k: es.enter_context(nc.semaphore(f"dma_{k}")) for k in self.dma_cnt}
            block = es.enter_context(nc.Block())

            def run(ename, eng):
                waited = {}
                for o in self.ops[ename]:
                    for pe_, pi in o.deps.items():
                        val = self.ops[pe_][pi].sigval
                        key = ("e", pe_)
                        if waited.get(key, 0) < val:
                            eng.wait_ge(esem[pe_], val)
                            waited[key] = val
                    for k, c in o.dma_deps.items():
                        key = ("d", k)
                        if waited.get(key, 0) < 16 * c:
                            eng.wait_ge(dsem[k], 16 * c)
                            waited[key] = 16 * c
                    ins = o.fn(eng)
                    try:
                        self.ins_stage[ins.ins.name] = (ename, o.stage)
                    except Exception:
                        pass
                    if o.dma_key is not None:
                        ins.then_inc(dsem[o.dma_key], 16)
                    elif o.signal:
                        ins.then_inc(esem[ename], 1)
                if ename == "sp":
                    for k, c in self.dma_cnt.items():
                        if waited.get(("d", k), 0) < 16 * c:
                            eng.wait_ge(dsem[k], 16 * c)

            @block.tensor
            def _(eng):
                run("pe", eng)

            @block.scalar
            def _(eng):
                run("act", eng)

            @block.vector
            def _(eng):
                run("dve", eng)

            @block.gpsimd
            def _(eng):
                run("pool", eng)

            @block.sync
            def _(eng):
                run("sp", eng)


class PsTT(TT):
    def __init__(self, bank_handle, shape, dtype, bank, byte_off):
        n_el = 2048 // ESZ[dtype]
        full = bank_handle[:].bitcast(dtype) if dtype != F32 else bank_handle[:]
        self.full = full
        self.shape = list(shape)
        self.dtype = dtype
        self.space = "P"
        self.base = bank * 2048 + byte_off
        self.cell = PS_CELL
        self.tid = "P"
        self._cache = {}
        esz = ESZ[dtype]
        self.esz = esz
        dims = self.shape[1:]
        st = []
        acc = esz
        for d in reversed(dims):
            st.append(acc)
            acc *= d
        self.strides = list(reversed(st))
        self.dims = dims
        n = 1
        for d in dims:
            n *= d
        e0 = byte_off // esz
        flat = full[:, e0:e0 + n]
        if len(dims) == 1:
            self.view = flat
        elif len(dims) == 2:
            self.view = flat.rearrange("p (a b) -> p a b", b=dims[1])
        elif len(dims) == 3:
            self.view = flat.rearrange("p (a b c) -> p a b c", b=dims[1], c=dims[2])
        else:
            raise ValueError

    def __getitem__(self, idx):
        if not isinstance(idx, tuple):
            idx = (idx,)
        key = tuple((i.start, i.stop, i.step) if isinstance(i, slice) else i for i in idx)
        c = self._cache.get(key)
        if c is None:
            c = self._cells(idx)
            self._cache[key] = c
        return V(self.view[idx], c)


SLOT_ELEMS = NF * 128
NSLOT = 6
PREFETCH = 4


class WStream:
    def __init__(self, prog):
        self.prog = prog
        self.plan_list = []
        self.i_next = 0
        self.i_issued = 0
        self.slots = None

    def reset_for_real(self):
        self.i_next = 0
        self.i_issued = 0

    def _issue(self, i):
        src_fn, nelem = self.plan_list[i]
        s = i % NSLOT
        dst = self.slots[:, s, 0:nelem]
        src = src_fn()
        self.prog.op("sp", lambda e, d=dst, s_=src: e.dma_start(out=d.ap, in_=s_.ap),
                     reads=[src], writes=[dst], dma_key=f"w{s}")

    def next(self, src_fn, nelem):
        if self.prog.plan:
            self.plan_list.append((src_fn, nelem))
            return self.slots[:, 0, 0:nelem], 0
        i = self.i_next
        self.i_next += 1
        while self.i_issued < min(len(self.plan_list), i + 1 + PREFETCH):
            self._issue(self.i_issued)
            self.i_issued += 1
        return self.slots[:, i % NSLOT, 0:nelem], i % NSLOT


def _cols(v):
    v = np.asarray(v, np.float32)
    return np.ascontiguousarray(v.reshape(-1, 128).T)


VEC_LAYOUT = {}


N_CMAT = 15


def build_vecs(inp):
    cols = []
    pos = 0
    VEC_LAYOUT.clear()

    def add(name, arr):
        nonlocal pos
        arr = np.asarray(arr, np.float32)
        VEC_LAYOUT[name] = (pos, arr.shape[1])
        cols.append(arr)
        pos += arr.shape[1]

    def bc(v):
        v = np.asarray(v, np.float32).reshape(1, -1)
        return np.broadcast_to(v, (128, v.shape[1]))

    for l in range(DEPTH):
        add(f"ffn1_norm{l}", _cols(inp["ffn1_norm"][l]))
        add(f"mix_norm{l}", _cols(inp["mix_norm"][l]))
        add(f"ffn2_norm{l}", _cols(inp["ffn2_norm"][l]))
    add("final_norm", _cols(inp["final_norm"]))
    for l in range(DEPTH):
        for j in range(4):
            add(f"conv_w{l}_{j}", _cols(inp["conv_w"][l][j]))
        add(f"conv_b{l}", _cols(inp["conv_b"][l]))
        add(f"ssd_norm{l}", _cols(inp["ssd_norm"][l]))
        add(f"dcol{l}", _cols(np.repeat(np.asarray(inp["d_skip"][l], np.float32), 64)))
        add(f"sgu_ln_g{l}", _cols(inp["sgu_ln_g"][l]))
        add(f"dt_bias{l}", bc(np.tile(np.asarray(inp["dt_bias"][l], np.float32), 4)))
        add(f"a_log{l}", bc(np.tile(np.asarray(inp["a_log"][l], np.float32), 4)))
        add(f"sgu_ln_b{l}", bc(inp["sgu_ln_b"][l]))
    return np.ascontiguousarray(np.concatenate(cols, axis=1))


def n_vec_cols():
    return DEPTH * 3 * NK + NK + DEPTH * (28 + 7 + 3 + 3 + 2 + 24 + 24 + 256)


def build_cmat():
    i = np.arange(128)
    m = []
    m.append(np.eye(128))
    m.append((i[:, None] <= i[None, :]) * 1.0)
    m.append(np.where(i[:, None] <= i[None, :], 0.0, -30000.0))
    m.append((i[None, :] >= i[:, None]) * 1.0)
    m.append((i[None, :] <= i[:, None]) * 1.0)
    for k in range(3):
        for mm in range(3):
            gi = (128 * k + i) // 192
            go = (128 * mm + i) // 192
            m.append((gi[:, None] == go[None, :]) * 1.0)
    m.append((i[None, :] <= i[:, None]) * 1.0)
    return np.ascontiguousarray(np.concatenate(m, axis=1).astype(np.float32))


class Builder:
    def __init__(self, stages=None, dump=None, parts=("att", "ssd", "sgu")):
        self.parts = parts
        self.nc = bass.Bass("TRN2", target_bir_lowering=False, dynamic_dma_scratch_size=64)
        self.P = Prog(self.nc)
        self.ws = WStream(self.P)
        self.stages = stages
        self.dump = dump
        self.rr = 0

    def declare(self):
        P = self.P
        nc = self.nc
        ext = lambda n, s, dt=F32: P.dram(n, s, dt, kind="ExternalInput")
        self.x_in = ext("x", [NSEQ, SEQ, D])
        self.w = {}
        for nm in ("ffn1", "ffn2"):
            self.w[nm + "_w_gate"] = ext(nm + "_w_gate", [DEPTH, D, DFF])
            self.w[nm + "_w_up"] = ext(nm + "_w_up", [DEPTH, D, DFF])
            self.w[nm + "_w_down"] = ext(nm + "_w_down", [DEPTH, DFF, D])
        self.w["w_in"] = ext("w_in", [DEPTH, D, D_IN])
        self.w["w_out"] = ext("w_out", [DEPTH, D, D])
        self.vecs_in = ext("vecs", [128, n_vec_cols()])
        self.out = P.dram("out", [NSEQ, SEQ, D], F32, kind="ExternalOutput")
        self.WguR = P.dram("WguR", [DEPTH * 2, NF, 128, 2 * NK * 128], BF16, cell=128 * 2 * NK * 128 * 2)
        self.WdR = P.dram("WdR", [DEPTH * 2, NK, 128, NF * 128], BF16, cell=128 * NF * 128 * 2)
        self.WinR = P.dram("WinR", [DEPTH, 24, 128, NK * 128], BF16, cell=128 * NK * 128 * 2)
        self.WoutR = P.dram("WoutR", [DEPTH, NK, 128, NK * 128], BF16, cell=128 * NK * 128 * 2)
        self.cmat_in = ext("cmat", [128, N_CMAT * 128])
        self.sgu_w_in = ext("sgu_w", [DEPTH, 4, 128, 128])
        self.sgu_b_in = ext("sgu_b", [DEPTH, 4 * 128])

    def alloc_global(self):
        P = self.P
        P.init_mem(229056)
        self.xT = P.sb("xT", [128, NK, SEQ], F32)
        self.vecs = P.sb("vecs_sb", [128, n_vec_cols()], F32)
        self.cmat = P.sb("cmat_sb", [128, N_CMAT, 128], F32)
        self.ident = self.cmat_view(0)
        self.identbf = P.sb("identbf", [128, 128], BF16)
        self.mask2 = P.sb("mask2", [128, 256], BF16)
        self.ones32 = P.sb("ones32", [128, 128], F32)
        self.selbf = P.sb("selbf", [128, 9, 128], BF16)
        self.ones_bf = P.sb("ones_bf", [128, 128], BF16)
        self.cst = P.sb("cst", [128, 8], F32)
        self.ws.slots = P.sb("wslots", [128, NSLOT, SLOT_ELEMS], BF16)
        self.glob_end = P.sb_ptr

    def cmat_view(self, i):
        class _C:
            def __getitem__(s_, idx):
                if not isinstance(idx, tuple):
                    idx = (idx,)
                return self.cmat[(idx[0], i) + tuple(idx[1:])]
        return _C()

    def phase(self):
        self.P.sb_ptr = self.glob_end
        self.phase_id = getattr(self, "phase_id", 0) + 1
        return f"ph{self.phase_id}_"

    def vcol(self, name, k):
        pos, n = VEC_LAYOUT[name]
        return self.vecs[:, pos + k:pos + k + 1]

    def any_eng(self):
        self.rr += 1
        return ["act", "dve", "pool"][self.rr % 3]

    def load_consts(self):
        P = self.P
        P.op("sp", lambda e: e.dma_start(out=self.vecs[:, :].ap, in_=self.vecs_in[:, :].ap),
             reads=[], writes=[self.vecs[:, :]], dma_key="c0")
        cm = self.cmat[:, :, :]
        P.op("sp", lambda e: e.dma_start(out=cm.ap, in_=self.cmat_in[:, :].m(
            lambda a: a.rearrange("p (c i) -> p c i", i=128)).ap), reads=[], writes=[cm], dma_key="c0")
        P.op("dve", lambda e: e.tensor_copy(out=self.identbf[:, :].ap, in_=self.ident[:, :].ap),
             reads=[self.ident[:, :]], writes=[self.identbf[:, :]])
        P.op("dve", lambda e: e.tensor_copy(out=self.mask2[:, :].ap, in_=self.cmat[:, 3:5, :].m(
            lambda a: a.rearrange("p c i -> p (c i)")).ap), reads=[self.cmat[:, 3:5, :]], writes=[self.mask2[:, :]])
        P.op("pool", lambda e: e.memset(self.ones32[:, :].ap, 1.0), writes=[self.ones32[:, :]])
        P.op("dve", lambda e: e.tensor_copy(out=self.selbf[:, :, :].ap, in_=self.cmat[:, 5:14, :].ap),
             reads=[self.cmat[:, 5:14, :]], writes=[self.selbf[:, :, :]])
        P.op("pool", lambda e: e.memset(self.cst[:, 3:4].ap, 1.0), writes=[self.cst[:, 3:4]])
        P.op("pool", lambda e: e.memset(self.ones_bf[:, :].ap, 1.0), writes=[self.ones_bf[:, :]])
        P.op("pool", lambda e: e.memset(self.cst[:, 0:1].ap, RMS_EPS), writes=[self.cst[:, 0:1]])
        P.op("pool", lambda e: e.memset(self.cst[:, 1:2].ap, LN_EPS), writes=[self.cst[:, 1:2]])
        P.op("pool", lambda e: e.memset(self.cst[:, 2:3].ap, 0.0), writes=[self.cst[:, 2:3]])

    def prepass(self, sl):
        P = self.P
        pre = self.phase()
        CW = 256
        NB_ = 3
        st32 = [P.sb(pre + f"st32_{i}", [128, NF, CW], F32) for i in range(NB_)]
        stbf = [P.sb(pre + f"stbf_{i}", [128, 2, NF, 128], BF16) for i in range(NB_)]
        it = 0

        def one(src_t, l, nk, c0, dst_fn, ncols=CW):
            nonlocal it
            b = it % NB_
            eng = "dve" if it % 2 == 0 else "act"
            it += 1
            s32 = st32[b][:, 0:nk, 0:ncols]
            src = src_t[l, :, c0:c0 + ncols].m(lambda a: a.rearrange("(k p) n -> p k n", p=128))
            P.op("sp", lambda e: e.dma_start(out=s32.ap, in_=src.ap), reads=[], writes=[s32], dma_key=f"pi{b}")
            if ncols == CW:
                sbf = stbf[b][:, :, 0:nk, :]
                s32p = s32.m(lambda a: a.rearrange("p k (j i) -> p j k i", j=2))
            else:
                sbf = stbf[b][:, 0, 0:nk, 0:ncols]
                s32p = s32
            if eng == "act":
                P.op("act", lambda e: e.copy(out=sbf.ap, in_=s32p.ap), reads=[s32], writes=[sbf])
            else:
                P.op(eng, lambda e: e.tensor_copy(out=sbf.ap, in_=s32p.ap), reads=[s32], writes=[sbf])
            dst = dst_fn()
            P.op("act", lambda e: e.dma_start(out=dst.ap, in_=sbf.ap), reads=[sbf], writes=[dst], dma_key=f"po{b}")

        for l in range(DEPTH):
            for fi, nm in enumerate(("ffn1", "ffn2")):
                if (nm, l) not in sl:
                    continue
                idx = l * 2 + fi
                for gu, wn in enumerate(("_w_gate", "_w_up")):
                    for g in range(NF // 2):
                        one(self.w[nm + wn], l, NK, g * CW,
                            lambda idx=idx, g=g, gu=gu: self.WguR[idx, 2 * g:2 * g + 2, :, gu * NK * 128:(gu + 1) * NK * 128]
                            .m(lambda a: a.rearrange("f p (k i) -> p f k i", i=128)))
                for g in range(NK // 2):
                    one(self.w[nm + "_w_down"], l, NF, g * CW,
                        lambda idx=idx, g=g: self.WdR[idx, 2 * g:2 * g + 2, :, :]
                        .m(lambda a: a.rearrange("c p (f i) -> p c f i", i=128)))
        for l in range(DEPTH):
            if ("mix", l) not in sl:
                continue

            def win_dst(l, c, n):
                if n == 2:
                    return lambda: self.WinR[l, c:c + 2, :, :].m(lambda a: a.rearrange("c p (k i) -> p c k i", i=128))
                return None
            for g in range(9):
                one(self.w["w_in"], l, NK, g * CW, win_dst(l, 2 * g, 2))
            one(self.w["w_in"], l, NK, 2304, lambda l=l: self.WinR[l, 18, :, :].m(
                lambda a: a.rearrange("p (k i) -> p k i", i=128)), ncols=128)
            one(self.w["w_in"], l, NK, 2432, lambda l=l: self.WinR[l, 19, :, :].m(
                lambda a: a.rearrange("p (k i) -> p k i", i=128)), ncols=128)
            one(self.w["w_in"], l, NK, 2438, win_dst(l, 20, 2))
            one(self.w["w_in"], l, NK, 2694, win_dst(l, 22, 2))
            for g in range(4):
                one(self.w["w_out"], l, NK, g * CW,
                    lambda l=l, g=g: self.WoutR[l, 2 * g:2 * g + 2, :, :]
                    .m(lambda a: a.rearrange("c p (k i) -> p c k i", i=128)))

    def load_x(self, s):
        P = self.P
        pre = self.phase()
        stg = [P.sb(pre + f"xs{i}", [128, D], F32) for i in range(3)]
        pst = [P.ps(f"tp{i}", i, [128, 4, 128], F32) for i in range(4)]
        n = 0
        for b in range(SEQ // 128):
            sg = stg[b % 3]
            src = self.x_in[s, b * 128:(b + 1) * 128, :]
            P.op("sp", lambda e, sg=sg, src=src: e.dma_start(out=sg[:, :].ap, in_=src.ap),
                 reads=[], writes=[sg[:, :]], dma_key=f"xi{b % 3}")
            for half in range(2):
                pt = pst[n % 4]
                n += 1
                for j in range(4):
                    k = half * 4 + j
                    P.op("pe", lambda e, pt=pt, j=j, k=k, sg=sg: e.transpose(
                        out=pt[:, j, :].ap, in_=sg[:, k * 128:(k + 1) * 128].ap, identity=self.ident[:, :].ap),
                        reads=[sg[:, k * 128:(k + 1) * 128], self.ident[:, :]], writes=[pt[:, j, :]])
                dst = self.xT[:, half * 4:half * 4 + 4, b * 128:(b + 1) * 128]
                if n % 2 == 0:
                    P.op("act", lambda e, pt=pt, dst=dst: e.copy(out=dst.ap, in_=pt[:, :, :].ap),
                         reads=[pt[:, :, :]], writes=[dst])
                else:
                    P.op("dve", lambda e, pt=pt, dst=dst: e.tensor_copy(out=dst.ap, in_=pt[:, :, :].ap),
                         reads=[pt[:, :, :]], writes=[dst])

    def rms_tile(self, pre, t, gname, hT, ss_ps, sq, rstd, dst_fn=None):
        P = self.P
        tl = slice(t * TT_, (t + 1) * TT_)
        for k in range(NK):
            q = sq[k % 2]
            xin = self.xT[:, k, tl]
            P.op("act", lambda e, q=q, xin=xin: e.activation(out=q[:, :].ap, in_=xin.ap, func=AF.Square),
                 reads=[xin], writes=[q[:, :]])
            P.op("pe", lambda e, q=q, k=k: e.matmul(ss_ps[:, :].ap, lhsT=self.ones_bf[:, :].ap, rhs=q[:, :].ap,
                                                    start=(k == 0), stop=(k == NK - 1)),
                 reads=[q[:, :], self.ones_bf[:, :]], writes=[ss_ps[:, :]])
        self.rsqrt_mean(rstd[:, :], ss_ps[:, :], 1.0 / D, RMS_EPS)
        for k in range(NK):
            xin = self.xT[:, k, tl]
            g = self.vcol(gname, k)
            dst = hT[:, k, :] if dst_fn is None else dst_fn(k)
            eng = "dve"
            P.op(eng, lambda e, xin=xin, g=g, dst=dst: e.scalar_tensor_tensor(
                out=dst.ap, in0=xin.ap, scalar=g.ap, in1=rstd[:, :].ap, op0=ALU.mult, op1=ALU.mult),
                reads=[xin, g, rstd[:, :]], writes=[dst])

    def rsqrt_mean(self, dst, src_ps, scale, eps):
        P = self.P
        ec = self.cst[:, 0:1] if eps == RMS_EPS else self.cst[:, 1:2]
        P.op("act", lambda e: e.activation(out=dst.ap, in_=src_ps.ap, func=AF.Ln, bias=ec.ap, scale=scale),
             reads=[src_ps, ec], writes=[dst])
        P.op("act", lambda e: e.activation(out=dst.ap, in_=dst.ap, func=AF.Exp, scale=-0.5),
             reads=[dst], writes=[dst])

    def ffn(self, l, fi):
        P = self.P
        pre = self.phase()
        idx = l * 2 + fi
        gname = f"ffn{fi + 1}_norm{l}"
        hT = [P.sb(pre + f"hT{i}", [128, NK, TT_], BF16) for i in range(2)]
        sq = [P.sb(pre + f"sq{i}", [128, TT_], BF16) for i in range(2)]
        rstd = P.sb(pre + "rstd", [128, TT_], F32)
        sg = [P.sb(pre + f"sg{i}", [128, TT_], F32) for i in range(2)]
        aT = P.sb(pre + "aT", [128, NF, TT_], BF16)
        ss_ps = P.ps("ffn_ss", 0, [128, TT_], F32)
        pg = [P.ps(f"ffn_pg{i}", 1 + i, [128, TT_], F32) for i in range(2)]
        pu = [P.ps(f"ffn_pu{i}", 3 + i, [128, TT_], F32) for i in range(2)]
        py = [P.ps(f"ffn_py{i}", 5 + i, [128, TT_], F32) for i in range(2)]
        self.rms_tile(pre, 0, gname, hT[0], ss_ps, sq, rstd)
        for t in range(NT):
            tl = slice(t * TT_, (t + 1) * TT_)
            h = hT[t % 2]
            for f in range(NF):
                wv, _ = self.ws.next(lambda f=f: self.WguR[idx, f, :, :], 2 * NK * 128)
                g_ps = pg[f % 2]
                u_ps = pu[f % 2]
                for gu, ps_ in ((0, g_ps), (1, u_ps)):
                    for k in range(NK):
                        lw = wv.m(lambda a, gu=gu, k=k: a[:, (gu * NK + k) * 128:(gu * NK + k + 1) * 128])
                        P.op("pe", lambda e, ps_=ps_, lw=lw, h=h, k=k: e.matmul(
                            ps_[:, :].ap, lhsT=lw.ap, rhs=h[:, k, :].ap, start=(k == 0), stop=(k == NK - 1)),
                            reads=[wv, h[:, k, :]], writes=[ps_[:, :]])
                s_ = sg[f % 2]
                P.op("act", lambda e, s_=s_, g_ps=g_ps: e.activation(out=s_[:, :].ap, in_=g_ps[:, :].ap, func=AF.Silu),
                     reads=[g_ps[:, :]], writes=[s_[:, :]])
                dst = aT[:, f, :]
                P.op("dve", lambda e, s_=s_, u_ps=u_ps, dst=dst: e.tensor_tensor(
                    out=dst.ap, in0=u_ps[:, :].ap, in1=s_[:, :].ap, op=ALU.mult),
                    reads=[u_ps[:, :], s_[:, :]], writes=[dst])
            if t + 1 < NT:
                self.rms_tile(pre, t + 1, gname, hT[(t + 1) % 2], ss_ps, sq, rstd)
            for c in range(NK):
                wv, _ = self.ws.next(lambda c=c: self.WdR[idx, c, :, :], NF * 128)
                y_ps = py[c % 2]
                for f in range(NF):
                    lw = wv.m(lambda a, f=f: a[:, f * 128:(f + 1) * 128])
                    P.op("pe", lambda e, y_ps=y_ps, lw=lw, f=f: e.matmul(
                        y_ps[:, :].ap, lhsT=lw.ap, rhs=aT[:, f, :].ap, start=(f == 0), stop=(f == NF - 1)),
                        reads=[wv, aT[:, f, :]], writes=[y_ps[:, :]])
                xin = self.xT[:, c, tl]
                P.op("dve", lambda e, y_ps=y_ps, xin=xin: e.scalar_tensor_tensor(
                    out=xin.ap, in0=y_ps[:, :].ap, scalar=0.5, in1=xin.ap, op0=ALU.mult, op1=ALU.add),
                    reads=[y_ps[:, :], xin], writes=[xin])

    def store_out(self, s, normalize=True):
        P = self.P
        pre = self.phase()
        sq = [P.sb(pre + f"sq{i}", [128, TT_], BF16) for i in range(2)]
        rstd = P.sb(pre + "rstd", [128, TT_], F32)
        yT = [P.sb(pre + f"yT{i}", [128, NK, TT_], F32) for i in range(2)]
        stg = [P.sb(pre + f"os{i}", [128, D], F32) for i in range(3)]
        ss_ps = P.ps("ffn_ss", 0, [128, TT_], F32)
        pst = [P.ps(f"otp{i}", 1 + i, [128, 4, 128], F32) for i in range(4)]
        n = 0
        nb = 0
        for t in range(NT):
            tl = slice(t * TT_, (t + 1) * TT_)
            y = yT[t % 2]
            if normalize:
                for k in range(NK):
                    q = sq[k % 2]
                    xin = self.xT[:, k, tl]
                    P.op("act", lambda e, q=q, xin=xin: e.activation(out=q[:, :].ap, in_=xin.ap, func=AF.Square),
                         reads=[xin], writes=[q[:, :]])
                    P.op("pe", lambda e, q=q, k=k: e.matmul(ss_ps[:, :].ap, lhsT=self.ones_bf[:, :].ap, rhs=q[:, :].ap,
                                                            start=(k == 0), stop=(k == NK - 1)),
                         reads=[q[:, :], self.ones_bf[:, :]], writes=[ss_ps[:, :]])
                self.rsqrt_mean(rstd[:, :], ss_ps[:, :], 1.0 / D, RMS_EPS)
                for k in range(NK):
                    xin = self.xT[:, k, tl]
                    g = self.vcol("final_norm", k)
                    dst = y[:, k, :]
                    eng = "dve"
                    P.op(eng, lambda e, xin=xin, g=g, dst=dst: e.scalar_tensor_tensor(
                        out=dst.ap, in0=xin.ap, scalar=g.ap, in1=rstd[:, :].ap, op0=ALU.mult, op1=ALU.mult),
                        reads=[xin, g, rstd[:, :]], writes=[dst])
            for bb in range(TT_ // 128):
                b = t * (TT_ // 128) + bb
                sg = stg[nb % 3]
                nb += 1
                for half in range(2):
                    pt = pst[n % 4]
                    n += 1
                    for j in range(4):
                        k = half * 4 + j
                        src = (y[:, k, bb * 128:(bb + 1) * 128] if normalize
                               else self.xT[:, k, b * 128:(b + 1) * 128])
                        P.op("pe", lambda e, pt=pt, j=j, src=src: e.transpose(
                            out=pt[:, j, :].ap, in_=src.ap, identity=self.ident[:, :].ap),
                            reads=[src, self.ident[:, :]], writes=[pt[:, j, :]])
                    dst = sg[:, half * 512:(half + 1) * 512]
                    pin = pt[:, :, :].m(lambda a: a.rearrange("p a b -> p (a b)"))
                    if n % 2 == 0:
                        P.op("act", lambda e, pin=pin, dst=dst: e.copy(out=dst.ap, in_=pin.ap),
                             reads=[pin], writes=[dst])
                    else:
                        P.op("dve", lambda e, pin=pin, dst=dst: e.tensor_copy(out=dst.ap, in_=pin.ap),
                             reads=[pin], writes=[dst])
                dsto = self.out[s, b * 128:(b + 1) * 128, :]
                P.op("sp", lambda e, sg=sg, dsto=dsto: e.dma_start(out=dsto.ap, in_=sg[:, :].ap),
                     reads=[sg[:, :]], writes=[], dma_key=f"xo{(nb - 1) % 3}")

    def body(self):
        st = self.stages
        full = [(n, l) for l in range(DEPTH) for n in ("ffn1", "mix", "ffn2")]
        if isinstance(st, list):
            sl = st
        elif st is None:
            sl = full
        else:
            sl = full[:st]
        self.P.stage = "consts"
        self.load_consts()
        self.P.stage = "prepass"
        self.prepass(sl)
        for s in range(NSEQ):
            self.P.stage = f"s{s}.load_x"
            self.load_x(s)
            for name, l in sl:
                self.P.stage = f"s{s}.{name}{l}"
                if name == "ffn1":
                    self.ffn(l, 0)
                elif name == "ffn2":
                    self.ffn(l, 1)
                else:
                    self.mixer(l)
            self.P.stage = f"s{s}.store"
            self.store_out(s, normalize=(st is None))

    def mixer(self, l):
        P = self.P
        pre = self.phase()
        parts = self.parts
        hT = P.sb(pre + "hT", [128, NK, SEQ], BF16)
        ymix = P.sb(pre + "ymix", [128, NK, SEQ], BF16)
        self.mix_base = P.sb_ptr
        sq = [P.sb(pre + f"sq{i}", [128, TT_], BF16) for i in range(2)]
        rstd = P.sb(pre + "rstd", [128, TT_], F32)
        ss_ps = P.ps("ffn_ss", 0, [128, TT_], F32)
        for t in range(NT):
            tl = slice(t * TT_, (t + 1) * TT_)
            self.rms_tile(pre, t, f"mix_norm{l}", None, ss_ps, sq, rstd, dst_fn=lambda k, tl=tl: hT[:, k, tl])
        if len(parts) < 3:
            for k in range(NK):
                P.op("pool", lambda e, k=k: e.memset(ymix[:, k, :].ap, 0.0), writes=[ymix[:, k, :]])
        st0 = P.stage
        if "att" in parts:
            P.stage = st0 + ".att"
            self.attention(l, pre, hT, ymix)
        if "ssd" in parts:
            P.stage = st0 + ".ssd"
            self.ssd(l, pre, hT, ymix)
        if "sgu" in parts:
            P.stage = st0 + ".sgu"
            self.sgu(l, pre, hT, ymix)
        P.stage = st0 + ".wout"
        self.wout(l, ymix)

    def proj_w(self, l, c):
        wv, _ = self.ws.next(lambda: self.WinR[l, c, :, :], NK * 128)
        return wv

    def proj_mm(self, wv, ncols, rhs_fn, ps):
        P = self.P
        for k in range(NK):
            lw = wv.m(lambda a, k=k: a[:, k * 128:k * 128 + ncols])
            r = rhs_fn(k)
            P.op("pe", lambda e, lw=lw, r=r, k=k: e.matmul(ps.ap, lhsT=lw.ap, rhs=r.ap, start=(k == 0),
                                                           stop=(k == NK - 1)),
                 reads=[wv, r], writes=[ps])

    def wout(self, l, ymix):
        P = self.P
        py = [P.ps(f"ffn_py{i}", 5 + i, [128, TT_], F32) for i in range(2)]
        n = 0
        for co in range(NK):
            wv, _ = self.ws.next(lambda co=co: self.WoutR[l, co, :, :], NK * 128)
            for t in range(NT):
                tl = slice(t * TT_, (t + 1) * TT_)
                y_ps = py[n % 2]
                n += 1
                self.proj_mm(wv, 128, lambda k, tl=tl: ymix[:, k, tl], y_ps[:, :])
                xin = self.xT[:, co, tl]
                P.op("dve", lambda e, y_ps=y_ps, xin=xin: e.tensor_tensor(
                    out=xin.ap, in0=y_ps[:, :].ap, in1=xin.ap, op=ALU.add),
                    reads=[y_ps[:, :], xin], writes=[xin])

    def attention(self, l, pre0, hT, ymix):
        P = self.P
        P.sb_ptr = self.mix_base
        pre = pre0 + "att_"
        qT = P.sb(pre + "qT", [128, SEQ], BF16)
        kT = P.sb(pre + "kT", [128, SEQ], BF16)
        vT = P.sb(pre + "vT", [128, SEQ], BF16)
        Vtok = P.sb(pre + "Vtok", [128, 16, 128], BF16)
        PT = [P.sb(pre + f"PT{i}", [128, 256], BF16) for i in range(6)]
        accN = P.sb(pre + "accN", [128, SEQ], F32)
        accD = P.sb(pre + "accD", [128, SEQ], F32)
        ps_proj = [P.ps(f"att_pp{i}", i, [128, 512], F32) for i in range(2)]
        ps_tr = [P.ps(f"att_tr{i}", 0, [128, 4, 128], BF16, byte_off=(i % 2) * 1024) for i in range(2)]
        ps_S = [P.ps(f"att_S{i}", 1 + i, [128, 256], F32) for i in range(3)]
        ps_ON = [P.ps(f"att_ON{i}", 4 + i, [128, 512], F32) for i in range(2)]
        ps_OD = [P.ps(f"att_OD{i}", 6 + i, [128, 512], F32) for i in range(2)]
        cnt = {"pp": 0, "tr": 0, "S": 0, "PT": 0, "ev": 0}
        gbase = 0
        for c in range(3):
            for which, dstT in ((0, qT), (1, kT), (2, vT)):
                wv = self.proj_w(l, 3 * which + c)
                for t in range(NT):
                    tl = slice(t * TT_, (t + 1) * TT_)
                    ps = ps_proj[cnt["pp"] % 2]
                    cnt["pp"] += 1
                    self.proj_mm(wv, 128, lambda k, tl=tl: hT[:, k, tl], ps[:, :])
                    dst = dstT[:, tl]
                    if which == 0:
                        P.op("act", lambda e, ps=ps, dst=dst: e.mul(out=dst.ap, in_=ps[:, :].ap, mul=0.125),
                             reads=[ps[:, :]], writes=[dst])
                    elif which == 1:
                        P.op("dve", lambda e, ps=ps, dst=dst: e.tensor_copy(out=dst.ap, in_=ps[:, :].ap),
                             reads=[ps[:, :]], writes=[dst])
                    else:
                        P.op("act", lambda e, ps=ps, dst=dst: e.copy(out=dst.ap, in_=ps[:, :].ap),
                             reads=[ps[:, :]], writes=[dst])
            for br, d in enumerate((1, 4, 16)):
                nb = 16 // d

                def tokstart(blk):
                    if d == 1:
                        return 128 * blk
                    if d == 4:
                        return (blk // 4) + 512 * (blk % 4)
                    return blk
                for g4 in range(4):
                    pt = ps_tr[cnt["tr"] % 2]
                    cnt["tr"] += 1
                    for j in range(4):
                        st = tokstart(g4 * 4 + j)
                        src = vT[:, st:st + 127 * d + 1:d]
                        P.op("pe", lambda e, pt=pt, j=j, src=src: e.transpose(
                            out=pt[:, j, :].ap, in_=src.ap, identity=self.identbf[:, :].ap),
                            reads=[src, self.identbf[:, :]], writes=[pt[:, j, :]])
                    dst = Vtok[:, g4 * 4:(g4 + 1) * 4, :]
                    if g4 % 2 == 0:
                        P.op("act", lambda e, pt=pt, dst=dst: e.copy(out=dst.ap, in_=pt[:, :, :].ap),
                             reads=[pt[:, :, :]], writes=[dst])
                    else:
                        P.op("dve", lambda e, pt=pt, dst=dst: e.tensor_copy(out=dst.ap, in_=pt[:, :, :].ap),
                             reads=[pt[:, :, :]], writes=[dst])
                its = []
                for r in range(d):
                    for n in range(nb):
                        for h in range(2):
                            its.append((r, n, h))
                LAG = 4
                pend = {}
                for ii in range(len(its) + LAG):
                    if ii < len(its):
                        r, n, h = its[ii]
                        st = r + d * 128 * n
                        nq = 256 if n + 1 < nb else 128
                        kset = slice(st, st + 127 * d + 1, d)
                        qset = slice(st, st + (nq - 1) * d + 1, d)
                        hp = slice(64 * h, 64 * h + 64)
                        S = ps_S[cnt["S"] % len(ps_S)]
                        cnt["S"] += 1
                        kk = kT[hp, kset]
                        qq = qT[hp, qset]
                        Sv = S[:, 0:nq]
                        P.op("pe", lambda e, Sv=Sv, kk=kk, qq=qq: e.matmul(Sv.ap, lhsT=kk.ap, rhs=qq.ap,
                                                                           start=True, stop=True),
                             reads=[kk, qq], writes=[Sv])
                        pt_ = PT[cnt["PT"] % len(PT)]
                        cnt["PT"] += 1
                        pv = pt_[:, 0:nq]
                        P.op("act", lambda e, pv=pv, Sv=Sv: e.activation(out=pv.ap, in_=Sv.ap, func=AF.Exp),
                             reads=[Sv], writes=[pv])
                        mk = self.mask2[:, 0:nq]
                        P.op("pool", lambda e, pv=pv, mk=mk: e.tensor_tensor(out=pv.ap, in0=pv.ap, in1=mk.ap,
                                                                             op=ALU.mult),
                             reads=[pv, mk], writes=[pv])
                        pend[ii] = pt_
                    jj = ii - LAG
                    if jj < 0:
                        continue
                    r, n, h = its[jj]
                    pt_ = pend.pop(jj)
                    blk = n if d == 1 else (r * 4 + n if d == 4 else r)
                    nq = 256 if n + 1 < nb else 128
                    hp = slice(64 * h, 64 * h + 64)
                    vv = Vtok[:, blk, 64 * h:64 * h + 64]
                    on1 = self.ones_bf[:, 0:64]
                    for half in range(nq // 128):
                        qi = blk + half
                        G = gbase + qi // 4
                        pos = qi % 4
                        cs_ = slice(pos * 128, (pos + 1) * 128)
                        rhs = pt_[:, half * 128:(half + 1) * 128]
                        if half == 0:
                            fl = dict(start=(n == 0), stop=True)
                        else:
                            fl = dict(start=True, stop=False)
                        for lhs, dstp in ((vv, ps_ON[G % 2][hp, cs_]), (on1, ps_OD[G % 2][hp, cs_])):
                            P.op("pe", lambda e, lhs=lhs, dstp=dstp, rhs=rhs, fl=fl: e.matmul(
                                dstp.ap, lhsT=lhs.ap, rhs=rhs.ap, skip_group_check=True, **fl),
                                reads=[lhs, rhs], writes=[dstp])
                    if h == 1 and blk % 4 == 3:
                        Gl = blk // 4
                        G = gbase + Gl
                        if d == 1:
                            vw = lambda a, Gl=Gl: a[:, 512 * Gl:512 * Gl + 512]
                            pw = lambda a: a
                        elif d == 4:
                            vw = lambda a, Gl=Gl: a[:, Gl:SEQ:4]
                            pw = lambda a: a
                        else:
                            vw = lambda a, Gl=Gl: a.rearrange("p (i r) -> p r i", r=16)[:, 4 * Gl:4 * Gl + 4, :]
                            pw = lambda a: a.rearrange("p (r i) -> p r i", i=128)
                        for acc, psb, eng0 in ((accN, ps_ON[G % 2], "act"), (accD, ps_OD[G % 2], "dve")):
                            full = acc[:, :]
                            if d == 1:
                                av = acc[:, 512 * Gl:512 * Gl + 512]
                            else:
                                av = V(vw(full.ap), full.cells)
                            pp = psb[:, :]
                            pin = V(pw(pp.ap), pp.cells)
                            if br == 0:
                                if eng0 == "act":
                                    P.op("act", lambda e, av=av, pin=pin: e.copy(out=av.ap, in_=pin.ap),
                                         reads=[pin], writes=[av])
                                else:
                                    P.op("dve", lambda e, av=av, pin=pin: e.tensor_copy(out=av.ap, in_=pin.ap),
                                         reads=[pin], writes=[av])
                            else:
                                P.op("dve", lambda e, av=av, pin=pin: e.tensor_tensor(
                                    out=av.ap, in0=pin.ap, in1=av.ap, op=ALU.add),
                                    reads=[pin, av], writes=[av])
                gbase += 4
            for t in range(NT):
                tl = slice(t * TT_, (t + 1) * TT_)
                P.op("dve", lambda e, tl=tl: e.reciprocal(out=accD[:, tl].ap, in_=accD[:, tl].ap),
                     reads=[accD[:, tl]], writes=[accD[:, tl]])
                dst = ymix[:, c, tl]
                P.op("dve", lambda e, tl=tl, dst=dst: e.tensor_tensor(out=dst.ap, in0=accN[:, tl].ap,
                                                                      in1=accD[:, tl].ap, op=ALU.mult),
                     reads=[accN[:, tl], accD[:, tl]], writes=[dst])

    def sgu(self, l, pre0, hT, ymix):
        P = self.P
        P.sb_ptr = self.mix_base
        pre = pre0 + "sgu_"
        wraw = P.sb(pre + "wraw", [128, 4, 128], F32)
        WcT32 = P.sb(pre + "WcT32", [128, 4, 128], F32)
        WcTb = P.sb(pre + "WcTb", [128, 4, 128], BF16)
        bsrow = P.sb(pre + "bsrow", [128, 512], F32)
        Kt4 = P.sb(pre + "Kt4", [128, 2, 4, 128], F32)
        gu = [P.sb(pre + f"gu{i}", [128, 2, TT_], BF16) for i in range(2)]
        vg = [P.sb(pre + f"vg{i}", [128, 256], F32) for i in range(2)]
        cen = P.sb(pre + "cen", [128, 4, 256], F32)
        sqc = P.sb(pre + "sqc", [128, 256], F32)
        stat = [P.sb(pre + f"stat{i}", [128, 16], F32) for i in range(2)]
        nbf = [P.sb(pre + f"nbf{i}", [128, 256], BF16) for i in range(2)]
        tmp = P.sb(pre + "tmp", [128, TT_], F32)
        pp = [P.ps(f"att_pp{i}", i, [128, 512], F32) for i in range(2)]
        vps = [P.ps(f"sgu_v{i}", 2, [128, 256], F32, byte_off=i * 1024) for i in range(2)]
        mps = [P.ps(f"sgu_m{i}", 3 + i, [128, 512], F32) for i in range(2)]
        trps = P.ps("sgu_tr", 5, [128, 128], F32)
        kps = P.ps("sgu_k", 5, [128, 128], F32, byte_off=1024)
        tril = self.cmat_view(14)
        lbpos = VEC_LAYOUT[f"sgu_ln_b{l}"][0]
        P.op("sp", lambda e: e.dma_start(out=wraw[:, :, :].ap, in_=self.sgu_w_in[l, :, :, :].m(
            lambda a: a.rearrange("g t s -> t g s")).ap), reads=[], writes=[wraw[:, :, :]], dma_key="sg")
        P.op("sp", lambda e: e.dma_start(out=bsrow[0:1, :].ap, in_=self.sgu_b_in[l:l + 1, :].ap),
             reads=[], writes=[bsrow[0:1, :]], dma_key="sg")
        if SGU_STOP <= 1:
            return
        for g in range(4):
            w_ = wraw[:, g, :]
            if "nomask" not in SGU_VAR:
                P.op("dve", lambda e, w_=w_: e.tensor_tensor(out=w_.ap, in0=w_.ap, in1=tril[:, :].ap, op=ALU.mult),
                     reads=[w_, tril[:, :]], writes=[w_])
            if "notr" in SGU_VAR:
                continue
            P.op("pe", lambda e, w_=w_: e.transpose(out=trps[:, :].ap, in_=w_.ap, identity=self.ident[:, :].ap),
                 reads=[w_, self.ident[:, :]], writes=[trps[:, :]])
            P.op("act", lambda e, g=g: e.copy(out=WcT32[:, g, :].ap, in_=trps[:, :].ap),
                 reads=[trps[:, :]], writes=[WcT32[:, g, :]])
            P.op("dve", lambda e, g=g: e.tensor_copy(out=WcTb[:, g, :].ap, in_=WcT32[:, g, :].ap),
                 reads=[WcT32[:, g, :]], writes=[WcTb[:, g, :]])
        if SGU_STOP <= 2:
            return
        for cc in range(2):
            for gg in range(2):
                g = 2 * cc + gg
                kp = kps[64 * gg:64 * gg + 64, :]
                lb = self.vecs[:, lbpos + g * 64:lbpos + (g + 1) * 64]
                P.op("pe", lambda e, kp=kp, lb=lb, g=g: e.matmul(kp.ap, lhsT=lb.ap, rhs=WcT32[:, g, :].ap,
                                                                 start=True, stop=False, skip_group_check=True),
                     reads=[lb, WcT32[:, g, :]], writes=[kp])
                on = self.ones32[0:1, 0:64]
                br_ = bsrow[0:1, g * 128:(g + 1) * 128]
                P.op("pe", lambda e, kp=kp, on=on, br_=br_: e.matmul(kp.ap, lhsT=on.ap, rhs=br_.ap,
                                                                     start=False, stop=True, skip_group_check=True),
                     reads=[on, br_], writes=[kp])
            for b in range(4):
                dst = Kt4[:, cc, b, :]
                if b % 2 == 0:
                    P.op("act", lambda e, dst=dst: e.copy(out=dst.ap, in_=kps[:, :].ap), reads=[kps[:, :]], writes=[dst])
                else:
                    P.op("dve", lambda e, dst=dst: e.tensor_copy(out=dst.ap, in_=kps[:, :].ap),
                         reads=[kps[:, :]], writes=[dst])
        nb_ = 0
        if SGU_STOP <= 3:
            return
        for t in range(NT):
            tl = slice(t * TT_, (t + 1) * TT_)
            gut = gu[t % 2]
            st_ = stat[t % 2]
            for cc in range(2):
                wv = self.proj_w(l, 20 + cc)
                ps = pp[cc]
                self.proj_mm(wv, 128, lambda k, tl=tl: hT[:, k, tl], ps[:, :])
                P.op("act", lambda e, ps=ps, cc=cc, gut=gut: e.activation(out=gut[:, cc, :].ap, in_=ps[:, :].ap,
                                                                          func=AF.Gelu),
                     reads=[ps[:, :]], writes=[gut[:, cc, :]])
            if SGU_STOP <= 4:
                continue
            w0 = self.proj_w(l, 22)
            w1 = self.proj_w(l, 23)
            for b in range(4):
                bl = slice(t * TT_ + b * 128, t * TT_ + (b + 1) * 128)
                vp = vps[b % 2]
                for half, w in ((0, w0), (1, w1)):
                    vph = vp[:, half * 128:(half + 1) * 128]
                    for k in range(NK):
                        hk = hT[:, k, bl]
                        wk = w.m(lambda a, k=k: a[:, k * 128:(k + 1) * 128])
                        P.op("pe", lambda e, vph=vph, hk=hk, wk=wk, k=k: e.matmul(
                            vph.ap, lhsT=hk.ap, rhs=wk.ap, start=(k == 0), stop=(k == NK - 1)),
                            reads=[hk, w], writes=[vph])
                v_ = vg[b % 2]
                P.op("act", lambda e, v_=v_, vp=vp: e.activation(out=v_[:, :].ap, in_=vp[:, :].ap, func=AF.Gelu),
                     reads=[vp[:, :]], writes=[v_[:, :]])
                sm = st_[:, b:b + 1]
                nm = st_[:, 4 + b:5 + b]
                vs = st_[:, 8 + b:9 + b]
                P.op("dve", lambda e, v_=v_, sm=sm: e.reduce_sum(out=sm.ap, in_=v_[:, :].ap, axis=mybir.AxisListType.X),
                     reads=[v_[:, :]], writes=[sm])
                P.op("dve", lambda e, sm=sm, nm=nm: e.tensor_scalar(out=nm.ap, in0=sm.ap, scalar1=-1.0 / 256,
                                                                     scalar2=None, op0=ALU.mult),
                     reads=[sm], writes=[nm])
                cb = cen[:, b, :]
                P.op("dve", lambda e, cb=cb, v_=v_, nm=nm: e.tensor_scalar(out=cb.ap, in0=v_[:, :].ap, scalar1=nm.ap,
                                                                           scalar2=None, op0=ALU.add),
                     reads=[v_[:, :], nm], writes=[cb])
                P.op("pool", lambda e, cb=cb: e.tensor_tensor(out=sqc[:, :].ap, in0=cb.ap, in1=cb.ap, op=ALU.mult),
                     reads=[cb], writes=[sqc[:, :]])
                P.op("dve", lambda e, vs=vs: e.reduce_sum(out=vs.ap, in_=sqc[:, :].ap, axis=mybir.AxisListType.X),
                     reads=[sqc[:, :]], writes=[vs])
            if SGU_STOP <= 5:
                continue
            rs = st_[:, 12:16]
            ec = self.cst[:, 1:2]
            P.op("act", lambda e, rs=rs, st_=st_, ec=ec: e.activation(out=rs.ap, in_=st_[:, 8:12].ap, func=AF.Sqrt,
                                                                      bias=ec.ap, scale=1.0 / 256),
                 reads=[st_[:, 8:12], ec], writes=[rs])
            P.op("dve", lambda e, rs=rs: e.reciprocal(out=rs.ap, in_=rs.ap), reads=[rs], writes=[rs])
            if SGU_STOP <= 6:
                continue
            for b in range(4):
                n_ = nbf[nb_ % 2]
                nb_ += 1
                cb = cen[:, b, :]
                rb = st_[:, 12 + b:13 + b]
                P.op("dve", lambda e, n_=n_, cb=cb, rb=rb: e.tensor_scalar(out=n_[:, :].ap, in0=cb.ap, scalar1=rb.ap,
                                                                           scalar2=None, op0=ALU.mult),
                     reads=[cb, rb], writes=[n_[:, :]])
                for g in range(4):
                    mp = mps[g // 2][64 * (g % 2):64 * (g % 2) + 64, b * 128:(b + 1) * 128]
                    ng = n_[:, g * 64:(g + 1) * 64]
                    P.op("pe", lambda e, mp=mp, ng=ng, g=g: e.matmul(mp.ap, lhsT=ng.ap, rhs=WcTb[:, g, :].ap,
                                                                     start=True, stop=True, skip_group_check=True),
                         reads=[ng, WcTb[:, g, :]], writes=[mp])
            for cc in range(2):
                gcol = self.vcol(f"sgu_ln_g{l}", cc)
                k4 = Kt4[:, cc, :, :].m(lambda a: a.rearrange("p b i -> p (b i)"))
                P.op("dve", lambda e, cc=cc, gcol=gcol, k4=k4: e.scalar_tensor_tensor(
                    out=tmp[:, :].ap, in0=mps[cc][:, :].ap, scalar=gcol.ap, in1=k4.ap, op0=ALU.mult, op1=ALU.add),
                    reads=[mps[cc][:, :], gcol, k4], writes=[tmp[:, :]])
                dst = ymix[:, 6 + cc, tl]
                P.op("dve", lambda e, cc=cc, dst=dst, gut=gut: e.tensor_tensor(
                    out=dst.ap, in0=tmp[:, :].ap, in1=gut[:, cc, :].ap, op=ALU.mult),
                    reads=[tmp[:, :], gut[:, cc, :]], writes=[dst])

    def ssd(self, l, pre0, hT, ymix):
        P = self.P
        P.sb_ptr = self.mix_base
        pre = pre0 + "ssd_"
        zs = P.sb(pre + "zs", [128, 3, TT_], BF16)
        stgb = [P.sb(pre + f"stg{i}", [128, TT_ + 3], F32) for i in range(2)]
        halo = P.sb(pre + "halo", [128, 7, 4], F32)
        cacc = [P.sb(pre + f"cacc{i}", [128, TT_], F32) for i in range(2)]
        xact = P.sb(pre + "xact", [128, 7, TT_], BF16)
        negA = P.sb(pre + "negA", [128, 24], F32)
        sm = [{n: P.sb(pre + f"{n}{i}", [128, 24], F32) for n in ("t1", "dt", "a", "acs", "last", "dte", "dA", "dtd")}
              for i in range(2)]
        NBUF = 3
        arep = [P.sb(pre + f"arep{i}", [128, 128], F32) for i in range(NBUF)]
        tmpL = [P.sb(pre + f"tmpL{i}", [128, 128], F32) for i in range(NBUF)]
        LT = [P.sb(pre + f"LT{i}", [128, 128], F32) for i in range(NBUF)]
        Eb = [P.sb(pre + f"E{i}", [128, 128], BF16) for i in range(NBUF)]
        MT = [P.sb(pre + f"MT{i}", [128, 128], BF16) for i in range(NBUF)]
        CsT = [P.sb(pre + f"CsT{i}", [128, 128], BF16) for i in range(NBUF)]
        Xs = [P.sb(pre + f"X{i}", [128, 384], BF16) for i in range(2)]
        Xd = [P.sb(pre + f"Xd{i}", [128, 384], BF16) for i in range(2)]
        Btok = [P.sb(pre + f"Btok{i}", [128, 2, 128], BF16) for i in range(2)]
        H32 = P.sb(pre + "H32", [128, 6, 64], F32)
        Hbf = P.sb(pre + "Hbf", [128, 6, 64], BF16)
        ycat = cacc[0]
        rst = cacc[1]
        yg = P.sb(pre + "yg", [128, 3, TT_], F32)
        sqg = P.sb(pre + "sqg", [128, 3, TT_], BF16)
        pp = [P.ps(f"att_pp{i}", i, [128, 512], F32) for i in range(2)]
        yps = [P.ps(f"ssd_y{i}", 2 + i, [128, 512], F32) for i in range(3)]
        BCp = [P.ps(f"ssd_bc{i}", b_, [128, 128], F32) for i, b_ in enumerate((0, 1, 7))]
        GTp = [P.ps(f"ssd_gt{i}", 5, [128, 128], F32, byte_off=1024 * i) for i in range(2)]
        trp = [P.ps(f"ssd_tr{i}", 6, [128, 128], BF16, byte_off=256 * i) for i in range(4)]
        Hps = [P.ps(f"ssd_h{i}", 6, [128, 64], F32, byte_off=1024 + 256 * i) for i in range(2)]
        dtp = P.ps("ssd_dt", 6, [128, 24], F32, byte_off=1536)
        acp = P.ps("ssd_ac", 6, [128, 24], F32, byte_off=1664)
        lap = P.ps("ssd_la", 6, [128, 24], F32, byte_off=1792)
        ssp = P.ps("ssd_ss", 7, [128, 512], F32)
        tri = self.cmat_view(1)
        negm = self.cmat_view(2)
        one_c = self.cst[:, 3:4]
        p0 = VEC_LAYOUT[f"dt_bias{l}"][0]
        dtb = self.vecs[:, p0:p0 + 24]
        p1 = VEC_LAYOUT[f"a_log{l}"][0]
        alog = self.vecs[:, p1:p1 + 24]
        P.op("act", lambda e: e.activation(out=negA[:, :].ap, in_=alog.ap, func=AF.Exp), reads=[alog], writes=[negA[:, :]])
        P.op("dve", lambda e: e.tensor_scalar(out=negA[:, :].ap, in0=negA[:, :].ap, scalar1=-1.0, scalar2=None,
                                              op0=ALU.mult), reads=[negA[:, :]], writes=[negA[:, :]])
        P.op("pool", lambda e: e.memset(H32[:, :, :].ap, 0.0), writes=[H32[:, :, :]])
        P.op("pool", lambda e: e.memset(Hbf[:, :, :].ap, 0.0), writes=[Hbf[:, :, :]])
        P.op("pool", lambda e: e.memset(halo[:, :, :].ap, 0.0), writes=[halo[:, :, :]])
        cn = {"pp": 0, "tr": 0, "ch": 0, "hd": 0, "hp": 0}
        for t in range(NT):
            tl = slice(t * TT_, (t + 1) * TT_)
            for c in range(3):
                wv = self.proj_w(l, 9 + c)
                ps = pp[cn["pp"] % 2]
                cn["pp"] += 1
                self.proj_mm(wv, 128, lambda k, tl=tl: hT[:, k, tl], ps[:, :])
                P.op("act", lambda e, ps=ps, c=c: e.activation(out=zs[:, c, :].ap, in_=ps[:, :].ap, func=AF.Silu),
                     reads=[ps[:, :]], writes=[zs[:, c, :]])
            for c in range(7):
                wv = self.proj_w(l, 12 + c)
                ps = pp[cn["pp"] % 2]
                cn["pp"] += 1
                self.proj_mm(wv, 128, lambda k, tl=tl: hT[:, k, tl], ps[:, :])
                stg = stgb[c % 2]
                sg_ = stg[:, 3:TT_ + 3]
                P.op("pool", lambda e, stg=stg, c=c: e.tensor_copy(out=stg[:, 0:3].ap, in_=halo[:, c, 0:3].ap),
                     reads=[halo[:, c, 0:3]], writes=[stg[:, 0:3]])
                P.op("act", lambda e, ps=ps, sg_=sg_: e.copy(out=sg_.ap, in_=ps[:, :].ap), reads=[ps[:, :]], writes=[sg_])
                ca = cacc[c % 2]
                for j in range(4):
                    wj = self.vcol(f"conv_w{l}_{j}", c)
                    sj = stg[:, j:j + TT_]
                    if j == 0:
                        P.op("dve", lambda e, ca=ca, sj=sj, wj=wj: e.tensor_scalar(
                            out=ca[:, :].ap, in0=sj.ap, scalar1=wj.ap, scalar2=None, op0=ALU.mult),
                            reads=[sj, wj], writes=[ca[:, :]])
                    else:
                        P.op("dve", lambda e, ca=ca, sj=sj, wj=wj: e.scalar_tensor_tensor(
                            out=ca[:, :].ap, in0=sj.ap, scalar=wj.ap, in1=ca[:, :].ap, op0=ALU.mult, op1=ALU.add),
                            reads=[sj, wj, ca[:, :]], writes=[ca[:, :]])
                cb_ = self.vcol(f"conv_b{l}", c)
                P.op("act", lambda e, ca=ca, c=c, cb_=cb_: e.activation(out=xact[:, c, :].ap, in_=ca[:, :].ap,
                                                                        func=AF.Silu, bias=cb_.ap, scale=1.0),
                     reads=[ca[:, :], cb_], writes=[xact[:, c, :]])
                P.op("pool", lambda e, c=c, stg=stg: e.tensor_copy(out=halo[:, c, 0:3].ap, in_=stg[:, TT_:TT_ + 3].ap),
                     reads=[stg[:, TT_:TT_ + 3]], writes=[halo[:, c, 0:3]])
            wdt = self.proj_w(l, 19)
            S_ = sm[t % 2]
            for ch in range(4):
                tok = slice(t * TT_ + ch * 128, t * TT_ + (ch + 1) * 128)
                dpc = dtp[:, ch * 6:(ch + 1) * 6]
                for k in range(NK):
                    hk = hT[:, k, tok]
                    wk = wdt.m(lambda a, k=k: a[:, k * 128:k * 128 + 6])
                    P.op("pe", lambda e, hk=hk, wk=wk, k=k, dpc=dpc: e.matmul(dpc.ap, lhsT=hk.ap, rhs=wk.ap,
                                                                              start=(k == 0), stop=(k == NK - 1)),
                         reads=[hk, wdt], writes=[dpc])
            t1, dt, a_, acs, last, dte, dA, dtd = (S_[n][:, :] for n in ("t1", "dt", "a", "acs", "last", "dte", "dA", "dtd"))
            P.op("dve", lambda e, t1=t1: e.tensor_tensor(out=t1.ap, in0=dtp[:, :].ap, in1=dtb.ap, op=ALU.add),
                 reads=[dtp[:, :], dtb], writes=[t1])
            P.op("act", lambda e, t1=t1: e.activation(out=t1.ap, in_=t1.ap, func=AF.Exp), reads=[t1], writes=[t1])
            P.op("act", lambda e, t1=t1, dt=dt: e.activation(out=dt.ap, in_=t1.ap, func=AF.Ln, bias=one_c.ap, scale=1.0),
                 reads=[t1, one_c], writes=[dt])
            P.op("dve", lambda e, a_=a_, dt=dt: e.tensor_tensor(out=a_.ap, in0=dt.ap, in1=negA[:, :].ap, op=ALU.mult),
                 reads=[dt, negA[:, :]], writes=[a_])
            P.op("pe", lambda e, a_=a_: e.matmul(acp[:, :].ap, lhsT=tri[:, :].ap, rhs=a_.ap, start=True, stop=True),
                 reads=[tri[:, :], a_], writes=[acp[:, :]])
            P.op("pe", lambda e, a_=a_: e.matmul(lap[:, :].ap, lhsT=self.ones32[:, :].ap, rhs=a_.ap, start=True, stop=True),
                 reads=[self.ones32[:, :], a_], writes=[lap[:, :]])
            P.op("dve", lambda e, acs=acs: e.tensor_copy(out=acs.ap, in_=acp[:, :].ap), reads=[acp[:, :]], writes=[acs])
            P.op("dve", lambda e, last=last: e.tensor_copy(out=last.ap, in_=lap[:, :].ap), reads=[lap[:, :]], writes=[last])
            P.op("dve", lambda e, dte=dte, last=last, acs=acs: e.tensor_tensor(out=dte.ap, in0=last.ap, in1=acs.ap,
                                                                              op=ALU.subtract),
                 reads=[last, acs], writes=[dte])
            P.op("act", lambda e, dte=dte: e.activation(out=dte.ap, in_=dte.ap, func=AF.Exp), reads=[dte], writes=[dte])
            P.op("act", lambda e, dA=dA, last=last: e.activation(out=dA.ap, in_=last.ap, func=AF.Exp),
                 reads=[last], writes=[dA])
            P.op("dve", lambda e, dtd=dtd, dt=dt, dte=dte: e.tensor_tensor(out=dtd.ap, in0=dt.ap, in1=dte.ap, op=ALU.mult),
                 reads=[dt, dte], writes=[dtd])
            for ch in range(4):
                lt = slice(ch * 128, (ch + 1) * 128)
                X = Xs[cn["ch"] % 2]
                XD = Xd[cn["ch"] % 2]
                BT = Btok[cn["ch"] % 2]
                cn["ch"] += 1
                for c in range(3):
                    tp = trp[cn["tr"] % 4]
                    cn["tr"] += 1
                    src = xact[:, c, lt]
                    P.op("pe", lambda e, tp=tp, src=src: e.transpose(out=tp[:, :].ap, in_=src.ap,
                                                                     identity=self.identbf[:, :].ap),
                         reads=[src, self.identbf[:, :]], writes=[tp[:, :]])
                    for hh in range(2):
                        h = 2 * c + hh
                        xh = X[:, h * 64:(h + 1) * 64]
                        xdh = XD[:, h * 64:(h + 1) * 64]
                        tph = tp[:, hh * 64:(hh + 1) * 64]
                        dth = S_["dt"][:, ch * 6 + h:ch * 6 + h + 1]
                        ddh = S_["dtd"][:, ch * 6 + h:ch * 6 + h + 1]
                        P.op("dve", lambda e, xh=xh, tph=tph, dth=dth: e.tensor_scalar(
                            out=xh.ap, in0=tph.ap, scalar1=dth.ap, scalar2=None, op0=ALU.mult),
                            reads=[tph, dth], writes=[xh])
                        P.op("dve", lambda e, xdh=xdh, tph=tph, ddh=ddh: e.tensor_scalar(
                            out=xdh.ap, in0=tph.ap, scalar1=ddh.ap, scalar2=None, op0=ALU.mult),
                            reads=[tph, ddh], writes=[xdh])
                for g in range(2):
                    tp = trp[cn["tr"] % 4]
                    cn["tr"] += 1
                    src = xact[:, 3 + g, lt]
                    P.op("pe", lambda e, tp=tp, src=src: e.transpose(out=tp[:, :].ap, in_=src.ap,
                                                                     identity=self.identbf[:, :].ap),
                         reads=[src, self.identbf[:, :]], writes=[tp[:, :]])
                    P.op("act", lambda e, tp=tp, g=g, BT=BT: e.copy(out=BT[:, g, :].ap, in_=tp[:, :].ap),
                         reads=[tp[:, :]], writes=[BT[:, g, :]])
                for g in range(2):
                    gt = GTp[g]
                    bT = xact[:, 3 + g, lt]
                    cT = xact[:, 5 + g, lt]
                    P.op("pe", lambda e, gt=gt, bT=bT, cT=cT: e.matmul(gt[:, :].ap, lhsT=bT.ap, rhs=cT.ap,
                                                                       start=True, stop=True),
                         reads=[bT, cT], writes=[gt[:, :]])
                LAG = 2
                bufs = {}
                for ii in range(6 + LAG):
                    if ii < 6:
                        h = ii
                        g = h // 3
                        gt = GTp[g]
                        cT = xact[:, 5 + g, lt]
                        i3 = cn["hd"] % NBUF
                        cn["hd"] += 1
                        ar, tL, L_, E_, M_, C_ = arep[i3], tmpL[i3], LT[i3], Eb[i3], MT[i3], CsT[i3]
                        bc = BCp[i3]
                        ah = S_["a"][:, ch * 6 + h:ch * 6 + h + 1]
                        ach = S_["acs"][:, ch * 6 + h:ch * 6 + h + 1]
                        P.op("act", lambda e, ar=ar, ah=ah: e.activation(out=ar[:, :].ap, in_=self.ones32[:, :].ap,
                                                                         func=AF.Copy, scale=ah.ap),
                             reads=[self.ones32[:, :], ah], writes=[ar[:, :]])
                        P.op("pe", lambda e, bc=bc, ar=ar: e.matmul(bc[:, :].ap, lhsT=ar[:, :].ap, rhs=tri[:, :].ap,
                                                                    start=True, stop=True),
                             reads=[ar[:, :], tri[:, :]], writes=[bc[:, :]])
                        P.op("dve", lambda e, tL=tL, bc=bc, ach=ach: e.scalar_tensor_tensor(
                            out=tL[:, :].ap, in0=bc[:, :].ap, scalar=ach.ap, in1=negm[:, :].ap,
                            op0=ALU.subtract, op1=ALU.add),
                            reads=[bc[:, :], ach, negm[:, :]], writes=[tL[:, :]])
                        P.op("act", lambda e, E_=E_, bc=bc: e.activation(out=E_[:, :].ap, in_=bc[:, :].ap, func=AF.Exp),
                             reads=[bc[:, :]], writes=[E_[:, :]])
                        P.op("act", lambda e, L_=L_, tL=tL: e.activation(out=L_[:, :].ap, in_=tL[:, :].ap, func=AF.Exp),
                             reads=[tL[:, :]], writes=[L_[:, :]])
                        P.op("dve", lambda e, M_=M_, gt=gt, L_=L_: e.tensor_tensor(out=M_[:, :].ap, in0=gt[:, :].ap,
                                                                                   in1=L_[:, :].ap, op=ALU.mult),
                             reads=[gt[:, :], L_[:, :]], writes=[M_[:, :]])
                        P.op("pool", lambda e, C_=C_, cT=cT, E_=E_: e.tensor_tensor(out=C_[:, :].ap, in0=cT.ap,
                                                                                    in1=E_[:, :].ap, op=ALU.mult),
                             reads=[cT, E_[:, :]], writes=[C_[:, :]])
                        bufs[ii] = (M_, C_)
                    jj = ii - LAG
                    if jj < 0:
                        continue
                    h = jj
                    g = h // 3
                    c, hh = h // 2, h % 2
                    M_, C_ = bufs.pop(jj)
                    dah = S_["dA"][:, ch * 6 + h:ch * 6 + h + 1]
                    yp = yps[c][64 * hh:64 * hh + 64, lt]
                    xh = X[:, h * 64:(h + 1) * 64]
                    xdh = XD[:, h * 64:(h + 1) * 64]
                    hb = Hbf[:, h, :]
                    P.op("pe", lambda e, yp=yp, xh=xh, M_=M_: e.matmul(yp.ap, lhsT=xh.ap, rhs=M_[:, :].ap, start=True,
                                                                       stop=False, skip_group_check=True),
                         reads=[xh, M_[:, :]], writes=[yp])
                    P.op("pe", lambda e, yp=yp, hb=hb, C_=C_: e.matmul(yp.ap, lhsT=hb.ap, rhs=C_[:, :].ap, start=False,
                                                                       stop=True, skip_group_check=True),
                         reads=[hb, C_[:, :]], writes=[yp])
                    hp_ = Hps[cn["hp"] % 2]
                    cn["hp"] += 1
                    P.op("pe", lambda e, hp_=hp_, BT=BT, g=g, xdh=xdh: e.matmul(hp_[:, :].ap, lhsT=BT[:, g, :].ap,
                                                                                rhs=xdh.ap, start=True, stop=True),
                         reads=[BT[:, g, :], xdh], writes=[hp_[:, :]])
                    h32 = H32[:, h, :]
                    P.op("dve", lambda e, h32=h32, dah=dah, hp_=hp_: e.scalar_tensor_tensor(
                        out=h32.ap, in0=h32.ap, scalar=dah.ap, in1=hp_[:, :].ap, op0=ALU.mult, op1=ALU.add),
                        reads=[h32, dah, hp_[:, :]], writes=[h32])
                    P.op("pool", lambda e, hb=hb, h32=h32: e.tensor_copy(out=hb.ap, in_=h32.ap),
                         reads=[h32], writes=[hb])
            for c in range(3):
                dc = self.vcol(f"dcol{l}", c)
                P.op("dve", lambda e, c=c, dc=dc: e.scalar_tensor_tensor(
                    out=ycat[:, :].ap, in0=xact[:, c, :].ap, scalar=dc.ap, in1=yps[c][:, :].ap, op0=ALU.mult, op1=ALU.add),
                    reads=[xact[:, c, :], dc, yps[c][:, :]], writes=[ycat[:, :]])
                P.op("dve", lambda e, c=c: e.tensor_tensor(out=yg[:, c, :].ap, in0=ycat[:, :].ap, in1=zs[:, c, :].ap,
                                                           op=ALU.mult),
                     reads=[ycat[:, :], zs[:, c, :]], writes=[yg[:, c, :]])
                P.op("act", lambda e, c=c: e.activation(out=sqg[:, c, :].ap, in_=yg[:, c, :].ap, func=AF.Square),
                     reads=[yg[:, c, :]], writes=[sqg[:, c, :]])
            for m in range(3):
                ks = [k for k in range(3) if abs(k - m) <= 1]
                for i, k in enumerate(ks):
                    sel = self.selbf[:, 3 * k + m, :]
                    P.op("pe", lambda e, sel=sel, k=k, i=i, ks=ks: e.matmul(ssp[:, :].ap, lhsT=sel.ap, rhs=sqg[:, k, :].ap,
                                                                           start=(i == 0), stop=(i == len(ks) - 1)),
                         reads=[sel, sqg[:, k, :]], writes=[ssp[:, :]])
                self.rsqrt_mean(rst[:, :], ssp[:, :], 1.0 / 192, RMS_EPS)
                gcol = self.vcol(f"ssd_norm{l}", m)
                dst = ymix[:, 3 + m, tl]
                P.op("dve", lambda e, m=m, gcol=gcol, dst=dst: e.scalar_tensor_tensor(
                    out=dst.ap, in0=yg[:, m, :].ap, scalar=gcol.ap, in1=rst[:, :].ap, op0=ALU.mult, op1=ALU.mult),
                    reads=[yg[:, m, :], gcol, rst[:, :]], writes=[dst])

    def build(self):
        self.declare()
        self.alloc_global()
        self.P.plan = True
        self.body()
        self.P.plan = False
        self.ws.reset_for_real()
        self.phase_id = 0
        self.rr = 0
        self.body()
        self.P.emit()
        return self.nc


def make_in_maps(inp):
    vecs = build_vecs(inp)
    x = np.ascontiguousarray(np.asarray(inp["x"], np.float32))
    shared = {}
    for nm in ("ffn1", "ffn2"):
        for wn in ("_w_gate", "_w_up", "_w_down"):
            shared[nm + wn] = np.ascontiguousarray(np.asarray(inp[nm + wn], np.float32))
    shared["w_in"] = np.ascontiguousarray(np.asarray(inp["w_in"], np.float32))
    shared["w_out"] = np.ascontiguousarray(np.asarray(inp["w_out"], np.float32))
    shared["vecs"] = vecs
    shared["cmat"] = build_cmat()
    shared["sgu_w"] = np.ascontiguousarray(np.asarray(inp["sgu_w"], np.float32))
    shared["sgu_b"] = np.ascontiguousarray(np.asarray(inp["sgu_b"], np.float32).reshape(DEPTH, 4 * 128))
    maps = []
    for c in range(NCORES):
        m = dict(shared)
        m["x"] = x[c * NSEQ:(c + 1) * NSEQ]
        maps.append(m)
    return maps


LAST_BUILDER = None


def run(inp, stages=None, trace=False, parts=("att", "ssd", "sgu")):
    global LAST_BUILDER
    b = Builder(stages=stages, parts=parts)
    LAST_BUILDER = b
    maps = make_in_maps(inp)
    nc = b.build()
    res = run_bass_kernel_spmd(nc, maps, core_ids=list(range(NCORES)), trace=trace)
    out = np.concatenate([np.asarray(r["out"]) for r in res.results], axis=0)
    return out.astype(np.float32), res


def kernel(**inputs):
    out, _ = run(inputs)
    return out
```
